# Optimizing a Trainium2 kernel written in Bass

```python
import jax, jax.numpy as jnp
from jax import lax
import numpy as np

D_MODEL = 1024
BATCH = 2
SEQ = 8192
DEPTH = 2

N_EVEN = (DEPTH + 1) // 2
N_ODD = DEPTH // 2
D_PLE = 256
EPS = 1e-6

CONV_GROUPS = 8
CONV_GROUP_DIM = 64
CONV_DIM = CONV_GROUPS * CONV_GROUP_DIM
CONV_WIDTH = 3
GMLP_HEADS = 8
GMLP_HEAD_DIM = 64
GMLP_DIM = GMLP_HEADS * GMLP_HEAD_DIM
GMLP_CHUNK = 128
EVEN_IN = 3 * CONV_DIM + 2 * GMLP_DIM
EVEN_MIX = CONV_DIM + GMLP_DIM

MLA_HEADS = 8
MLA_NOPE = 64
MLA_ROPE = 32
MLA_V = 64
MLA_QK = MLA_NOPE + MLA_ROPE
Q_LORA = 384
KV_LORA = 256
ROPE_THETA = 10000.0
ATTN_BLOCK = 128
MLSTM_HEADS = 4
MLSTM_QK = 64
MLSTM_V = 128
MLSTM_CHUNK = 64
ODD_IN = Q_LORA + KV_LORA + MLA_ROPE + 2 * MLSTM_HEADS * MLSTM_QK + 2 * MLSTM_HEADS * MLSTM_V + 2 * MLSTM_HEADS
ODD_MIX = MLA_HEADS * MLA_V + MLSTM_HEADS * MLSTM_V

D_FF = -(-8 * D_MODEL // (3 * 256)) * 256

kernel_name = 'hybrid_conv_gmlp_mla_mlstm'


def rmsnorm(x, g):
    xf = x.astype(jnp.float32)
    y = xf * lax.rsqrt(jnp.mean(xf * xf, axis=-1, keepdims=True) + EPS)
    return (y * g.astype(jnp.float32)).astype(x.dtype)


def swiglu(h, w_gate, w_up, w_down):
    return (jax.nn.silu(h @ w_gate) * (h @ w_up)) @ w_down


def short_gated_conv(b_gate, c_gate, x_in, w_conv):
    z = c_gate * x_in
    conv = lax.conv_general_dilated(
        z, w_conv[:, None, :].astype(z.dtype), window_strides=(1,),
        padding=[(CONV_WIDTH - 1, 0)], dimension_numbers=('NWC', 'WIO', 'NWC'),
        feature_group_count=CONV_DIM)
    return b_gate * conv


def chunked_spatial_gating(u, v, g_v, w_s, b_s):
    bsz, seq, _ = u.shape
    n_chunks = seq // GMLP_CHUNK
    shape5 = (bsz, n_chunks, GMLP_CHUNK, GMLP_HEADS, GMLP_HEAD_DIM)
    u = jax.nn.gelu(u).reshape(shape5)
    v = rmsnorm(jax.nn.gelu(v).reshape(shape5), g_v)
    causal = jnp.tril(jnp.ones((GMLP_CHUNK, GMLP_CHUNK), dtype=w_s.dtype))
    mixed = jnp.einsum('hts,bnshd->bnthd', w_s * causal, v) + b_s.T[:, :, None]
    return (u * mixed).reshape(bsz, seq, GMLP_DIM)


def rope_tables(positions, dim, dtype):
    inv_freq = ROPE_THETA ** (-jnp.arange(0, dim, 2, dtype=jnp.float32) / dim)
    ang = positions.astype(jnp.float32)[..., None] * inv_freq
    return jnp.cos(ang)[:, :, None, :].astype(dtype), jnp.sin(ang)[:, :, None, :].astype(dtype)


def apply_rope(x, cos, sin):
    half = x.shape[-1] // 2
    x1, x2 = x[..., :half], x[..., half:]
    return jnp.concatenate([x1 * cos - x2 * sin, x2 * cos + x1 * sin], axis=-1)


def causal_block_attention(q, k, v):
    bsz, seq, heads, dk = q.shape
    n_blocks = seq // ATTN_BLOCK
    q_blocks = q.reshape(bsz, n_blocks, ATTN_BLOCK, heads, dk).transpose(1, 0, 2, 3, 4)
    k_idx = jnp.arange(seq)
    scale = dk ** -0.5

    def one_block(args):
        qi, bi = args
        s = jnp.einsum('bqhd,bkhd->bhqk', qi, k).astype(jnp.float32) * scale
        q_idx = bi * ATTN_BLOCK + jnp.arange(ATTN_BLOCK)
        s = jnp.where(k_idx[None, :] <= q_idx[:, None], s, -jnp.inf)
        pr = jax.nn.softmax(s, axis=-1).astype(v.dtype)
        return jnp.einsum('bhqk,bkhv->bqhv', pr, v)

    out = lax.map(one_block, (q_blocks, jnp.arange(n_blocks)))
    return out.transpose(1, 0, 2, 3, 4).reshape(bsz, seq, heads, v.shape[-1])


def latent_attention(q_lat, kv_lat, k_pe, positions, g_qa, g_kva, w_q_up, w_kv_up, g_q, g_k):
    bsz, seq, _ = q_lat.shape
    q = (rmsnorm(q_lat, g_qa) @ w_q_up).reshape(bsz, seq, MLA_HEADS, MLA_QK)
    kv = (rmsnorm(kv_lat, g_kva) @ w_kv_up).reshape(bsz, seq, MLA_HEADS, MLA_NOPE + MLA_V)
    k_nope, v = kv[..., :MLA_NOPE], kv[..., MLA_NOPE:]
    cos, sin = rope_tables(positions, MLA_ROPE, q.dtype)
    q_nope = rmsnorm(q[..., :MLA_NOPE], g_q[:MLA_NOPE])
    q_pe = apply_rope(rmsnorm(q[..., MLA_NOPE:], g_q[MLA_NOPE:]), cos, sin)
    k_nope = rmsnorm(k_nope, g_k[:MLA_NOPE])
    k_pe = apply_rope(rmsnorm(k_pe[:, :, None, :], g_k[MLA_NOPE:]), cos, sin)
    q = jnp.concatenate([q_nope, q_pe], axis=-1)
    k = jnp.concatenate([k_nope, jnp.broadcast_to(k_pe, (bsz, seq, MLA_HEADS, MLA_ROPE))], axis=-1)
    return causal_block_attention(q, k, v).reshape(bsz, seq, MLA_HEADS * MLA_V)


def mlstm_chunkwise(q, k, v, i_pre, f_pre):
    out_dtype = v.dtype
    bsz, seq, heads, dk = q.shape
    dv = v.shape[-1]
    L = MLSTM_CHUNK
    nc = seq // L
    f32 = jnp.float32

    def to_chunks(a):
        return a.astype(f32).reshape(bsz, nc, L, heads, -1).transpose(1, 0, 3, 2, 4)

    def gate_chunks(a):
        return a.astype(f32).reshape(bsz, nc, L, heads).transpose(1, 0, 3, 2)

    qc = to_chunks(q) * (dk ** -0.5)
    kc, vc = to_chunks(k), to_chunks(v)
    ic = gate_chunks(i_pre)
    ac = jnp.cumsum(jax.nn.log_sigmoid(gate_chunks(f_pre)), axis=-1)
    causal = jnp.tril(jnp.ones((L, L), dtype=bool))

    def step(carry, xs):
        c_mat, n_vec, m = carry
        qb, kb, vb, ib, ab = xs
        d = jnp.where(causal, ab[..., :, None] - ab[..., None, :] + ib[..., None, :], -jnp.inf)
        inter = ab + m[..., None]
        m_t = jnp.maximum(inter, jnp.max(d, axis=-1))
        w_intra = jnp.exp(d - m_t[..., None])
        w_inter = jnp.exp(inter - m_t)
        s = jnp.einsum('bhtd,bhsd->bhts', qb, kb) * w_intra
        num = jnp.einsum('bhts,bhsv->bhtv', s, vb) + w_inter[..., None] * jnp.einsum('bhtd,bhvd->bhtv', qb, c_mat)
        den = jnp.sum(s, axis=-1) + w_inter * jnp.einsum('bhtd,bhd->bht', qb, n_vec)
        h = num / jnp.maximum(jnp.abs(den), jnp.exp(-m_t))[..., None]
        a_last = ab[..., -1]
        m_new = m_t[..., -1]
        w_s = jnp.exp(a_last[..., None] - ab + ib - m_new[..., None])
        decay = jnp.exp(a_last + m - m_new)
        c_new = decay[..., None, None] * c_mat + jnp.einsum('bhs,bhsv,bhsd->bhvd', w_s, vb, kb)
        n_new = decay[..., None] * n_vec + jnp.einsum('bhs,bhsd->bhd', w_s, kb)
        return (c_new, n_new, m_new), h

    init = (jnp.zeros((bsz, heads, dv, dk), f32), jnp.zeros((bsz, heads, dk), f32), jnp.zeros((bsz, heads), f32))
    _, hs = lax.scan(step, init, (qc, kc, vc, ic, ac))
    return hs.transpose(1, 0, 3, 2, 4).reshape(bsz, seq, heads, dv).astype(out_dtype)


def even_mixer(hn, w_in, w_conv, g_v, w_s, b_s, w_out):
    z = hn @ w_in
    b_gate, c_gate, x_in, u, v = jnp.split(z, [CONV_DIM, 2 * CONV_DIM, 3 * CONV_DIM, 3 * CONV_DIM + GMLP_DIM], axis=-1)
    y_a = short_gated_conv(b_gate, c_gate, x_in, w_conv)
    y_b = chunked_spatial_gating(u, v, g_v, w_s, b_s)
    return jnp.concatenate([y_a, y_b], axis=-1) @ w_out


def odd_mixer(hn, positions, w_in, b_gate, g_qa, g_kva, w_q_up, w_kv_up, g_q, g_k, g_mh, w_out):
    bsz, seq, _ = hn.shape
    z = hn @ w_in
    sizes = [Q_LORA, KV_LORA, MLA_ROPE, MLSTM_HEADS * MLSTM_QK, MLSTM_HEADS * MLSTM_QK,
             MLSTM_HEADS * MLSTM_V, MLSTM_HEADS * MLSTM_V, MLSTM_HEADS, MLSTM_HEADS]
    q_lat, kv_lat, k_pe, mq, mk, mv, mo, mi, mf = jnp.split(z, np.cumsum(sizes)[:-1].tolist(), axis=-1)
    y_c = latent_attention(q_lat, kv_lat, k_pe, positions, g_qa, g_kva, w_q_up, w_kv_up, g_q, g_k)
    h_m = mlstm_chunkwise(mq.reshape(bsz, seq, MLSTM_HEADS, MLSTM_QK),
                          mk.reshape(bsz, seq, MLSTM_HEADS, MLSTM_QK),
                          mv.reshape(bsz, seq, MLSTM_HEADS, MLSTM_V),
                          mi + b_gate[:MLSTM_HEADS], mf + b_gate[MLSTM_HEADS:])
    y_d = jax.nn.sigmoid(mo) * rmsnorm(h_m, g_mh).reshape(bsz, seq, MLSTM_HEADS * MLSTM_V)
    return jnp.concatenate([y_c, y_d], axis=-1) @ w_out


def setup_inputs(seed: int = 0) -> dict:
    key = jax.random.key(seed)
    ks = list(jax.random.split(key, 40))
    f32 = jnp.float32

    def normal(shape):
        return jax.random.normal(ks.pop(), shape, f32)

    def dense(shape, fan_in):
        return normal(shape) * fan_in ** -0.5

    def gain(shape):
        return 1.0 + 0.05 * normal(shape)

    x = normal((BATCH, SEQ, D_MODEL))
    p = normal((DEPTH, BATCH, SEQ, D_PLE))
    positions = jnp.arange(SEQ, dtype=jnp.int32)[None, :] + jax.random.randint(ks.pop(), (BATCH, 1), 0, SEQ, dtype=jnp.int32)
    od_b_gate = jnp.concatenate([0.1 * normal((N_ODD, MLSTM_HEADS)), 3.0 + 0.1 * normal((N_ODD, MLSTM_HEADS))], axis=-1)
    return {
        'x': x,
        'p': p,
        'positions': positions,
        'g_mix': gain((DEPTH, D_MODEL)),
        'g_ffn': gain((DEPTH, D_MODEL)),
        'g_ple': gain((DEPTH, D_MODEL)),
        'ev_w_in': dense((N_EVEN, D_MODEL, EVEN_IN), D_MODEL),
        'ev_w_conv': dense((N_EVEN, CONV_WIDTH, CONV_DIM), CONV_WIDTH),
        'ev_g_v': gain((N_EVEN, GMLP_HEADS, GMLP_HEAD_DIM)),
        'ev_w_s': dense((N_EVEN, GMLP_HEADS, GMLP_CHUNK, GMLP_CHUNK), GMLP_CHUNK),
        'ev_b_s': 1.0 + 0.1 * normal((N_EVEN, GMLP_HEADS, GMLP_CHUNK)),
        'ev_w_out': dense((N_EVEN, EVEN_MIX, D_MODEL), EVEN_MIX),
        'od_w_in': dense((N_ODD, D_MODEL, ODD_IN), D_MODEL),
        'od_b_gate': od_b_gate,
        'od_g_qa': gain((N_ODD, Q_LORA)),
        'od_g_kva': gain((N_ODD, KV_LORA)),
        'od_w_q_up': dense((N_ODD, Q_LORA, MLA_HEADS * MLA_QK), Q_LORA),
        'od_w_kv_up': dense((N_ODD, KV_LORA, MLA_HEADS * (MLA_NOPE + MLA_V)), KV_LORA),
        'od_g_q': gain((N_ODD, MLA_QK)),
        'od_g_k': gain((N_ODD, MLA_QK)),
        'od_g_mh': gain((N_ODD, MLSTM_HEADS, MLSTM_V)),
        'od_w_out': dense((N_ODD, ODD_MIX, D_MODEL), ODD_MIX),
        'w_gate': dense((DEPTH, D_MODEL, D_FF), D_MODEL),
        'w_up': dense((DEPTH, D_MODEL, D_FF), D_MODEL),
        'w_down': dense((DEPTH, D_FF, D_MODEL), D_FF),
        'w_ple_proj': dense((DEPTH, D_PLE, D_MODEL), D_PLE),
        'w_ple_gate': dense((DEPTH, D_MODEL, D_MODEL), D_MODEL),
    }


def reference(x, p, positions, g_mix, g_ffn, g_ple, ev_w_in, ev_w_conv, ev_g_v, ev_w_s, ev_b_s, ev_w_out,
              od_w_in, od_b_gate, od_g_qa, od_g_kva, od_w_q_up, od_w_kv_up, od_g_q, od_g_k, od_g_mh, od_w_out,
              w_gate, w_up, w_down, w_ple_proj, w_ple_gate):
    h = x
    for layer in range(DEPTH):
        j = layer // 2
        hn = rmsnorm(h, g_mix[layer])
        if layer % 2 == 0:
            mix = even_mixer(hn, ev_w_in[j], ev_w_conv[j], ev_g_v[j], ev_w_s[j], ev_b_s[j], ev_w_out[j])
        else:
            mix = odd_mixer(hn, positions, od_w_in[j], od_b_gate[j], od_g_qa[j], od_g_kva[j], od_w_q_up[j],
                            od_w_kv_up[j], od_g_q[j], od_g_k[j], od_g_mh[j], od_w_out[j])
        h = h + mix
        h = h + swiglu(rmsnorm(h, g_ffn[layer]), w_gate[layer], w_up[layer], w_down[layer])
        gate = jax.nn.sigmoid(rmsnorm(h, g_ple[layer]) @ w_ple_gate[layer])
        h = h + gate * (p[layer] @ w_ple_proj[layer])
    return h
```

```python
import numpy as np
import ml_dtypes
import concourse.bass as bass
import concourse.mybir as mybir
from concourse.bass_utils import run_bass_kernel_spmd

F32 = mybir.dt.float32
BF16 = mybir.dt.bfloat16
I32 = mybir.dt.int32
AF = mybir.ActivationFunctionType
ALU = mybir.AluOpType
AX = mybir.AxisListType
EPS = 1e-6
TWO_PI = 6.283185307179586
PI = 3.141592653589793
NTOK = 2048
TS = 512
class TT:
    __slots__ = ("h", "name", "last_w", "readers", "dsem", "dcount")

    def __init__(self, h, name):
        self.h = h
        self.name = name
        self.last_w = None
        self.readers = []
        self.dsem = None
        self.dcount = 0

    def __getitem__(self, idx):
        return self.h[idx]


class Prog:
    ENGS = ("pe", "act", "dve", "pool", "sp")

    def __init__(self, nc, prefix=""):
        self.nc = nc
        self.prefix = prefix
        self.ops = {e: [] for e in self.ENGS}
        self.count = {e: 0 for e in self.ENGS}
        self.waited = {e: {} for e in self.ENGS}
        self.sem_names = ["eng_" + e for e in self.ENGS]
        self.tiles = []
        self._ctx = []
        self.n_dsem = 0

    def sb(self, name, shape, dt=F32):
        g = self.nc.sbuf_tensor(self.prefix + "s_" + name, list(shape), dt)
        h = g.__enter__()
        self._ctx.append(g)
        t = TT(h, name)
        self.tiles.append(t)
        return t

    def ps(self, name, shape, dt=F32):
        g = self.nc.psum_tensor(self.prefix + "p_" + name, list(shape), dt)
        h = g.__enter__()
        self._ctx.append(g)
        t = TT(h, name)
        self.tiles.append(t)
        return t

    def _deps(self, eng, reads, writes):
        deps = []
        for r in reads:
            if r.last_w is not None:
                deps.append(r.last_w)
        for w in writes:
            if w.last_w is not None:
                deps.append(w.last_w)
            deps.extend(w.readers)
        waits = []
        wd = self.waited[eng]
        best = {}
        for (k, v) in deps:
            if eng == "pe" and k == "eng_pe":
                continue
            if wd.get(k, 0) >= v:
                continue
            if best.get(k, 0) < v:
                best[k] = v
        for k, v in best.items():
            wd[k] = v
            waits.append((k, v))
        return waits

    def op(self, eng, fn, reads=(), writes=()):
        waits = self._deps(eng, reads, writes)
        self.count[eng] += 1
        me = ("eng_" + eng, self.count[eng])
        self.ops[eng].append((waits, fn, (me[0], 1)))
        for r in reads:
            r.readers.append(me)
        for w in writes:
            w.last_w = me
            w.readers = []
        return me

    def dma(self, eng, out_ap, in_ap, reads=(), writes=(), **kw):
        waits = self._deps(eng, reads, writes)
        owner = (list(writes) + list(reads))[0]
        if owner.dsem is None:
            owner.dsem = "d%d" % self.n_dsem
            self.n_dsem += 1
            self.sem_names.append(owner.dsem)
        owner.dcount += 1
        me = (owner.dsem, 16 * owner.dcount)

        def fn(e, out_ap=out_ap, in_ap=in_ap, kw=kw):
            o_ = out_ap() if callable(out_ap) else out_ap
            i_ = in_ap() if callable(in_ap) else in_ap
            return e.dma_start(out=o_, in_=i_, **kw)
        self.ops[eng].append((waits, fn, (owner.dsem, 16)))
        for r in reads:
            r.readers.append(me)
        for w in writes:
            w.last_w = me
            w.readers = []
        return me

    def raw(self, eng, fn):
        self.ops[eng].append(([], fn, "raw"))

    def dram(self, name, shape, dt=F32):
        h = self.nc.dram_tensor(name, list(shape), dt).ap()
        t = TT(h, name)
        self.tiles.append(t)
        return t

    def collective(self, kind, in_t, out_t, groups):
        waits = self._deps("pool", [in_t], [out_t])
        key = "cc%d" % self.n_dsem
        self.n_dsem += 1
        self.sem_names.append(key)
        me = (key, 1)

        def fn(e):
            return e.collective_compute(kind, ALU.bypass, replica_groups=groups, ins=[in_t.h.opt()], outs=[out_t.h.opt()])
        self.ops["pool"].append((waits, fn, (key, None)))
        in_t.readers.append(me)
        out_t.last_w = me
        out_t.readers = []
        return me

    def gather(self, out_t, out_ap, in_ap, idx_t, idx_ap):
        waits = self._deps("pool", [idx_t], [out_t])
        if out_t.dsem is None:
            out_t.dsem = "d%d" % self.n_dsem
            self.n_dsem += 1
            self.sem_names.append(out_t.dsem)
        out_t.dcount += 1
        me = (out_t.dsem, 16 * out_t.dcount)

        def fn(e):
            return e.indirect_dma_start(out=out_ap, out_offset=None, in_=in_ap, in_offset=bass.IndirectOffsetOnAxis(ap=idx_ap, axis=0))
        self.ops["pool"].append((waits, fn, (out_t.dsem, 16)))
        idx_t.readers.append(me)
        out_t.last_w = me
        out_t.readers = []
        return me

    def collective_raw(self, kind, in_ap, out_ap, groups, wait=True):
        self.wait_all("pool", self.tiles)
        key = "cc%d" % self.n_dsem
        self.n_dsem += 1
        self.sem_names.append(key)

        def fn(e):
            return e.collective_compute(kind, ALU.bypass, replica_groups=groups, ins=[in_ap.opt()], outs=[out_ap.opt()])
        self.ops["pool"].append(([], fn, (key, None)))
        if wait:
            self.ops["pool"].append(([(key, 1)], None, None))
        return key

    def wait_all(self, eng, tiles):
        deps = []
        for t in tiles:
            if t.last_w is not None:
                deps.append(t.last_w)
            deps.extend(t.readers)
        wd = self.waited[eng]
        best = {}
        for k, v in deps:
            if wd.get(k, 0) < v and best.get(k, 0) < v:
                best[k] = v
        waits = []
        for k, v in best.items():
            wd[k] = v
            waits.append((k, v))
        self.ops[eng].append((waits, None, None))

    def emit(self):
        nc = self.nc
        sems = {}
        for n in self.sem_names:
            sems[n] = nc.alloc_semaphore(name=self.prefix + n)
        blk = nc.Block()
        block = blk.__enter__()

        def run(engname):
            def body(e):
                for waits, fn, inc in self.ops[engname]:
                    for k, v in waits:
                        e.wait_ge(sems[k], v)
                    if fn is not None:
                        ins = fn(e)
                        if inc == "raw":
                            continue
                        if inc[1] is None:
                            ins.then_inc(sems[inc[0]])
                        else:
                            ins.then_inc(sems[inc[0]], inc[1])
            return body

        block.tensor(run("pe"))
        block.scalar(run("act"))
        block.vector(run("dve"))
        block.gpsimd(run("pool"))
        block.sync(run("sp"))
        blk.__exit__(None, None, None)
        nc.all_engine_barrier()
        nc.clear_and_free_semaphores(list(sems.values()))
        nc.all_engine_barrier()
        for g in reversed(self._ctx):
            g.__exit__(None, None, None)
        self._ctx = []


class TV:
    def __init__(self, base, ap):
        self.__dict__["base"] = base
        self.__dict__["ap"] = ap

    def __getitem__(self, idx):
        return self.ap[idx]

    def __getattr__(self, k):
        return getattr(self.base, k)

    def __setattr__(self, k, v):
        setattr(self.base, k, v)


class Ctx:
    def __init__(self, P, nbanks=8):
        self.P = P
        self.nb = nbanks
        self.pbanks = [P.ps("pb%d" % i, [128, 512], F32) for i in range(nbanks)]
        self.pi = 0
        self.pinned = set()
        self.rings = {}

    def psum(self, pin=False):
        while (self.pi % self.nb) in self.pinned:
            self.pi += 1
        t = self.pbanks[self.pi % self.nb]
        if pin:
            self.pinned.add(self.pi % self.nb)
        self.pi += 1
        return t

    def unpin(self):
        self.pinned = set()

    def tmp(self, key, shape, dt=F32, n=2, own=False):
        if dt == F32 and len(shape) == 2 and shape[1] == 512 and not own:
            base = self.tmp("T32", [128, 512, 1], F32, 8)
            return TV(base, base.h[0:shape[0], :, 0])
        if key not in self.rings:
            self.rings[key] = [[self.P.sb("%s_%d" % (key, i), shape, dt) for i in range(n)], 0]
        r = self.rings[key]
        t = r[0][r[1] % len(r[0])]
        r[1] += 1
        return t


def dram_in(nc, name, shape, dt=F32):
    return nc.dram_tensor(name, list(shape), dt, kind="ExternalInput").ap()


def dram_out(nc, name, shape, dt=F32):
    return nc.dram_tensor(name, list(shape), dt, kind="ExternalOutput").ap()


def build_tok(mode, fz=None):
    nc = fz["nc"] if fz else bass.Bass("TRN2", target_bir_lowering=False)
    P = Prog(nc, mode + "_")
    C = Ctx(P)
    op = P.op
    L = 0 if mode == "A" else 1

    D = dict(fz["D"]) if fz else {}
    def din(name, shape, dt=F32):
        if name not in D:
            D[name] = dram_in(nc, name, shape, dt)
        return D[name]
    def dout(name, shape, dt=F32):
        if name not in D:
            D[name] = dram_out(nc, name, shape, dt)
        return D[name]

    din("hT", [1024, NTOK])
    din("pT", [256, NTOK])
    din("gains", [128, 32])
    din("consts", [128, 512])
    din("w_gate", [1024, 2816]); din("w_up", [1024, 2816]); din("w_down", [2816, 1024])
    din("w_ple_gate", [1024, 1024]); din("w_ple_proj", [256, 1024])
    din("w_out", [1024, 1024])
    if mode == "A":
        din("xhT", [1024, 2])
        din("posb", [96, NTOK], I32)
        din("ev_w_in", [1024, 2560])
        din("wconv", [128, 12]); din("gvb", [128, 512]); din("wsT", [128, 8, 128]); din("bsb", [128, 4, 128])
        din("od_w_in", [1024, 2248])
        din("g_lat", [128, 5])
        din("w_q_up", [384, 768]); din("w_q_up_sw", [384, 768]); din("w_kv_up", [256, 1024])
        din("gsm", [128, 8])
        dout("h1T", [1024, NTOK])
        dout("qT", [8, 96, NTOK], BF16); dout("knT", [512, NTOK], BF16); dout("kpeT", [32, NTOK], BF16)
        dout("vT", [512, NTOK], BF16)
        dout("mqT", [256, NTOK], BF16); dout("mkT", [256, NTOK], BF16)
        dout("mvT", [512, NTOK], BF16); dout("soT", [512, NTOK], BF16)
        dout("mifT", [8, NTOK])
    else:
        if not fz:
            din("ymT", [1024, NTOK], BF16)
        else:
            idxc = P.sb("idxc", [128, 64], I32)
            P.dma("sp", idxc[:, :], D["idx"], writes=[idxc])
            ystg = [P.sb("ystg%d" % c, [128, NTOK], BF16) for c in range(8)]
            for c in range(8):
                P.gather(ystg[c], ystg[c][:, :], D["recv2"][:, :], idxc, idxc[:, 48 + c:49 + c])
                if "dbg_y" in D:
                    P.dma("sp", D["dbg_y"][c * 128:(c + 1) * 128, :], ystg[c][:, :], reads=[ystg[c]])
        dout("outT", [1024, NTOK])

    h = [[P.sb("h%d_%d" % (s, c), [128, TS]) for c in range(8)] for s in range(2)]
    hn = [[P.sb("hn%d_%d" % (s, c), [128, TS], BF16) for c in range(8)] for s in range(2)]
    act = [[P.sb("act%d_%d" % (s, j), [128, TS], BF16) for j in range(22)] for s in range(2)]
    y = [a[0:8] for a in act]
    ringA = [P.sb("wA%d" % i, [128, 8, 512], BF16) for i in range(3)]
    ringD = [P.sb("wD%d" % i, [128, 22, 128], BF16) for i in range(2)]
    st = {"a": 0, "d": 0}
    wpp = P.sb("wpp", [128, 2, 1024], BF16)
    gains = P.sb("gains", [128, 32])
    cst = P.sb("cst", [128, 512])
    ones_bf = P.sb("ones_bf", [128, 128], BF16)
    P.dma("sp", gains[:, :], D["gains"], writes=[gains])
    P.dma("sp", cst[:, :], D["consts"], writes=[cst])
    op("pool", lambda e: e.memset(ones_bf[:, :], 1.0), writes=[ones_bf])
    P.dma("pool", wpp[:, :, :], D["w_ple_proj"].rearrange("(kc p) n -> p kc n", p=128), writes=[wpp])

    def slotA():
        t = ringA[st["a"] % 3]; st["a"] += 1; return t

    def slotD():
        t = ringD[st["d"] % 2]; st["d"] += 1; return t

    def loadA(w, c0, c1, dst=None, off=0):
        t = dst if dst is not None else slotA()
        P.dma("pool", t[:, :, off:off + (c1 - c0)], w.rearrange("(kc p) n -> p kc n", p=128)[:, :, c0:c1], writes=[t])
        return t

    def mm(ps_ap, lhs_fn, rhs_fn, nk, reads, ps):
        for kc in range(nk):
            l_ = lhs_fn(kc); r_ = rhs_fn(kc)
            op("pe", lambda e, kc=kc, l_=l_, r_=r_: e.matmul(ps_ap, l_, r_, start=(kc == 0), stop=(kc == nk - 1)),
               reads=reads, writes=[ps])

    def rstd_from_ps(ps, rows, n, scale, tag):
        sd = C.tmp("sd" + tag, [128, 512])
        rp = C.tmp("rp" + tag, [128, 512])
        rd = [ps] + ([cst] if not isinstance(scale, float) else [])
        op("act", lambda e: e.activation(sd[0:rows, 0:n], ps[0:rows, 0:n], AF.Sqrt, bias=EPS, scale=scale), reads=rd, writes=[sd])
        op("dve", lambda e: e.reciprocal(rp[0:rows, 0:n], sd[0:rows, 0:n]), reads=[sd], writes=[rp])
        return rp

    def norm(s, gcol, n=TS, src=None, dst=None):
        src = src or h[s]; dst = dst or hn[s]
        ps = C.psum()
        for c in range(8):
            sq = C.tmp("sq", [128, 512], BF16, 3)
            op("pool", lambda e, c=c, sq=sq: e.tensor_tensor(sq[:, 0:n], src[c][:, 0:n], src[c][:, 0:n], op=ALU.mult), reads=[src[c]], writes=[sq])
            op("pe", lambda e, c=c, sq=sq: e.matmul(ps[:, 0:n], ones_bf[:, :], sq[:, 0:n], start=(c == 0), stop=(c == 7)), reads=[sq, ones_bf], writes=[ps])
        rp = rstd_from_ps(ps, 128, n, 1.0 / 1024.0, "n")
        for c in range(8):
            op("dve", lambda e, c=c: e.scalar_tensor_tensor(dst[c][:, 0:n], src[c][:, 0:n], gains[:, gcol + c:gcol + c + 1], rp[:, 0:n], op0=ALU.mult, op1=ALU.mult),
               reads=[src[c], gains, rp], writes=[dst[c]])

    def resid_proj(w, src):
        for blk in range(2):
            slot = loadA(w, blk * 512, blk * 512 + 512)
            for s in range(2):
                for m in range(4):
                    ps = C.psum()
                    mm(ps[:, :], lambda kc, m=m: slot[:, kc, m * 128:(m + 1) * 128], lambda kc, s=s: src[s][kc][:, :], 8, [slot] + src[s], ps)
                    hc = h[s][blk * 4 + m]
                    op("dve", lambda e, ps=ps, hc=hc: e.tensor_tensor(hc[:, :], ps[:, :], hc[:, :], op=ALU.add), reads=[ps, hc], writes=[hc])

    def ffn(gcol):
        for s in range(2):
            norm(s, gcol)
        for j in range(11):
            slot = slotA()
            loadA(D["w_gate"], j * 256, j * 256 + 256, dst=slot, off=0)
            loadA(D["w_up"], j * 256, j * 256 + 256, dst=slot, off=256)
            for s in range(2):
                for jj in range(2):
                    pg = C.psum(); pu = C.psum()
                    mm(pg[:, :], lambda kc, jj=jj: slot[:, kc, jj * 128:(jj + 1) * 128], lambda kc, s=s: hn[s][kc][:, :], 8, [slot] + hn[s], pg)
                    mm(pu[:, :], lambda kc, jj=jj: slot[:, kc, 256 + jj * 128:256 + (jj + 1) * 128], lambda kc, s=s: hn[s][kc][:, :], 8, [slot] + hn[s], pu)
                    sg = C.tmp("sg", [128, 512], F32, 3)
                    op("act", lambda e, pg=pg, sg=sg: e.activation(sg[:, :], pg[:, :], AF.Silu), reads=[pg], writes=[sg])
                    a = act[s][2 * j + jj]
                    op("dve", lambda e, pu=pu, sg=sg, a=a: e.tensor_tensor(a[:, :], pu[:, :], sg[:, :], op=ALU.mult), reads=[pu, sg], writes=[a])
        for mb in range(8):
            slot = slotD()
            P.dma("pool", slot[:, :, :], D["w_down"].rearrange("(kc p) n -> p kc n", p=128)[:, :, mb * 128:(mb + 1) * 128], writes=[slot])
            for s in range(2):
                ps = C.psum()
                mm(ps[:, :], lambda kc: slot[:, kc, :], lambda kc, s=s: act[s][kc][:, :], 22, [slot] + act[s], ps)
                hc = h[s][mb]
                op("dve", lambda e, ps=ps, hc=hc: e.tensor_tensor(hc[:, :], ps[:, :], hc[:, :], op=ALU.add), reads=[ps, hc], writes=[hc])

    def ple(gcol, tok0):
        pt = []
        for s in range(2):
            norm(s, gcol)
            t = C.tmp("pt", [128, 2, TS], BF16, 2)
            P.dma("pool", t[:, :, :], D["pT"].rearrange("(kc p) n -> p kc n", p=128)[:, :, tok0 + s * TS:tok0 + (s + 1) * TS], writes=[t])
            pt.append(t)
        for blk in range(2):
            slot = loadA(D["w_ple_gate"], blk * 512, blk * 512 + 512)
            for s in range(2):
                for m in range(4):
                    mg = blk * 4 + m
                    pg = C.psum(); pp = C.psum()
                    mm(pg[:, :], lambda kc, m=m: slot[:, kc, m * 128:(m + 1) * 128], lambda kc, s=s: hn[s][kc][:, :], 8, [slot] + hn[s], pg)
                    mm(pp[:, :], lambda kc, mg=mg: wpp[:, kc, mg * 128:(mg + 1) * 128], lambda kc, s=s: pt[s][:, kc, :], 2, [wpp, pt[s]], pp)
                    sg = C.tmp("sg", [128, 512], F32, 3)
                    op("act", lambda e, pg=pg, sg=sg: e.activation(sg[:, :], pg[:, :], AF.Sigmoid), reads=[pg], writes=[sg])
                    t2 = C.tmp("t2", [128, 512], F32, 3)
                    op("dve", lambda e, pp=pp, sg=sg, t2=t2: e.tensor_tensor(t2[:, :], pp[:, :], sg[:, :], op=ALU.mult), reads=[pp, sg], writes=[t2])
                    hc = h[s][mg]
                    op("pool", lambda e, t2=t2, hc=hc: e.tensor_tensor(hc[:, :], t2[:, :], hc[:, :], op=ALU.add), reads=[t2, hc], writes=[hc])

    if mode == "A":
        wconv = P.sb("wconv", [128, 12]); gvb = P.sb("gvb", [128, 512]); bsb = P.sb("bsb", [128, 4, 128])
        wsT = P.sb("wsT", [128, 8, 128], BF16); maskb = P.sb("maskb", [128, 128], BF16)
        g_lat = P.sb("g_lat", [128, 5]); gsm = P.sb("gsm", [128, 8])
        b96 = P.sb("b96", [96, 96], BF16); bd64 = P.sb("bd64", [128, 128], BF16)
        for t, nm in ((wconv, "wconv"), (gvb, "gvb"), (g_lat, "g_lat"), (gsm, "gsm")):
            P.dma("sp", t[:, :], D[nm], writes=[t])
        P.dma("sp", bsb[:, :, :], D["bsb"], writes=[bsb])
        P.dma("pool", wsT[:, :, :], D["wsT"], writes=[wsT])
        op("dve", lambda e: e.tensor_copy(maskb[:, :], cst[:, 0:128]), reads=[cst], writes=[maskb])
        for hh in range(8):
            op("dve", lambda e, hh=hh: e.tensor_tensor(wsT[:, hh, :], wsT[:, hh, :], maskb[:, :], op=ALU.mult), reads=[wsT, maskb], writes=[wsT])
        op("dve", lambda e: e.tensor_copy(b96[:, :], cst[0:96, 128:224]), reads=[cst], writes=[b96])
        op("dve", lambda e: e.tensor_copy(bd64[:, :], cst[:, 224:352]), reads=[cst], writes=[bd64])
        hal = [P.sb("hal%d" % cc, [128, 2]) for cc in range(4)]
        gu = [act[s][8:12] for s in range(2)]
        hh_t = [P.sb("hh%d" % c, [128, 2]) for c in range(8)]
        hhn = [P.sb("hhn%d" % c, [128, 2], BF16) for c in range(8)]

    def even_mixer(sti, tok0):
        for s in range(2):
            norm(s, 0)
        if sti == 0:
            for c in range(8):
                P.dma("sp", hh_t[c][:, :], D["xhT"][c * 128:(c + 1) * 128, :], writes=[hh_t[c]])
            norm(0, 0, n=2, src=hh_t, dst=hhn)
        for cc in range(4):
            slot = loadA(D["ev_w_in"], cc * 384, cc * 384 + 384)
            for s in range(2):
                z = C.tmp("zt", [128, TS + 2], F32, 2)
                if not (s == 0 and sti == 0):
                    op("pool", lambda e, cc=cc, z=z: e.tensor_copy(z[:, 0:2], hal[cc][:, :]), reads=[hal[cc]], writes=[z])
                else:
                    pc = C.psum(); px = C.psum()
                    mm(pc[:, 0:2], lambda kc: slot[:, kc, 128:256], lambda kc: hhn[kc][:, :], 8, [slot] + hhn, pc)
                    mm(px[:, 0:2], lambda kc: slot[:, kc, 256:384], lambda kc: hhn[kc][:, :], 8, [slot] + hhn, px)
                    cs = C.tmp("cs", [128, 512], F32, 2)
                    op("act", lambda e, pc=pc, cs=cs: e.activation(cs[:, 0:2], pc[:, 0:2], AF.Copy), reads=[pc], writes=[cs])
                    op("dve", lambda e, px=px, cs=cs, z=z: e.tensor_tensor(z[:, 0:2], px[:, 0:2], cs[:, 0:2], op=ALU.mult), reads=[px, cs], writes=[z])
                pb = C.psum(); pc = C.psum(); px = C.psum()
                for pp_, c0 in ((pc, 128), (px, 256), (pb, 0)):
                    mm(pp_[:, :], lambda kc, c0=c0: slot[:, kc, c0:c0 + 128], lambda kc, s=s: hn[s][kc][:, :], 8, [slot] + hn[s], pp_)
                cs = C.tmp("cs", [128, 512], F32, 2)
                op("act", lambda e, pc=pc, cs=cs: e.activation(cs[:, :], pc[:, :], AF.Copy), reads=[pc], writes=[cs])
                op("dve", lambda e, px=px, cs=cs, z=z: e.tensor_tensor(z[:, 2:TS + 2], px[:, :], cs[:, :], op=ALU.mult), reads=[px, cs], writes=[z])
                op("pool", lambda e, cc=cc, z=z: e.tensor_copy(hal[cc][:, :], z[:, TS:TS + 2]), reads=[z], writes=[hal[cc]])
                acc = C.tmp("acc", [128, 512], F32, 2)
                op("pool", lambda e, z=z, acc=acc, cc=cc: e.tensor_scalar(acc[:, :], z[:, 0:TS], wconv[:, cc * 3:cc * 3 + 1], None, op0=ALU.mult), reads=[z, wconv], writes=[acc])
                op("dve", lambda e, z=z, acc=acc, cc=cc: e.scalar_tensor_tensor(acc[:, :], z[:, 1:TS + 1], wconv[:, cc * 3 + 1:cc * 3 + 2], acc[:, :], op0=ALU.mult, op1=ALU.add), reads=[z, wconv, acc], writes=[acc])
                op("dve", lambda e, z=z, acc=acc, cc=cc: e.scalar_tensor_tensor(acc[:, :], z[:, 2:TS + 2], wconv[:, cc * 3 + 2:cc * 3 + 3], acc[:, :], op0=ALU.mult, op1=ALU.add), reads=[z, wconv, acc], writes=[acc])
                yt = y[s][cc]
                op("dve", lambda e, pb=pb, acc=acc, yt=yt: e.tensor_tensor(yt[:, :], pb[:, :], acc[:, :], op=ALU.mult), reads=[pb, acc], writes=[yt])
        slot = loadA(D["ev_w_in"], 1536, 2048)
        for s in range(2):
            for uc in range(4):
                pu = C.psum()
                mm(pu[:, :], lambda kc, uc=uc: slot[:, kc, uc * 128:(uc + 1) * 128], lambda kc, s=s: hn[s][kc][:, :], 8, [slot] + hn[s], pu)
                g_ = gu[s][uc]
                op("act", lambda e, pu=pu, g_=g_: e.activation(g_[:, :], pu[:, :], AF.Gelu), reads=[pu], writes=[g_])
        slot = loadA(D["ev_w_in"], 2048, 2560)
        for s in range(2):
            pm = [C.psum(pin=True) for _ in range(4)]
            for tb in range(4):
                pv = C.psum()
                mm(pv[:, :], lambda kc, s=s, tb=tb: hn[s][kc][:, tb * 128:(tb + 1) * 128], lambda kc: slot[:, kc, :], 8, [slot] + hn[s], pv)
                gv = C.tmp("gv", [128, 512], F32, 2)
                op("act", lambda e, pv=pv, gv=gv: e.activation(gv[:, :], pv[:, :], AF.Gelu), reads=[pv], writes=[gv])
                sqv = C.tmp("sqv", [128, 512], F32, 2)
                op("pool", lambda e, gv=gv, sqv=sqv: e.tensor_tensor(sqv[:, :], gv[:, :], gv[:, :], op=ALU.mult), reads=[gv], writes=[sqv])
                ss = C.tmp("ss", [128, 8], F32, 2); sd = C.tmp("ssd", [128, 8], F32, 2); rs = C.tmp("srs", [128, 8], F32, 2)
                op("dve", lambda e, sqv=sqv, ss=ss: e.tensor_reduce(ss[:, :], sqv[:, :].rearrange("p (h d) -> p h d", d=64), axis=AX.X, op=ALU.add), reads=[sqv], writes=[ss])
                op("act", lambda e, ss=ss, sd=sd: e.activation(sd[:, :], ss[:, :], AF.Sqrt, bias=EPS, scale=1.0 / 64.0), reads=[ss], writes=[sd])
                op("dve", lambda e, sd=sd, rs=rs: e.reciprocal(rs[:, :], sd[:, :]), reads=[sd], writes=[rs])
                op("dve", lambda e, gv=gv, rs=rs: e.tensor_tensor(gv[:, :].rearrange("p (h d) -> p h d", d=64), gv[:, :].rearrange("p (h d) -> p h d", d=64),
                                                                 rs[:, :].unsqueeze(2).to_broadcast([128, 8, 64]), op=ALU.mult), reads=[gv, rs], writes=[gv])
                vn = C.tmp("vn", [128, 512], BF16, 2)
                op("pool", lambda e, gv=gv, vn=vn: e.tensor_tensor(vn[:, :], gv[:, :], gvb[:, :], op=ALU.mult), reads=[gv, gvb], writes=[vn])
                for hd in range(8):
                    op("pe", lambda e, hd=hd, tb=tb, vn=vn, pm=pm: e.matmul(pm[hd // 2][64 * (hd % 2):64 * (hd % 2) + 64, tb * 128:(tb + 1) * 128], vn[:, hd * 64:(hd + 1) * 64], wsT[:, hd, :], start=True, stop=True),
                       reads=[vn, wsT], writes=[pm[hd // 2]])
            for hc in range(4):
                t1 = C.tmp("t2", [128, 512], F32, 3)
                op("dve", lambda e, hc=hc, t1=t1, pm=pm: e.tensor_tensor(t1[:, :].rearrange("p (b t) -> p b t", t=128), pm[hc][:, :].rearrange("p (b t) -> p b t", t=128),
                                                                bsb[:, hc, :].unsqueeze(1).to_broadcast([128, 4, 128]), op=ALU.add), reads=[pm[hc], bsb], writes=[t1])
                yt = y[s][4 + hc]
                op("pool", lambda e, t1=t1, yt=yt, s=s, hc=hc: e.tensor_tensor(yt[:, :], t1[:, :], gu[s][hc][:, :], op=ALU.mult), reads=[t1, gu[s][hc]], writes=[yt])
            C.unpin()
        resid_proj(D["w_out"], y)

    def O(nm, r0, r1, t0):
        if fz and (nm + "_h0") in D:
            hf = t0 // 1024
            return D[nm + "_h%d" % hf][r0:r1, (t0 % 1024):(t0 % 1024) + TS]
        return D[nm][r0:r1, t0:t0 + TS]

    def odd_front(tok0):
        W = D["od_w_in"]
        for s in range(2):
            norm(s, 24)
        tabs = []
        for s in range(2):
            pi_ = C.tmp("ti", [96, TS], I32, 1)
            P.dma("sp", pi_[:, :], D["posb"][:, tok0 + s * TS:tok0 + (s + 1) * TS], writes=[pi_])
            ang = C.tmp("angp", [96, TS], F32, 1, own=True)
            op("dve", lambda e, pi_=pi_, ang=ang: e.tensor_copy(ang[:, :], pi_[:, :]), reads=[pi_], writes=[ang])
            op("dve", lambda e, ang=ang: e.tensor_scalar(ang[:, :], ang[:, :], cst[0:96, 353:354], None, op0=ALU.mult), reads=[ang, cst], writes=[ang])
            pair = []
            for nm, shift in (("cos", PI / 2.0), ("sin", 0.0)):
                a2 = C.tmp("a2", [96, TS], F32, 1); tf = C.tmp("tf", [96, TS], F32, 1); ti = C.tmp("ti", [96, TS], I32, 1)
                tab = C.tmp("tab" + nm, [96, TS], F32, 2, own=True)
                op("dve", lambda e, a2=a2, ang=ang, shift=shift: e.tensor_scalar(a2[:, :], ang[:, :], shift, None, op0=ALU.add), reads=[ang], writes=[a2])
                op("dve", lambda e, a2=a2, tf=tf: e.tensor_scalar(tf[:, :], a2[:, :], 1.0 / TWO_PI, None, op0=ALU.mult), reads=[a2], writes=[tf])
                op("dve", lambda e, tf=tf, ti=ti: e.tensor_copy(ti[:, :], tf[:, :]), reads=[tf], writes=[ti])
                op("dve", lambda e, tf=tf, ti=ti: e.tensor_copy(tf[:, :], ti[:, :]), reads=[ti], writes=[tf])
                op("dve", lambda e, tf=tf, a2=a2: e.scalar_tensor_tensor(a2[:, :], tf[:, :], -TWO_PI, a2[:, :], op0=ALU.mult, op1=ALU.add), reads=[tf, a2], writes=[a2])
                op("dve", lambda e, a2=a2: e.tensor_scalar(a2[:, :], a2[:, :], -PI, PI, op0=ALU.max, op1=ALU.min), reads=[a2], writes=[a2])
                if nm == "sin":
                    op("act", lambda e, a2=a2, tab=tab: e.activation(tab[:, :], a2[:, :], AF.Sin, scale=cst[0:96, 354:355]), reads=[a2, cst], writes=[tab])
                else:
                    op("act", lambda e, a2=a2, tab=tab: e.activation(tab[:, :], a2[:, :], AF.Sin), reads=[a2], writes=[tab])
                pair.append(tab)
            tabs.append(pair)

        def fm_out(ps, rows, dram_ap, scale=1.0, tag="fo"):
            ob = C.tmp(tag, [128, TS], BF16, 3)
            op("act", lambda e: e.mul(ob[0:rows, :], ps[0:rows, :], float(scale)), reads=[ps], writes=[ob])
            P.dma("sp", dram_ap, ob[0:rows, :], reads=[ob])

        def tok_out(ps, ncols, dram_fn, stage, tb, col0, func=AF.Copy):
            op("act", lambda e: e.activation(stage[:, tb, col0:col0 + ncols], ps[:, 0:ncols], func), reads=[ps], writes=[stage])

        def lat_norm(raws, nch, gc0, D_, outs, tag):
            ps = C.psum()
            for c in range(nch):
                sq = C.tmp("sq", [128, 512], BF16, 3)
                op("pool", lambda e, c=c, sq=sq: e.tensor_tensor(sq[:, :], raws[c][:, :], raws[c][:, :], op=ALU.mult), reads=[raws[c]], writes=[sq])
                op("pe", lambda e, c=c, sq=sq: e.matmul(ps[:, :], ones_bf[:, :], sq[:, :], start=(c == 0), stop=(c == nch - 1)), reads=[sq, ones_bf], writes=[ps])
            rp = rstd_from_ps(ps, 128, TS, 1.0 / D_, tag)
            for c in range(nch):
                op("dve", lambda e, c=c: e.scalar_tensor_tensor(outs[c][:, :], raws[c][:, :], g_lat[:, gc0 + c:gc0 + c + 1], rp[:, :], op0=ALU.mult, op1=ALU.mult),
                   reads=[raws[c], g_lat, rp], writes=[outs[c]])

        qlr = [act[s][12:15] for s in range(2)]
        kvr = [act[s][15:17] for s in range(2)]
        qln = [act[s][17:20] for s in range(2)]
        kvn = [act[s][20:22] for s in range(2)]

        def raw_copy(ps, rows, dst):
            op("act", lambda e: e.activation(dst[0:rows, :], ps[0:rows, :], AF.Copy), reads=[ps], writes=[dst])

        slot = loadA(W, 0, 512)
        for s in range(2):
            for c in range(4):
                ps = C.psum()
                mm(ps[:, :], lambda kc, c=c: slot[:, kc, c * 128:(c + 1) * 128], lambda kc, s=s: hn[s][kc][:, :], 8, [slot] + hn[s], ps)
                raw_copy(ps, 128, qlr[s][c] if c < 3 else kvr[s][0])
            lat_norm(qlr[s], 3, 0, 384.0, qln[s], "q")
        slot = loadA(W, 512, 960)
        for s in range(2):
            t0 = tok0 + s * TS
            cosT, sinT = tabs[s]
            ps = C.psum()
            mm(ps[:, :], lambda kc: slot[:, kc, 0:128], lambda kc, s=s: hn[s][kc][:, :], 8, [slot] + hn[s], ps)
            raw_copy(ps, 128, kvr[s][1])
            lat_norm(kvr[s], 2, 3, 256.0, kvn[s], "k")
            pk = C.psum(); pks = C.psum()
            mm(pk[0:32, :], lambda kc: slot[:, kc, 128:160], lambda kc, s=s: hn[s][kc][:, :], 8, [slot] + hn[s], pk)
            mm(pks[0:32, :], lambda kc: slot[:, kc, 160:192], lambda kc, s=s: hn[s][kc][:, :], 8, [slot] + hn[s], pks)
            kr = C.tmp("kr", [32, 512], F32)
            raw_copy(pk, 32, kr)
            sq = C.tmp("sq", [128, 512], BF16, 3)
            op("pool", lambda e, sq=sq, kr=kr: e.tensor_tensor(sq[0:32, :], kr[:, :], kr[:, :], op=ALU.mult), reads=[kr], writes=[sq])
            pn = C.psum()
            op("pe", lambda e, sq=sq, pn=pn: e.matmul(pn[0:32, :], ones_bf[0:32, 0:32], sq[0:32, :], start=True, stop=True), reads=[sq, ones_bf], writes=[pn])
            rp = rstd_from_ps(pn, 32, TS, 1.0 / 32.0, "kp")
            a = C.tmp("kpa", [32, 512], F32); b_ = C.tmp("kpb", [32, 512], F32)
            op("dve", lambda e, a=a, rp=rp, kr=kr: e.scalar_tensor_tensor(a[:, :], kr[:, :], gsm[0:32, 4:5], rp[0:32, :], op0=ALU.mult, op1=ALU.mult), reads=[kr, gsm, rp], writes=[a])
            op("dve", lambda e, b_=b_, rp=rp, pks=pks: e.scalar_tensor_tensor(b_[:, :], pks[0:32, :], gsm[0:32, 5:6], rp[0:32, :], op0=ALU.mult, op1=ALU.mult), reads=[pks, gsm, rp], writes=[b_])
            op("pool", lambda e, a=a, cosT=cosT: e.tensor_tensor(a[:, :], a[:, :], cosT[0:32, :], op=ALU.mult), reads=[a, cosT], writes=[a])
            op("pool", lambda e, b_=b_, sinT=sinT: e.tensor_tensor(b_[:, :], b_[:, :], sinT[0:32, :], op=ALU.mult), reads=[b_, sinT], writes=[b_])
            ob = C.tmp("fo", [128, TS], BF16, 3)
            op("pool", lambda e, a=a, b_=b_, ob=ob: e.tensor_tensor(ob[0:32, :], a[:, :], b_[:, :], op=ALU.add), reads=[a, b_], writes=[ob])
            P.dma("sp", O("kpeT", 0, 32, t0), ob[0:32, :], reads=[ob])
            for c in range(2):
                ps = C.psum()
                mm(ps[:, :], lambda kc, c=c: slot[:, kc, 192 + c * 128:192 + (c + 1) * 128], lambda kc, s=s: hn[s][kc][:, :], 8, [slot] + hn[s], ps)
                fm_out(ps, 128, O("mqT", c * 128, (c + 1) * 128, t0), scale=0.125)
        slot = loadA(W, 960, 1224)
        for s in range(2):
            t0 = tok0 + s * TS
            for c in range(2):
                ps = C.psum()
                mm(ps[:, :], lambda kc, c=c: slot[:, kc, c * 128:(c + 1) * 128], lambda kc, s=s: hn[s][kc][:, :], 8, [slot] + hn[s], ps)
                fm_out(ps, 128, O("mkT", c * 128, (c + 1) * 128, t0))
            ps = C.psum()
            mm(ps[0:8, :], lambda kc: slot[:, kc, 256:264], lambda kc, s=s: hn[s][kc][:, :], 8, [slot] + hn[s], ps)
            mo_ = C.tmp("mif", [8, 512], F32)
            raw_copy(ps, 8, mo_)
            P.dma("sp", D["mifT"][:, t0:t0 + TS], mo_[:, :], reads=[mo_])
        for gi_, (c0, nm) in enumerate(((1224, "mvT"), (1736, "soT"))):
            slot = loadA(W, c0, c0 + 512)
            for s in range(2):
                t0 = tok0 + s * TS
                for c in range(4):
                    ps = C.psum()
                    mm(ps[:, :], lambda kc, c=c: slot[:, kc, c * 128:(c + 1) * 128], lambda kc, s=s: hn[s][kc][:, :], 8, [slot] + hn[s], ps)
                    ob = C.tmp("fo", [128, TS], BF16, 3)
                    if gi_ == 0 and c % 2 == 0:
                        op("dve", lambda e, ps=ps, ob=ob: e.tensor_copy(ob[:, :], ps[:, :]), reads=[ps], writes=[ob])
                    else:
                        fn_ = AF.Copy if gi_ == 0 else AF.Sigmoid
                        op("act", lambda e, ps=ps, ob=ob, fn_=fn_: e.activation(ob[:, :], ps[:, :], fn_), reads=[ps], writes=[ob])
                    P.dma("sp", O(nm, c * 128, (c + 1) * 128, t0), ob[:, :], reads=[ob])
        def sview(slot, nk, n):
            return slot.h[:, :, :].rearrange("p k n -> p (k n)")[:, 0:nk * n].rearrange("p (k n) -> p k n", n=n)
        wq_t = slotA(); wqs_t = slotA(); wkv_t = slotA()
        wq = TV(wq_t, sview(wq_t, 3, 768)); wqs = TV(wqs_t, sview(wqs_t, 3, 768)); wkv = TV(wkv_t, sview(wkv_t, 2, 1024))
        P.dma("pool", wq[:, :, :], D["w_q_up"].rearrange("(kc p) n -> p kc n", p=128), writes=[wq])
        P.dma("pool", wqs[:, :, :], D["w_q_up_sw"].rearrange("(kc p) n -> p kc n", p=128), writes=[wqs])
        P.dma("pool", wkv[:, :, :], D["w_kv_up"].rearrange("(kc p) n -> p kc n", p=128), writes=[wkv])
        for hd in range(8):
            for s in range(2):
                t0 = tok0 + s * TS
                cosT, sinT = tabs[s]
                pq = C.psum(); pqs = C.psum()
                mm(pq[0:96, :], lambda kc, hd=hd: wq[:, kc, hd * 96:(hd + 1) * 96], lambda kc, s=s: qln[s][kc][:, :], 3, [wq] + qln[s], pq)
                mm(pqs[0:96, :], lambda kc, hd=hd: wqs[:, kc, hd * 96:(hd + 1) * 96], lambda kc, s=s: qln[s][kc][:, :], 3, [wqs] + qln[s], pqs)
                qr = C.tmp("qr", [96, 512], F32)
                raw_copy(pq, 96, qr)
                sq = C.tmp("sq", [128, 512], BF16, 3)
                op("pool", lambda e, sq=sq, qr=qr: e.tensor_tensor(sq[0:96, :], qr[:, :], qr[:, :], op=ALU.mult), reads=[qr], writes=[sq])
                pn = C.psum()
                op("pe", lambda e, sq=sq, pn=pn: e.matmul(pn[0:96, :], b96[:, :], sq[0:96, :], start=True, stop=True), reads=[sq, b96], writes=[pn])
                rp = rstd_from_ps(pn, 96, TS, cst[0:96, 352:353], "qh")
                qn = C.tmp("qn", [96, 512], F32); sw = C.tmp("qsw", [96, 512], F32)
                op("dve", lambda e, qn=qn, qr=qr, rp=rp: e.scalar_tensor_tensor(qn[:, :], qr[:, :], gsm[0:96, 0:1], rp[0:96, :], op0=ALU.mult, op1=ALU.mult), reads=[qr, gsm, rp], writes=[qn])
                op("dve", lambda e, sw=sw, pqs=pqs, rp=rp: e.scalar_tensor_tensor(sw[64:96, :], pqs[64:96, :], gsm[64:96, 1:2], rp[64:96, :], op0=ALU.mult, op1=ALU.mult), reads=[pqs, gsm, rp], writes=[sw])
                op("pool", lambda e, qn=qn, cosT=cosT: e.tensor_tensor(qn[64:96, :], qn[64:96, :], cosT[64:96, :], op=ALU.mult), reads=[qn, cosT], writes=[qn])
                op("pool", lambda e, sw=sw, sinT=sinT: e.tensor_tensor(sw[64:96, :], sw[64:96, :], sinT[64:96, :], op=ALU.mult), reads=[sw, sinT], writes=[sw])
                op("pool", lambda e, qn=qn, sw=sw: e.tensor_tensor(qn[64:96, :], qn[64:96, :], sw[64:96, :], op=ALU.add), reads=[qn, sw], writes=[qn])
                ob = C.tmp("fo", [128, TS], BF16, 3)
                op("act", lambda e, ob=ob, qn=qn: e.mul(ob[0:96, :], qn[:, :], 96.0 ** -0.5), reads=[qn], writes=[ob])
                P.dma("sp", (O("qTf", hd * 96, (hd + 1) * 96, t0) if fz else D["qT"][hd, :, t0:t0 + TS]), ob[0:96, :], reads=[ob])
        for s in range(2):
            t0 = tok0 + s * TS
            for hp in range(4):
                pk = C.psum()
                mm(pk[:, :], lambda kc, hp=hp: wkv[:, kc, hp * 128:(hp + 1) * 128], lambda kc, s=s: kvn[s][kc][:, :], 2, [wkv] + kvn[s], pk)
                kr = C.tmp("kr", [128, 512], F32)
                raw_copy(pk, 128, kr)
                sq = C.tmp("sq", [128, 512], BF16, 3)
                op("pool", lambda e, sq=sq, kr=kr: e.tensor_tensor(sq[:, :], kr[:, :], kr[:, :], op=ALU.mult), reads=[kr], writes=[sq])
                pn = C.psum()
                op("pe", lambda e, sq=sq, pn=pn: e.matmul(pn[:, :], bd64[:, :], sq[:, :], start=True, stop=True), reads=[sq, bd64], writes=[pn])
                rp = rstd_from_ps(pn, 128, TS, 1.0 / 64.0, "kh")
                ob = C.tmp("fo", [128, TS], BF16, 3)
                op("dve", lambda e, ob=ob, kr=kr, rp=rp: e.scalar_tensor_tensor(ob[:, :], kr[:, :], gsm[:, 2:3], rp[:, :], op0=ALU.mult, op1=ALU.mult), reads=[kr, gsm, rp], writes=[ob])
                P.dma("sp", O("knT", hp * 128, (hp + 1) * 128, t0), ob[:, :], reads=[ob])
            for hp in range(4):
                pv = C.psum()
                mm(pv[:, :], lambda kc, hp=hp: wkv[:, kc, 512 + hp * 128:512 + (hp + 1) * 128], lambda kc, s=s: kvn[s][kc][:, :], 2, [wkv] + kvn[s], pv)
                ob = C.tmp("fo", [128, TS], BF16, 3)
                op("act", lambda e, pv=pv, ob=ob: e.activation(ob[:, :], pv[:, :], AF.Copy), reads=[pv], writes=[ob])
                P.dma("sp", O("vT", hp * 128, (hp + 1) * 128, t0), ob[:, :], reads=[ob])

    for sti in range(NTOK // (2 * TS)):
        tok0 = sti * 2 * TS
        for s in range(2):
            for c in range(8):
                P.dma("sp", h[s][c][:, :], D["hT"][c * 128:(c + 1) * 128, tok0 + s * TS:tok0 + (s + 1) * TS], writes=[h[s][c]])
        if mode == "A":
            even_mixer(sti, tok0)
            if fz and sti == 1 and "mid_hook" in fz:
                fz["mid_hook"](P, fz["snap"])
            ffn(8)
            ple(16, tok0)
            for s in range(2):
                for c in range(8):
                    P.dma("sp", D["h1T"][c * 128:(c + 1) * 128, tok0 + s * TS:tok0 + (s + 1) * TS], h[s][c][:, :], reads=[h[s][c]])
            odd_front(tok0)
            if fz and sti == 0 and "mid_hook" in fz:
                fz["snap"] = [(t.dsem, 16 * t.dcount) for t in P.tiles if t.dsem is not None]
        else:
            if fz:
                ysrc = [[TV(ystg[c], ystg[c].h[:, tok0 + s * TS:tok0 + (s + 1) * TS]) for c in range(8)] for s in range(2)]
            else:
                ysrc = y
                for s in range(2):
                    for c in range(8):
                        P.dma("pool", y[s][c][:, :], D["ymT"][c * 128:(c + 1) * 128, tok0 + s * TS:tok0 + (s + 1) * TS], writes=[y[s][c]])
            resid_proj(D["w_out"], ysrc)
            ffn(8)
            ple(16, tok0)
            for s in range(2):
                for c in range(8):
                    P.dma("sp", D["outT"][c * 128:(c + 1) * 128, tok0 + s * TS:tok0 + (s + 1) * TS], h[s][c][:, :], reads=[h[s][c]])
    P.wait_all("sp", P.tiles)
    if fz:
        return P
    P.emit()
    return nc


def _c(a):
    return np.ascontiguousarray(a)


def _chunkT(v):
    return _c(v.reshape(-1, 128).T)


def _consts():
    c = np.zeros((128, 512), np.float32)
    s = np.arange(128)
    c[:, 0:128] = (s[:, None] <= s[None, :]).astype(np.float32)
    k = np.arange(96)
    c[0:96, 128:224] = ((k[:, None] < 64) == (k[None, :] < 64)).astype(np.float32)
    c[:, 224:352] = ((s[:, None] // 64) == (s[None, :] // 64)).astype(np.float32)
    c[0:64, 352] = 1.0 / 64.0
    c[64:96, 352] = 1.0 / 32.0
    inv_freq = (10000.0 ** (-np.arange(0, 32, 2, dtype=np.float32) / np.float32(32))).astype(np.float32)
    c[0:96, 353] = inv_freq[np.arange(96) % 16]
    c[0:96, 354] = np.where((np.arange(96) % 32) < 16, -1.0, 1.0)
    return c


_SW = (np.arange(32) + 16) % 32


def prep_tok_common(inp, L, core):
    b, q = core // 4, core % 4
    s0 = q * NTOK
    m = {
        "pT": _c(inp["p"][L, b, s0:s0 + NTOK, :].T),
        "consts": _consts(),
        "w_gate": _c(inp["w_gate"][L]), "w_up": _c(inp["w_up"][L]), "w_down": _c(inp["w_down"][L]),
        "w_ple_gate": _c(inp["w_ple_gate"][L]), "w_ple_proj": _c(inp["w_ple_proj"][L]),
    }
    return m


def prep_A(inp):
    maps = []
    x = inp["x"]
    ev_w_in = inp["ev_w_in"][0]
    cols = []
    for cc in range(4):
        for base in (0, 512, 1024):
            cols.append(np.arange(base + cc * 128, base + (cc + 1) * 128))
    cols.append(np.arange(1536, 2560))
    ev_w_in_r = _c(ev_w_in[:, np.concatenate(cols)])
    od_w_in = inp["od_w_in"][0]
    ocols = np.concatenate([np.arange(0, 672), 640 + _SW, np.arange(672, 928), np.arange(928, 1184), np.arange(2208, 2216),
                            np.arange(1184, 1696), np.arange(1696, 2208)])
    od_ext = _c(od_w_in[:, ocols])
    wq = inp["od_w_q_up"][0]
    qcols = np.arange(768).reshape(8, 96).copy()
    qcols[:, 64:96] = qcols[:, 64:96][:, _SW]
    wq_sw = _c(wq[:, qcols.reshape(-1)])
    wkv = inp["od_w_kv_up"][0].reshape(256, 8, 128)
    wkv_r = _c(np.concatenate([wkv[:, :, :64].reshape(256, 512), wkv[:, :, 64:].reshape(256, 512)], axis=1))
    gq, gk = inp["od_g_q"][0], inp["od_g_k"][0]
    gsm = np.zeros((128, 8), np.float32)
    gsm[0:96, 0] = gq
    gsm[64:96, 1] = gq[64:96][_SW]
    gsm[0:64, 2] = gk[:64]; gsm[64:128, 2] = gk[:64]
    gsm[0:32, 4] = gk[64:96]; gsm[0:32, 5] = gk[64:96][_SW]
    g_lat = np.concatenate([_chunkT(inp["od_g_qa"][0]), _chunkT(inp["od_g_kva"][0])], axis=1)
    gains = np.zeros((128, 32), np.float32)
    gains[:, 0:8] = _chunkT(inp["g_mix"][0]); gains[:, 8:16] = _chunkT(inp["g_ffn"][0])
    gains[:, 16:24] = _chunkT(inp["g_ple"][0]); gains[:, 24:32] = _chunkT(inp["g_mix"][1])
    wconv = _c(inp["ev_w_conv"][0].T.reshape(4, 128, 3).transpose(1, 0, 2).reshape(128, 12))
    gvb = _c(np.tile(inp["ev_g_v"][0].reshape(1, 512), (128, 1)))
    wsT = _c(inp["ev_w_s"][0].transpose(2, 0, 1))
    bsb = _c(inp["ev_b_s"][0].reshape(4, 2, 1, 128).repeat(64, axis=2).reshape(4, 128, 128).transpose(1, 0, 2))
    for core in range(8):
        b, q = core // 4, core % 4
        s0 = q * NTOK
        m = prep_tok_common(inp, 0, core)
        m["hT"] = _c(x[b, s0:s0 + NTOK, :].T)
        m["xhT"] = _c(x[b, s0 - 2:s0, :].T) if q > 0 else np.zeros((1024, 2), np.float32)
        m["posb"] = _c(np.tile(inp["positions"][b, s0:s0 + NTOK].reshape(1, NTOK), (96, 1)).astype(np.int32))
        m.update({"gains": gains, "w_out": _c(inp["ev_w_out"][0]), "ev_w_in": ev_w_in_r, "wconv": wconv, "gvb": gvb, "wsT": wsT,
                  "bsb": bsb, "od_w_in": od_ext, "g_lat": _c(g_lat), "w_q_up": _c(wq), "w_q_up_sw": wq_sw, "w_kv_up": wkv_r, "gsm": gsm})
        maps.append(m)
    return maps


def prep_C(inp, h1T, ymT):
    maps = []
    gains = np.zeros((128, 32), np.float32)
    gains[:, 8:16] = _chunkT(inp["g_ffn"][1]); gains[:, 16:24] = _chunkT(inp["g_ple"][1])
    for core in range(8):
        m = prep_tok_common(inp, 1, core)
        m["hT"] = h1T[core]
        m["ymT"] = ymT[core]
        m["gains"] = gains
        m["w_out"] = _c(inp["od_w_out"][0])
        maps.append(m)
    return maps


SEQ = 8192


def build_mix(parts, fz):
    nc = fz["nc"]
    P = Prog(nc, ("M_" if "m" in parts else "T_"))
    C = Ctx(P, nbanks=6)
    op = P.op
    D = fz["D"]
    ptb = [P.ps("ptb%d" % i, [128, 1024], BF16) for i in range(2)]
    idx = P.sb("idx", [128, 64], I32)
    P.dma("sp", idx[:, :], D["idx"], writes=[idx])
    R1 = D["recv1"]

    def gat(t, rows, q, col):
        for hf in range(2):
            c0_ = q * NTOK + hf * 1024
            P.gather(t, t[0:rows, c0_:c0_ + 1024], R1[hf][:, :], idx, idx[0:rows, col:col + 1])
    send2 = D["send2"]

    cB = P.sb("cB", [128, 512])
    bcol = P.sb("bcol", [128, 2]); gmh = P.sb("gmh", [128, 128])
    P.dma("sp", cB[:, :], D["cB"], writes=[cB]); P.dma("sp", bcol[:, :], D["bcol"], writes=[bcol]); P.dma("sp", gmh[:, :], D["gmh"], writes=[gmh])
    maskb = P.sb("maskb", [128, 128], BF16)
    op("dve", lambda e: e.tensor_copy(maskb[:, :], cB[:, 0:128]), reads=[cB], writes=[maskb])
    idb = P.sb("idb", [128, 128], BF16)
    op("dve", lambda e: e.tensor_copy(idb[:, :], cB[:, 256:384]), reads=[cB], writes=[idb])
    ones_f = P.sb("ones_f", [128, 128])
    op("pool", lambda e: e.memset(ones_f[:, :], 1.0), writes=[ones_f])

    if True:
        pass
    if "s" in parts:
        gi = P.sb("gi", [128, 64]); gf = P.sb("gf", [128, 64])
        rmv = D["recvm"].rearrange("r (c t) -> (r c) t", t=64)
        P.gather(gi, gi[:, :], rmv, idx, idx[:, 40:41])
        P.gather(gf, gf[:, :], rmv, idx, idx[:, 41:42])
        zer = P.sb("zer", [128, 128]); op("pool", lambda e: e.memset(zer[:, :], 0.0), writes=[zer])
        nbf = P.sb("nbf", [128, 1])
        op("dve", lambda e: e.tensor_scalar(nbf[:, :], bcol[:, 1:2], -1.0, None, op0=ALU.mult), reads=[bcol], writes=[nbf])
        e1 = P.sb("e1", [128, 64]); lf = P.sb("lf", [128, 64]); Al = P.sb("Al", [128, 64]); vv = P.sb("vv", [128, 64])
        Ml = P.sb("Ml", [128, 64]); Mp = P.sb("Mp", [128, 64])
        op("act", lambda e: e.activation(e1[:, :], gf[:, :], AF.Exp, bias=nbf[:, 0:1], scale=-1.0), reads=[gf, nbf], writes=[e1])
        op("act", lambda e: e.activation(lf[:, :], e1[:, :], AF.Ln, bias=1.0), reads=[e1], writes=[lf])
        op("dve", lambda e: e.tensor_scalar(lf[:, :], lf[:, :], -1.0, None, op0=ALU.mult), reads=[lf], writes=[lf])
        op("dve", lambda e: e.tensor_tensor_scan(Al[:, :], lf[:, :], zer[:, 0:64], 0.0, op0=ALU.add, op1=ALU.add), reads=[lf, zer], writes=[Al])

        def col2row(col_ap, reads):
            ps = C.psum()
            op("pe", lambda e: e.matmul(ps[0:1, 0:128], col_ap, cB[:, 256:384], start=True, stop=True), reads=reads + [cB], writes=[ps])
            return ps

        rows = P.sb("rows", [1, 8, 128])
        ps = col2row(Al[:, 63:64], [Al])
        op("dve", lambda e, ps=ps: e.tensor_copy(rows[:, 0, :], ps[0:1, 0:128]), reads=[ps], writes=[rows])
        op("dve", lambda e: e.tensor_tensor_scan(rows[:, 1, :], rows[:, 0, :], zer[0:1, :], 0.0, op0=ALU.add, op1=ALU.add), reads=[rows, zer], writes=[rows])
        op("dve", lambda e: e.tensor_tensor(rows[:, 2, :], rows[:, 1, :], rows[:, 0, :], op=ALU.subtract), reads=[rows], writes=[rows])
        cols_ps = C.psum(pin=True)

        def row2col(k, j):
            op("pe", lambda e: e.matmul(cols_ps[:, j:j + 1], rows[0:1, k, :], ones_f[0:1, 0:1], start=True, stop=True), reads=[rows, ones_f], writes=[cols_ps])

        row2col(2, 0)
        cols = P.sb("cols", [128, 8])
        op("dve", lambda e: e.tensor_copy(cols[:, 0:1], cols_ps[:, 0:1]), reads=[cols_ps], writes=[cols])
        op("dve", lambda e: e.tensor_scalar(Al[:, :], Al[:, :], cols[:, 0:1], None, op0=ALU.add), reads=[Al, cols], writes=[Al])
        op("dve", lambda e: e.scalar_tensor_tensor(vv[:, :], gi[:, :], bcol[:, 0:1], Al[:, :], op0=ALU.add, op1=ALU.subtract), reads=[gi, bcol, Al], writes=[vv])
        op("dve", lambda e: e.tensor_tensor_scan(Ml[:, :], vv[:, :], vv[:, :], -1e30, op0=ALU.max, op1=ALU.max), reads=[vv], writes=[Ml])
        ps = col2row(Ml[:, 63:64], [Ml])
        op("dve", lambda e, ps=ps: e.tensor_copy(rows[:, 3, :], ps[0:1, 0:128]), reads=[ps], writes=[rows])
        op("dve", lambda e: e.tensor_tensor_scan(rows[:, 4, :], rows[:, 3, :], rows[:, 3, :], 0.0, op0=ALU.max, op1=ALU.max), reads=[rows], writes=[rows])
        op("dve", lambda e: e.memset(rows[:, 5, 0:1], 0.0), reads=[], writes=[rows])
        op("dve", lambda e: e.tensor_copy(rows[:, 5, 1:128], rows[:, 4, 0:127]), reads=[rows], writes=[rows])
        r4 = rows[:, 4, :].rearrange("p (c two) -> p c two", two=2)
        r6 = rows[:, 6, :].rearrange("p (c two) -> p c two", two=2)
        op("dve", lambda e: e.tensor_copy(r6[:, :, 0:1], r4[:, :, 1:2]), reads=[rows], writes=[rows])
        op("dve", lambda e: e.tensor_copy(r6[:, :, 1:2], r4[:, :, 1:2]), reads=[rows], writes=[rows])
        op("dve", lambda e: e.tensor_tensor(rows[:, 7, :], rows[:, 5, :], rows[:, 4, :], op=ALU.subtract), reads=[rows], writes=[rows])
        op("act", lambda e: e.activation(rows[:, 7, :], rows[:, 7, :], AF.Exp), reads=[rows], writes=[rows])
        row2col(5, 1); row2col(4, 2); row2col(6, 3)
        op("dve", lambda e: e.tensor_copy(cols[:, 1:4], cols_ps[:, 1:4]), reads=[cols_ps], writes=[cols])
        C.unpin()
        op("dve", lambda e: e.tensor_scalar(Mp[:, :], Ml[:, :], cols[:, 1:2], 0.0, op0=ALU.max, op1=ALU.max), reads=[Ml, cols], writes=[Mp])
        dec_b = P.sb("dec_b", [64, 128])
        ps = C.psum()
        op("pe", lambda e, ps=ps: e.matmul(ps[0:64, 0:128], ones_f[0:1, 0:64], rows[0:1, 7, :], start=True, stop=True), reads=[ones_f, rows], writes=[ps])
        op("dve", lambda e, ps=ps: e.tensor_copy(dec_b[:, :], ps[0:64, 0:128]), reads=[ps], writes=[dec_b])
        fac = P.sb("fac", [128, 5, 128])
        tmpc = P.sb("tmpc", [128, 64])
        op("dve", lambda e: e.tensor_scalar(tmpc[:, :], vv[:, :], cols[:, 3:4], None, op0=ALU.subtract), reads=[vv, cols], writes=[tmpc])
        op("act", lambda e: e.activation(fac[:, 0, 0:64], tmpc[:, :], AF.Exp), reads=[tmpc], writes=[fac])
        op("dve", lambda e: e.tensor_scalar(tmpc[:, :], Mp[:, :], cols[:, 3:4], None, op0=ALU.subtract), reads=[Mp, cols], writes=[tmpc])
        op("act", lambda e: e.activation(fac[:, 1, 0:64], tmpc[:, :], AF.Exp, scale=-1.0), reads=[tmpc], writes=[fac])
        op("dve", lambda e: e.tensor_scalar(tmpc[:, :], Mp[:, :], cols[:, 1:2], None, op0=ALU.subtract), reads=[Mp, cols], writes=[tmpc])
        op("act", lambda e: e.activation(fac[:, 2, 0:64], tmpc[:, :], AF.Exp, scale=-1.0), reads=[tmpc], writes=[fac])
        op("dve", lambda e: e.tensor_scalar(tmpc[:, :], vv[:, :], cols[:, 2:3], None, op0=ALU.subtract), reads=[vv, cols], writes=[tmpc])
        op("act", lambda e: e.activation(fac[:, 3, 0:64], tmpc[:, :], AF.Exp), reads=[tmpc], writes=[fac])
        op("dve", lambda e: e.tensor_tensor(tmpc[:, :], Al[:, :], Mp[:, :], op=ALU.add), reads=[Al, Mp], writes=[tmpc])
        op("act", lambda e: e.activation(fac[:, 4, 0:64], tmpc[:, :], AF.Exp, scale=-1.0), reads=[tmpc], writes=[fac])
        op("dve", lambda e: e.tensor_copy(fac[:, :, 64:128], fac[:, :, 0:64]), reads=[fac], writes=[fac])
        fcol = P.sb("fcol", [128, 5, 64])
        for k in range(5):
            ps = C.psum()
            op("pe", lambda e, k=k, ps=ps: e.transpose(ps[:, 0:128], fac[:, k, :], cB[:, 256:384]), reads=[fac, cB], writes=[ps])
            pv = ps[:, 0:128].rearrange("p (c two) -> p c two", two=2)
            op("dve", lambda e, k=k, ps=ps: e.tensor_copy(fcol[0:64, k, :], ps[0:64, 0:128].rearrange("p (c two) -> p c two", two=2)[:, :, 0]), reads=[ps], writes=[fcol])
            op("dve", lambda e, k=k, ps=ps: e.tensor_copy(fcol[64:128, k, :], ps[64:128, 0:128].rearrange("p (c two) -> p c two", two=2)[:, :, 1]), reads=[ps], writes=[fcol])

    if "m" in parts:
        mq = P.sb("mq", [64, SEQ], BF16); mk = P.sb("mk", [64, SEQ], BF16)
        ktok = P.sb("ktok", [128, 64, 64], BF16); vext = P.sb("vext", [128, 64, 129], BF16); so = P.sb("so", [128, 64, 128], BF16)
        Call = P.sb("Call", [64, 128, 129], BF16); Sxs = [P.sb("Sx%d" % i, [64, 129]) for i in range(2)]
        mvs = P.sb("mvs", [128, SEQ], BF16); sos = P.sb("sos", [128, SEQ], BF16)
        for q in range(4):
            cs_ = slice(q * NTOK, (q + 1) * NTOK)
            gat(mq, 64, q, 24 + q); gat(mk, 64, q, 28 + q); gat(mvs, 128, q, 32 + q); gat(sos, 128, q, 36 + q)
        op("pool", lambda e: e.memset(vext[:, :, :], 1.0), writes=[vext])
        tcount = [0]
        def tr_group(src, rows, nblk_per, width, dst_fn, g):
            pt_ = ptb[tcount[0] % 2]; tcount[0] += 1
            for b_ in range(nblk_per):
                blk = g * nblk_per + b_
                op("pe", lambda e, pt_=pt_, b_=b_, blk=blk: e.transpose(pt_[:, b_ * width:(b_ + 1) * width], src[0:rows, blk * 128:(blk + 1) * 128], idb[0:rows, 0:rows]),
                   reads=[src, idb], writes=[pt_])
            dst_t, dst_ap = dst_fn(g)
            eng = "act" if tcount[0] % 2 == 0 else "dve"
            if eng == "act":
                op("act", lambda e, pt_=pt_, dst_ap=dst_ap: e.activation(dst_ap, pt_[:, 0:nblk_per * width].rearrange("p (b w) -> p b w", w=width), AF.Copy), reads=[pt_], writes=[dst_t])
            else:
                op("dve", lambda e, pt_=pt_, dst_ap=dst_ap: e.tensor_copy(dst_ap, pt_[:, 0:nblk_per * width].rearrange("p (b w) -> p b w", w=width)), reads=[pt_], writes=[dst_t])
        for g in range(4):
            tr_group(mk, 64, 16, 64, lambda g: (ktok, ktok[:, g * 16:(g + 1) * 16, :]), g)
        for g in range(8):
            tr_group(mvs, 128, 8, 128, lambda g: (vext, vext[:, g * 8:(g + 1) * 8, 0:128]), g)
            tr_group(sos, 128, 8, 128, lambda g: (so, so[:, g * 8:(g + 1) * 8, :]), g)
        op("pool", lambda e: e.memset(Sxs[0][:, :], 0.0), writes=[Sxs[0]])
        for blk in range(64):
            kw = C.tmp("kw", [128, 64], BF16, 3)
            op("dve", lambda e, blk=blk, kw=kw: e.tensor_scalar(kw[:, :], ktok[:, blk, :], fcol[:, 3, blk:blk + 1], None, op0=ALU.mult), reads=[ktok, fcol], writes=[kw])
            for half in range(2):
                c = 2 * blk + half
                Sa = Sxs[c % 2]; Sb = Sxs[(c + 1) % 2]
                op("act", lambda e, c=c, Sa=Sa: e.activation(Call[:, c, :], Sa[:, :], AF.Copy), reads=[Sa], writes=[Call])
                pu = C.psum()
                op("pe", lambda e, pu=pu, kw=kw, half=half, blk=blk: e.matmul(pu[0:64, 0:129], kw[64 * half:64 * half + 64, :], vext[64 * half:64 * half + 64, blk, :], start=True, stop=True),
                   reads=[kw, vext], writes=[pu])
                op("dve", lambda e, pu=pu, c=c, Sa=Sa, Sb=Sb: e.scalar_tensor_tensor(Sb[:, :], Sa[:, :], dec_b[:, c:c + 1], pu[0:64, 0:129], op0=ALU.mult, op1=ALU.add), reads=[Sa, dec_b, pu], writes=[Sb])
        if "2" in parts:
            def stX(blk):
                t0 = blk * 128
                pS = C.psum()
                op("pe", lambda e, pS=pS, t0=t0: e.matmul(pS[:, 0:128], mk[:, t0:t0 + 128], mq[:, t0:t0 + 128], start=True, stop=True), reads=[mk, mq], writes=[pS])
                pt = C.tmp("mpt", [128, 128], BF16, 3)
                op("dve", lambda e, pS=pS, pt=pt, blk=blk: e.scalar_tensor_tensor(pt[:, :], pS[:, 0:128], fcol[:, 0, blk:blk + 1], cB[:, 128:256], op0=ALU.mult, op1=ALU.mult), reads=[pS, fcol, cB], writes=[pt])
                return pt

            def stY(blk, pt):
                t0 = blk * 128
                po1 = C.psum(); po2 = C.psum()
                op("pe", lambda e, po1=po1, pt=pt, blk=blk: e.matmul(po1[:, 0:129], pt[:, :], vext[:, blk, :], start=True, stop=True), reads=[pt, vext], writes=[po1])
                for half in range(2):
                    op("pe", lambda e, po2=po2, half=half, blk=blk, t0=t0: e.matmul(po2[64 * half:64 * half + 64, 0:129], mq[:, t0 + 64 * half:t0 + 64 * half + 64], Call[:, 2 * blk + half, :], start=True, stop=True),
                       reads=[mq, Call], writes=[po2])
                o1 = C.tmp("o1", [128, 129], F32, 2); hnum = C.tmp("hnum", [128, 129], F32, 2)
                op("dve", lambda e, o1=o1, po1=po1, blk=blk: e.tensor_scalar(o1[:, :], po1[:, 0:129], fcol[:, 1, blk:blk + 1], None, op0=ALU.mult), reads=[po1, fcol], writes=[o1])
                op("dve", lambda e, o1=o1, po2=po2, hnum=hnum, blk=blk: e.scalar_tensor_tensor(hnum[:, :], po2[:, 0:129], fcol[:, 2, blk:blk + 1], o1[:, :], op0=ALU.mult, op1=ALU.add), reads=[po2, fcol, o1], writes=[hnum])
                sm = C.tmp("sm", [128, 8], F32, 2)
                op("dve", lambda e, sm=sm, hnum=hnum: e.scalar_tensor_tensor(sm[:, 6:7], hnum[:, 128:129], -1.0, hnum[:, 128:129], op0=ALU.mult, op1=ALU.max), reads=[hnum], writes=[sm])
                op("dve", lambda e, sm=sm, blk=blk: e.tensor_tensor(sm[:, 0:1], sm[:, 6:7], fcol[:, 4, blk:blk + 1], op=ALU.max), reads=[sm, fcol], writes=[sm])
                op("dve", lambda e, sm=sm: e.reciprocal(sm[:, 1:2], sm[:, 0:1]), reads=[sm], writes=[sm])
                junk = C.tmp("junk", [128, 128], F32, 2)
                op("dve", lambda e, hnum=hnum, sm=sm: e.tensor_scalar(hnum[:, 0:128], hnum[:, 0:128], sm[:, 1:2], None, op0=ALU.mult), reads=[hnum, sm], writes=[hnum])
                op("act", lambda e, junk=junk, hnum=hnum, sm=sm: e.activation(junk[:, :], hnum[:, 0:128], AF.Square, accum_out=sm[:, 2:3]), reads=[hnum], writes=[junk, sm])
                op("act", lambda e, sm=sm: e.activation(sm[:, 3:4], sm[:, 2:3], AF.Sqrt, bias=EPS, scale=1.0 / 128.0), reads=[sm], writes=[sm])
                op("dve", lambda e, sm=sm: e.reciprocal(sm[:, 4:5], sm[:, 3:4]), reads=[sm], writes=[sm])
                yt = C.tmp("yt", [128, 128], F32, 2); yd = C.tmp("yd", [128, 128], F32, 3)
                op("dve", lambda e, yt=yt, hnum=hnum, sm=sm: e.scalar_tensor_tensor(yt[:, :], hnum[:, 0:128], sm[:, 4:5], gmh[:, :], op0=ALU.mult, op1=ALU.mult), reads=[hnum, sm, gmh], writes=[yt])
                op("pool", lambda e, yt=yt, yd=yd, blk=blk: e.tensor_tensor(yd[:, :], yt[:, :], so[:, blk, :], op=ALU.mult), reads=[yt, so], writes=[yd])
                return yd

            zst = {"pT": None}

            def stZ(blk, yd):
                g4, j4 = blk // 4, blk % 4
                if j4 == 0:
                    zst["pT"] = C.psum(pin=True)
                pT = zst["pT"]
                op("pe", lambda e, pT=pT, yd=yd, j4=j4: e.transpose(pT[:, j4 * 128:(j4 + 1) * 128], yd[:, :], cB[:, 256:384]), reads=[yd, cB], writes=[pT])
                if j4 == 3:
                    ob = C.tmp("ydo", [128, 512], BF16, 2)
                    op("act", lambda e, ob=ob, pT=pT: e.activation(ob[:, :], pT[:, :], AF.Copy), reads=[pT], writes=[ob])
                    P.dma("sp", send2[(g4 // 4) * 256 + 128:(g4 // 4) * 256 + 256, (g4 % 4) * 512:(g4 % 4) * 512 + 512], ob[:, :], reads=[ob])
                    C.unpin()

            pts = {0: stX(0)}
            yds = {}
            for blk in range(64):
                if blk + 1 < 64:
                    pts[blk + 1] = stX(blk + 1)
                yds[blk] = stY(blk, pts.pop(blk))
                if blk >= 1:
                    stZ(blk - 1, yds.pop(blk - 1))
            stZ(63, yds.pop(63))

    if "a" in parts:
        qTs = [P.sb("qT%d" % i, [96, SEQ], BF16) for i in range(2)]; kTs = [P.sb("kT%d" % i, [96, SEQ], BF16) for i in range(2)]
        V = P.sb("V", [128, 64, 65], BF16); vTss = [P.sb("vTs%d" % i, [64, SEQ], BF16) for i in range(2)]
        for hh in range(2):
            for q in range(4):
                cs_ = slice(q * NTOK, (q + 1) * NTOK)
                gat(qTs[hh], 96, q, hh * 4 + q); gat(kTs[hh], 96, q, 8 + hh * 4 + q); gat(vTss[hh], 64, q, 16 + hh * 4 + q)
        tails = []
        for hh in range(2):
            qT = qTs[hh]; kT = kTs[hh]; vTs = vTss[hh]
            op("pool", lambda e: e.memset(V[:, :, :], 1.0), writes=[V])
            for g in range(4):
                pt_ = ptb[g % 2]
                for b_ in range(16):
                    blk = g * 16 + b_
                    op("pe", lambda e, pt_=pt_, b_=b_, blk=blk, vTs=vTs: e.transpose(pt_[:, b_ * 64:(b_ + 1) * 64], vTs[0:64, blk * 128:(blk + 1) * 128], idb[0:64, 0:64]), reads=[vTs, idb], writes=[pt_])
                op("dve", lambda e, pt_=pt_, g=g: e.tensor_copy(V[:, g * 16:(g + 1) * 16, 0:64], pt_[:, :].rearrange("p (b w) -> p b w", w=64)), reads=[pt_], writes=[V])
            for qt in range(16):
                po = C.psum(pin=True)
                nkb = 4 * qt + 4
                LOOK = 3
                pend = []
                for it in range(nkb + LOOK):
                    if it < nkb:
                        kb = it
                        c0 = max(0, kb - 4 * qt) * 128
                        ps = C.psum()
                        op("pe", lambda e, ps=ps, kb=kb, c0=c0, qt=qt, kT=kT, qT=qT: e.matmul(ps[:, c0:512], kT[:, kb * 128:(kb + 1) * 128], qT[:, qt * 512 + c0:qt * 512 + 512], start=True, stop=True), reads=[kT, qT], writes=[ps])
                        pt = C.tmp("apt", [128, 512], BF16, 6)
                        op("act", lambda e, ps=ps, pt=pt, c0=c0: e.activation(pt[:, c0:512], ps[:, c0:512], AF.Exp), reads=[ps], writes=[pt])
                        if kb >= 4 * qt:
                            op("pool", lambda e, pt=pt, c0=c0: e.tensor_tensor(pt[:, c0:c0 + 128], pt[:, c0:c0 + 128], maskb[:, :], op=ALU.mult), reads=[pt, maskb], writes=[pt])
                        pend.append((kb, c0, pt))
                    if it == 2 and tails:
                        tails.pop(0)()
                    if it >= LOOK:
                        kb, c0, pt = pend[it - LOOK]
                        op("pe", lambda e, po=po, pt=pt, kb=kb, c0=c0, nkb=nkb: e.matmul(po[0:65, c0:512], V[:, kb, :], pt[:, c0:512], start=(kb == 0), stop=(kb == nkb - 1)), reads=[V, pt], writes=[po])
                osb = C.tmp("osb", [65, 512], F32, 2, own=True)
                op("act", lambda e, osb=osb, po=po: e.activation(osb[:, :], po[0:65, :], AF.Copy), reads=[po], writes=[osb])
                C.unpin()
                op("dve", lambda e, osb=osb: e.reciprocal(osb[64:65, :], osb[64:65, :]), reads=[osb], writes=[osb])

                def tail(osb=osb, hh=hh, qt=qt):
                    pb = C.psum()
                    op("pe", lambda e, pb=pb, osb=osb: e.matmul(pb[0:64, :], ones_f[64:65, 0:64], osb[64:65, :], start=True, stop=True), reads=[ones_f, osb], writes=[pb])
                    yc = C.tmp("yc", [64, 512], BF16, 2)
                    op("dve", lambda e, yc=yc, osb=osb, pb=pb: e.tensor_tensor(yc[:, :], osb[0:64, :], pb[0:64, :], op=ALU.mult), reads=[osb, pb], writes=[yc])
                    P.dma("sp", send2[(qt // 4) * 256 + hh * 64:(qt // 4) * 256 + hh * 64 + 64, (qt % 4) * 512:(qt % 4) * 512 + 512], yc[:, :], reads=[yc])
                tails.append(tail)
        while tails:
            tails.pop(0)()
    P.wait_all("sp", P.tiles)
    return P


R1ROWS = 3360
Q0, KN0, KP0, V0, MQ0, MK0, MV0, SO0 = 0, 768, 1280, 1312, 1824, 2080, 2336, 2848
GROUPS = [[0, 1, 2, 3], [4, 5, 6, 7]]
CCH = 240
CCH1 = 480


def _chunks(total, cch=CCH):
    out = []
    start = 0
    base = 0
    while start < total:
        size = min(cch, total - start)
        out.append((start, size, base))
        base += 4 * size
        start += size
    return out


def _rowmap(total, r, q, cch=CCH):
    k = r // cch
    start = k * cch
    size = np.minimum(cch, total - start)
    return 4 * start + q * size + (r - start)


def _gather_all(P, send, recv, total, cch=CCH, wait=True):
    keys = []
    for (start, size, base) in _chunks(total, cch):
        keys.append(P.collective_raw("AllGather", send[start:start + size, :], recv[base:base + 4 * size, :], GROUPS, wait=False))
    if wait:
        P.ops["pool"].append(([(k, 1) for k in keys], None, None))
    return keys


def build_fused(phases="AMTC"):
    nc = bass.Bass("TRN2", target_bir_lowering=False)
    send1 = [nc.dram_tensor("send1_%d" % i, [R1ROWS, 1024], BF16).ap() for i in range(2)]
    recv1 = [nc.dram_tensor("recv1_%d" % i, [4 * R1ROWS, 1024], BF16).ap() for i in range(2)]
    sendm = nc.dram_tensor("sendm", [8, NTOK], F32).ap()
    recvm = nc.dram_tensor("recvm", [32, NTOK], F32).ap()
    send2 = nc.dram_tensor("send2", [1024, NTOK], BF16).ap()
    recv2 = nc.dram_tensor("recv2", [4096, NTOK], BF16).ap()
    h1d = nc.dram_tensor("h1d", [1024, NTOK], F32).ap()
    idx = dram_in(nc, "idx", [128, 64], I32)
    consts = dram_in(nc, "consts", [128, 512])

    def tokD(L):
        sfx = str(L)
        return {
            "pT": dram_in(nc, "pT" + sfx, [256, NTOK]), "gains": dram_in(nc, "gains" + sfx, [128, 32]), "consts": consts,
            "w_gate": dram_in(nc, "w_gate" + sfx, [1024, 2816]), "w_up": dram_in(nc, "w_up" + sfx, [1024, 2816]),
            "w_down": dram_in(nc, "w_down" + sfx, [2816, 1024]), "w_ple_gate": dram_in(nc, "w_ple_gate" + sfx, [1024, 1024]),
            "w_ple_proj": dram_in(nc, "w_ple_proj" + sfx, [256, 1024]), "w_out": dram_in(nc, "w_out" + sfx, [1024, 1024]),
        }
    DA = tokD(0)
    DA.update({
        "hT": dram_in(nc, "xT", [1024, NTOK]), "h1T": h1d,
        "mifT": sendm,
    })
    for hf in range(2):
        for nm, r0, nr in (("qTf", Q0, 768), ("knT", KN0, 512), ("kpeT", KP0, 32), ("vT", V0, 512), ("mqT", MQ0, 256), ("mkT", MK0, 256),
                           ("mvT", MV0, 512), ("soT", SO0, 512)):
            DA["%s_h%d" % (nm, hf)] = send1[hf][r0:r0 + nr, :]
    for nm in ("qT", "knT", "kpeT", "vT", "mqT", "mkT", "mvT", "soT"):
        DA[nm] = send1[0]
    keys1 = []

    def mid_hook(P, snap):
        wd = P.waited["pool"]
        waits = [(k, v) for (k, v) in snap if wd.get(k, 0) < v]
        for k, v in waits:
            wd[k] = v
        P.ops["pool"].append((waits, None, None))
        for (start, size, base) in _chunks(R1ROWS, CCH1):
            key = "cc%d" % P.n_dsem
            P.n_dsem += 1
            P.sem_names.append(key)
            i_ap = send1[0][start:start + size, :]; o_ap = recv1[0][base:base + 4 * size, :]
            P.ops["pool"].append(([], (lambda e, i_ap=i_ap, o_ap=o_ap: e.collective_compute("AllGather", ALU.bypass, replica_groups=GROUPS, ins=[i_ap.opt()], outs=[o_ap.opt()])), (key, None)))
            keys1.append(key)
    PA = build_tok("A", {"nc": nc, "D": DA, "mid_hook": mid_hook})
    PA.wait_all("pool", PA.tiles)
    keys1.extend(_gather_all(PA, send1[1], recv1[1], R1ROWS, CCH1, wait=False))
    PA.ops["pool"].append(([(k, 1) for k in keys1], None, None))
    PA.collective_raw("AllGather", sendm, recvm, GROUPS)
    PA.emit()
    nc.all_engine_barrier()
    if "M" not in phases and "T" not in phases and "C" not in phases:
        dram_out(nc, "outT", [1024, NTOK])
        return nc

    DB = {"idx": idx, "recv1": recv1, "recvm": recvm, "send2": send2,
          "cB": dram_in(nc, "cB", [128, 512]), "bcol": dram_in(nc, "bcol", [128, 2]), "gmh": dram_in(nc, "gmh", [128, 128])}
    if "M" in phases:
        PM = build_mix("sm2", {"nc": nc, "D": DB})
        PM.wait_all("pool", PM.tiles)
        PM.emit()
        nc.all_engine_barrier()
    if "T" in phases:
        PT = build_mix("a", {"nc": nc, "D": DB})
        _gather_all(PT, send2, recv2, 1024)
        PT.emit()
        nc.all_engine_barrier()
    if "C" not in phases:
        dram_out(nc, "outT", [1024, NTOK])
        return nc

    DC = tokD(1)
    DC.update({"hT": h1d, "idx": idx, "recv2": recv2, "outT": dram_out(nc, "outT", [1024, NTOK])})
    if "D" in phases:
        DC["dbg_y"] = dram_out(nc, "dbg_y", [1024, NTOK], BF16)
    PC = build_tok("C", {"nc": nc, "D": DC})
    PC.wait_all("pool", PC.tiles)
    if "D" in phases:
        d1 = dram_out(nc, "dbg_send1", [R1ROWS, NTOK], BF16); d2 = dram_out(nc, "dbg_send2", [1024, NTOK], BF16); d3 = dram_out(nc, "dbg_sendm", [8, NTOK]); d4 = dram_out(nc, "dbg_h1", [1024, NTOK])
        dummy = PC.sb("dbgdummy", [1, 8])
        PC.dma("sp", d1, send1, reads=[dummy]); PC.dma("sp", d2, send2, reads=[dummy]); PC.dma("sp", d3, sendm, reads=[dummy]); PC.dma("sp", d4, h1d, reads=[dummy])
        PC.wait_all("sp", [dummy])
    PC.emit()
    return nc


def prep_fused(inp):
    mapsA = prep_A(inp)
    maps = []
    cB = np.zeros((128, 512), np.float32)
    s = np.arange(128)
    cB[:, 0:128] = (s[:, None] <= s[None, :])
    cB[:, 128:256] = (s[:, None] <= s[None, :]) & ((s[:, None] // 64) == (s[None, :] // 64))
    cB[:, 256:384] = np.eye(128, dtype=np.float32)
    gains1 = np.zeros((128, 32), np.float32)
    gains1[:, 8:16] = _chunkT(inp["g_ffn"][1]); gains1[:, 16:24] = _chunkT(inp["g_ple"][1])
    bg = inp["od_b_gate"][0]
    p = np.arange(128)
    for core in range(8):
        b, j = core // 4, core % 4
        q_ = j
        a = mapsA[core]
        m = {"xT": a["hT"], "xhT": a["xhT"], "posb": a["posb"], "consts": a["consts"], "idx": None,
             "pT0": a["pT"], "gains0": a["gains"], "w_gate0": a["w_gate"], "w_up0": a["w_up"], "w_down0": a["w_down"],
             "w_ple_gate0": a["w_ple_gate"], "w_ple_proj0": a["w_ple_proj"], "w_out0": a["w_out"]}
        for k in ("ev_w_in", "wconv", "gvb", "wsT", "bsb", "od_w_in", "g_lat", "w_q_up", "w_q_up_sw", "w_kv_up", "gsm"):
            m[k] = a[k]
        c1 = prep_tok_common(inp, 1, core)
        m.update({"pT1": c1["pT"], "gains1": gains1, "w_gate1": c1["w_gate"], "w_up1": c1["w_up"], "w_down1": c1["w_down"],
                  "w_ple_gate1": c1["w_ple_gate"], "w_ple_proj1": c1["w_ple_proj"], "w_out1": _c(inp["od_w_out"][0])})
        m["cB"] = cB
        m["bcol"] = _c(np.tile(np.array([[bg[j], bg[4 + j]]], np.float32), (128, 1)))
        m["gmh"] = _c(np.tile(inp["od_g_mh"][0][j].reshape(1, 128), (128, 1)))
        ix = np.zeros((128, 64), np.int32)
        def rm(r, q):
            return _rowmap(R1ROWS, np.asarray(r), q, CCH1)
        for q in range(4):
            for hh in range(2):
                hd = 2 * j + hh
                ix[0:96, hh * 4 + q] = rm(Q0 + hd * 96 + p[0:96], q)
                ix[0:64, 8 + hh * 4 + q] = rm(KN0 + hd * 64 + p[0:64], q)
                ix[64:96, 8 + hh * 4 + q] = rm(KP0 + p[0:32], q)
                ix[0:64, 16 + hh * 4 + q] = rm(V0 + hd * 64 + p[0:64], q)
            ix[0:64, 24 + q] = rm(MQ0 + j * 64 + p[0:64], q)
            ix[0:64, 28 + q] = rm(MK0 + j * 64 + p[0:64], q)
            ix[:, 32 + q] = rm(MV0 + j * 128 + p, q)
            ix[:, 36 + q] = rm(SO0 + j * 128 + p, q)
            ix[q * 32:(q + 1) * 32, 40] = (q * 8 + j) * 32 + p[0:32]
            ix[q * 32:(q + 1) * 32, 41] = (q * 8 + 4 + j) * 32 + p[0:32]
        for c in range(8):
            if c < 4:
                ix[:, 48 + c] = _rowmap(1024, q_ * 256 + p, c)
            else:
                ix[:, 48 + c] = _rowmap(1024, q_ * 256 + 128 + p, c - 4)
        m["idx"] = ix
        maps.append(m)
    return maps


_NC_CACHE = {}


def kernel(**inputs):
    inp = {k: np.asarray(v) for k, v in inputs.items()}
    if "nc" not in _NC_CACHE:
        _NC_CACHE["nc"] = build_fused()
    res = run_bass_kernel_spmd(_NC_CACHE["nc"], prep_fused(inp), core_ids=list(range(8))).results
    out = np.zeros((2, SEQ, 1024), np.float32)
    for core in range(8):
        b, q = core // 4, core % 4
        out[b, q * NTOK:(q + 1) * NTOK, :] = res[core]["outT"].T
    return out
```

```python
import numpy as np
import ml_dtypes
import concourse.bass as bass
import concourse.mybir as mybir
from concourse.bass_utils import run_bass_kernel_spmd

F32 = mybir.dt.float32
BF16 = mybir.dt.bfloat16
I32 = mybir.dt.int32
AF = mybir.ActivationFunctionType
ALU = mybir.AluOpType
AX = mybir.AxisListType
EPS = 1e-6
TWO_PI = 6.283185307179586
PI = 3.141592653589793
NTOK = 2048
TS = 512
class TT:
    __slots__ = ("h", "name", "last_w", "readers", "dsem", "dcount")

    def __init__(self, h, name):
        self.h = h
        self.name = name
        self.last_w = None
        self.readers = []
        self.dsem = None
        self.dcount = 0

    def __getitem__(self, idx):
        return self.h[idx]


class Prog:
    ENGS = ("pe", "act", "dve", "pool", "sp")

    def __init__(self, nc, prefix=""):
        self.nc = nc
        self.prefix = prefix
        self.ops = {e: [] for e in self.ENGS}
        self.count = {e: 0 for e in self.ENGS}
        self.waited = {e: {} for e in self.ENGS}
        self.sem_names = ["eng_" + e for e in self.ENGS]
        self.tiles = []
        self._ctx = []
        self.n_dsem = 0

    def sb(self, name, shape, dt=F32):
        g = self.nc.sbuf_tensor(self.prefix + "s_" + name, list(shape), dt)
        h = g.__enter__()
        self._ctx.append(g)
        t = TT(h, name)
        self.tiles.append(t)
        return t

    def ps(self, name, shape, dt=F32):
        g = self.nc.psum_tensor(self.prefix + "p_" + name, list(shape), dt)
        h = g.__enter__()
        self._ctx.append(g)
        t = TT(h, name)
        self.tiles.append(t)
        return t

    def _deps(self, eng, reads, writes):
        deps = []
        for r in reads:
            if r.last_w is not None:
                deps.append(r.last_w)
        for w in writes:
            if w.last_w is not None:
                deps.append(w.last_w)
            deps.extend(w.readers)
        waits = []
        wd = self.waited[eng]
        best = {}
        for (k, v) in deps:
            if eng == "pe" and k == "eng_pe":
                continue
            if wd.get(k, 0) >= v:
                continue
            if best.get(k, 0) < v:
                best[k] = v
        for k, v in best.items():
            wd[k] = v
            waits.append((k, v))
        return waits

    def op(self, eng, fn, reads=(), writes=()):
        waits = self._deps(eng, reads, writes)
        self.count[eng] += 1
        me = ("eng_" + eng, self.count[eng])
        self.ops[eng].append((waits, fn, (me[0], 1)))
        for r in reads:
            r.readers.append(me)
        for w in writes:
            w.last_w = me
            w.readers = []
        return me

    def dma(self, eng, out_ap, in_ap, reads=(), writes=(), **kw):
        waits = self._deps(eng, reads, writes)
        owner = (list(writes) + list(reads))[0]
        if owner.dsem is None:
            owner.dsem = "d%d" % self.n_dsem
            self.n_dsem += 1
            self.sem_names.append(owner.dsem)
        owner.dcount += 1
        me = (owner.dsem, 16 * owner.dcount)

        def fn(e, out_ap=out_ap, in_ap=in_ap, kw=kw):
            o_ = out_ap() if callable(out_ap) else out_ap
            i_ = in_ap() if callable(in_ap) else in_ap
            return e.dma_start(out=o_, in_=i_, **kw)
        self.ops[eng].append((waits, fn, (owner.dsem, 16)))
        for r in reads:
            r.readers.append(me)
        for w in writes:
            w.last_w = me
            w.readers = []
        return me

    def raw(self, eng, fn):
        self.ops[eng].append(([], fn, "raw"))

    def dram(self, name, shape, dt=F32):
        h = self.nc.dram_tensor(name, list(shape), dt).ap()
        t = TT(h, name)
        self.tiles.append(t)
        return t

    def collective(self, kind, in_t, out_t, groups):
        waits = self._deps("pool", [in_t], [out_t])
        key = "cc%d" % self.n_dsem
        self.n_dsem += 1
        self.sem_names.append(key)
        me = (key, 1)

        def fn(e):
            return e.collective_compute(kind, ALU.bypass, replica_groups=groups, ins=[in_t.h.opt()], outs=[out_t.h.opt()])
        self.ops["pool"].append((waits, fn, (key, None)))
        in_t.readers.append(me)
        out_t.last_w = me
        out_t.readers = []
        return me

    def gather(self, out_t, out_ap, in_ap, idx_t, idx_ap):
        waits = self._deps("pool", [idx_t], [out_t])
        if out_t.dsem is None:
            out_t.dsem = "d%d" % self.n_dsem
            self.n_dsem += 1
            self.sem_names.append(out_t.dsem)
        out_t.dcount += 1
        me = (out_t.dsem, 16 * out_t.dcount)

        def fn(e):
            return e.indirect_dma_start(out=out_ap, out_offset=None, in_=in_ap, in_offset=bass.IndirectOffsetOnAxis(ap=idx_ap, axis=0))
        self.ops["pool"].append((waits, fn, (out_t.dsem, 16)))
        idx_t.readers.append(me)
        out_t.last_w = me
        out_t.readers = []
        return me

    def collective_raw(self, kind, in_ap, out_ap, groups, wait=True):
        self.wait_all("pool", self.tiles)
        key = "cc%d" % self.n_dsem
        self.n_dsem += 1
        self.sem_names.append(key)

        def fn(e):
            return e.collective_compute(kind, ALU.bypass, replica_groups=groups, ins=[in_ap.opt()], outs=[out_ap.opt()])
        self.ops["pool"].append(([], fn, (key, None)))
        if wait:
            self.ops["pool"].append(([(key, 1)], None, None))
        return key

    def wait_all(self, eng, tiles):
        deps = []
        for t in tiles:
            if t.last_w is not None:
                deps.append(t.last_w)
            deps.extend(t.readers)
        wd = self.waited[eng]
        best = {}
        for k, v in deps:
            if wd.get(k, 0) < v and best.get(k, 0) < v:
                best[k] = v
        waits = []
        for k, v in best.items():
            wd[k] = v
            waits.append((k, v))
        self.ops[eng].append((waits, None, None))

    def emit(self):
        nc = self.nc
        sems = {}
        for n in self.sem_names:
            sems[n] = nc.alloc_semaphore(name=self.prefix + n)
        blk = nc.Block()
        block = blk.__enter__()

        def run(engname):
            def body(e):
                for waits, fn, inc in self.ops[engname]:
                    for k, v in waits:
                        e.wait_ge(sems[k], v)
                    if fn is not None:
                        ins = fn(e)
                        if inc == "raw":
                            continue
                        if inc[1] is None:
                            ins.then_inc(sems[inc[0]])
                        else:
                            ins.then_inc(sems[inc[0]], inc[1])
            return body

        block.tensor(run("pe"))
        block.scalar(run("act"))
        block.vector(run("dve"))
        block.gpsimd(run("pool"))
        block.sync(run("sp"))
        blk.__exit__(None, None, None)
        nc.all_engine_barrier()
        nc.clear_and_free_semaphores(list(sems.values()))
        nc.all_engine_barrier()
        for g in reversed(self._ctx):
            g.__exit__(None, None, None)
        self._ctx = []


class TV:
    def __init__(self, base, ap):
        self.__dict__["base"] = base
        self.__dict__["ap"] = ap

    def __getitem__(self, idx):
        return self.ap[idx]

    def __getattr__(self, k):
        return getattr(self.base, k)

    def __setattr__(self, k, v):
        setattr(self.base, k, v)


class Ctx:
    def __init__(self, P, nbanks=8):
        self.P = P
        self.nb = nbanks
        self.pbanks = [P.ps("pb%d" % i, [128, 512], F32) for i in range(nbanks)]
        self.pi = 0
        self.pinned = set()
        self.rings = {}

    def psum(self, pin=False):
        while (self.pi % self.nb) in self.pinned:
            self.pi += 1
        t = self.pbanks[self.pi % self.nb]
        if pin:
            self.pinned.add(self.pi % self.nb)
        self.pi += 1
        return t

    def unpin(self):
        self.pinned = set()

    def tmp(self, key, shape, dt=F32, n=2, own=False):
        if dt == F32 and len(shape) == 2 and shape[1] == 512 and not own:
            base = self.tmp("T32", [128, 512, 1], F32, 8)
            return TV(base, base.h[0:shape[0], :, 0])
        if key not in self.rings:
            self.rings[key] = [[self.P.sb("%s_%d" % (key, i), shape, dt) for i in range(n)], 0]
        r = self.rings[key]
        t = r[0][r[1] % len(r[0])]
        r[1] += 1
        return t


def dram_in(nc, name, shape, dt=F32):
    return nc.dram_tensor(name, list(shape), dt, kind="ExternalInput").ap()


def dram_out(nc, name, shape, dt=F32):
    return nc.dram_tensor(name, list(shape), dt, kind="ExternalOutput").ap()


def build_tok(mode, fz=None):
    nc = fz["nc"] if fz else bass.Bass("TRN2", target_bir_lowering=False)
    P = Prog(nc, mode + "_")
    C = Ctx(P)
    op = P.op
    L = 0 if mode == "A" else 1

    D = dict(fz["D"]) if fz else {}
    def din(name, shape, dt=F32):
        if name not in D:
            D[name] = dram_in(nc, name, shape, dt)
        return D[name]
    def dout(name, shape, dt=F32):
        if name not in D:
            D[name] = dram_out(nc, name, shape, dt)
        return D[name]

    din("hT", [1024, NTOK])
    din("pT", [256, NTOK])
    din("gains", [128, 32])
    din("consts", [128, 512])
    din("w_gate", [1024, 2816]); din("w_up", [1024, 2816]); din("w_down", [2816, 1024])
    din("w_ple_gate", [1024, 1024]); din("w_ple_proj", [256, 1024])
    din("w_out", [1024, 1024])
    if mode == "A":
        din("xhT", [1024, 2])
        din("posb", [96, NTOK], I32)
        din("ev_w_in", [1024, 2560])
        din("wconv", [128, 12]); din("gvb", [128, 512]); din("wsT", [128, 8, 128]); din("bsb", [128, 4, 128])
        din("od_w_in", [1024, 2248])
        din("g_lat", [128, 5])
        din("w_q_up", [384, 768]); din("w_q_up_sw", [384, 768]); din("w_kv_up", [256, 1024])
        din("gsm", [128, 8])
        dout("h1T", [1024, NTOK])
        dout("qT", [8, 96, NTOK], BF16); dout("knT", [512, NTOK], BF16); dout("kpeT", [32, NTOK], BF16)
        dout("vT", [512, NTOK], BF16)
        dout("mqT", [256, NTOK], BF16); dout("mkT", [256, NTOK], BF16)
        dout("mvT", [512, NTOK], BF16); dout("soT", [512, NTOK], BF16)
        dout("mifT", [8, NTOK])
    else:
        if not fz:
            din("ymT", [1024, NTOK], BF16)
        else:
            idxc = P.sb("idxc", [128, 64], I32)
            P.dma("sp", idxc[:, :], D["idx"], writes=[idxc])
            ystg = [P.sb("ystg%d" % c, [128, NTOK], BF16) for c in range(8)]
            for c in range(8):
                P.gather(ystg[c], ystg[c][:, :], D["recv2"][:, :], idxc, idxc[:, 48 + c:49 + c])
                if "dbg_y" in D:
                    P.dma("sp", D["dbg_y"][c * 128:(c + 1) * 128, :], ystg[c][:, :], reads=[ystg[c]])
        dout("outT", [1024, NTOK])

    h = [[P.sb("h%d_%d" % (s, c), [128, TS]) for c in range(8)] for s in range(2)]
    hn = [[P.sb("hn%d_%d" % (s, c), [128, TS], BF16) for c in range(8)] for s in range(2)]
    act = [[P.sb("act%d_%d" % (s, j), [128, TS], BF16) for j in range(22)] for s in range(2)]
    y = [a[0:8] for a in act]
    ringA = [P.sb("wA%d" % i, [128, 8, 512], BF16) for i in range(3)]
    ringD = [P.sb("wD%d" % i, [128, 22, 128], BF16) for i in range(2)]
    st = {"a": 0, "d": 0}
    wpp = P.sb("wpp", [128, 2, 1024], BF16)
    gains = P.sb("gains", [128, 32])
    cst = P.sb("cst", [128, 512])
    ones_bf = P.sb("ones_bf", [128, 128], BF16)
    P.dma("sp", gains[:, :], D["gains"], writes=[gains])
    P.dma("sp", cst[:, :], D["consts"], writes=[cst])
    op("pool", lambda e: e.memset(ones_bf[:, :], 1.0), writes=[ones_bf])
    P.dma("pool", wpp[:, :, :], D["w_ple_proj"].rearrange("(kc p) n -> p kc n", p=128), writes=[wpp])

    def slotA():
        t = ringA[st["a"] % 3]; st["a"] += 1; return t

    def slotD():
        t = ringD[st["d"] % 2]; st["d"] += 1; return t

    def loadA(w, c0, c1, dst=None, off=0):
        t = dst if dst is not None else slotA()
        P.dma("pool", t[:, :, off:off + (c1 - c0)], w.rearrange("(kc p) n -> p kc n", p=128)[:, :, c0:c1], writes=[t])
        return t

    def mm(ps_ap, lhs_fn, rhs_fn, nk, reads, ps):
        for kc in range(nk):
            l_ = lhs_fn(kc); r_ = rhs_fn(kc)
            op("pe", lambda e, kc=kc, l_=l_, r_=r_: e.matmul(ps_ap, l_, r_, start=(kc == 0), stop=(kc == nk - 1)),
               reads=reads, writes=[ps])

    def rstd_from_ps(ps, rows, n, scale, tag):
        sd = C.tmp("sd" + tag, [128, 512])
        rp = C.tmp("rp" + tag, [128, 512])
        rd = [ps] + ([cst] if not isinstance(scale, float) else [])
        op("act", lambda e: e.activation(sd[0:rows, 0:n], ps[0:rows, 0:n], AF.Sqrt, bias=EPS, scale=scale), reads=rd, writes=[sd])
        op("dve", lambda e: e.reciprocal(rp[0:rows, 0:n], sd[0:rows, 0:n]), reads=[sd], writes=[rp])
        return rp

    def norm(s, gcol, n=TS, src=None, dst=None):
        src = src or h[s]; dst = dst or hn[s]
        ps = C.psum()
        for c in range(8):
            sq = C.tmp("sq", [128, 512], BF16, 3)
            op("act", lambda e, c=c, sq=sq: e.activation(sq[:, 0:n], src[c][:, 0:n], AF.Square), reads=[src[c]], writes=[sq])
            op("pe", lambda e, c=c, sq=sq: e.matmul(ps[:, 0:n], ones_bf[:, :], sq[:, 0:n], start=(c == 0), stop=(c == 7)), reads=[sq, ones_bf], writes=[ps])
        rp = rstd_from_ps(ps, 128, n, 1.0 / 1024.0, "n")
        for c in range(8):
            op("dve", lambda e, c=c: e.scalar_tensor_tensor(dst[c][:, 0:n], src[c][:, 0:n], gains[:, gcol + c:gcol + c + 1], rp[:, 0:n], op0=ALU.mult, op1=ALU.mult),
               reads=[src[c], gains, rp], writes=[dst[c]])

    def resid_proj(w, src):
        for blk in range(2):
            slot = loadA(w, blk * 512, blk * 512 + 512)
            for s in range(2):
                for m in range(4):
                    ps = C.psum()
                    mm(ps[:, :], lambda kc, m=m: slot[:, kc, m * 128:(m + 1) * 128], lambda kc, s=s: src[s][kc][:, :], 8, [slot] + src[s], ps)
                    hc = h[s][blk * 4 + m]
                    op("dve", lambda e, ps=ps, hc=hc: e.tensor_tensor(hc[:, :], ps[:, :], hc[:, :], op=ALU.add), reads=[ps, hc], writes=[hc])

    def ffn(gcol):
        for s in range(2):
            norm(s, gcol)
        for j in range(11):
            slot = slotA()
            loadA(D["w_gate"], j * 256, j * 256 + 256, dst=slot, off=0)
            loadA(D["w_up"], j * 256, j * 256 + 256, dst=slot, off=256)
            for s in range(2):
                for jj in range(2):
                    pg = C.psum(); pu = C.psum()
                    mm(pg[:, :], lambda kc, jj=jj: slot[:, kc, jj * 128:(jj + 1) * 128], lambda kc, s=s: hn[s][kc][:, :], 8, [slot] + hn[s], pg)
                    mm(pu[:, :], lambda kc, jj=jj: slot[:, kc, 256 + jj * 128:256 + (jj + 1) * 128], lambda kc, s=s: hn[s][kc][:, :], 8, [slot] + hn[s], pu)
                    sg = C.tmp("sg", [128, 512], F32, 3)
                    op("act", lambda e, pg=pg, sg=sg: e.activation(sg[:, :], pg[:, :], AF.Silu), reads=[pg], writes=[sg])
                    a = act[s][2 * j + jj]
                    op("dve", lambda e, pu=pu, sg=sg, a=a: e.tensor_tensor(a[:, :], pu[:, :], sg[:, :], op=ALU.mult), reads=[pu, sg], writes=[a])
        for mb in range(8):
            slot = slotD()
            P.dma("pool", slot[:, :, :], D["w_down"].rearrange("(kc p) n -> p kc n", p=128)[:, :, mb * 128:(mb + 1) * 128], writes=[slot])
            for s in range(2):
                ps = C.psum()
                mm(ps[:, :], lambda kc: slot[:, kc, :], lambda kc, s=s: act[s][kc][:, :], 22, [slot] + act[s], ps)
                hc = h[s][mb]
                op("dve", lambda e, ps=ps, hc=hc: e.tensor_tensor(hc[:, :], ps[:, :], hc[:, :], op=ALU.add), reads=[ps, hc], writes=[hc])

    def ple(gcol, tok0):
        pt = []
        for s in range(2):
            norm(s, gcol)
            t = C.tmp("pt", [128, 2, TS], BF16, 2)
            P.dma("pool", t[:, :, :], D["pT"].rearrange("(kc p) n -> p kc n", p=128)[:, :, tok0 + s * TS:tok0 + (s + 1) * TS], writes=[t])
            pt.append(t)
        for blk in range(2):
            slot = loadA(D["w_ple_gate"], blk * 512, blk * 512 + 512)
            for s in range(2):
                for m in range(4):
                    mg = blk * 4 + m
                    pg = C.psum(); pp = C.psum()
                    mm(pg[:, :], lambda kc, m=m: slot[:, kc, m * 128:(m + 1) * 128], lambda kc, s=s: hn[s][kc][:, :], 8, [slot] + hn[s], pg)
                    mm(pp[:, :], lambda kc, mg=mg: wpp[:, kc, mg * 128:(mg + 1) * 128], lambda kc, s=s: pt[s][:, kc, :], 2, [wpp, pt[s]], pp)
                    sg = C.tmp("sg", [128, 512], F32, 3)
                    op("act", lambda e, pg=pg, sg=sg: e.activation(sg[:, :], pg[:, :], AF.Sigmoid), reads=[pg], writes=[sg])
                    t2 = C.tmp("t2", [128, 512], F32, 3)
                    op("dve", lambda e, pp=pp, sg=sg, t2=t2: e.tensor_tensor(t2[:, :], pp[:, :], sg[:, :], op=ALU.mult), reads=[pp, sg], writes=[t2])
                    hc = h[s][mg]
                    op("pool", lambda e, t2=t2, hc=hc: e.tensor_tensor(hc[:, :], t2[:, :], hc[:, :], op=ALU.add), reads=[t2, hc], writes=[hc])

    if mode == "A":
        wconv = P.sb("wconv", [128, 12]); gvb = P.sb("gvb", [128, 512]); bsb = P.sb("bsb", [128, 4, 128])
        wsT = P.sb("wsT", [128, 8, 128], BF16); maskb = P.sb("maskb", [128, 128], BF16)
        g_lat = P.sb("g_lat", [128, 5]); gsm = P.sb("gsm", [128, 8])
        b96 = P.sb("b96", [96, 96], BF16); bd64 = P.sb("bd64", [128, 128], BF16)
        for t, nm in ((wconv, "wconv"), (gvb, "gvb"), (g_lat, "g_lat"), (gsm, "gsm")):
            P.dma("sp", t[:, :], D[nm], writes=[t])
        P.dma("sp", bsb[:, :, :], D["bsb"], writes=[bsb])
        P.dma("pool", wsT[:, :, :], D["wsT"], writes=[wsT])
        op("dve", lambda e: e.tensor_copy(maskb[:, :], cst[:, 0:128]), reads=[cst], writes=[maskb])
        for hh in range(8):
            op("dve", lambda e, hh=hh: e.tensor_tensor(wsT[:, hh, :], wsT[:, hh, :], maskb[:, :], op=ALU.mult), reads=[wsT, maskb], writes=[wsT])
        op("dve", lambda e: e.tensor_copy(b96[:, :], cst[0:96, 128:224]), reads=[cst], writes=[b96])
        op("dve", lambda e: e.tensor_copy(bd64[:, :], cst[:, 224:352]), reads=[cst], writes=[bd64])
        hal = [P.sb("hal%d" % cc, [128, 2]) for cc in range(4)]
        gu = [act[s][8:12] for s in range(2)]
        hh_t = [P.sb("hh%d" % c, [128, 2]) for c in range(8)]
        hhn = [P.sb("hhn%d" % c, [128, 2], BF16) for c in range(8)]

    def even_mixer(sti, tok0):
        for s in range(2):
            norm(s, 0)
        if sti == 0:
            for c in range(8):
                P.dma("sp", hh_t[c][:, :], D["xhT"][c * 128:(c + 1) * 128, :], writes=[hh_t[c]])
            norm(0, 0, n=2, src=hh_t, dst=hhn)
        for cc in range(4):
            slot = loadA(D["ev_w_in"], cc * 384, cc * 384 + 384)
            for s in range(2):
                z = C.tmp("zt", [128, TS + 2], F32, 2)
                if not (s == 0 and sti == 0):
                    op("pool", lambda e, cc=cc, z=z: e.tensor_copy(z[:, 0:2], hal[cc][:, :]), reads=[hal[cc]], writes=[z])
                else:
                    pc = C.psum(); px = C.psum()
                    mm(pc[:, 0:2], lambda kc: slot[:, kc, 128:256], lambda kc: hhn[kc][:, :], 8, [slot] + hhn, pc)
                    mm(px[:, 0:2], lambda kc: slot[:, kc, 256:384], lambda kc: hhn[kc][:, :], 8, [slot] + hhn, px)
                    cs = C.tmp("cs", [128, 512], F32, 2)
                    op("act", lambda e, pc=pc, cs=cs: e.activation(cs[:, 0:2], pc[:, 0:2], AF.Copy), reads=[pc], writes=[cs])
                    op("dve", lambda e, px=px, cs=cs, z=z: e.tensor_tensor(z[:, 0:2], px[:, 0:2], cs[:, 0:2], op=ALU.mult), reads=[px, cs], writes=[z])
                pb = C.psum(); pc = C.psum(); px = C.psum()
                for pp_, c0 in ((pc, 128), (px, 256), (pb, 0)):
                    mm(pp_[:, :], lambda kc, c0=c0: slot[:, kc, c0:c0 + 128], lambda kc, s=s: hn[s][kc][:, :], 8, [slot] + hn[s], pp_)
                cs = C.tmp("cs", [128, 512], F32, 2)
                op("act", lambda e, pc=pc, cs=cs: e.activation(cs[:, :], pc[:, :], AF.Copy), reads=[pc], writes=[cs])
                op("dve", lambda e, px=px, cs=cs, z=z: e.tensor_tensor(z[:, 2:TS + 2], px[:, :], cs[:, :], op=ALU.mult), reads=[px, cs], writes=[z])
                op("pool", lambda e, cc=cc, z=z: e.tensor_copy(hal[cc][:, :], z[:, TS:TS + 2]), reads=[z], writes=[hal[cc]])
                acc = C.tmp("acc", [128, 512], F32, 2)
                op("pool", lambda e, z=z, acc=acc, cc=cc: e.tensor_scalar(acc[:, :], z[:, 0:TS], wconv[:, cc * 3:cc * 3 + 1], None, op0=ALU.mult), reads=[z, wconv], writes=[acc])
                op("dve", lambda e, z=z, acc=acc, cc=cc: e.scalar_tensor_tensor(acc[:, :], z[:, 1:TS + 1], wconv[:, cc * 3 + 1:cc * 3 + 2], acc[:, :], op0=ALU.mult, op1=ALU.add), reads=[z, wconv, acc], writes=[acc])
                op("dve", lambda e, z=z, acc=acc, cc=cc: e.scalar_tensor_tensor(acc[:, :], z[:, 2:TS + 2], wconv[:, cc * 3 + 2:cc * 3 + 3], acc[:, :], op0=ALU.mult, op1=ALU.add), reads=[z, wconv, acc], writes=[acc])
                yt = y[s][cc]
                op("dve", lambda e, pb=pb, acc=acc, yt=yt: e.tensor_tensor(yt[:, :], pb[:, :], acc[:, :], op=ALU.mult), reads=[pb, acc], writes=[yt])
        slot = loadA(D["ev_w_in"], 1536, 2048)
        for s in range(2):
            for uc in range(4):
                pu = C.psum()
                mm(pu[:, :], lambda kc, uc=uc: slot[:, kc, uc * 128:(uc + 1) * 128], lambda kc, s=s: hn[s][kc][:, :], 8, [slot] + hn[s], pu)
                g_ = gu[s][uc]
                op("act", lambda e, pu=pu, g_=g_: e.activation(g_[:, :], pu[:, :], AF.Gelu), reads=[pu], writes=[g_])
        slot = loadA(D["ev_w_in"], 2048, 2560)
        for s in range(2):
            pm = [C.psum(pin=True) for _ in range(4)]
            for tb in range(4):
                pv = C.psum()
                mm(pv[:, :], lambda kc, s=s, tb=tb: hn[s][kc][:, tb * 128:(tb + 1) * 128], lambda kc: slot[:, kc, :], 8, [slot] + hn[s], pv)
                gv = C.tmp("gv", [128, 512], F32, 2)
                op("act", lambda e, pv=pv, gv=gv: e.activation(gv[:, :], pv[:, :], AF.Gelu), reads=[pv], writes=[gv])
                sqv = C.tmp("sqv", [128, 512], F32, 2)
                op("pool", lambda e, gv=gv, sqv=sqv: e.tensor_tensor(sqv[:, :], gv[:, :], gv[:, :], op=ALU.mult), reads=[gv], writes=[sqv])
                ss = C.tmp("ss", [128, 8], F32, 2); sd = C.tmp("ssd", [128, 8], F32, 2); rs = C.tmp("srs", [128, 8], F32, 2)
                op("dve", lambda e, sqv=sqv, ss=ss: e.tensor_reduce(ss[:, :], sqv[:, :].rearrange("p (h d) -> p h d", d=64), axis=AX.X, op=ALU.add), reads=[sqv], writes=[ss])
                op("act", lambda e, ss=ss, sd=sd: e.activation(sd[:, :], ss[:, :], AF.Sqrt, bias=EPS, scale=1.0 / 64.0), reads=[ss], writes=[sd])
                op("dve", lambda e, sd=sd, rs=rs: e.reciprocal(rs[:, :], sd[:, :]), reads=[sd], writes=[rs])
                op("dve", lambda e, gv=gv, rs=rs: e.tensor_tensor(gv[:, :].rearrange("p (h d) -> p h d", d=64), gv[:, :].rearrange("p (h d) -> p h d", d=64),
                                                                 rs[:, :].unsqueeze(2).to_broadcast([128, 8, 64]), op=ALU.mult), reads=[gv, rs], writes=[gv])
                vn = C.tmp("vn", [128, 512], BF16, 2)
                op("pool", lambda e, gv=gv, vn=vn: e.tensor_tensor(vn[:, :], gv[:, :], gvb[:, :], op=ALU.mult), reads=[gv, gvb], writes=[vn])
                for hd in range(8):
                    op("pe", lambda e, hd=hd, tb=tb, vn=vn, pm=pm: e.matmul(pm[hd // 2][64 * (hd % 2):64 * (hd % 2) + 64, tb * 128:(tb + 1) * 128], vn[:, hd * 64:(hd + 1) * 64], wsT[:, hd, :], start=True, stop=True),
                       reads=[vn, wsT], writes=[pm[hd // 2]])
            for hc in range(4):
                t1 = C.tmp("t2", [128, 512], F32, 3)
                op("dve", lambda e, hc=hc, t1=t1, pm=pm: e.tensor_tensor(t1[:, :].rearrange("p (b t) -> p b t", t=128), pm[hc][:, :].rearrange("p (b t) -> p b t", t=128),
                                                                bsb[:, hc, :].unsqueeze(1).to_broadcast([128, 4, 128]), op=ALU.add), reads=[pm[hc], bsb], writes=[t1])
                yt = y[s][4 + hc]
                op("pool", lambda e, t1=t1, yt=yt, s=s, hc=hc: e.tensor_tensor(yt[:, :], t1[:, :], gu[s][hc][:, :], op=ALU.mult), reads=[t1, gu[s][hc]], writes=[yt])
            C.unpin()
        resid_proj(D["w_out"], y)

    def O(nm, r0, r1, t0):
        if fz and (nm + "_h0") in D:
            hf = t0 // 1024
            return D[nm + "_h%d" % hf][r0:r1, (t0 % 1024):(t0 % 1024) + TS]
        return D[nm][r0:r1, t0:t0 + TS]

    def odd_front(tok0):
        W = D["od_w_in"]
        for s in range(2):
            norm(s, 24)
        tabs = []
        for s in range(2):
            pi_ = C.tmp("ti", [96, TS], I32, 1)
            P.dma("sp", pi_[:, :], D["posb"][:, tok0 + s * TS:tok0 + (s + 1) * TS], writes=[pi_])
            ang = C.tmp("angp", [96, TS], F32, 1, own=True)
            op("dve", lambda e, pi_=pi_, ang=ang: e.tensor_copy(ang[:, :], pi_[:, :]), reads=[pi_], writes=[ang])
            op("dve", lambda e, ang=ang: e.tensor_scalar(ang[:, :], ang[:, :], cst[0:96, 353:354], None, op0=ALU.mult), reads=[ang, cst], writes=[ang])
            pair = []
            for nm, shift in (("cos", PI / 2.0), ("sin", 0.0)):
                a2 = C.tmp("a2", [96, TS], F32, 1); tf = C.tmp("tf", [96, TS], F32, 1); ti = C.tmp("ti", [96, TS], I32, 1)
                tab = C.tmp("tab" + nm, [96, TS], F32, 2, own=True)
                op("dve", lambda e, a2=a2, ang=ang, shift=shift: e.tensor_scalar(a2[:, :], ang[:, :], shift, None, op0=ALU.add), reads=[ang], writes=[a2])
                op("dve", lambda e, a2=a2, tf=tf: e.tensor_scalar(tf[:, :], a2[:, :], 1.0 / TWO_PI, None, op0=ALU.mult), reads=[a2], writes=[tf])
                op("dve", lambda e, tf=tf, ti=ti: e.tensor_copy(ti[:, :], tf[:, :]), reads=[tf], writes=[ti])
                op("dve", lambda e, tf=tf, ti=ti: e.tensor_copy(tf[:, :], ti[:, :]), reads=[ti], writes=[tf])
                op("dve", lambda e, tf=tf, a2=a2: e.scalar_tensor_tensor(a2[:, :], tf[:, :], -TWO_PI, a2[:, :], op0=ALU.mult, op1=ALU.add), reads=[tf, a2], writes=[a2])
                op("dve", lambda e, a2=a2: e.tensor_scalar(a2[:, :], a2[:, :], -PI, PI, op0=ALU.max, op1=ALU.min), reads=[a2], writes=[a2])
                if nm == "sin":
                    op("act", lambda e, a2=a2, tab=tab: e.activation(tab[:, :], a2[:, :], AF.Sin, scale=cst[0:96, 354:355]), reads=[a2, cst], writes=[tab])
                else:
                    op("act", lambda e, a2=a2, tab=tab: e.activation(tab[:, :], a2[:, :], AF.Sin), reads=[a2], writes=[tab])
                pair.append(tab)
            tabs.append(pair)

        def fm_out(ps, rows, dram_ap, scale=1.0, tag="fo"):
            ob = C.tmp(tag, [128, TS], BF16, 3)
            op("act", lambda e: e.mul(ob[0:rows, :], ps[0:rows, :], float(scale)), reads=[ps], writes=[ob])
            P.dma("sp", dram_ap, ob[0:rows, :], reads=[ob])

        def tok_out(ps, ncols, dram_fn, stage, tb, col0, func=AF.Copy):
            op("act", lambda e: e.activation(stage[:, tb, col0:col0 + ncols], ps[:, 0:ncols], func), reads=[ps], writes=[stage])

        def lat_norm(raws, nch, gc0, D_, outs, tag):
            ps = C.psum()
            for c in range(nch):
                sq = C.tmp("sq", [128, 512], BF16, 3)
                op("act", lambda e, c=c, sq=sq: e.activation(sq[:, :], raws[c][:, :], AF.Square), reads=[raws[c]], writes=[sq])
                op("pe", lambda e, c=c, sq=sq: e.matmul(ps[:, :], ones_bf[:, :], sq[:, :], start=(c == 0), stop=(c == nch - 1)), reads=[sq, ones_bf], writes=[ps])
            rp = rstd_from_ps(ps, 128, TS, 1.0 / D_, tag)
            for c in range(nch):
                op("dve", lambda e, c=c: e.scalar_tensor_tensor(outs[c][:, :], raws[c][:, :], g_lat[:, gc0 + c:gc0 + c + 1], rp[:, :], op0=ALU.mult, op1=ALU.mult),
                   reads=[raws[c], g_lat, rp], writes=[outs[c]])

        qlr = [act[s][12:15] for s in range(2)]
        kvr = [act[s][15:17] for s in range(2)]
        qln = [act[s][17:20] for s in range(2)]
        kvn = [act[s][20:22] for s in range(2)]

        def raw_copy(ps, rows, dst):
            op("act", lambda e: e.activation(dst[0:rows, :], ps[0:rows, :], AF.Copy), reads=[ps], writes=[dst])

        slot = loadA(W, 0, 512)
        for s in range(2):
            for c in range(4):
                ps = C.psum()
                mm(ps[:, :], lambda kc, c=c: slot[:, kc, c * 128:(c + 1) * 128], lambda kc, s=s: hn[s][kc][:, :], 8, [slot] + hn[s], ps)
                raw_copy(ps, 128, qlr[s][c] if c < 3 else kvr[s][0])
            lat_norm(qlr[s], 3, 0, 384.0, qln[s], "q")
        slot = loadA(W, 512, 960)
        for s in range(2):
            t0 = tok0 + s * TS
            cosT, sinT = tabs[s]
            ps = C.psum()
            mm(ps[:, :], lambda kc: slot[:, kc, 0:128], lambda kc, s=s: hn[s][kc][:, :], 8, [slot] + hn[s], ps)
            raw_copy(ps, 128, kvr[s][1])
            lat_norm(kvr[s], 2, 3, 256.0, kvn[s], "k")
            pk = C.psum(); pks = C.psum()
            mm(pk[0:32, :], lambda kc: slot[:, kc, 128:160], lambda kc, s=s: hn[s][kc][:, :], 8, [slot] + hn[s], pk)
            mm(pks[0:32, :], lambda kc: slot[:, kc, 160:192], lambda kc, s=s: hn[s][kc][:, :], 8, [slot] + hn[s], pks)
            kr = C.tmp("kr", [32, 512], F32)
            raw_copy(pk, 32, kr)
            sq = C.tmp("sq", [128, 512], BF16, 3)
            op("pool", lambda e, sq=sq, kr=kr: e.tensor_tensor(sq[0:32, :], kr[:, :], kr[:, :], op=ALU.mult), reads=[kr], writes=[sq])
            pn = C.psum()
            op("pe", lambda e, sq=sq, pn=pn: e.matmul(pn[0:32, :], ones_bf[0:32, 0:32], sq[0:32, :], start=True, stop=True), reads=[sq, ones_bf], writes=[pn])
            rp = rstd_from_ps(pn, 32, TS, 1.0 / 32.0, "kp")
            a = C.tmp("kpa", [32, 512], F32); b_ = C.tmp("kpb", [32, 512], F32)
            op("dve", lambda e, a=a, rp=rp, kr=kr: e.scalar_tensor_tensor(a[:, :], kr[:, :], gsm[0:32, 4:5], rp[0:32, :], op0=ALU.mult, op1=ALU.mult), reads=[kr, gsm, rp], writes=[a])
            op("dve", lambda e, b_=b_, rp=rp, pks=pks: e.scalar_tensor_tensor(b_[:, :], pks[0:32, :], gsm[0:32, 5:6], rp[0:32, :], op0=ALU.mult, op1=ALU.mult), reads=[pks, gsm, rp], writes=[b_])
            op("pool", lambda e, a=a, cosT=cosT: e.tensor_tensor(a[:, :], a[:, :], cosT[0:32, :], op=ALU.mult), reads=[a, cosT], writes=[a])
            op("pool", lambda e, b_=b_, sinT=sinT: e.tensor_tensor(b_[:, :], b_[:, :], sinT[0:32, :], op=ALU.mult), reads=[b_, sinT], writes=[b_])
            ob = C.tmp("fo", [128, TS], BF16, 3)
            op("pool", lambda e, a=a, b_=b_, ob=ob: e.tensor_tensor(ob[0:32, :], a[:, :], b_[:, :], op=ALU.add), reads=[a, b_], writes=[ob])
            P.dma("sp", O("kpeT", 0, 32, t0), ob[0:32, :], reads=[ob])
            for c in range(2):
                ps = C.psum()
                mm(ps[:, :], lambda kc, c=c: slot[:, kc, 192 + c * 128:192 + (c + 1) * 128], lambda kc, s=s: hn[s][kc][:, :], 8, [slot] + hn[s], ps)
                fm_out(ps, 128, O("mqT", c * 128, (c + 1) * 128, t0), scale=0.125)
        slot = loadA(W, 960, 1224)
        for s in range(2):
            t0 = tok0 + s * TS
            for c in range(2):
                ps = C.psum()
                mm(ps[:, :], lambda kc, c=c: slot[:, kc, c * 128:(c + 1) * 128], lambda kc, s=s: hn[s][kc][:, :], 8, [slot] + hn[s], ps)
                fm_out(ps, 128, O("mkT", c * 128, (c + 1) * 128, t0))
            ps = C.psum()
            mm(ps[0:8, :], lambda kc: slot[:, kc, 256:264], lambda kc, s=s: hn[s][kc][:, :], 8, [slot] + hn[s], ps)
            mo_ = C.tmp("mif", [8, 512], F32)
            raw_copy(ps, 8, mo_)
            P.dma("sp", D["mifT"][:, t0:t0 + TS], mo_[:, :], reads=[mo_])
        for gi_, (c0, nm) in enumerate(((1224, "mvT"), (1736, "soT"))):
            slot = loadA(W, c0, c0 + 512)
            for s in range(2):
                t0 = tok0 + s * TS
                for c in range(4):
                    ps = C.psum()
                    mm(ps[:, :], lambda kc, c=c: slot[:, kc, c * 128:(c + 1) * 128], lambda kc, s=s: hn[s][kc][:, :], 8, [slot] + hn[s], ps)
                    ob = C.tmp("fo", [128, TS], BF16, 3)
                    if gi_ == 0 and c % 2 == 0:
                        op("dve", lambda e, ps=ps, ob=ob: e.tensor_copy(ob[:, :], ps[:, :]), reads=[ps], writes=[ob])
                    else:
                        fn_ = AF.Copy if gi_ == 0 else AF.Sigmoid
                        op("act", lambda e, ps=ps, ob=ob, fn_=fn_: e.activation(ob[:, :], ps[:, :], fn_), reads=[ps], writes=[ob])
                    P.dma("sp", O(nm, c * 128, (c + 1) * 128, t0), ob[:, :], reads=[ob])
        def sview(slot, nk, n):
            return slot.h[:, :, :].rearrange("p k n -> p (k n)")[:, 0:nk * n].rearrange("p (k n) -> p k n", n=n)
        wq_t = slotA(); wqs_t = slotA(); wkv_t = slotA()
        wq = TV(wq_t, sview(wq_t, 3, 768)); wqs = TV(wqs_t, sview(wqs_t, 3, 768)); wkv = TV(wkv_t, sview(wkv_t, 2, 1024))
        P.dma("pool", wq[:, :, :], D["w_q_up"].rearrange("(kc p) n -> p kc n", p=128), writes=[wq])
        P.dma("pool", wqs[:, :, :], D["w_q_up_sw"].rearrange("(kc p) n -> p kc n", p=128), writes=[wqs])
        P.dma("pool", wkv[:, :, :], D["w_kv_up"].rearrange("(kc p) n -> p kc n", p=128), writes=[wkv])
        for hd in range(8):
            for s in range(2):
                t0 = tok0 + s * TS
                cosT, sinT = tabs[s]
                pq = C.psum(); pqs = C.psum()
                mm(pq[0:96, :], lambda kc, hd=hd: wq[:, kc, hd * 96:(hd + 1) * 96], lambda kc, s=s: qln[s][kc][:, :], 3, [wq] + qln[s], pq)
                mm(pqs[0:96, :], lambda kc, hd=hd: wqs[:, kc, hd * 96:(hd + 1) * 96], lambda kc, s=s: qln[s][kc][:, :], 3, [wqs] + qln[s], pqs)
                qr = C.tmp("qr", [96, 512], F32)
                raw_copy(pq, 96, qr)
                sq = C.tmp("sq", [128, 512], BF16, 3)
                op("pool", lambda e, sq=sq, qr=qr: e.tensor_tensor(sq[0:96, :], qr[:, :], qr[:, :], op=ALU.mult), reads=[qr], writes=[sq])
                pn = C.psum()
                op("pe", lambda e, sq=sq, pn=pn: e.matmul(pn[0:96, :], b96[:, :], sq[0:96, :], start=True, stop=True), reads=[sq, b96], writes=[pn])
                rp = rstd_from_ps(pn, 96, TS, cst[0:96, 352:353], "qh")
                qn = C.tmp("qn", [96, 512], F32); sw = C.tmp("qsw", [96, 512], F32)
                op("dve", lambda e, qn=qn, qr=qr, rp=rp: e.scalar_tensor_tensor(qn[:, :], qr[:, :], gsm[0:96, 0:1], rp[0:96, :], op0=ALU.mult, op1=ALU.mult), reads=[qr, gsm, rp], writes=[qn])
                op("dve", lambda e, sw=sw, pqs=pqs, rp=rp: e.scalar_tensor_tensor(sw[64:96, :], pqs[64:96, :], gsm[64:96, 1:2], rp[64:96, :], op0=ALU.mult, op1=ALU.mult), reads=[pqs, gsm, rp], writes=[sw])
                op("pool", lambda e, qn=qn, cosT=cosT: e.tensor_tensor(qn[64:96, :], qn[64:96, :], cosT[64:96, :], op=ALU.mult), reads=[qn, cosT], writes=[qn])
                op("pool", lambda e, sw=sw, sinT=sinT: e.tensor_tensor(sw[64:96, :], sw[64:96, :], sinT[64:96, :], op=ALU.mult), reads=[sw, sinT], writes=[sw])
                op("pool", lambda e, qn=qn, sw=sw: e.tensor_tensor(qn[64:96, :], qn[64:96, :], sw[64:96, :], op=ALU.add), reads=[qn, sw], writes=[qn])
                ob = C.tmp("fo", [128, TS], BF16, 3)
                op("act", lambda e, ob=ob, qn=qn: e.mul(ob[0:96, :], qn[:, :], 96.0 ** -0.5), reads=[qn], writes=[ob])
                P.dma("sp", (O("qTf", hd * 96, (hd + 1) * 96, t0) if fz else D["qT"][hd, :, t0:t0 + TS]), ob[0:96, :], reads=[ob])
        for s in range(2):
            t0 = tok0 + s * TS
            for hp in range(4):
                pk = C.psum()
                mm(pk[:, :], lambda kc, hp=hp: wkv[:, kc, hp * 128:(hp + 1) * 128], lambda kc, s=s: kvn[s][kc][:, :], 2, [wkv] + kvn[s], pk)
                kr = C.tmp("kr", [128, 512], F32)
                raw_copy(pk, 128, kr)
                sq = C.tmp("sq", [128, 512], BF16, 3)
                op("pool", lambda e, sq=sq, kr=kr: e.tensor_tensor(sq[:, :], kr[:, :], kr[:, :], op=ALU.mult), reads=[kr], writes=[sq])
                pn = C.psum()
                op("pe", lambda e, sq=sq, pn=pn: e.matmul(pn[:, :], bd64[:, :], sq[:, :], start=True, stop=True), reads=[sq, bd64], writes=[pn])
                rp = rstd_from_ps(pn, 128, TS, 1.0 / 64.0, "kh")
                ob = C.tmp("fo", [128, TS], BF16, 3)
                op("dve", lambda e, ob=ob, kr=kr, rp=rp: e.scalar_tensor_tensor(ob[:, :], kr[:, :], gsm[:, 2:3], rp[:, :], op0=ALU.mult, op1=ALU.mult), reads=[kr, gsm, rp], writes=[ob])
                P.dma("sp", O("knT", hp * 128, (hp + 1) * 128, t0), ob[:, :], reads=[ob])
            for hp in range(4):
                pv = C.psum()
                mm(pv[:, :], lambda kc, hp=hp: wkv[:, kc, 512 + hp * 128:512 + (hp + 1) * 128], lambda kc, s=s: kvn[s][kc][:, :], 2, [wkv] + kvn[s], pv)
                ob = C.tmp("fo", [128, TS], BF16, 3)
                op("act", lambda e, pv=pv, ob=ob: e.activation(ob[:, :], pv[:, :], AF.Copy), reads=[pv], writes=[ob])
                P.dma("sp", O("vT", hp * 128, (hp + 1) * 128, t0), ob[:, :], reads=[ob])

    for sti in range(NTOK // (2 * TS)):
        tok0 = sti * 2 * TS
        for s in range(2):
            for c in range(8):
                P.dma("sp", h[s][c][:, :], D["hT"][c * 128:(c + 1) * 128, tok0 + s * TS:tok0 + (s + 1) * TS], writes=[h[s][c]])
        if mode == "A":
            even_mixer(sti, tok0)
            ffn(8)
            ple(16, tok0)
            for s in range(2):
                for c in range(8):
                    P.dma("sp", D["h1T"][c * 128:(c + 1) * 128, tok0 + s * TS:tok0 + (s + 1) * TS], h[s][c][:, :], reads=[h[s][c]])
            odd_front(tok0)
            if fz and sti == 0 and "mid_hook" in fz:
                fz["mid_hook"](P)
        else:
            if fz:
                ysrc = [[TV(ystg[c], ystg[c].h[:, tok0 + s * TS:tok0 + (s + 1) * TS]) for c in range(8)] for s in range(2)]
            else:
                ysrc = y
                for s in range(2):
                    for c in range(8):
                        P.dma("pool", y[s][c][:, :], D["ymT"][c * 128:(c + 1) * 128, tok0 + s * TS:tok0 + (s + 1) * TS], writes=[y[s][c]])
            resid_proj(D["w_out"], ysrc)
            ffn(8)
            ple(16, tok0)
            for s in range(2):
                for c in range(8):
                    P.dma("sp", D["outT"][c * 128:(c + 1) * 128, tok0 + s * TS:tok0 + (s + 1) * TS], h[s][c][:, :], reads=[h[s][c]])
    P.wait_all("sp", P.tiles)
    if fz:
        return P
    P.emit()
    return nc


def _c(a):
    return np.ascontiguousarray(a)


def _chunkT(v):
    return _c(v.reshape(-1, 128).T)


def _consts():
    c = np.zeros((128, 512), np.float32)
    s = np.arange(128)
    c[:, 0:128] = (s[:, None] <= s[None, :]).astype(np.float32)
    k = np.arange(96)
    c[0:96, 128:224] = ((k[:, None] < 64) == (k[None, :] < 64)).astype(np.float32)
    c[:, 224:352] = ((s[:, None] // 64) == (s[None, :] // 64)).astype(np.float32)
    c[0:64, 352] = 1.0 / 64.0
    c[64:96, 352] = 1.0 / 32.0
    inv_freq = (10000.0 ** (-np.arange(0, 32, 2, dtype=np.float32) / np.float32(32))).astype(np.float32)
    c[0:96, 353] = inv_freq[np.arange(96) % 16]
    c[0:96, 354] = np.where((np.arange(96) % 32) < 16, -1.0, 1.0)
    return c


_SW = (np.arange(32) + 16) % 32


def prep_tok_common(inp, L, core):
    b, q = core // 4, core % 4
    s0 = q * NTOK
    m = {
        "pT": _c(inp["p"][L, b, s0:s0 + NTOK, :].T),
        "consts": _consts(),
        "w_gate": _c(inp["w_gate"][L]), "w_up": _c(inp["w_up"][L]), "w_down": _c(inp["w_down"][L]),
        "w_ple_gate": _c(inp["w_ple_gate"][L]), "w_ple_proj": _c(inp["w_ple_proj"][L]),
    }
    return m


def prep_A(inp):
    maps = []
    x = inp["x"]
    ev_w_in = inp["ev_w_in"][0]
    cols = []
    for cc in range(4):
        for base in (0, 512, 1024):
            cols.append(np.arange(base + cc * 128, base + (cc + 1) * 128))
    cols.append(np.arange(1536, 2560))
    ev_w_in_r = _c(ev_w_in[:, np.concatenate(cols)])
    od_w_in = inp["od_w_in"][0]
    ocols = np.concatenate([np.arange(0, 672), 640 + _SW, np.arange(672, 928), np.arange(928, 1184), np.arange(2208, 2216),
                            np.arange(1184, 1696), np.arange(1696, 2208)])
    od_ext = _c(od_w_in[:, ocols])
    wq = inp["od_w_q_up"][0]
    qcols = np.arange(768).reshape(8, 96).copy()
    qcols[:, 64:96] = qcols[:, 64:96][:, _SW]
    wq_sw = _c(wq[:, qcols.reshape(-1)])
    wkv = inp["od_w_kv_up"][0].reshape(256, 8, 128)
    wkv_r = _c(np.concatenate([wkv[:, :, :64].reshape(256, 512), wkv[:, :, 64:].reshape(256, 512)], axis=1))
    gq, gk = inp["od_g_q"][0], inp["od_g_k"][0]
    gsm = np.zeros((128, 8), np.float32)
    gsm[0:96, 0] = gq
    gsm[64:96, 1] = gq[64:96][_SW]
    gsm[0:64, 2] = gk[:64]; gsm[64:128, 2] = gk[:64]
    gsm[0:32, 4] = gk[64:96]; gsm[0:32, 5] = gk[64:96][_SW]
    g_lat = np.concatenate([_chunkT(inp["od_g_qa"][0]), _chunkT(inp["od_g_kva"][0])], axis=1)
    gains = np.zeros((128, 32), np.float32)
    gains[:, 0:8] = _chunkT(inp["g_mix"][0]); gains[:, 8:16] = _chunkT(inp["g_ffn"][0])
    gains[:, 16:24] = _chunkT(inp["g_ple"][0]); gains[:, 24:32] = _chunkT(inp["g_mix"][1])
    wconv = _c(inp["ev_w_conv"][0].T.reshape(4, 128, 3).transpose(1, 0, 2).reshape(128, 12))
    gvb = _c(np.tile(inp["ev_g_v"][0].reshape(1, 512), (128, 1)))
    wsT = _c(inp["ev_w_s"][0].transpose(2, 0, 1))
    bsb = _c(inp["ev_b_s"][0].reshape(4, 2, 1, 128).repeat(64, axis=2).reshape(4, 128, 128).transpose(1, 0, 2))
    for core in range(8):
        b, q = core // 4, core % 4
        s0 = q * NTOK
        m = prep_tok_common(inp, 0, core)
        m["hT"] = _c(x[b, s0:s0 + NTOK, :].T)
        m["xhT"] = _c(x[b, s0 - 2:s0, :].T) if q > 0 else np.zeros((1024, 2), np.float32)
        m["posb"] = _c(np.tile(inp["positions"][b, s0:s0 + NTOK].reshape(1, NTOK), (96, 1)).astype(np.int32))
        m.update({"gains": gains, "w_out": _c(inp["ev_w_out"][0]), "ev_w_in": ev_w_in_r, "wconv": wconv, "gvb": gvb, "wsT": wsT,
                  "bsb": bsb, "od_w_in": od_ext, "g_lat": _c(g_lat), "w_q_up": _c(wq), "w_q_up_sw": wq_sw, "w_kv_up": wkv_r, "gsm": gsm})
        maps.append(m)
    return maps


def prep_C(inp, h1T, ymT):
    maps = []
    gains = np.zeros((128, 32), np.float32)
    gains[:, 8:16] = _chunkT(inp["g_ffn"][1]); gains[:, 16:24] = _chunkT(inp["g_ple"][1])
    for core in range(8):
        m = prep_tok_common(inp, 1, core)
        m["hT"] = h1T[core]
        m["ymT"] = ymT[core]
        m["gains"] = gains
        m["w_out"] = _c(inp["od_w_out"][0])
        maps.append(m)
    return maps


SEQ = 8192


def build_mix(parts, fz):
    nc = fz["nc"]
    P = Prog(nc, ("M_" if "m" in parts else "T_"))
    C = Ctx(P, nbanks=6)
    op = P.op
    D = fz["D"]
    ptb = [P.ps("ptb%d" % i, [128, 1024], BF16) for i in range(2)]
    idx = P.sb("idx", [128, 64], I32)
    P.dma("sp", idx[:, :], D["idx"], writes=[idx])
    R1 = D["recv1"]

    def gat(t, rows, q, col):
        for hf in range(2):
            c0_ = q * NTOK + hf * 1024
            P.gather(t, t[0:rows, c0_:c0_ + 1024], R1[hf][:, :], idx, idx[0:rows, col:col + 1])
    send2 = D["send2"]

    cB = P.sb("cB", [128, 512])
    bcol = P.sb("bcol", [128, 2]); gmh = P.sb("gmh", [128, 128])
    P.dma("sp", cB[:, :], D["cB"], writes=[cB]); P.dma("sp", bcol[:, :], D["bcol"], writes=[bcol]); P.dma("sp", gmh[:, :], D["gmh"], writes=[gmh])
    maskb = P.sb("maskb", [128, 128], BF16)
    op("dve", lambda e: e.tensor_copy(maskb[:, :], cB[:, 0:128]), reads=[cB], writes=[maskb])
    idb = P.sb("idb", [128, 128], BF16)
    op("dve", lambda e: e.tensor_copy(idb[:, :], cB[:, 256:384]), reads=[cB], writes=[idb])
    ones_f = P.sb("ones_f", [128, 128])
    op("pool", lambda e: e.memset(ones_f[:, :], 1.0), writes=[ones_f])

    if True:
        pass
    if "s" in parts:
        gi = P.sb("gi", [128, 64]); gf = P.sb("gf", [128, 64])
        rmv = D["recvm"].rearrange("r (c t) -> (r c) t", t=64)
        P.gather(gi, gi[:, :], rmv, idx, idx[:, 40:41])
        P.gather(gf, gf[:, :], rmv, idx, idx[:, 41:42])
        zer = P.sb("zer", [128, 128]); op("pool", lambda e: e.memset(zer[:, :], 0.0), writes=[zer])
        nbf = P.sb("nbf", [128, 1])
        op("dve", lambda e: e.tensor_scalar(nbf[:, :], bcol[:, 1:2], -1.0, None, op0=ALU.mult), reads=[bcol], writes=[nbf])
        e1 = P.sb("e1", [128, 64]); lf = P.sb("lf", [128, 64]); Al = P.sb("Al", [128, 64]); vv = P.sb("vv", [128, 64])
        Ml = P.sb("Ml", [128, 64]); Mp = P.sb("Mp", [128, 64])
        op("act", lambda e: e.activation(e1[:, :], gf[:, :], AF.Exp, bias=nbf[:, 0:1], scale=-1.0), reads=[gf, nbf], writes=[e1])
        op("act", lambda e: e.activation(lf[:, :], e1[:, :], AF.Ln, bias=1.0), reads=[e1], writes=[lf])
        op("dve", lambda e: e.tensor_scalar(lf[:, :], lf[:, :], -1.0, None, op0=ALU.mult), reads=[lf], writes=[lf])
        op("dve", lambda e: e.tensor_tensor_scan(Al[:, :], lf[:, :], zer[:, 0:64], 0.0, op0=ALU.add, op1=ALU.add), reads=[lf, zer], writes=[Al])

        def col2row(col_ap, reads):
            ps = C.psum()
            op("pe", lambda e: e.matmul(ps[0:1, 0:128], col_ap, cB[:, 256:384], start=True, stop=True), reads=reads + [cB], writes=[ps])
            return ps

        rows = P.sb("rows", [1, 8, 128])
        ps = col2row(Al[:, 63:64], [Al])
        op("dve", lambda e, ps=ps: e.tensor_copy(rows[:, 0, :], ps[0:1, 0:128]), reads=[ps], writes=[rows])
        op("dve", lambda e: e.tensor_tensor_scan(rows[:, 1, :], rows[:, 0, :], zer[0:1, :], 0.0, op0=ALU.add, op1=ALU.add), reads=[rows, zer], writes=[rows])
        op("dve", lambda e: e.tensor_tensor(rows[:, 2, :], rows[:, 1, :], rows[:, 0, :], op=ALU.subtract), reads=[rows], writes=[rows])
        cols_ps = C.psum(pin=True)

        def row2col(k, j):
            op("pe", lambda e: e.matmul(cols_ps[:, j:j + 1], rows[0:1, k, :], ones_f[0:1, 0:1], start=True, stop=True), reads=[rows, ones_f], writes=[cols_ps])

        row2col(2, 0)
        cols = P.sb("cols", [128, 8])
        op("dve", lambda e: e.tensor_copy(cols[:, 0:1], cols_ps[:, 0:1]), reads=[cols_ps], writes=[cols])
        op("dve", lambda e: e.tensor_scalar(Al[:, :], Al[:, :], cols[:, 0:1], None, op0=ALU.add), reads=[Al, cols], writes=[Al])
        op("dve", lambda e: e.scalar_tensor_tensor(vv[:, :], gi[:, :], bcol[:, 0:1], Al[:, :], op0=ALU.add, op1=ALU.subtract), reads=[gi, bcol, Al], writes=[vv])
        op("dve", lambda e: e.tensor_tensor_scan(Ml[:, :], vv[:, :], vv[:, :], -1e30, op0=ALU.max, op1=ALU.max), reads=[vv], writes=[Ml])
        ps = col2row(Ml[:, 63:64], [Ml])
        op("dve", lambda e, ps=ps: e.tensor_copy(rows[:, 3, :], ps[0:1, 0:128]), reads=[ps], writes=[rows])
        op("dve", lambda e: e.tensor_tensor_scan(rows[:, 4, :], rows[:, 3, :], rows[:, 3, :], 0.0, op0=ALU.max, op1=ALU.max), reads=[rows], writes=[rows])
        op("dve", lambda e: e.memset(rows[:, 5, 0:1], 0.0), reads=[], writes=[rows])
        op("dve", lambda e: e.tensor_copy(rows[:, 5, 1:128], rows[:, 4, 0:127]), reads=[rows], writes=[rows])
        r4 = rows[:, 4, :].rearrange("p (c two) -> p c two", two=2)
        r6 = rows[:, 6, :].rearrange("p (c two) -> p c two", two=2)
        op("dve", lambda e: e.tensor_copy(r6[:, :, 0:1], r4[:, :, 1:2]), reads=[rows], writes=[rows])
        op("dve", lambda e: e.tensor_copy(r6[:, :, 1:2], r4[:, :, 1:2]), reads=[rows], writes=[rows])
        op("dve", lambda e: e.tensor_tensor(rows[:, 7, :], rows[:, 5, :], rows[:, 4, :], op=ALU.subtract), reads=[rows], writes=[rows])
        op("act", lambda e: e.activation(rows[:, 7, :], rows[:, 7, :], AF.Exp), reads=[rows], writes=[rows])
        row2col(5, 1); row2col(4, 2); row2col(6, 3)
        op("dve", lambda e: e.tensor_copy(cols[:, 1:4], cols_ps[:, 1:4]), reads=[cols_ps], writes=[cols])
        C.unpin()
        op("dve", lambda e: e.tensor_scalar(Mp[:, :], Ml[:, :], cols[:, 1:2], 0.0, op0=ALU.max, op1=ALU.max), reads=[Ml, cols], writes=[Mp])
        dec_b = P.sb("dec_b", [64, 128])
        ps = C.psum()
        op("pe", lambda e, ps=ps: e.matmul(ps[0:64, 0:128], ones_f[0:1, 0:64], rows[0:1, 7, :], start=True, stop=True), reads=[ones_f, rows], writes=[ps])
        op("dve", lambda e, ps=ps: e.tensor_copy(dec_b[:, :], ps[0:64, 0:128]), reads=[ps], writes=[dec_b])
        fac = P.sb("fac", [128, 5, 128])
        tmpc = P.sb("tmpc", [128, 64])
        op("dve", lambda e: e.tensor_scalar(tmpc[:, :], vv[:, :], cols[:, 3:4], None, op0=ALU.subtract), reads=[vv, cols], writes=[tmpc])
        op("act", lambda e: e.activation(fac[:, 0, 0:64], tmpc[:, :], AF.Exp), reads=[tmpc], writes=[fac])
        op("dve", lambda e: e.tensor_scalar(tmpc[:, :], Mp[:, :], cols[:, 3:4], None, op0=ALU.subtract), reads=[Mp, cols], writes=[tmpc])
        op("act", lambda e: e.activation(fac[:, 1, 0:64], tmpc[:, :], AF.Exp, scale=-1.0), reads=[tmpc], writes=[fac])
        op("dve", lambda e: e.tensor_scalar(tmpc[:, :], Mp[:, :], cols[:, 1:2], None, op0=ALU.subtract), reads=[Mp, cols], writes=[tmpc])
        op("act", lambda e: e.activation(fac[:, 2, 0:64], tmpc[:, :], AF.Exp, scale=-1.0), reads=[tmpc], writes=[fac])
        op("dve", lambda e: e.tensor_scalar(tmpc[:, :], vv[:, :], cols[:, 2:3], None, op0=ALU.subtract), reads=[vv, cols], writes=[tmpc])
        op("act", lambda e: e.activation(fac[:, 3, 0:64], tmpc[:, :], AF.Exp), reads=[tmpc], writes=[fac])
        op("dve", lambda e: e.tensor_tensor(tmpc[:, :], Al[:, :], Mp[:, :], op=ALU.add), reads=[Al, Mp], writes=[tmpc])
        op("act", lambda e: e.activation(fac[:, 4, 0:64], tmpc[:, :], AF.Exp, scale=-1.0), reads=[tmpc], writes=[fac])
        op("dve", lambda e: e.tensor_copy(fac[:, :, 64:128], fac[:, :, 0:64]), reads=[fac], writes=[fac])
        fcol = P.sb("fcol", [128, 5, 64])
        for k in range(5):
            ps = C.psum()
            op("pe", lambda e, k=k, ps=ps: e.transpose(ps[:, 0:128], fac[:, k, :], cB[:, 256:384]), reads=[fac, cB], writes=[ps])
            pv = ps[:, 0:128].rearrange("p (c two) -> p c two", two=2)
            op("dve", lambda e, k=k, ps=ps: e.tensor_copy(fcol[0:64, k, :], ps[0:64, 0:128].rearrange("p (c two) -> p c two", two=2)[:, :, 0]), reads=[ps], writes=[fcol])
            op("dve", lambda e, k=k, ps=ps: e.tensor_copy(fcol[64:128, k, :], ps[64:128, 0:128].rearrange("p (c two) -> p c two", two=2)[:, :, 1]), reads=[ps], writes=[fcol])

    if "m" in parts:
        mq = P.sb("mq", [64, SEQ], BF16); mk = P.sb("mk", [64, SEQ], BF16)
        ktok = P.sb("ktok", [128, 64, 64], BF16); vext = P.sb("vext", [128, 64, 129], BF16); so = P.sb("so", [128, 64, 128], BF16)
        Call = P.sb("Call", [64, 128, 129], BF16); Sxs = [P.sb("Sx%d" % i, [64, 129]) for i in range(2)]
        mvs = P.sb("mvs", [128, SEQ], BF16); sos = P.sb("sos", [128, SEQ], BF16)
        for q in range(4):
            cs_ = slice(q * NTOK, (q + 1) * NTOK)
            gat(mq, 64, q, 24 + q); gat(mk, 64, q, 28 + q); gat(mvs, 128, q, 32 + q); gat(sos, 128, q, 36 + q)
        op("pool", lambda e: e.memset(vext[:, :, :], 1.0), writes=[vext])
        tcount = [0]
        def tr_group(src, rows, nblk_per, width, dst_fn, g):
            pt_ = ptb[tcount[0] % 2]; tcount[0] += 1
            for b_ in range(nblk_per):
                blk = g * nblk_per + b_
                op("pe", lambda e, pt_=pt_, b_=b_, blk=blk: e.transpose(pt_[:, b_ * width:(b_ + 1) * width], src[0:rows, blk * 128:(blk + 1) * 128], idb[0:rows, 0:rows]),
                   reads=[src, idb], writes=[pt_])
            dst_t, dst_ap = dst_fn(g)
            eng = "act" if tcount[0] % 2 == 0 else "dve"
            if eng == "act":
                op("act", lambda e, pt_=pt_, dst_ap=dst_ap: e.activation(dst_ap, pt_[:, 0:nblk_per * width].rearrange("p (b w) -> p b w", w=width), AF.Copy), reads=[pt_], writes=[dst_t])
            else:
                op("dve", lambda e, pt_=pt_, dst_ap=dst_ap: e.tensor_copy(dst_ap, pt_[:, 0:nblk_per * width].rearrange("p (b w) -> p b w", w=width)), reads=[pt_], writes=[dst_t])
        for g in range(4):
            tr_group(mk, 64, 16, 64, lambda g: (ktok, ktok[:, g * 16:(g + 1) * 16, :]), g)
        for g in range(8):
            tr_group(mvs, 128, 8, 128, lambda g: (vext, vext[:, g * 8:(g + 1) * 8, 0:128]), g)
            tr_group(sos, 128, 8, 128, lambda g: (so, so[:, g * 8:(g + 1) * 8, :]), g)
        op("pool", lambda e: e.memset(Sxs[0][:, :], 0.0), writes=[Sxs[0]])
        for blk in range(64):
            kw = C.tmp("kw", [128, 64], BF16, 3)
            op("dve", lambda e, blk=blk, kw=kw: e.tensor_scalar(kw[:, :], ktok[:, blk, :], fcol[:, 3, blk:blk + 1], None, op0=ALU.mult), reads=[ktok, fcol], writes=[kw])
            for half in range(2):
                c = 2 * blk + half
                Sa = Sxs[c % 2]; Sb = Sxs[(c + 1) % 2]
                op("act", lambda e, c=c, Sa=Sa: e.activation(Call[:, c, :], Sa[:, :], AF.Copy), reads=[Sa], writes=[Call])
                pu = C.psum()
                op("pe", lambda e, pu=pu, kw=kw, half=half, blk=blk: e.matmul(pu[0:64, 0:129], kw[64 * half:64 * half + 64, :], vext[64 * half:64 * half + 64, blk, :], start=True, stop=True),
                   reads=[kw, vext], writes=[pu])
                op("dve", lambda e, pu=pu, c=c, Sa=Sa, Sb=Sb: e.scalar_tensor_tensor(Sb[:, :], Sa[:, :], dec_b[:, c:c + 1], pu[0:64, 0:129], op0=ALU.mult, op1=ALU.add), reads=[Sa, dec_b, pu], writes=[Sb])
        if "2" in parts:
            def stX(blk):
                t0 = blk * 128
                pS = C.psum()
                op("pe", lambda e, pS=pS, t0=t0: e.matmul(pS[:, 0:128], mk[:, t0:t0 + 128], mq[:, t0:t0 + 128], start=True, stop=True), reads=[mk, mq], writes=[pS])
                pt = C.tmp("mpt", [128, 128], BF16, 3)
                op("dve", lambda e, pS=pS, pt=pt, blk=blk: e.scalar_tensor_tensor(pt[:, :], pS[:, 0:128], fcol[:, 0, blk:blk + 1], cB[:, 128:256], op0=ALU.mult, op1=ALU.mult), reads=[pS, fcol, cB], writes=[pt])
                return pt

            def stY(blk, pt):
                t0 = blk * 128
                po1 = C.psum(); po2 = C.psum()
                op("pe", lambda e, po1=po1, pt=pt, blk=blk: e.matmul(po1[:, 0:129], pt[:, :], vext[:, blk, :], start=True, stop=True), reads=[pt, vext], writes=[po1])
                for half in range(2):
                    op("pe", lambda e, po2=po2, half=half, blk=blk, t0=t0: e.matmul(po2[64 * half:64 * half + 64, 0:129], mq[:, t0 + 64 * half:t0 + 64 * half + 64], Call[:, 2 * blk + half, :], start=True, stop=True),
                       reads=[mq, Call], writes=[po2])
                o1 = C.tmp("o1", [128, 129], F32, 2); hnum = C.tmp("hnum", [128, 129], F32, 2)
                op("dve", lambda e, o1=o1, po1=po1, blk=blk: e.tensor_scalar(o1[:, :], po1[:, 0:129], fcol[:, 1, blk:blk + 1], None, op0=ALU.mult), reads=[po1, fcol], writes=[o1])
                op("dve", lambda e, o1=o1, po2=po2, hnum=hnum, blk=blk: e.scalar_tensor_tensor(hnum[:, :], po2[:, 0:129], fcol[:, 2, blk:blk + 1], o1[:, :], op0=ALU.mult, op1=ALU.add), reads=[po2, fcol, o1], writes=[hnum])
                sm = C.tmp("sm", [128, 8], F32, 2)
                op("dve", lambda e, sm=sm, hnum=hnum: e.scalar_tensor_tensor(sm[:, 6:7], hnum[:, 128:129], -1.0, hnum[:, 128:129], op0=ALU.mult, op1=ALU.max), reads=[hnum], writes=[sm])
                op("dve", lambda e, sm=sm, blk=blk: e.tensor_tensor(sm[:, 0:1], sm[:, 6:7], fcol[:, 4, blk:blk + 1], op=ALU.max), reads=[sm, fcol], writes=[sm])
                op("dve", lambda e, sm=sm: e.reciprocal(sm[:, 1:2], sm[:, 0:1]), reads=[sm], writes=[sm])
                junk = C.tmp("junk", [128, 128], F32, 2)
                op("dve", lambda e, hnum=hnum, sm=sm: e.tensor_scalar(hnum[:, 0:128], hnum[:, 0:128], sm[:, 1:2], None, op0=ALU.mult), reads=[hnum, sm], writes=[hnum])
                op("act", lambda e, junk=junk, hnum=hnum, sm=sm: e.activation(junk[:, :], hnum[:, 0:128], AF.Square, accum_out=sm[:, 2:3]), reads=[hnum], writes=[junk, sm])
                op("act", lambda e, sm=sm: e.activation(sm[:, 3:4], sm[:, 2:3], AF.Sqrt, bias=EPS, scale=1.0 / 128.0), reads=[sm], writes=[sm])
                op("dve", lambda e, sm=sm: e.reciprocal(sm[:, 4:5], sm[:, 3:4]), reads=[sm], writes=[sm])
                yt = C.tmp("yt", [128, 128], F32, 2); yd = C.tmp("yd", [128, 128], F32, 3)
                op("dve", lambda e, yt=yt, hnum=hnum, sm=sm: e.scalar_tensor_tensor(yt[:, :], hnum[:, 0:128], sm[:, 4:5], gmh[:, :], op0=ALU.mult, op1=ALU.mult), reads=[hnum, sm, gmh], writes=[yt])
                op("pool", lambda e, yt=yt, yd=yd, blk=blk: e.tensor_tensor(yd[:, :], yt[:, :], so[:, blk, :], op=ALU.mult), reads=[yt, so], writes=[yd])
                return yd

            zst = {"pT": None}

            def stZ(blk, yd):
                g4, j4 = blk // 4, blk % 4
                if j4 == 0:
                    zst["pT"] = C.psum(pin=True)
                pT = zst["pT"]
                op("pe", lambda e, pT=pT, yd=yd, j4=j4: e.transpose(pT[:, j4 * 128:(j4 + 1) * 128], yd[:, :], cB[:, 256:384]), reads=[yd, cB], writes=[pT])
                if j4 == 3:
                    ob = C.tmp("ydo", [128, 512], BF16, 2)
                    op("act", lambda e, ob=ob, pT=pT: e.activation(ob[:, :], pT[:, :], AF.Copy), reads=[pT], writes=[ob])
                    P.dma("sp", send2[(g4 // 4) * 256 + 128:(g4 // 4) * 256 + 256, (g4 % 4) * 512:(g4 % 4) * 512 + 512], ob[:, :], reads=[ob])
                    C.unpin()

            pts = {0: stX(0)}
            yds = {}
            for blk in range(64):
                if blk + 1 < 64:
                    pts[blk + 1] = stX(blk + 1)
                yds[blk] = stY(blk, pts.pop(blk))
                if blk >= 1:
                    stZ(blk - 1, yds.pop(blk - 1))
            stZ(63, yds.pop(63))

    if "a" in parts:
        qTs = [P.sb("qT%d" % i, [96, SEQ], BF16) for i in range(2)]; kTs = [P.sb("kT%d" % i, [96, SEQ], BF16) for i in range(2)]
        V = P.sb("V", [128, 64, 65], BF16); vTss = [P.sb("vTs%d" % i, [64, SEQ], BF16) for i in range(2)]
        for hh in range(2):
            for q in range(4):
                cs_ = slice(q * NTOK, (q + 1) * NTOK)
                gat(qTs[hh], 96, q, hh * 4 + q); gat(kTs[hh], 96, q, 8 + hh * 4 + q); gat(vTss[hh], 64, q, 16 + hh * 4 + q)
        tails = []
        for hh in range(2):
            qT = qTs[hh]; kT = kTs[hh]; vTs = vTss[hh]
            op("pool", lambda e: e.memset(V[:, :, :], 1.0), writes=[V])
            for g in range(4):
                pt_ = ptb[g % 2]
                for b_ in range(16):
                    blk = g * 16 + b_
                    op("pe", lambda e, pt_=pt_, b_=b_, blk=blk, vTs=vTs: e.transpose(pt_[:, b_ * 64:(b_ + 1) * 64], vTs[0:64, blk * 128:(blk + 1) * 128], idb[0:64, 0:64]), reads=[vTs, idb], writes=[pt_])
                op("dve", lambda e, pt_=pt_, g=g: e.tensor_copy(V[:, g * 16:(g + 1) * 16, 0:64], pt_[:, :].rearrange("p (b w) -> p b w", w=64)), reads=[pt_], writes=[V])
            for qt in range(16):
                po = C.psum(pin=True)
                nkb = 4 * qt + 4
                LOOK = 3
                pend = []
                for it in range(nkb + LOOK):
                    if it < nkb:
                        kb = it
                        c0 = max(0, kb - 4 * qt) * 128
                        ps = C.psum()
                        op("pe", lambda e, ps=ps, kb=kb, c0=c0, qt=qt, kT=kT, qT=qT: e.matmul(ps[:, c0:512], kT[:, kb * 128:(kb + 1) * 128], qT[:, qt * 512 + c0:qt * 512 + 512], start=True, stop=True), reads=[kT, qT], writes=[ps])
                        pt = C.tmp("apt", [128, 512], BF16, 6)
                        op("act", lambda e, ps=ps, pt=pt, c0=c0: e.activation(pt[:, c0:512], ps[:, c0:512], AF.Exp), reads=[ps], writes=[pt])
                        if kb >= 4 * qt:
                            op("pool", lambda e, pt=pt, c0=c0: e.tensor_tensor(pt[:, c0:c0 + 128], pt[:, c0:c0 + 128], maskb[:, :], op=ALU.mult), reads=[pt, maskb], writes=[pt])
                        pend.append((kb, c0, pt))
                    if it == 2 and tails:
                        tails.pop(0)()
                    if it >= LOOK:
                        kb, c0, pt = pend[it - LOOK]
                        op("pe", lambda e, po=po, pt=pt, kb=kb, c0=c0, nkb=nkb: e.matmul(po[0:65, c0:512], V[:, kb, :], pt[:, c0:512], start=(kb == 0), stop=(kb == nkb - 1)), reads=[V, pt], writes=[po])
                osb = C.tmp("osb", [65, 512], F32, 2, own=True)
                op("act", lambda e, osb=osb, po=po: e.activation(osb[:, :], po[0:65, :], AF.Copy), reads=[po], writes=[osb])
                C.unpin()
                op("dve", lambda e, osb=osb: e.reciprocal(osb[64:65, :], osb[64:65, :]), reads=[osb], writes=[osb])

                def tail(osb=osb, hh=hh, qt=qt):
                    pb = C.psum()
                    op("pe", lambda e, pb=pb, osb=osb: e.matmul(pb[0:64, :], ones_f[64:65, 0:64], osb[64:65, :], start=True, stop=True), reads=[ones_f, osb], writes=[pb])
                    yc = C.tmp("yc", [64, 512], BF16, 2)
                    op("dve", lambda e, yc=yc, osb=osb, pb=pb: e.tensor_tensor(yc[:, :], osb[0:64, :], pb[0:64, :], op=ALU.mult), reads=[osb, pb], writes=[yc])
                    P.dma("sp", send2[(qt // 4) * 256 + hh * 64:(qt // 4) * 256 + hh * 64 + 64, (qt % 4) * 512:(qt % 4) * 512 + 512], yc[:, :], reads=[yc])
                tails.append(tail)
        while tails:
            tails.pop(0)()
    P.wait_all("sp", P.tiles)
    return P


R1ROWS = 3360
Q0, KN0, KP0, V0, MQ0, MK0, MV0, SO0 = 0, 768, 1280, 1312, 1824, 2080, 2336, 2848
GROUPS = [[0, 1, 2, 3], [4, 5, 6, 7]]
CCH = 240
CCH1 = 480


def _chunks(total, cch=CCH):
    out = []
    start = 0
    base = 0
    while start < total:
        size = min(cch, total - start)
        out.append((start, size, base))
        base += 4 * size
        start += size
    return out


def _rowmap(total, r, q, cch=CCH):
    k = r // cch
    start = k * cch
    size = np.minimum(cch, total - start)
    return 4 * start + q * size + (r - start)


def _gather_all(P, send, recv, total, cch=CCH, wait=True):
    keys = []
    for (start, size, base) in _chunks(total, cch):
        keys.append(P.collective_raw("AllGather", send[start:start + size, :], recv[base:base + 4 * size, :], GROUPS, wait=False))
    if wait:
        P.ops["pool"].append(([(k, 1) for k in keys], None, None))
    return keys


def build_fused(phases="AMTC"):
    nc = bass.Bass("TRN2", target_bir_lowering=False)
    send1 = [nc.dram_tensor("send1_%d" % i, [R1ROWS, 1024], BF16).ap() for i in range(2)]
    recv1 = [nc.dram_tensor("recv1_%d" % i, [4 * R1ROWS, 1024], BF16).ap() for i in range(2)]
    sendm = nc.dram_tensor("sendm", [8, NTOK], F32).ap()
    recvm = nc.dram_tensor("recvm", [32, NTOK], F32).ap()
    send2 = nc.dram_tensor("send2", [1024, NTOK], BF16).ap()
    recv2 = nc.dram_tensor("recv2", [4096, NTOK], BF16).ap()
    h1d = nc.dram_tensor("h1d", [1024, NTOK], F32).ap()
    idx = dram_in(nc, "idx", [128, 64], I32)
    consts = dram_in(nc, "consts", [128, 512])

    def tokD(L):
        sfx = str(L)
        return {
            "pT": dram_in(nc, "pT" + sfx, [256, NTOK]), "gains": dram_in(nc, "gains" + sfx, [128, 32]), "consts": consts,
            "w_gate": dram_in(nc, "w_gate" + sfx, [1024, 2816]), "w_up": dram_in(nc, "w_up" + sfx, [1024, 2816]),
            "w_down": dram_in(nc, "w_down" + sfx, [2816, 1024]), "w_ple_gate": dram_in(nc, "w_ple_gate" + sfx, [1024, 1024]),
            "w_ple_proj": dram_in(nc, "w_ple_proj" + sfx, [256, 1024]), "w_out": dram_in(nc, "w_out" + sfx, [1024, 1024]),
        }
    DA = tokD(0)
    DA.update({
        "hT": dram_in(nc, "xT", [1024, NTOK]), "h1T": h1d,
        "mifT": sendm,
    })
    for hf in range(2):
        for nm, r0, nr in (("qTf", Q0, 768), ("knT", KN0, 512), ("kpeT", KP0, 32), ("vT", V0, 512), ("mqT", MQ0, 256), ("mkT", MK0, 256),
                           ("mvT", MV0, 512), ("soT", SO0, 512)):
            DA["%s_h%d" % (nm, hf)] = send1[hf][r0:r0 + nr, :]
    for nm in ("qT", "knT", "kpeT", "vT", "mqT", "mkT", "mvT", "soT"):
        DA[nm] = send1[0]
    keys1 = []

    def mid_hook(P):
        P.wait_all("pool", P.tiles)
        keys1.extend(_gather_all(P, send1[0], recv1[0], R1ROWS, CCH1, wait=False))
    PA = build_tok("A", {"nc": nc, "D": DA, "mid_hook": mid_hook})
    PA.wait_all("pool", PA.tiles)
    keys1.extend(_gather_all(PA, send1[1], recv1[1], R1ROWS, CCH1, wait=False))
    PA.ops["pool"].append(([(k, 1) for k in keys1], None, None))
    PA.collective_raw("AllGather", sendm, recvm, GROUPS)
    PA.emit()
    nc.all_engine_barrier()
    if "M" not in phases and "T" not in phases and "C" not in phases:
        dram_out(nc, "outT", [1024, NTOK])
        return nc

    DB = {"idx": idx, "recv1": recv1, "recvm": recvm, "send2": send2,
          "cB": dram_in(nc, "cB", [128, 512]), "bcol": dram_in(nc, "bcol", [128, 2]), "gmh": dram_in(nc, "gmh", [128, 128])}
    if "M" in phases:
        PM = build_mix("sm2", {"nc": nc, "D": DB})
        PM.wait_all("pool", PM.tiles)
        PM.emit()
        nc.all_engine_barrier()
    if "T" in phases:
        PT = build_mix("a", {"nc": nc, "D": DB})
        _gather_all(PT, send2, recv2, 1024)
        PT.emit()
        nc.all_engine_barrier()
    if "C" not in phases:
        dram_out(nc, "outT", [1024, NTOK])
        return nc

    DC = tokD(1)
    DC.update({"hT": h1d, "idx": idx, "recv2": recv2, "outT": dram_out(nc, "outT", [1024, NTOK])})
    if "D" in phases:
        DC["dbg_y"] = dram_out(nc, "dbg_y", [1024, NTOK], BF16)
    PC = build_tok("C", {"nc": nc, "D": DC})
    PC.wait_all("pool", PC.tiles)
    if "D" in phases:
        d1 = dram_out(nc, "dbg_send1", [R1ROWS, NTOK], BF16); d2 = dram_out(nc, "dbg_send2", [1024, NTOK], BF16); d3 = dram_out(nc, "dbg_sendm", [8, NTOK]); d4 = dram_out(nc, "dbg_h1", [1024, NTOK])
        dummy = PC.sb("dbgdummy", [1, 8])
        PC.dma("sp", d1, send1, reads=[dummy]); PC.dma("sp", d2, send2, reads=[dummy]); PC.dma("sp", d3, sendm, reads=[dummy]); PC.dma("sp", d4, h1d, reads=[dummy])
        PC.wait_all("sp", [dummy])
    PC.emit()
    return nc


def prep_fused(inp):
    mapsA = prep_A(inp)
    maps = []
    cB = np.zeros((128, 512), np.float32)
    s = np.arange(128)
    cB[:, 0:128] = (s[:, None] <= s[None, :])
    cB[:, 128:256] = (s[:, None] <= s[None, :]) & ((s[:, None] // 64) == (s[None, :] // 64))
    cB[:, 256:384] = np.eye(128, dtype=np.float32)
    gains1 = np.zeros((128, 32), np.float32)
    gains1[:, 8:16] = _chunkT(inp["g_ffn"][1]); gains1[:, 16:24] = _chunkT(inp["g_ple"][1])
    bg = inp["od_b_gate"][0]
    p = np.arange(128)
    for core in range(8):
        b, j = core // 4, core % 4
        q_ = j
        a = mapsA[core]
        m = {"xT": a["hT"], "xhT": a["xhT"], "posb": a["posb"], "consts": a["consts"], "idx": None,
             "pT0": a["pT"], "gains0": a["gains"], "w_gate0": a["w_gate"], "w_up0": a["w_up"], "w_down0": a["w_down"],
             "w_ple_gate0": a["w_ple_gate"], "w_ple_proj0": a["w_ple_proj"], "w_out0": a["w_out"]}
        for k in ("ev_w_in", "wconv", "gvb", "wsT", "bsb", "od_w_in", "g_lat", "w_q_up", "w_q_up_sw", "w_kv_up", "gsm"):
            m[k] = a[k]
        c1 = prep_tok_common(inp, 1, core)
        m.update({"pT1": c1["pT"], "gains1": gains1, "w_gate1": c1["w_gate"], "w_up1": c1["w_up"], "w_down1": c1["w_down"],
                  "w_ple_gate1": c1["w_ple_gate"], "w_ple_proj1": c1["w_ple_proj"], "w_out1": _c(inp["od_w_out"][0])})
        m["cB"] = cB
        m["bcol"] = _c(np.tile(np.array([[bg[j], bg[4 + j]]], np.float32), (128, 1)))
        m["gmh"] = _c(np.tile(inp["od_g_mh"][0][j].reshape(1, 128), (128, 1)))
        ix = np.zeros((128, 64), np.int32)
        def rm(r, q):
            return _rowmap(R1ROWS, np.asarray(r), q, CCH1)
        for q in range(4):
            for hh in range(2):
                hd = 2 * j + hh
                ix[0:96, hh * 4 + q] = rm(Q0 + hd * 96 + p[0:96], q)
                ix[0:64, 8 + hh * 4 + q] = rm(KN0 + hd * 64 + p[0:64], q)
                ix[64:96, 8 + hh * 4 + q] = rm(KP0 + p[0:32], q)
                ix[0:64, 16 + hh * 4 + q] = rm(V0 + hd * 64 + p[0:64], q)
            ix[0:64, 24 + q] = rm(MQ0 + j * 64 + p[0:64], q)
            ix[0:64, 28 + q] = rm(MK0 + j * 64 + p[0:64], q)
            ix[:, 32 + q] = rm(MV0 + j * 128 + p, q)
            ix[:, 36 + q] = rm(SO0 + j * 128 + p, q)
            ix[q * 32:(q + 1) * 32, 40] = (q * 8 + j) * 32 + p[0:32]
            ix[q * 32:(q + 1) * 32, 41] = (q * 8 + 4 + j) * 32 + p[0:32]
        for c in range(8):
            if c < 4:
                ix[:, 48 + c] = _rowmap(1024, q_ * 256 + p, c)
            else:
                ix[:, 48 + c] = _rowmap(1024, q_ * 256 + 128 + p, c - 4)
        m["idx"] = ix
        maps.append(m)
    return maps


_NC_CACHE = {}


def kernel(**inputs):
    inp = {k: np.asarray(v) for k, v in inputs.items()}
    if "nc" not in _NC_CACHE:
        _NC_CACHE["nc"] = build_fused()
    res = run_bass_kernel_spmd(_NC_CACHE["nc"], prep_fused(inp), core_ids=list(range(8))).results
    out = np.zeros((2, SEQ, 1024), np.float32)
    for core in range(8):
        b, q = core // 4, core % 4
        out[b, q * NTOK:(q + 1) * NTOK, :] = res[core]["outT"].T
    return out
```

```python
import numpy as np
import ml_dtypes
import concourse.bass as bass
import concourse.mybir as mybir
from concourse.bass_utils import run_bass_kernel_spmd

F32 = mybir.dt.float32
BF16 = mybir.dt.bfloat16
I32 = mybir.dt.int32
AF = mybir.ActivationFunctionType
ALU = mybir.AluOpType
AX = mybir.AxisListType
EPS = 1e-6
TWO_PI = 6.283185307179586
PI = 3.141592653589793
NTOK = 2048
TS = 512
class TT:
    __slots__ = ("h", "name", "last_w", "readers", "dsem", "dcount")

    def __init__(self, h, name):
        self.h = h
        self.name = name
        self.last_w = None
        self.readers = []
        self.dsem = None
        self.dcount = 0

    def __getitem__(self, idx):
        return self.h[idx]


class Prog:
    ENGS = ("pe", "act", "dve", "pool", "sp")

    def __init__(self, nc, prefix=""):
        self.nc = nc
        self.prefix = prefix
        self.ops = {e: [] for e in self.ENGS}
        self.count = {e: 0 for e in self.ENGS}
        self.waited = {e: {} for e in self.ENGS}
        self.sem_names = ["eng_" + e for e in self.ENGS]
        self.tiles = []
        self._ctx = []
        self.n_dsem = 0

    def sb(self, name, shape, dt=F32):
        g = self.nc.sbuf_tensor(self.prefix + "s_" + name, list(shape), dt)
        h = g.__enter__()
        self._ctx.append(g)
        t = TT(h, name)
        self.tiles.append(t)
        return t

    def ps(self, name, shape, dt=F32):
        g = self.nc.psum_tensor(self.prefix + "p_" + name, list(shape), dt)
        h = g.__enter__()
        self._ctx.append(g)
        t = TT(h, name)
        self.tiles.append(t)
        return t

    def _deps(self, eng, reads, writes):
        deps = []
        for r in reads:
            if r.last_w is not None:
                deps.append(r.last_w)
        for w in writes:
            if w.last_w is not None:
                deps.append(w.last_w)
            deps.extend(w.readers)
        waits = []
        wd = self.waited[eng]
        best = {}
        for (k, v) in deps:
            if eng == "pe" and k == "eng_pe":
                continue
            if wd.get(k, 0) >= v:
                continue
            if best.get(k, 0) < v:
                best[k] = v
        for k, v in best.items():
            wd[k] = v
            waits.append((k, v))
        return waits

    def op(self, eng, fn, reads=(), writes=()):
        waits = self._deps(eng, reads, writes)
        self.count[eng] += 1
        me = ("eng_" + eng, self.count[eng])
        self.ops[eng].append((waits, fn, (me[0], 1)))
        for r in reads:
            r.readers.append(me)
        for w in writes:
            w.last_w = me
            w.readers = []
        return me

    def dma(self, eng, out_ap, in_ap, reads=(), writes=(), **kw):
        waits = self._deps(eng, reads, writes)
        owner = (list(writes) + list(reads))[0]
        if owner.dsem is None:
            owner.dsem = "d%d" % self.n_dsem
            self.n_dsem += 1
            self.sem_names.append(owner.dsem)
        owner.dcount += 1
        me = (owner.dsem, 16 * owner.dcount)

        def fn(e, out_ap=out_ap, in_ap=in_ap, kw=kw):
            o_ = out_ap() if callable(out_ap) else out_ap
            i_ = in_ap() if callable(in_ap) else in_ap
            return e.dma_start(out=o_, in_=i_, **kw)
        self.ops[eng].append((waits, fn, (owner.dsem, 16)))
        for r in reads:
            r.readers.append(me)
        for w in writes:
            w.last_w = me
            w.readers = []
        return me

    def raw(self, eng, fn):
        self.ops[eng].append(([], fn, "raw"))

    def dram(self, name, shape, dt=F32):
        h = self.nc.dram_tensor(name, list(shape), dt).ap()
        t = TT(h, name)
        self.tiles.append(t)
        return t

    def collective(self, kind, in_t, out_t, groups):
        waits = self._deps("pool", [in_t], [out_t])
        key = "cc%d" % self.n_dsem
        self.n_dsem += 1
        self.sem_names.append(key)
        me = (key, 1)

        def fn(e):
            return e.collective_compute(kind, ALU.bypass, replica_groups=groups, ins=[in_t.h.opt()], outs=[out_t.h.opt()])
        self.ops["pool"].append((waits, fn, (key, None)))
        in_t.readers.append(me)
        out_t.last_w = me
        out_t.readers = []
        return me

    def gather(self, out_t, out_ap, in_ap, idx_t, idx_ap):
        waits = self._deps("pool", [idx_t], [out_t])
        if out_t.dsem is None:
            out_t.dsem = "d%d" % self.n_dsem
            self.n_dsem += 1
            self.sem_names.append(out_t.dsem)
        out_t.dcount += 1
        me = (out_t.dsem, 16 * out_t.dcount)

        def fn(e):
            return e.indirect_dma_start(out=out_ap, out_offset=None, in_=in_ap, in_offset=bass.IndirectOffsetOnAxis(ap=idx_ap, axis=0))
        self.ops["pool"].append((waits, fn, (out_t.dsem, 16)))
        idx_t.readers.append(me)
        out_t.last_w = me
        out_t.readers = []
        return me

    def collective_raw(self, kind, in_ap, out_ap, groups, wait=True):
        self.wait_all("pool", self.tiles)
        key = "cc%d" % self.n_dsem
        self.n_dsem += 1
        self.sem_names.append(key)

        def fn(e):
            return e.collective_compute(kind, ALU.bypass, replica_groups=groups, ins=[in_ap.opt()], outs=[out_ap.opt()])
        self.ops["pool"].append(([], fn, (key, None)))
        if wait:
            self.ops["pool"].append(([(key, 1)], None, None))
        return key

    def wait_all(self, eng, tiles):
        deps = []
        for t in tiles:
            if t.last_w is not None:
                deps.append(t.last_w)
            deps.extend(t.readers)
        wd = self.waited[eng]
        best = {}
        for k, v in deps:
            if wd.get(k, 0) < v and best.get(k, 0) < v:
                best[k] = v
        waits = []
        for k, v in best.items():
            wd[k] = v
            waits.append((k, v))
        self.ops[eng].append((waits, None, None))

    def emit(self):
        nc = self.nc
        sems = {}
        for n in self.sem_names:
            sems[n] = nc.alloc_semaphore(name=self.prefix + n)
        blk = nc.Block()
        block = blk.__enter__()

        def run(engname):
            def body(e):
                for waits, fn, inc in self.ops[engname]:
                    for k, v in waits:
                        e.wait_ge(sems[k], v)
                    if fn is not None:
                        ins = fn(e)
                        if inc == "raw":
                            continue
                        if inc[1] is None:
                            ins.then_inc(sems[inc[0]])
                        else:
                            ins.then_inc(sems[inc[0]], inc[1])
            return body

        block.tensor(run("pe"))
        block.scalar(run("act"))
        block.vector(run("dve"))
        block.gpsimd(run("pool"))
        block.sync(run("sp"))
        blk.__exit__(None, None, None)
        nc.all_engine_barrier()
        nc.clear_and_free_semaphores(list(sems.values()))
        nc.all_engine_barrier()
        for g in reversed(self._ctx):
            g.__exit__(None, None, None)
        self._ctx = []


class TV:
    def __init__(self, base, ap):
        self.__dict__["base"] = base
        self.__dict__["ap"] = ap

    def __getitem__(self, idx):
        return self.ap[idx]

    def __getattr__(self, k):
        return getattr(self.base, k)

    def __setattr__(self, k, v):
        setattr(self.base, k, v)


class Ctx:
    def __init__(self, P, nbanks=8):
        self.P = P
        self.nb = nbanks
        self.pbanks = [P.ps("pb%d" % i, [128, 512], F32) for i in range(nbanks)]
        self.pi = 0
        self.pinned = set()
        self.rings = {}

    def psum(self, pin=False):
        while (self.pi % self.nb) in self.pinned:
            self.pi += 1
        t = self.pbanks[self.pi % self.nb]
        if pin:
            self.pinned.add(self.pi % self.nb)
        self.pi += 1
        return t

    def unpin(self):
        self.pinned = set()

    def tmp(self, key, shape, dt=F32, n=2, own=False):
        if dt == F32 and len(shape) == 2 and shape[1] == 512 and not own:
            base = self.tmp("T32", [128, 512, 1], F32, 8)
            return TV(base, base.h[0:shape[0], :, 0])
        if key not in self.rings:
            self.rings[key] = [[self.P.sb("%s_%d" % (key, i), shape, dt) for i in range(n)], 0]
        r = self.rings[key]
        t = r[0][r[1] % len(r[0])]
        r[1] += 1
        return t


def dram_in(nc, name, shape, dt=F32):
    return nc.dram_tensor(name, list(shape), dt, kind="ExternalInput").ap()


def dram_out(nc, name, shape, dt=F32):
    return nc.dram_tensor(name, list(shape), dt, kind="ExternalOutput").ap()


def build_tok(mode, fz=None):
    nc = fz["nc"] if fz else bass.Bass("TRN2", target_bir_lowering=False)
    P = Prog(nc, mode + "_")
    C = Ctx(P)
    op = P.op
    L = 0 if mode == "A" else 1

    D = dict(fz["D"]) if fz else {}
    def din(name, shape, dt=F32):
        if name not in D:
            D[name] = dram_in(nc, name, shape, dt)
        return D[name]
    def dout(name, shape, dt=F32):
        if name not in D:
            D[name] = dram_out(nc, name, shape, dt)
        return D[name]

    din("hT", [1024, NTOK])
    din("pT", [256, NTOK])
    din("gains", [128, 32])
    din("consts", [128, 512])
    din("w_gate", [1024, 2816]); din("w_up", [1024, 2816]); din("w_down", [2816, 1024])
    din("w_ple_gate", [1024, 1024]); din("w_ple_proj", [256, 1024])
    din("w_out", [1024, 1024])
    if mode == "A":
        din("xhT", [1024, 2])
        din("posb", [96, NTOK], I32)
        din("ev_w_in", [1024, 2560])
        din("wconv", [128, 12]); din("gvb", [128, 512]); din("wsT", [128, 8, 128]); din("bsb", [128, 4, 128])
        din("od_w_in", [1024, 2248])
        din("g_lat", [128, 5])
        din("w_q_up", [384, 768]); din("w_q_up_sw", [384, 768]); din("w_kv_up", [256, 1024])
        din("gsm", [128, 8])
        dout("h1T", [1024, NTOK])
        dout("qT", [8, 96, NTOK], BF16); dout("knT", [512, NTOK], BF16); dout("kpeT", [32, NTOK], BF16)
        dout("vT", [512, NTOK], BF16)
        dout("mqT", [256, NTOK], BF16); dout("mkT", [256, NTOK], BF16)
        dout("mvT", [512, NTOK], BF16); dout("soT", [512, NTOK], BF16)
        dout("mifT", [8, NTOK])
    else:
        if not fz:
            din("ymT", [1024, NTOK], BF16)
        else:
            idxc = P.sb("idxc", [128, 64], I32)
            P.dma("sp", idxc[:, :], D["idx"], writes=[idxc])
            ystg = [P.sb("ystg%d" % c, [128, NTOK], BF16) for c in range(8)]
            for c in range(8):
                P.gather(ystg[c], ystg[c][:, :], D["recv2"][:, :], idxc, idxc[:, 48 + c:49 + c])
                if "dbg_y" in D:
                    P.dma("sp", D["dbg_y"][c * 128:(c + 1) * 128, :], ystg[c][:, :], reads=[ystg[c]])
        dout("outT", [1024, NTOK])

    h = [[P.sb("h%d_%d" % (s, c), [128, TS]) for c in range(8)] for s in range(2)]
    hn = [[P.sb("hn%d_%d" % (s, c), [128, TS], BF16) for c in range(8)] for s in range(2)]
    act = [[P.sb("act%d_%d" % (s, j), [128, TS], BF16) for j in range(22)] for s in range(2)]
    y = [a[0:8] for a in act]
    ringA = [P.sb("wA%d" % i, [128, 8, 512], BF16) for i in range(3)]
    ringD = [P.sb("wD%d" % i, [128, 22, 128], BF16) for i in range(2)]
    st = {"a": 0, "d": 0}
    wpp = P.sb("wpp", [128, 2, 1024], BF16)
    gains = P.sb("gains", [128, 32])
    cst = P.sb("cst", [128, 512])
    ones_bf = P.sb("ones_bf", [128, 128], BF16)
    P.dma("sp", gains[:, :], D["gains"], writes=[gains])
    P.dma("sp", cst[:, :], D["consts"], writes=[cst])
    op("pool", lambda e: e.memset(ones_bf[:, :], 1.0), writes=[ones_bf])
    P.dma("pool", wpp[:, :, :], D["w_ple_proj"].rearrange("(kc p) n -> p kc n", p=128), writes=[wpp])

    def slotA():
        t = ringA[st["a"] % 3]; st["a"] += 1; return t

    def slotD():
        t = ringD[st["d"] % 2]; st["d"] += 1; return t

    def loadA(w, c0, c1, dst=None, off=0):
        t = dst if dst is not None else slotA()
        P.dma("pool", t[:, :, off:off + (c1 - c0)], w.rearrange("(kc p) n -> p kc n", p=128)[:, :, c0:c1], writes=[t])
        return t

    def mm(ps_ap, lhs_fn, rhs_fn, nk, reads, ps):
        for kc in range(nk):
            l_ = lhs_fn(kc); r_ = rhs_fn(kc)
            op("pe", lambda e, kc=kc, l_=l_, r_=r_: e.matmul(ps_ap, l_, r_, start=(kc == 0), stop=(kc == nk - 1)),
               reads=reads, writes=[ps])

    def rstd_from_ps(ps, rows, n, scale, tag):
        sd = C.tmp("sd" + tag, [128, 512])
        rp = C.tmp("rp" + tag, [128, 512])
        rd = [ps] + ([cst] if not isinstance(scale, float) else [])
        op("act", lambda e: e.activation(sd[0:rows, 0:n], ps[0:rows, 0:n], AF.Sqrt, bias=EPS, scale=scale), reads=rd, writes=[sd])
        op("dve", lambda e: e.reciprocal(rp[0:rows, 0:n], sd[0:rows, 0:n]), reads=[sd], writes=[rp])
        return rp

    def norm(s, gcol, n=TS, src=None, dst=None):
        src = src or h[s]; dst = dst or hn[s]
        ps = C.psum()
        for c in range(8):
            sq = C.tmp("sq", [128, 512], BF16, 3)
            op("act", lambda e, c=c, sq=sq: e.activation(sq[:, 0:n], src[c][:, 0:n], AF.Square), reads=[src[c]], writes=[sq])
            op("pe", lambda e, c=c, sq=sq: e.matmul(ps[:, 0:n], ones_bf[:, :], sq[:, 0:n], start=(c == 0), stop=(c == 7)), reads=[sq, ones_bf], writes=[ps])
        rp = rstd_from_ps(ps, 128, n, 1.0 / 1024.0, "n")
        for c in range(8):
            op("dve", lambda e, c=c: e.scalar_tensor_tensor(dst[c][:, 0:n], src[c][:, 0:n], gains[:, gcol + c:gcol + c + 1], rp[:, 0:n], op0=ALU.mult, op1=ALU.mult),
               reads=[src[c], gains, rp], writes=[dst[c]])

    def resid_proj(w, src):
        for blk in range(2):
            slot = loadA(w, blk * 512, blk * 512 + 512)
            for s in range(2):
                for m in range(4):
                    ps = C.psum()
                    mm(ps[:, :], lambda kc, m=m: slot[:, kc, m * 128:(m + 1) * 128], lambda kc, s=s: src[s][kc][:, :], 8, [slot] + src[s], ps)
                    hc = h[s][blk * 4 + m]
                    op("dve", lambda e, ps=ps, hc=hc: e.tensor_tensor(hc[:, :], ps[:, :], hc[:, :], op=ALU.add), reads=[ps, hc], writes=[hc])

    def ffn(gcol):
        for s in range(2):
            norm(s, gcol)
        for j in range(11):
            slot = slotA()
            loadA(D["w_gate"], j * 256, j * 256 + 256, dst=slot, off=0)
            loadA(D["w_up"], j * 256, j * 256 + 256, dst=slot, off=256)
            for s in range(2):
                for jj in range(2):
                    pg = C.psum(); pu = C.psum()
                    mm(pg[:, :], lambda kc, jj=jj: slot[:, kc, jj * 128:(jj + 1) * 128], lambda kc, s=s: hn[s][kc][:, :], 8, [slot] + hn[s], pg)
                    mm(pu[:, :], lambda kc, jj=jj: slot[:, kc, 256 + jj * 128:256 + (jj + 1) * 128], lambda kc, s=s: hn[s][kc][:, :], 8, [slot] + hn[s], pu)
                    sg = C.tmp("sg", [128, 512], F32, 3)
                    op("act", lambda e, pg=pg, sg=sg: e.activation(sg[:, :], pg[:, :], AF.Silu), reads=[pg], writes=[sg])
                    a = act[s][2 * j + jj]
                    op("dve", lambda e, pu=pu, sg=sg, a=a: e.tensor_tensor(a[:, :], pu[:, :], sg[:, :], op=ALU.mult), reads=[pu, sg], writes=[a])
        for mb in range(8):
            slot = slotD()
            P.dma("pool", slot[:, :, :], D["w_down"].rearrange("(kc p) n -> p kc n", p=128)[:, :, mb * 128:(mb + 1) * 128], writes=[slot])
            for s in range(2):
                ps = C.psum()
                mm(ps[:, :], lambda kc: slot[:, kc, :], lambda kc, s=s: act[s][kc][:, :], 22, [slot] + act[s], ps)
                hc = h[s][mb]
                op("dve", lambda e, ps=ps, hc=hc: e.tensor_tensor(hc[:, :], ps[:, :], hc[:, :], op=ALU.add), reads=[ps, hc], writes=[hc])

    def ple(gcol, tok0):
        pt = []
        for s in range(2):
            norm(s, gcol)
            t = C.tmp("pt", [128, 2, TS], BF16, 2)
            P.dma("pool", t[:, :, :], D["pT"].rearrange("(kc p) n -> p kc n", p=128)[:, :, tok0 + s * TS:tok0 + (s + 1) * TS], writes=[t])
            pt.append(t)
        for blk in range(2):
            slot = loadA(D["w_ple_gate"], blk * 512, blk * 512 + 512)
            for s in range(2):
                for m in range(4):
                    mg = blk * 4 + m
                    pg = C.psum(); pp = C.psum()
                    mm(pg[:, :], lambda kc, m=m: slot[:, kc, m * 128:(m + 1) * 128], lambda kc, s=s: hn[s][kc][:, :], 8, [slot] + hn[s], pg)
                    mm(pp[:, :], lambda kc, mg=mg: wpp[:, kc, mg * 128:(mg + 1) * 128], lambda kc, s=s: pt[s][:, kc, :], 2, [wpp, pt[s]], pp)
                    sg = C.tmp("sg", [128, 512], F32, 3)
                    op("act", lambda e, pg=pg, sg=sg: e.activation(sg[:, :], pg[:, :], AF.Sigmoid), reads=[pg], writes=[sg])
                    t2 = C.tmp("t2", [128, 512], F32, 3)
                    op("dve", lambda e, pp=pp, sg=sg, t2=t2: e.tensor_tensor(t2[:, :], pp[:, :], sg[:, :], op=ALU.mult), reads=[pp, sg], writes=[t2])
                    hc = h[s][mg]
                    op("dve", lambda e, t2=t2, hc=hc: e.tensor_tensor(hc[:, :], t2[:, :], hc[:, :], op=ALU.add), reads=[t2, hc], writes=[hc])

    if mode == "A":
        wconv = P.sb("wconv", [128, 12]); gvb = P.sb("gvb", [128, 512]); bsb = P.sb("bsb", [128, 4, 128])
        wsT = P.sb("wsT", [128, 8, 128], BF16); maskb = P.sb("maskb", [128, 128], BF16)
        g_lat = P.sb("g_lat", [128, 5]); gsm = P.sb("gsm", [128, 8])
        b96 = P.sb("b96", [96, 96], BF16); bd64 = P.sb("bd64", [128, 128], BF16)
        for t, nm in ((wconv, "wconv"), (gvb, "gvb"), (g_lat, "g_lat"), (gsm, "gsm")):
            P.dma("sp", t[:, :], D[nm], writes=[t])
        P.dma("sp", bsb[:, :, :], D["bsb"], writes=[bsb])
        P.dma("pool", wsT[:, :, :], D["wsT"], writes=[wsT])
        op("dve", lambda e: e.tensor_copy(maskb[:, :], cst[:, 0:128]), reads=[cst], writes=[maskb])
        for hh in range(8):
            op("dve", lambda e, hh=hh: e.tensor_tensor(wsT[:, hh, :], wsT[:, hh, :], maskb[:, :], op=ALU.mult), reads=[wsT, maskb], writes=[wsT])
        op("dve", lambda e: e.tensor_copy(b96[:, :], cst[0:96, 128:224]), reads=[cst], writes=[b96])
        op("dve", lambda e: e.tensor_copy(bd64[:, :], cst[:, 224:352]), reads=[cst], writes=[bd64])
        hal = [P.sb("hal%d" % cc, [128, 2]) for cc in range(4)]
        gu = [act[s][8:12] for s in range(2)]
        hh_t = [P.sb("hh%d" % c, [128, 2]) for c in range(8)]
        hhn = [P.sb("hhn%d" % c, [128, 2], BF16) for c in range(8)]

    def even_mixer(sti, tok0):
        for s in range(2):
            norm(s, 0)
        if sti == 0:
            for c in range(8):
                P.dma("sp", hh_t[c][:, :], D["xhT"][c * 128:(c + 1) * 128, :], writes=[hh_t[c]])
            norm(0, 0, n=2, src=hh_t, dst=hhn)
        for cc in range(4):
            slot = loadA(D["ev_w_in"], cc * 384, cc * 384 + 384)
            for s in range(2):
                z = C.tmp("zt", [128, TS + 2], F32, 2)
                if not (s == 0 and sti == 0):
                    op("pool", lambda e, cc=cc, z=z: e.tensor_copy(z[:, 0:2], hal[cc][:, :]), reads=[hal[cc]], writes=[z])
                else:
                    pc = C.psum(); px = C.psum()
                    mm(pc[:, 0:2], lambda kc: slot[:, kc, 128:256], lambda kc: hhn[kc][:, :], 8, [slot] + hhn, pc)
                    mm(px[:, 0:2], lambda kc: slot[:, kc, 256:384], lambda kc: hhn[kc][:, :], 8, [slot] + hhn, px)
                    cs = C.tmp("cs", [128, 512], F32, 2)
                    op("act", lambda e, pc=pc, cs=cs: e.activation(cs[:, 0:2], pc[:, 0:2], AF.Copy), reads=[pc], writes=[cs])
                    op("dve", lambda e, px=px, cs=cs, z=z: e.tensor_tensor(z[:, 0:2], px[:, 0:2], cs[:, 0:2], op=ALU.mult), reads=[px, cs], writes=[z])
                pb = C.psum(); pc = C.psum(); px = C.psum()
                for pp_, c0 in ((pc, 128), (px, 256), (pb, 0)):
                    mm(pp_[:, :], lambda kc, c0=c0: slot[:, kc, c0:c0 + 128], lambda kc, s=s: hn[s][kc][:, :], 8, [slot] + hn[s], pp_)
                cs = C.tmp("cs", [128, 512], F32, 2)
                op("act", lambda e, pc=pc, cs=cs: e.activation(cs[:, :], pc[:, :], AF.Copy), reads=[pc], writes=[cs])
                op("dve", lambda e, px=px, cs=cs, z=z: e.tensor_tensor(z[:, 2:TS + 2], px[:, :], cs[:, :], op=ALU.mult), reads=[px, cs], writes=[z])
                op("pool", lambda e, cc=cc, z=z: e.tensor_copy(hal[cc][:, :], z[:, TS:TS + 2]), reads=[z], writes=[hal[cc]])
                acc = C.tmp("acc", [128, 512], F32, 2)
                op("pool", lambda e, z=z, acc=acc, cc=cc: e.tensor_scalar(acc[:, :], z[:, 0:TS], wconv[:, cc * 3:cc * 3 + 1], None, op0=ALU.mult), reads=[z, wconv], writes=[acc])
                op("dve", lambda e, z=z, acc=acc, cc=cc: e.scalar_tensor_tensor(acc[:, :], z[:, 1:TS + 1], wconv[:, cc * 3 + 1:cc * 3 + 2], acc[:, :], op0=ALU.mult, op1=ALU.add), reads=[z, wconv, acc], writes=[acc])
                op("dve", lambda e, z=z, acc=acc, cc=cc: e.scalar_tensor_tensor(acc[:, :], z[:, 2:TS + 2], wconv[:, cc * 3 + 2:cc * 3 + 3], acc[:, :], op0=ALU.mult, op1=ALU.add), reads=[z, wconv, acc], writes=[acc])
                yt = y[s][cc]
                op("dve", lambda e, pb=pb, acc=acc, yt=yt: e.tensor_tensor(yt[:, :], pb[:, :], acc[:, :], op=ALU.mult), reads=[pb, acc], writes=[yt])
        slot = loadA(D["ev_w_in"], 1536, 2048)
        for s in range(2):
            for uc in range(4):
                pu = C.psum()
                mm(pu[:, :], lambda kc, uc=uc: slot[:, kc, uc * 128:(uc + 1) * 128], lambda kc, s=s: hn[s][kc][:, :], 8, [slot] + hn[s], pu)
                g_ = gu[s][uc]
                op("act", lambda e, pu=pu, g_=g_: e.activation(g_[:, :], pu[:, :], AF.Gelu), reads=[pu], writes=[g_])
        slot = loadA(D["ev_w_in"], 2048, 2560)
        for s in range(2):
            pm = [C.psum(pin=True) for _ in range(4)]
            for tb in range(4):
                pv = C.psum()
                mm(pv[:, :], lambda kc, s=s, tb=tb: hn[s][kc][:, tb * 128:(tb + 1) * 128], lambda kc: slot[:, kc, :], 8, [slot] + hn[s], pv)
                gv = C.tmp("gv", [128, 512], F32, 2)
                op("act", lambda e, pv=pv, gv=gv: e.activation(gv[:, :], pv[:, :], AF.Gelu), reads=[pv], writes=[gv])
                sqv = C.tmp("sqv", [128, 512], F32, 2)
                op("act", lambda e, gv=gv, sqv=sqv: e.activation(sqv[:, :], gv[:, :], AF.Square), reads=[gv], writes=[sqv])
                ss = C.tmp("ss", [128, 8], F32, 2); sd = C.tmp("ssd", [128, 8], F32, 2); rs = C.tmp("srs", [128, 8], F32, 2)
                op("dve", lambda e, sqv=sqv, ss=ss: e.tensor_reduce(ss[:, :], sqv[:, :].rearrange("p (h d) -> p h d", d=64), axis=AX.X, op=ALU.add), reads=[sqv], writes=[ss])
                op("act", lambda e, ss=ss, sd=sd: e.activation(sd[:, :], ss[:, :], AF.Sqrt, bias=EPS, scale=1.0 / 64.0), reads=[ss], writes=[sd])
                op("dve", lambda e, sd=sd, rs=rs: e.reciprocal(rs[:, :], sd[:, :]), reads=[sd], writes=[rs])
                op("dve", lambda e, gv=gv, rs=rs: e.tensor_tensor(gv[:, :].rearrange("p (h d) -> p h d", d=64), gv[:, :].rearrange("p (h d) -> p h d", d=64),
                                                                 rs[:, :].unsqueeze(2).to_broadcast([128, 8, 64]), op=ALU.mult), reads=[gv, rs], writes=[gv])
                vn = C.tmp("vn", [128, 512], BF16, 2)
                op("pool", lambda e, gv=gv, vn=vn: e.tensor_tensor(vn[:, :], gv[:, :], gvb[:, :], op=ALU.mult), reads=[gv, gvb], writes=[vn])
                for hd in range(8):
                    op("pe", lambda e, hd=hd, tb=tb, vn=vn, pm=pm: e.matmul(pm[hd // 2][64 * (hd % 2):64 * (hd % 2) + 64, tb * 128:(tb + 1) * 128], vn[:, hd * 64:(hd + 1) * 64], wsT[:, hd, :], start=True, stop=True),
                       reads=[vn, wsT], writes=[pm[hd // 2]])
            for hc in range(4):
                t1 = C.tmp("t2", [128, 512], F32, 3)
                op("dve", lambda e, hc=hc, t1=t1, pm=pm: e.tensor_tensor(t1[:, :].rearrange("p (b t) -> p b t", t=128), pm[hc][:, :].rearrange("p (b t) -> p b t", t=128),
                                                                bsb[:, hc, :].unsqueeze(1).to_broadcast([128, 4, 128]), op=ALU.add), reads=[pm[hc], bsb], writes=[t1])
                yt = y[s][4 + hc]
                op("pool", lambda e, t1=t1, yt=yt, s=s, hc=hc: e.tensor_tensor(yt[:, :], t1[:, :], gu[s][hc][:, :], op=ALU.mult), reads=[t1, gu[s][hc]], writes=[yt])
            C.unpin()
        resid_proj(D["w_out"], y)

    def O(nm, r0, r1, t0):
        if fz and (nm + "_h0") in D:
            hf = t0 // 1024
            return D[nm + "_h%d" % hf][r0:r1, (t0 % 1024):(t0 % 1024) + TS]
        return D[nm][r0:r1, t0:t0 + TS]

    def odd_front(tok0):
        W = D["od_w_in"]
        for s in range(2):
            norm(s, 24)
        tabs = []
        for s in range(2):
            pi_ = C.tmp("ti", [96, TS], I32, 1)
            P.dma("sp", pi_[:, :], D["posb"][:, tok0 + s * TS:tok0 + (s + 1) * TS], writes=[pi_])
            ang = C.tmp("angp", [96, TS], F32, 1, own=True)
            op("dve", lambda e, pi_=pi_, ang=ang: e.tensor_copy(ang[:, :], pi_[:, :]), reads=[pi_], writes=[ang])
            op("dve", lambda e, ang=ang: e.tensor_scalar(ang[:, :], ang[:, :], cst[0:96, 353:354], None, op0=ALU.mult), reads=[ang, cst], writes=[ang])
            pair = []
            for nm, shift in (("cos", PI / 2.0), ("sin", 0.0)):
                a2 = C.tmp("a2", [96, TS], F32, 1); tf = C.tmp("tf", [96, TS], F32, 1); ti = C.tmp("ti", [96, TS], I32, 1)
                tab = C.tmp("tab" + nm, [96, TS], F32, 2, own=True)
                op("dve", lambda e, a2=a2, ang=ang, shift=shift: e.tensor_scalar(a2[:, :], ang[:, :], shift, None, op0=ALU.add), reads=[ang], writes=[a2])
                op("dve", lambda e, a2=a2, tf=tf: e.tensor_scalar(tf[:, :], a2[:, :], 1.0 / TWO_PI, None, op0=ALU.mult), reads=[a2], writes=[tf])
                op("dve", lambda e, tf=tf, ti=ti: e.tensor_copy(ti[:, :], tf[:, :]), reads=[tf], writes=[ti])
                op("dve", lambda e, tf=tf, ti=ti: e.tensor_copy(tf[:, :], ti[:, :]), reads=[ti], writes=[tf])
                op("dve", lambda e, tf=tf, a2=a2: e.scalar_tensor_tensor(a2[:, :], tf[:, :], -TWO_PI, a2[:, :], op0=ALU.mult, op1=ALU.add), reads=[tf, a2], writes=[a2])
                op("dve", lambda e, a2=a2: e.tensor_scalar(a2[:, :], a2[:, :], -PI, PI, op0=ALU.max, op1=ALU.min), reads=[a2], writes=[a2])
                if nm == "sin":
                    op("act", lambda e, a2=a2, tab=tab: e.activation(tab[:, :], a2[:, :], AF.Sin, scale=cst[0:96, 354:355]), reads=[a2, cst], writes=[tab])
                else:
                    op("act", lambda e, a2=a2, tab=tab: e.activation(tab[:, :], a2[:, :], AF.Sin), reads=[a2], writes=[tab])
                pair.append(tab)
            tabs.append(pair)

        def fm_out(ps, rows, dram_ap, scale=1.0, tag="fo"):
            ob = C.tmp(tag, [128, TS], BF16, 3)
            op("act", lambda e: e.mul(ob[0:rows, :], ps[0:rows, :], float(scale)), reads=[ps], writes=[ob])
            P.dma("sp", dram_ap, ob[0:rows, :], reads=[ob])

        def tok_out(ps, ncols, dram_fn, stage, tb, col0, func=AF.Copy):
            op("act", lambda e: e.activation(stage[:, tb, col0:col0 + ncols], ps[:, 0:ncols], func), reads=[ps], writes=[stage])

        def lat_norm(raws, nch, gc0, D_, outs, tag):
            ps = C.psum()
            for c in range(nch):
                sq = C.tmp("sq", [128, 512], BF16, 3)
                op("act", lambda e, c=c, sq=sq: e.activation(sq[:, :], raws[c][:, :], AF.Square), reads=[raws[c]], writes=[sq])
                op("pe", lambda e, c=c, sq=sq: e.matmul(ps[:, :], ones_bf[:, :], sq[:, :], start=(c == 0), stop=(c == nch - 1)), reads=[sq, ones_bf], writes=[ps])
            rp = rstd_from_ps(ps, 128, TS, 1.0 / D_, tag)
            for c in range(nch):
                op("dve", lambda e, c=c: e.scalar_tensor_tensor(outs[c][:, :], raws[c][:, :], g_lat[:, gc0 + c:gc0 + c + 1], rp[:, :], op0=ALU.mult, op1=ALU.mult),
                   reads=[raws[c], g_lat, rp], writes=[outs[c]])

        qlr = [act[s][12:15] for s in range(2)]
        kvr = [act[s][15:17] for s in range(2)]
        qln = [act[s][17:20] for s in range(2)]
        kvn = [act[s][20:22] for s in range(2)]

        def raw_copy(ps, rows, dst):
            op("act", lambda e: e.activation(dst[0:rows, :], ps[0:rows, :], AF.Copy), reads=[ps], writes=[dst])

        slot = loadA(W, 0, 512)
        for s in range(2):
            for c in range(4):
                ps = C.psum()
                mm(ps[:, :], lambda kc, c=c: slot[:, kc, c * 128:(c + 1) * 128], lambda kc, s=s: hn[s][kc][:, :], 8, [slot] + hn[s], ps)
                raw_copy(ps, 128, qlr[s][c] if c < 3 else kvr[s][0])
            lat_norm(qlr[s], 3, 0, 384.0, qln[s], "q")
        slot = loadA(W, 512, 960)
        for s in range(2):
            t0 = tok0 + s * TS
            cosT, sinT = tabs[s]
            ps = C.psum()
            mm(ps[:, :], lambda kc: slot[:, kc, 0:128], lambda kc, s=s: hn[s][kc][:, :], 8, [slot] + hn[s], ps)
            raw_copy(ps, 128, kvr[s][1])
            lat_norm(kvr[s], 2, 3, 256.0, kvn[s], "k")
            pk = C.psum(); pks = C.psum()
            mm(pk[0:32, :], lambda kc: slot[:, kc, 128:160], lambda kc, s=s: hn[s][kc][:, :], 8, [slot] + hn[s], pk)
            mm(pks[0:32, :], lambda kc: slot[:, kc, 160:192], lambda kc, s=s: hn[s][kc][:, :], 8, [slot] + hn[s], pks)
            kr = C.tmp("kr", [32, 512], F32)
            raw_copy(pk, 32, kr)
            sq = C.tmp("sq", [128, 512], BF16, 3)
            op("act", lambda e, sq=sq, kr=kr: e.activation(sq[0:32, :], kr[:, :], AF.Square), reads=[kr], writes=[sq])
            pn = C.psum()
            op("pe", lambda e, sq=sq, pn=pn: e.matmul(pn[0:32, :], ones_bf[0:32, 0:32], sq[0:32, :], start=True, stop=True), reads=[sq, ones_bf], writes=[pn])
            rp = rstd_from_ps(pn, 32, TS, 1.0 / 32.0, "kp")
            a = C.tmp("kpa", [32, 512], F32); b_ = C.tmp("kpb", [32, 512], F32)
            op("dve", lambda e, a=a, rp=rp, kr=kr: e.scalar_tensor_tensor(a[:, :], kr[:, :], gsm[0:32, 4:5], rp[0:32, :], op0=ALU.mult, op1=ALU.mult), reads=[kr, gsm, rp], writes=[a])
            op("dve", lambda e, b_=b_, rp=rp, pks=pks: e.scalar_tensor_tensor(b_[:, :], pks[0:32, :], gsm[0:32, 5:6], rp[0:32, :], op0=ALU.mult, op1=ALU.mult), reads=[pks, gsm, rp], writes=[b_])
            op("pool", lambda e, a=a, cosT=cosT: e.tensor_tensor(a[:, :], a[:, :], cosT[0:32, :], op=ALU.mult), reads=[a, cosT], writes=[a])
            op("pool", lambda e, b_=b_, sinT=sinT: e.tensor_tensor(b_[:, :], b_[:, :], sinT[0:32, :], op=ALU.mult), reads=[b_, sinT], writes=[b_])
            ob = C.tmp("fo", [128, TS], BF16, 3)
            op("pool", lambda e, a=a, b_=b_, ob=ob: e.tensor_tensor(ob[0:32, :], a[:, :], b_[:, :], op=ALU.add), reads=[a, b_], writes=[ob])
            P.dma("sp", O("kpeT", 0, 32, t0), ob[0:32, :], reads=[ob])
            for c in range(2):
                ps = C.psum()
                mm(ps[:, :], lambda kc, c=c: slot[:, kc, 192 + c * 128:192 + (c + 1) * 128], lambda kc, s=s: hn[s][kc][:, :], 8, [slot] + hn[s], ps)
                fm_out(ps, 128, O("mqT", c * 128, (c + 1) * 128, t0), scale=0.125)
        slot = loadA(W, 960, 1224)
        for s in range(2):
            t0 = tok0 + s * TS
            for c in range(2):
                ps = C.psum()
                mm(ps[:, :], lambda kc, c=c: slot[:, kc, c * 128:(c + 1) * 128], lambda kc, s=s: hn[s][kc][:, :], 8, [slot] + hn[s], ps)
                fm_out(ps, 128, O("mkT", c * 128, (c + 1) * 128, t0))
            ps = C.psum()
            mm(ps[0:8, :], lambda kc: slot[:, kc, 256:264], lambda kc, s=s: hn[s][kc][:, :], 8, [slot] + hn[s], ps)
            mo_ = C.tmp("mif", [8, 512], F32)
            raw_copy(ps, 8, mo_)
            P.dma("sp", D["mifT"][:, t0:t0 + TS], mo_[:, :], reads=[mo_])
        for gi_, (c0, nm) in enumerate(((1224, "mvT"), (1736, "soT"))):
            slot = loadA(W, c0, c0 + 512)
            for s in range(2):
                t0 = tok0 + s * TS
                for c in range(4):
                    ps = C.psum()
                    mm(ps[:, :], lambda kc, c=c: slot[:, kc, c * 128:(c + 1) * 128], lambda kc, s=s: hn[s][kc][:, :], 8, [slot] + hn[s], ps)
                    ob = C.tmp("fo", [128, TS], BF16, 3)
                    if gi_ == 0 and c % 2 == 0:
                        op("dve", lambda e, ps=ps, ob=ob: e.tensor_copy(ob[:, :], ps[:, :]), reads=[ps], writes=[ob])
                    else:
                        fn_ = AF.Copy if gi_ == 0 else AF.Sigmoid
                        op("act", lambda e, ps=ps, ob=ob, fn_=fn_: e.activation(ob[:, :], ps[:, :], fn_), reads=[ps], writes=[ob])
                    P.dma("sp", O(nm, c * 128, (c + 1) * 128, t0), ob[:, :], reads=[ob])
        def sview(slot, nk, n):
            return slot.h[:, :, :].rearrange("p k n -> p (k n)")[:, 0:nk * n].rearrange("p (k n) -> p k n", n=n)
        wq_t = slotA(); wqs_t = slotA(); wkv_t = slotA()
        wq = TV(wq_t, sview(wq_t, 3, 768)); wqs = TV(wqs_t, sview(wqs_t, 3, 768)); wkv = TV(wkv_t, sview(wkv_t, 2, 1024))
        P.dma("pool", wq[:, :, :], D["w_q_up"].rearrange("(kc p) n -> p kc n", p=128), writes=[wq])
        P.dma("pool", wqs[:, :, :], D["w_q_up_sw"].rearrange("(kc p) n -> p kc n", p=128), writes=[wqs])
        P.dma("pool", wkv[:, :, :], D["w_kv_up"].rearrange("(kc p) n -> p kc n", p=128), writes=[wkv])
        for hd in range(8):
            for s in range(2):
                t0 = tok0 + s * TS
                cosT, sinT = tabs[s]
                pq = C.psum(); pqs = C.psum()
                mm(pq[0:96, :], lambda kc, hd=hd: wq[:, kc, hd * 96:(hd + 1) * 96], lambda kc, s=s: qln[s][kc][:, :], 3, [wq] + qln[s], pq)
                mm(pqs[0:96, :], lambda kc, hd=hd: wqs[:, kc, hd * 96:(hd + 1) * 96], lambda kc, s=s: qln[s][kc][:, :], 3, [wqs] + qln[s], pqs)
                qr = C.tmp("qr", [96, 512], F32)
                raw_copy(pq, 96, qr)
                sq = C.tmp("sq", [128, 512], BF16, 3)
                op("act", lambda e, sq=sq, qr=qr: e.activation(sq[0:96, :], qr[:, :], AF.Square), reads=[qr], writes=[sq])
                pn = C.psum()
                op("pe", lambda e, sq=sq, pn=pn: e.matmul(pn[0:96, :], b96[:, :], sq[0:96, :], start=True, stop=True), reads=[sq, b96], writes=[pn])
                rp = rstd_from_ps(pn, 96, TS, cst[0:96, 352:353], "qh")
                qn = C.tmp("qn", [96, 512], F32); sw = C.tmp("qsw", [96, 512], F32)
                op("dve", lambda e, qn=qn, qr=qr, rp=rp: e.scalar_tensor_tensor(qn[:, :], qr[:, :], gsm[0:96, 0:1], rp[0:96, :], op0=ALU.mult, op1=ALU.mult), reads=[qr, gsm, rp], writes=[qn])
                op("dve", lambda e, sw=sw, pqs=pqs, rp=rp: e.scalar_tensor_tensor(sw[64:96, :], pqs[64:96, :], gsm[64:96, 1:2], rp[64:96, :], op0=ALU.mult, op1=ALU.mult), reads=[pqs, gsm, rp], writes=[sw])
                op("pool", lambda e, qn=qn, cosT=cosT: e.tensor_tensor(qn[64:96, :], qn[64:96, :], cosT[64:96, :], op=ALU.mult), reads=[qn, cosT], writes=[qn])
                op("pool", lambda e, sw=sw, sinT=sinT: e.tensor_tensor(sw[64:96, :], sw[64:96, :], sinT[64:96, :], op=ALU.mult), reads=[sw, sinT], writes=[sw])
                op("pool", lambda e, qn=qn, sw=sw: e.tensor_tensor(qn[64:96, :], qn[64:96, :], sw[64:96, :], op=ALU.add), reads=[qn, sw], writes=[qn])
                ob = C.tmp("fo", [128, TS], BF16, 3)
                op("act", lambda e, ob=ob, qn=qn: e.mul(ob[0:96, :], qn[:, :], 96.0 ** -0.5), reads=[qn], writes=[ob])
                P.dma("sp", (O("qTf", hd * 96, (hd + 1) * 96, t0) if fz else D["qT"][hd, :, t0:t0 + TS]), ob[0:96, :], reads=[ob])
        for s in range(2):
            t0 = tok0 + s * TS
            for hp in range(4):
                pk = C.psum()
                mm(pk[:, :], lambda kc, hp=hp: wkv[:, kc, hp * 128:(hp + 1) * 128], lambda kc, s=s: kvn[s][kc][:, :], 2, [wkv] + kvn[s], pk)
                kr = C.tmp("kr", [128, 512], F32)
                raw_copy(pk, 128, kr)
                sq = C.tmp("sq", [128, 512], BF16, 3)
                op("act", lambda e, sq=sq, kr=kr: e.activation(sq[:, :], kr[:, :], AF.Square), reads=[kr], writes=[sq])
                pn = C.psum()
                op("pe", lambda e, sq=sq, pn=pn: e.matmul(pn[:, :], bd64[:, :], sq[:, :], start=True, stop=True), reads=[sq, bd64], writes=[pn])
                rp = rstd_from_ps(pn, 128, TS, 1.0 / 64.0, "kh")
                ob = C.tmp("fo", [128, TS], BF16, 3)
                op("dve", lambda e, ob=ob, kr=kr, rp=rp: e.scalar_tensor_tensor(ob[:, :], kr[:, :], gsm[:, 2:3], rp[:, :], op0=ALU.mult, op1=ALU.mult), reads=[kr, gsm, rp], writes=[ob])
                P.dma("sp", O("knT", hp * 128, (hp + 1) * 128, t0), ob[:, :], reads=[ob])
            for hp in range(4):
                pv = C.psum()
                mm(pv[:, :], lambda kc, hp=hp: wkv[:, kc, 512 + hp * 128:512 + (hp + 1) * 128], lambda kc, s=s: kvn[s][kc][:, :], 2, [wkv] + kvn[s], pv)
                ob = C.tmp("fo", [128, TS], BF16, 3)
                op("act", lambda e, pv=pv, ob=ob: e.activation(ob[:, :], pv[:, :], AF.Copy), reads=[pv], writes=[ob])
                P.dma("sp", O("vT", hp * 128, (hp + 1) * 128, t0), ob[:, :], reads=[ob])

    for sti in range(NTOK // (2 * TS)):
        tok0 = sti * 2 * TS
        for s in range(2):
            for c in range(8):
                P.dma("sp", h[s][c][:, :], D["hT"][c * 128:(c + 1) * 128, tok0 + s * TS:tok0 + (s + 1) * TS], writes=[h[s][c]])
        if mode == "A":
            even_mixer(sti, tok0)
            ffn(8)
            ple(16, tok0)
            for s in range(2):
                for c in range(8):
                    P.dma("sp", D["h1T"][c * 128:(c + 1) * 128, tok0 + s * TS:tok0 + (s + 1) * TS], h[s][c][:, :], reads=[h[s][c]])
            odd_front(tok0)
            if fz and sti == 0 and "mid_hook" in fz:
                fz["mid_hook"](P)
        else:
            if fz:
                ysrc = [[TV(ystg[c], ystg[c].h[:, tok0 + s * TS:tok0 + (s + 1) * TS]) for c in range(8)] for s in range(2)]
            else:
                ysrc = y
                for s in range(2):
                    for c in range(8):
                        P.dma("pool", y[s][c][:, :], D["ymT"][c * 128:(c + 1) * 128, tok0 + s * TS:tok0 + (s + 1) * TS], writes=[y[s][c]])
            resid_proj(D["w_out"], ysrc)
            ffn(8)
            ple(16, tok0)
            for s in range(2):
                for c in range(8):
                    P.dma("sp", D["outT"][c * 128:(c + 1) * 128, tok0 + s * TS:tok0 + (s + 1) * TS], h[s][c][:, :], reads=[h[s][c]])
    P.wait_all("sp", P.tiles)
    if fz:
        return P
    P.emit()
    return nc


def _c(a):
    return np.ascontiguousarray(a)


def _chunkT(v):
    return _c(v.reshape(-1, 128).T)


def _consts():
    c = np.zeros((128, 512), np.float32)
    s = np.arange(128)
    c[:, 0:128] = (s[:, None] <= s[None, :]).astype(np.float32)
    k = np.arange(96)
    c[0:96, 128:224] = ((k[:, None] < 64) == (k[None, :] < 64)).astype(np.float32)
    c[:, 224:352] = ((s[:, None] // 64) == (s[None, :] // 64)).astype(np.float32)
    c[0:64, 352] = 1.0 / 64.0
    c[64:96, 352] = 1.0 / 32.0
    inv_freq = (10000.0 ** (-np.arange(0, 32, 2, dtype=np.float32) / np.float32(32))).astype(np.float32)
    c[0:96, 353] = inv_freq[np.arange(96) % 16]
    c[0:96, 354] = np.where((np.arange(96) % 32) < 16, -1.0, 1.0)
    return c


_SW = (np.arange(32) + 16) % 32


def prep_tok_common(inp, L, core):
    b, q = core // 4, core % 4
    s0 = q * NTOK
    m = {
        "pT": _c(inp["p"][L, b, s0:s0 + NTOK, :].T),
        "consts": _consts(),
        "w_gate": _c(inp["w_gate"][L]), "w_up": _c(inp["w_up"][L]), "w_down": _c(inp["w_down"][L]),
        "w_ple_gate": _c(inp["w_ple_gate"][L]), "w_ple_proj": _c(inp["w_ple_proj"][L]),
    }
    return m


def prep_A(inp):
    maps = []
    x = inp["x"]
    ev_w_in = inp["ev_w_in"][0]
    cols = []
    for cc in range(4):
        for base in (0, 512, 1024):
            cols.append(np.arange(base + cc * 128, base + (cc + 1) * 128))
    cols.append(np.arange(1536, 2560))
    ev_w_in_r = _c(ev_w_in[:, np.concatenate(cols)])
    od_w_in = inp["od_w_in"][0]
    ocols = np.concatenate([np.arange(0, 672), 640 + _SW, np.arange(672, 928), np.arange(928, 1184), np.arange(2208, 2216),
                            np.arange(1184, 1696), np.arange(1696, 2208)])
    od_ext = _c(od_w_in[:, ocols])
    wq = inp["od_w_q_up"][0]
    qcols = np.arange(768).reshape(8, 96).copy()
    qcols[:, 64:96] = qcols[:, 64:96][:, _SW]
    wq_sw = _c(wq[:, qcols.reshape(-1)])
    wkv = inp["od_w_kv_up"][0].reshape(256, 8, 128)
    wkv_r = _c(np.concatenate([wkv[:, :, :64].reshape(256, 512), wkv[:, :, 64:].reshape(256, 512)], axis=1))
    gq, gk = inp["od_g_q"][0], inp["od_g_k"][0]
    gsm = np.zeros((128, 8), np.float32)
    gsm[0:96, 0] = gq
    gsm[64:96, 1] = gq[64:96][_SW]
    gsm[0:64, 2] = gk[:64]; gsm[64:128, 2] = gk[:64]
    gsm[0:32, 4] = gk[64:96]; gsm[0:32, 5] = gk[64:96][_SW]
    g_lat = np.concatenate([_chunkT(inp["od_g_qa"][0]), _chunkT(inp["od_g_kva"][0])], axis=1)
    gains = np.zeros((128, 32), np.float32)
    gains[:, 0:8] = _chunkT(inp["g_mix"][0]); gains[:, 8:16] = _chunkT(inp["g_ffn"][0])
    gains[:, 16:24] = _chunkT(inp["g_ple"][0]); gains[:, 24:32] = _chunkT(inp["g_mix"][1])
    wconv = _c(inp["ev_w_conv"][0].T.reshape(4, 128, 3).transpose(1, 0, 2).reshape(128, 12))
    gvb = _c(np.tile(inp["ev_g_v"][0].reshape(1, 512), (128, 1)))
    wsT = _c(inp["ev_w_s"][0].transpose(2, 0, 1))
    bsb = _c(inp["ev_b_s"][0].reshape(4, 2, 1, 128).repeat(64, axis=2).reshape(4, 128, 128).transpose(1, 0, 2))
    for core in range(8):
        b, q = core // 4, core % 4
        s0 = q * NTOK
        m = prep_tok_common(inp, 0, core)
        m["hT"] = _c(x[b, s0:s0 + NTOK, :].T)
        m["xhT"] = _c(x[b, s0 - 2:s0, :].T) if q > 0 else np.zeros((1024, 2), np.float32)
        m["posb"] = _c(np.tile(inp["positions"][b, s0:s0 + NTOK].reshape(1, NTOK), (96, 1)).astype(np.int32))
        m.update({"gains": gains, "w_out": _c(inp["ev_w_out"][0]), "ev_w_in": ev_w_in_r, "wconv": wconv, "gvb": gvb, "wsT": wsT,
                  "bsb": bsb, "od_w_in": od_ext, "g_lat": _c(g_lat), "w_q_up": _c(wq), "w_q_up_sw": wq_sw, "w_kv_up": wkv_r, "gsm": gsm})
        maps.append(m)
    return maps


def prep_C(inp, h1T, ymT):
    maps = []
    gains = np.zeros((128, 32), np.float32)
    gains[:, 8:16] = _chunkT(inp["g_ffn"][1]); gains[:, 16:24] = _chunkT(inp["g_ple"][1])
    for core in range(8):
        m = prep_tok_common(inp, 1, core)
        m["hT"] = h1T[core]
        m["ymT"] = ymT[core]
        m["gains"] = gains
        m["w_out"] = _c(inp["od_w_out"][0])
        maps.append(m)
    return maps


SEQ = 8192


def build_mix(parts, fz):
    nc = fz["nc"]
    P = Prog(nc, ("M_" if "m" in parts else "T_"))
    C = Ctx(P, nbanks=6)
    op = P.op
    D = fz["D"]
    ptb = [P.ps("ptb%d" % i, [128, 1024], BF16) for i in range(2)]
    idx = P.sb("idx", [128, 64], I32)
    P.dma("sp", idx[:, :], D["idx"], writes=[idx])
    R1 = D["recv1"]

    def gat(t, rows, q, col):
        for hf in range(2):
            c0_ = q * NTOK + hf * 1024
            P.gather(t, t[0:rows, c0_:c0_ + 1024], R1[hf][:, :], idx, idx[0:rows, col:col + 1])
    send2 = D["send2"]

    cB = P.sb("cB", [128, 512])
    bcol = P.sb("bcol", [128, 2]); gmh = P.sb("gmh", [128, 128])
    P.dma("sp", cB[:, :], D["cB"], writes=[cB]); P.dma("sp", bcol[:, :], D["bcol"], writes=[bcol]); P.dma("sp", gmh[:, :], D["gmh"], writes=[gmh])
    maskb = P.sb("maskb", [128, 128], BF16)
    op("dve", lambda e: e.tensor_copy(maskb[:, :], cB[:, 0:128]), reads=[cB], writes=[maskb])
    idb = P.sb("idb", [128, 128], BF16)
    op("dve", lambda e: e.tensor_copy(idb[:, :], cB[:, 256:384]), reads=[cB], writes=[idb])
    ones_f = P.sb("ones_f", [128, 128])
    op("pool", lambda e: e.memset(ones_f[:, :], 1.0), writes=[ones_f])

    if True:
        pass
    if "s" in parts:
        gi = P.sb("gi", [128, 64]); gf = P.sb("gf", [128, 64])
        rmv = D["recvm"].rearrange("r (c t) -> (r c) t", t=64)
        P.gather(gi, gi[:, :], rmv, idx, idx[:, 40:41])
        P.gather(gf, gf[:, :], rmv, idx, idx[:, 41:42])
        zer = P.sb("zer", [128, 128]); op("pool", lambda e: e.memset(zer[:, :], 0.0), writes=[zer])
        nbf = P.sb("nbf", [128, 1])
        op("dve", lambda e: e.tensor_scalar(nbf[:, :], bcol[:, 1:2], -1.0, None, op0=ALU.mult), reads=[bcol], writes=[nbf])
        e1 = P.sb("e1", [128, 64]); lf = P.sb("lf", [128, 64]); Al = P.sb("Al", [128, 64]); vv = P.sb("vv", [128, 64])
        Ml = P.sb("Ml", [128, 64]); Mp = P.sb("Mp", [128, 64])
        op("act", lambda e: e.activation(e1[:, :], gf[:, :], AF.Exp, bias=nbf[:, 0:1], scale=-1.0), reads=[gf, nbf], writes=[e1])
        op("act", lambda e: e.activation(lf[:, :], e1[:, :], AF.Ln, bias=1.0), reads=[e1], writes=[lf])
        op("dve", lambda e: e.tensor_scalar(lf[:, :], lf[:, :], -1.0, None, op0=ALU.mult), reads=[lf], writes=[lf])
        op("dve", lambda e: e.tensor_tensor_scan(Al[:, :], lf[:, :], zer[:, 0:64], 0.0, op0=ALU.add, op1=ALU.add), reads=[lf, zer], writes=[Al])

        def col2row(col_ap, reads):
            ps = C.psum()
            op("pe", lambda e: e.matmul(ps[0:1, 0:128], col_ap, cB[:, 256:384], start=True, stop=True), reads=reads + [cB], writes=[ps])
            return ps

        rows = P.sb("rows", [1, 8, 128])
        ps = col2row(Al[:, 63:64], [Al])
        op("dve", lambda e, ps=ps: e.tensor_copy(rows[:, 0, :], ps[0:1, 0:128]), reads=[ps], writes=[rows])
        op("dve", lambda e: e.tensor_tensor_scan(rows[:, 1, :], rows[:, 0, :], zer[0:1, :], 0.0, op0=ALU.add, op1=ALU.add), reads=[rows, zer], writes=[rows])
        op("dve", lambda e: e.tensor_tensor(rows[:, 2, :], rows[:, 1, :], rows[:, 0, :], op=ALU.subtract), reads=[rows], writes=[rows])
        cols_ps = C.psum(pin=True)

        def row2col(k, j):
            op("pe", lambda e: e.matmul(cols_ps[:, j:j + 1], rows[0:1, k, :], ones_f[0:1, 0:1], start=True, stop=True), reads=[rows, ones_f], writes=[cols_ps])

        row2col(2, 0)
        cols = P.sb("cols", [128, 8])
        op("dve", lambda e: e.tensor_copy(cols[:, 0:1], cols_ps[:, 0:1]), reads=[cols_ps], writes=[cols])
        op("dve", lambda e: e.tensor_scalar(Al[:, :], Al[:, :], cols[:, 0:1], None, op0=ALU.add), reads=[Al, cols], writes=[Al])
        op("dve", lambda e: e.scalar_tensor_tensor(vv[:, :], gi[:, :], bcol[:, 0:1], Al[:, :], op0=ALU.add, op1=ALU.subtract), reads=[gi, bcol, Al], writes=[vv])
        op("dve", lambda e: e.tensor_tensor_scan(Ml[:, :], vv[:, :], vv[:, :], -1e30, op0=ALU.max, op1=ALU.max), reads=[vv], writes=[Ml])
        ps = col2row(Ml[:, 63:64], [Ml])
        op("dve", lambda e, ps=ps: e.tensor_copy(rows[:, 3, :], ps[0:1, 0:128]), reads=[ps], writes=[rows])
        op("dve", lambda e: e.tensor_tensor_scan(rows[:, 4, :], rows[:, 3, :], rows[:, 3, :], 0.0, op0=ALU.max, op1=ALU.max), reads=[rows], writes=[rows])
        op("dve", lambda e: e.memset(rows[:, 5, 0:1], 0.0), reads=[], writes=[rows])
        op("dve", lambda e: e.tensor_copy(rows[:, 5, 1:128], rows[:, 4, 0:127]), reads=[rows], writes=[rows])
        r4 = rows[:, 4, :].rearrange("p (c two) -> p c two", two=2)
        r6 = rows[:, 6, :].rearrange("p (c two) -> p c two", two=2)
        op("dve", lambda e: e.tensor_copy(r6[:, :, 0:1], r4[:, :, 1:2]), reads=[rows], writes=[rows])
        op("dve", lambda e: e.tensor_copy(r6[:, :, 1:2], r4[:, :, 1:2]), reads=[rows], writes=[rows])
        op("dve", lambda e: e.tensor_tensor(rows[:, 7, :], rows[:, 5, :], rows[:, 4, :], op=ALU.subtract), reads=[rows], writes=[rows])
        op("act", lambda e: e.activation(rows[:, 7, :], rows[:, 7, :], AF.Exp), reads=[rows], writes=[rows])
        row2col(5, 1); row2col(4, 2); row2col(6, 3)
        op("dve", lambda e: e.tensor_copy(cols[:, 1:4], cols_ps[:, 1:4]), reads=[cols_ps], writes=[cols])
        C.unpin()
        op("dve", lambda e: e.tensor_scalar(Mp[:, :], Ml[:, :], cols[:, 1:2], 0.0, op0=ALU.max, op1=ALU.max), reads=[Ml, cols], writes=[Mp])
        dec_b = P.sb("dec_b", [64, 128])
        ps = C.psum()
        op("pe", lambda e, ps=ps: e.matmul(ps[0:64, 0:128], ones_f[0:1, 0:64], rows[0:1, 7, :], start=True, stop=True), reads=[ones_f, rows], writes=[ps])
        op("dve", lambda e, ps=ps: e.tensor_copy(dec_b[:, :], ps[0:64, 0:128]), reads=[ps], writes=[dec_b])
        fac = P.sb("fac", [128, 5, 128])
        tmpc = P.sb("tmpc", [128, 64])
        op("dve", lambda e: e.tensor_scalar(tmpc[:, :], vv[:, :], cols[:, 3:4], None, op0=ALU.subtract), reads=[vv, cols], writes=[tmpc])
        op("act", lambda e: e.activation(fac[:, 0, 0:64], tmpc[:, :], AF.Exp), reads=[tmpc], writes=[fac])
        op("dve", lambda e: e.tensor_scalar(tmpc[:, :], Mp[:, :], cols[:, 3:4], None, op0=ALU.subtract), reads=[Mp, cols], writes=[tmpc])
        op("act", lambda e: e.activation(fac[:, 1, 0:64], tmpc[:, :], AF.Exp, scale=-1.0), reads=[tmpc], writes=[fac])
        op("dve", lambda e: e.tensor_scalar(tmpc[:, :], Mp[:, :], cols[:, 1:2], None, op0=ALU.subtract), reads=[Mp, cols], writes=[tmpc])
        op("act", lambda e: e.activation(fac[:, 2, 0:64], tmpc[:, :], AF.Exp, scale=-1.0), reads=[tmpc], writes=[fac])
        op("dve", lambda e: e.tensor_scalar(tmpc[:, :], vv[:, :], cols[:, 2:3], None, op0=ALU.subtract), reads=[vv, cols], writes=[tmpc])
        op("act", lambda e: e.activation(fac[:, 3, 0:64], tmpc[:, :], AF.Exp), reads=[tmpc], writes=[fac])
        op("dve", lambda e: e.tensor_tensor(tmpc[:, :], Al[:, :], Mp[:, :], op=ALU.add), reads=[Al, Mp], writes=[tmpc])
        op("act", lambda e: e.activation(fac[:, 4, 0:64], tmpc[:, :], AF.Exp, scale=-1.0), reads=[tmpc], writes=[fac])
        op("dve", lambda e: e.tensor_copy(fac[:, :, 64:128], fac[:, :, 0:64]), reads=[fac], writes=[fac])
        fcol = P.sb("fcol", [128, 5, 64])
        for k in range(5):
            ps = C.psum()
            op("pe", lambda e, k=k, ps=ps: e.transpose(ps[:, 0:128], fac[:, k, :], cB[:, 256:384]), reads=[fac, cB], writes=[ps])
            pv = ps[:, 0:128].rearrange("p (c two) -> p c two", two=2)
            op("dve", lambda e, k=k, ps=ps: e.tensor_copy(fcol[0:64, k, :], ps[0:64, 0:128].rearrange("p (c two) -> p c two", two=2)[:, :, 0]), reads=[ps], writes=[fcol])
            op("dve", lambda e, k=k, ps=ps: e.tensor_copy(fcol[64:128, k, :], ps[64:128, 0:128].rearrange("p (c two) -> p c two", two=2)[:, :, 1]), reads=[ps], writes=[fcol])

    if "m" in parts:
        mq = P.sb("mq", [64, SEQ], BF16); mk = P.sb("mk", [64, SEQ], BF16)
        ktok = P.sb("ktok", [128, 64, 64], BF16); vext = P.sb("vext", [128, 64, 129], BF16); so = P.sb("so", [128, 64, 128], BF16)
        Call = P.sb("Call", [64, 128, 129], BF16); Sxs = [P.sb("Sx%d" % i, [64, 129]) for i in range(2)]
        mvs = P.sb("mvs", [128, SEQ], BF16); sos = P.sb("sos", [128, SEQ], BF16)
        for q in range(4):
            cs_ = slice(q * NTOK, (q + 1) * NTOK)
            gat(mq, 64, q, 24 + q); gat(mk, 64, q, 28 + q); gat(mvs, 128, q, 32 + q); gat(sos, 128, q, 36 + q)
        op("pool", lambda e: e.memset(vext[:, :, :], 1.0), writes=[vext])
        tcount = [0]
        def tr_group(src, rows, nblk_per, width, dst_fn, g):
            pt_ = ptb[tcount[0] % 2]; tcount[0] += 1
            for b_ in range(nblk_per):
                blk = g * nblk_per + b_
                op("pe", lambda e, pt_=pt_, b_=b_, blk=blk: e.transpose(pt_[:, b_ * width:(b_ + 1) * width], src[0:rows, blk * 128:(blk + 1) * 128], idb[0:rows, 0:rows]),
                   reads=[src, idb], writes=[pt_])
            dst_t, dst_ap = dst_fn(g)
            eng = "act" if tcount[0] % 2 == 0 else "dve"
            if eng == "act":
                op("act", lambda e, pt_=pt_, dst_ap=dst_ap: e.activation(dst_ap, pt_[:, 0:nblk_per * width].rearrange("p (b w) -> p b w", w=width), AF.Copy), reads=[pt_], writes=[dst_t])
            else:
                op("dve", lambda e, pt_=pt_, dst_ap=dst_ap: e.tensor_copy(dst_ap, pt_[:, 0:nblk_per * width].rearrange("p (b w) -> p b w", w=width)), reads=[pt_], writes=[dst_t])
        for g in range(4):
            tr_group(mk, 64, 16, 64, lambda g: (ktok, ktok[:, g * 16:(g + 1) * 16, :]), g)
        for g in range(8):
            tr_group(mvs, 128, 8, 128, lambda g: (vext, vext[:, g * 8:(g + 1) * 8, 0:128]), g)
            tr_group(sos, 128, 8, 128, lambda g: (so, so[:, g * 8:(g + 1) * 8, :]), g)
        op("pool", lambda e: e.memset(Sxs[0][:, :], 0.0), writes=[Sxs[0]])
        for blk in range(64):
            kw = C.tmp("kw", [128, 64], BF16, 3)
            op("dve", lambda e, blk=blk, kw=kw: e.tensor_scalar(kw[:, :], ktok[:, blk, :], fcol[:, 3, blk:blk + 1], None, op0=ALU.mult), reads=[ktok, fcol], writes=[kw])
            for half in range(2):
                c = 2 * blk + half
                Sa = Sxs[c % 2]; Sb = Sxs[(c + 1) % 2]
                op("act", lambda e, c=c, Sa=Sa: e.activation(Call[:, c, :], Sa[:, :], AF.Copy), reads=[Sa], writes=[Call])
                pu = C.psum()
                op("pe", lambda e, pu=pu, kw=kw, half=half, blk=blk: e.matmul(pu[0:64, 0:129], kw[64 * half:64 * half + 64, :], vext[64 * half:64 * half + 64, blk, :], start=True, stop=True),
                   reads=[kw, vext], writes=[pu])
                op("dve", lambda e, pu=pu, c=c, Sa=Sa, Sb=Sb: e.scalar_tensor_tensor(Sb[:, :], Sa[:, :], dec_b[:, c:c + 1], pu[0:64, 0:129], op0=ALU.mult, op1=ALU.add), reads=[Sa, dec_b, pu], writes=[Sb])
        if "2" in parts:
            def stX(blk):
                t0 = blk * 128
                pS = C.psum()
                op("pe", lambda e, pS=pS, t0=t0: e.matmul(pS[:, 0:128], mk[:, t0:t0 + 128], mq[:, t0:t0 + 128], start=True, stop=True), reads=[mk, mq], writes=[pS])
                pt = C.tmp("mpt", [128, 128], BF16, 3)
                op("dve", lambda e, pS=pS, pt=pt, blk=blk: e.scalar_tensor_tensor(pt[:, :], pS[:, 0:128], fcol[:, 0, blk:blk + 1], cB[:, 128:256], op0=ALU.mult, op1=ALU.mult), reads=[pS, fcol, cB], writes=[pt])
                return pt

            def stY(blk, pt):
                t0 = blk * 128
                po1 = C.psum(); po2 = C.psum()
                op("pe", lambda e, po1=po1, pt=pt, blk=blk: e.matmul(po1[:, 0:129], pt[:, :], vext[:, blk, :], start=True, stop=True), reads=[pt, vext], writes=[po1])
                for half in range(2):
                    op("pe", lambda e, po2=po2, half=half, blk=blk, t0=t0: e.matmul(po2[64 * half:64 * half + 64, 0:129], mq[:, t0 + 64 * half:t0 + 64 * half + 64], Call[:, 2 * blk + half, :], start=True, stop=True),
                       reads=[mq, Call], writes=[po2])
                o1 = C.tmp("o1", [128, 129], F32, 2); hnum = C.tmp("hnum", [128, 129], F32, 2)
                op("dve", lambda e, o1=o1, po1=po1, blk=blk: e.tensor_scalar(o1[:, :], po1[:, 0:129], fcol[:, 1, blk:blk + 1], None, op0=ALU.mult), reads=[po1, fcol], writes=[o1])
                op("dve", lambda e, o1=o1, po2=po2, hnum=hnum, blk=blk: e.scalar_tensor_tensor(hnum[:, :], po2[:, 0:129], fcol[:, 2, blk:blk + 1], o1[:, :], op0=ALU.mult, op1=ALU.add), reads=[po2, fcol, o1], writes=[hnum])
                sm = C.tmp("sm", [128, 8], F32, 2)
                op("dve", lambda e, sm=sm, hnum=hnum: e.scalar_tensor_tensor(sm[:, 6:7], hnum[:, 128:129], -1.0, hnum[:, 128:129], op0=ALU.mult, op1=ALU.max), reads=[hnum], writes=[sm])
                op("dve", lambda e, sm=sm, blk=blk: e.tensor_tensor(sm[:, 0:1], sm[:, 6:7], fcol[:, 4, blk:blk + 1], op=ALU.max), reads=[sm, fcol], writes=[sm])
                op("dve", lambda e, sm=sm: e.reciprocal(sm[:, 1:2], sm[:, 0:1]), reads=[sm], writes=[sm])
                junk = C.tmp("junk", [128, 128], F32, 2)
                op("dve", lambda e, hnum=hnum, sm=sm: e.tensor_scalar(hnum[:, 0:128], hnum[:, 0:128], sm[:, 1:2], None, op0=ALU.mult), reads=[hnum, sm], writes=[hnum])
                op("act", lambda e, junk=junk, hnum=hnum, sm=sm: e.activation(junk[:, :], hnum[:, 0:128], AF.Square, accum_out=sm[:, 2:3]), reads=[hnum], writes=[junk, sm])
                op("act", lambda e, sm=sm: e.activation(sm[:, 3:4], sm[:, 2:3], AF.Sqrt, bias=EPS, scale=1.0 / 128.0), reads=[sm], writes=[sm])
                op("dve", lambda e, sm=sm: e.reciprocal(sm[:, 4:5], sm[:, 3:4]), reads=[sm], writes=[sm])
                yt = C.tmp("yt", [128, 128], F32, 2); yd = C.tmp("yd", [128, 128], F32, 3)
                op("dve", lambda e, yt=yt, hnum=hnum, sm=sm: e.scalar_tensor_tensor(yt[:, :], hnum[:, 0:128], sm[:, 4:5], gmh[:, :], op0=ALU.mult, op1=ALU.mult), reads=[hnum, sm, gmh], writes=[yt])
                op("pool", lambda e, yt=yt, yd=yd, blk=blk: e.tensor_tensor(yd[:, :], yt[:, :], so[:, blk, :], op=ALU.mult), reads=[yt, so], writes=[yd])
                return yd

            zst = {"pT": None}

            def stZ(blk, yd):
                g4, j4 = blk // 4, blk % 4
                if j4 == 0:
                    zst["pT"] = C.psum(pin=True)
                pT = zst["pT"]
                op("pe", lambda e, pT=pT, yd=yd, j4=j4: e.transpose(pT[:, j4 * 128:(j4 + 1) * 128], yd[:, :], cB[:, 256:384]), reads=[yd, cB], writes=[pT])
                if j4 == 3:
                    ob = C.tmp("ydo", [128, 512], BF16, 2)
                    op("act", lambda e, ob=ob, pT=pT: e.activation(ob[:, :], pT[:, :], AF.Copy), reads=[pT], writes=[ob])
                    P.dma("sp", send2[(g4 // 4) * 256 + 128:(g4 // 4) * 256 + 256, (g4 % 4) * 512:(g4 % 4) * 512 + 512], ob[:, :], reads=[ob])
                    C.unpin()

            pts = {0: stX(0)}
            yds = {}
            for blk in range(64):
                if blk + 1 < 64:
                    pts[blk + 1] = stX(blk + 1)
                yds[blk] = stY(blk, pts.pop(blk))
                if blk >= 1:
                    stZ(blk - 1, yds.pop(blk - 1))
            stZ(63, yds.pop(63))

    if "a" in parts:
        qTs = [P.sb("qT%d" % i, [96, SEQ], BF16) for i in range(2)]; kTs = [P.sb("kT%d" % i, [96, SEQ], BF16) for i in range(2)]
        V = P.sb("V", [128, 64, 65], BF16); vTss = [P.sb("vTs%d" % i, [64, SEQ], BF16) for i in range(2)]
        for hh in range(2):
            for q in range(4):
                cs_ = slice(q * NTOK, (q + 1) * NTOK)
                gat(qTs[hh], 96, q, hh * 4 + q); gat(kTs[hh], 96, q, 8 + hh * 4 + q); gat(vTss[hh], 64, q, 16 + hh * 4 + q)
        tails = []
        for hh in range(2):
            qT = qTs[hh]; kT = kTs[hh]; vTs = vTss[hh]
            op("pool", lambda e: e.memset(V[:, :, :], 1.0), writes=[V])
            for g in range(4):
                pt_ = ptb[g % 2]
                for b_ in range(16):
                    blk = g * 16 + b_
                    op("pe", lambda e, pt_=pt_, b_=b_, blk=blk, vTs=vTs: e.transpose(pt_[:, b_ * 64:(b_ + 1) * 64], vTs[0:64, blk * 128:(blk + 1) * 128], idb[0:64, 0:64]), reads=[vTs, idb], writes=[pt_])
                op("dve", lambda e, pt_=pt_, g=g: e.tensor_copy(V[:, g * 16:(g + 1) * 16, 0:64], pt_[:, :].rearrange("p (b w) -> p b w", w=64)), reads=[pt_], writes=[V])
            for qt in range(16):
                po = C.psum(pin=True)
                nkb = 4 * qt + 4
                LOOK = 3
                pend = []
                for it in range(nkb + LOOK):
                    if it < nkb:
                        kb = it
                        c0 = max(0, kb - 4 * qt) * 128
                        ps = C.psum()
                        op("pe", lambda e, ps=ps, kb=kb, c0=c0, qt=qt, kT=kT, qT=qT: e.matmul(ps[:, c0:512], kT[:, kb * 128:(kb + 1) * 128], qT[:, qt * 512 + c0:qt * 512 + 512], start=True, stop=True), reads=[kT, qT], writes=[ps])
                        pt = C.tmp("apt", [128, 512], BF16, 6)
                        op("act", lambda e, ps=ps, pt=pt, c0=c0: e.activation(pt[:, c0:512], ps[:, c0:512], AF.Exp), reads=[ps], writes=[pt])
                        if kb >= 4 * qt:
                            op("pool", lambda e, pt=pt, c0=c0: e.tensor_tensor(pt[:, c0:c0 + 128], pt[:, c0:c0 + 128], maskb[:, :], op=ALU.mult), reads=[pt, maskb], writes=[pt])
                        pend.append((kb, c0, pt))
                    if it == 2 and tails:
                        tails.pop(0)()
                    if it >= LOOK:
                        kb, c0, pt = pend[it - LOOK]
                        op("pe", lambda e, po=po, pt=pt, kb=kb, c0=c0, nkb=nkb: e.matmul(po[0:65, c0:512], V[:, kb, :], pt[:, c0:512], start=(kb == 0), stop=(kb == nkb - 1)), reads=[V, pt], writes=[po])
                osb = C.tmp("osb", [65, 512], F32, 2, own=True)
                op("act", lambda e, osb=osb, po=po: e.activation(osb[:, :], po[0:65, :], AF.Copy), reads=[po], writes=[osb])
                C.unpin()
                op("dve", lambda e, osb=osb: e.reciprocal(osb[64:65, :], osb[64:65, :]), reads=[osb], writes=[osb])

                def tail(osb=osb, hh=hh, qt=qt):
                    pb = C.psum()
                    op("pe", lambda e, pb=pb, osb=osb: e.matmul(pb[0:64, :], ones_f[64:65, 0:64], osb[64:65, :], start=True, stop=True), reads=[ones_f, osb], writes=[pb])
                    yc = C.tmp("yc", [64, 512], BF16, 2)
                    op("dve", lambda e, yc=yc, osb=osb, pb=pb: e.tensor_tensor(yc[:, :], osb[0:64, :], pb[0:64, :], op=ALU.mult), reads=[osb, pb], writes=[yc])
                    P.dma("sp", send2[(qt // 4) * 256 + hh * 64:(qt // 4) * 256 + hh * 64 + 64, (qt % 4) * 512:(qt % 4) * 512 + 512], yc[:, :], reads=[yc])
                tails.append(tail)
        while tails:
            tails.pop(0)()
    P.wait_all("sp", P.tiles)
    return P


R1ROWS = 3360
Q0, KN0, KP0, V0, MQ0, MK0, MV0, SO0 = 0, 768, 1280, 1312, 1824, 2080, 2336, 2848
GROUPS = [[0, 1, 2, 3], [4, 5, 6, 7]]
CCH = 240
CCH1 = 480


def _chunks(total, cch=CCH):
    out = []
    start = 0
    base = 0
    while start < total:
        size = min(cch, total - start)
        out.append((start, size, base))
        base += 4 * size
        start += size
    return out


def _rowmap(total, r, q, cch=CCH):
    k = r // cch
    start = k * cch
    size = np.minimum(cch, total - start)
    return 4 * start + q * size + (r - start)


def _gather_all(P, send, recv, total, cch=CCH, wait=True):
    keys = []
    for (start, size, base) in _chunks(total, cch):
        keys.append(P.collective_raw("AllGather", send[start:start + size, :], recv[base:base + 4 * size, :], GROUPS, wait=False))
    if wait:
        P.ops["pool"].append(([(k, 1) for k in keys], None, None))
    return keys


def build_fused(phases="AMTC"):
    nc = bass.Bass("TRN2", target_bir_lowering=False)
    send1 = [nc.dram_tensor("send1_%d" % i, [R1ROWS, 1024], BF16).ap() for i in range(2)]
    recv1 = [nc.dram_tensor("recv1_%d" % i, [4 * R1ROWS, 1024], BF16).ap() for i in range(2)]
    sendm = nc.dram_tensor("sendm", [8, NTOK], F32).ap()
    recvm = nc.dram_tensor("recvm", [32, NTOK], F32).ap()
    send2 = nc.dram_tensor("send2", [1024, NTOK], BF16).ap()
    recv2 = nc.dram_tensor("recv2", [4096, NTOK], BF16).ap()
    h1d = nc.dram_tensor("h1d", [1024, NTOK], F32).ap()
    idx = dram_in(nc, "idx", [128, 64], I32)
    consts = dram_in(nc, "consts", [128, 512])

    def tokD(L):
        sfx = str(L)
        return {
            "pT": dram_in(nc, "pT" + sfx, [256, NTOK]), "gains": dram_in(nc, "gains" + sfx, [128, 32]), "consts": consts,
            "w_gate": dram_in(nc, "w_gate" + sfx, [1024, 2816]), "w_up": dram_in(nc, "w_up" + sfx, [1024, 2816]),
            "w_down": dram_in(nc, "w_down" + sfx, [2816, 1024]), "w_ple_gate": dram_in(nc, "w_ple_gate" + sfx, [1024, 1024]),
            "w_ple_proj": dram_in(nc, "w_ple_proj" + sfx, [256, 1024]), "w_out": dram_in(nc, "w_out" + sfx, [1024, 1024]),
        }
    DA = tokD(0)
    DA.update({
        "hT": dram_in(nc, "xT", [1024, NTOK]), "h1T": h1d,
        "mifT": sendm,
    })
    for hf in range(2):
        for nm, r0, nr in (("qTf", Q0, 768), ("knT", KN0, 512), ("kpeT", KP0, 32), ("vT", V0, 512), ("mqT", MQ0, 256), ("mkT", MK0, 256),
                           ("mvT", MV0, 512), ("soT", SO0, 512)):
            DA["%s_h%d" % (nm, hf)] = send1[hf][r0:r0 + nr, :]
    for nm in ("qT", "knT", "kpeT", "vT", "mqT", "mkT", "mvT", "soT"):
        DA[nm] = send1[0]
    keys1 = []

    def mid_hook(P):
        P.wait_all("pool", P.tiles)
        keys1.extend(_gather_all(P, send1[0], recv1[0], R1ROWS, CCH1, wait=False))
    PA = build_tok("A", {"nc": nc, "D": DA, "mid_hook": mid_hook})
    PA.wait_all("pool", PA.tiles)
    keys1.extend(_gather_all(PA, send1[1], recv1[1], R1ROWS, CCH1, wait=False))
    PA.ops["pool"].append(([(k, 1) for k in keys1], None, None))
    PA.collective_raw("AllGather", sendm, recvm, GROUPS)
    PA.emit()
    nc.all_engine_barrier()
    if "M" not in phases and "T" not in phases and "C" not in phases:
        dram_out(nc, "outT", [1024, NTOK])
        return nc

    DB = {"idx": idx, "recv1": recv1, "recvm": recvm, "send2": send2,
          "cB": dram_in(nc, "cB", [128, 512]), "bcol": dram_in(nc, "bcol", [128, 2]), "gmh": dram_in(nc, "gmh", [128, 128])}
    if "M" in phases:
        PM = build_mix("sm2", {"nc": nc, "D": DB})
        PM.wait_all("pool", PM.tiles)
        PM.emit()
        nc.all_engine_barrier()
    if "T" in phases:
        PT = build_mix("a", {"nc": nc, "D": DB})
        _gather_all(PT, send2, recv2, 1024)
        PT.emit()
        nc.all_engine_barrier()
    if "C" not in phases:
        dram_out(nc, "outT", [1024, NTOK])
        return nc

    DC = tokD(1)
    DC.update({"hT": h1d, "idx": idx, "recv2": recv2, "outT": dram_out(nc, "outT", [1024, NTOK])})
    if "D" in phases:
        DC["dbg_y"] = dram_out(nc, "dbg_y", [1024, NTOK], BF16)
    PC = build_tok("C", {"nc": nc, "D": DC})
    PC.wait_all("pool", PC.tiles)
    if "D" in phases:
        d1 = dram_out(nc, "dbg_send1", [R1ROWS, NTOK], BF16); d2 = dram_out(nc, "dbg_send2", [1024, NTOK], BF16); d3 = dram_out(nc, "dbg_sendm", [8, NTOK]); d4 = dram_out(nc, "dbg_h1", [1024, NTOK])
        dummy = PC.sb("dbgdummy", [1, 8])
        PC.dma("sp", d1, send1, reads=[dummy]); PC.dma("sp", d2, send2, reads=[dummy]); PC.dma("sp", d3, sendm, reads=[dummy]); PC.dma("sp", d4, h1d, reads=[dummy])
        PC.wait_all("sp", [dummy])
    PC.emit()
    return nc


def prep_fused(inp):
    mapsA = prep_A(inp)
    maps = []
    cB = np.zeros((128, 512), np.float32)
    s = np.arange(128)
    cB[:, 0:128] = (s[:, None] <= s[None, :])
    cB[:, 128:256] = (s[:, None] <= s[None, :]) & ((s[:, None] // 64) == (s[None, :] // 64))
    cB[:, 256:384] = np.eye(128, dtype=np.float32)
    gains1 = np.zeros((128, 32), np.float32)
    gains1[:, 8:16] = _chunkT(inp["g_ffn"][1]); gains1[:, 16:24] = _chunkT(inp["g_ple"][1])
    bg = inp["od_b_gate"][0]
    p = np.arange(128)
    for core in range(8):
        b, j = core // 4, core % 4
        q_ = j
        a = mapsA[core]
        m = {"xT": a["hT"], "xhT": a["xhT"], "posb": a["posb"], "consts": a["consts"], "idx": None,
             "pT0": a["pT"], "gains0": a["gains"], "w_gate0": a["w_gate"], "w_up0": a["w_up"], "w_down0": a["w_down"],
             "w_ple_gate0": a["w_ple_gate"], "w_ple_proj0": a["w_ple_proj"], "w_out0": a["w_out"]}
        for k in ("ev_w_in", "wconv", "gvb", "wsT", "bsb", "od_w_in", "g_lat", "w_q_up", "w_q_up_sw", "w_kv_up", "gsm"):
            m[k] = a[k]
        c1 = prep_tok_common(inp, 1, core)
        m.update({"pT1": c1["pT"], "gains1": gains1, "w_gate1": c1["w_gate"], "w_up1": c1["w_up"], "w_down1": c1["w_down"],
                  "w_ple_gate1": c1["w_ple_gate"], "w_ple_proj1": c1["w_ple_proj"], "w_out1": _c(inp["od_w_out"][0])})
        m["cB"] = cB
        m["bcol"] = _c(np.tile(np.array([[bg[j], bg[4 + j]]], np.float32), (128, 1)))
        m["gmh"] = _c(np.tile(inp["od_g_mh"][0][j].reshape(1, 128), (128, 1)))
        ix = np.zeros((128, 64), np.int32)
        def rm(r, q):
            return _rowmap(R1ROWS, np.asarray(r), q, CCH1)
        for q in range(4):
            for hh in range(2):
                hd = 2 * j + hh
                ix[0:96, hh * 4 + q] = rm(Q0 + hd * 96 + p[0:96], q)
                ix[0:64, 8 + hh * 4 + q] = rm(KN0 + hd * 64 + p[0:64], q)
                ix[64:96, 8 + hh * 4 + q] = rm(KP0 + p[0:32], q)
                ix[0:64, 16 + hh * 4 + q] = rm(V0 + hd * 64 + p[0:64], q)
            ix[0:64, 24 + q] = rm(MQ0 + j * 64 + p[0:64], q)
            ix[0:64, 28 + q] = rm(MK0 + j * 64 + p[0:64], q)
            ix[:, 32 + q] = rm(MV0 + j * 128 + p, q)
            ix[:, 36 + q] = rm(SO0 + j * 128 + p, q)
            ix[q * 32:(q + 1) * 32, 40] = (q * 8 + j) * 32 + p[0:32]
            ix[q * 32:(q + 1) * 32, 41] = (q * 8 + 4 + j) * 32 + p[0:32]
        for c in range(8):
            if c < 4:
                ix[:, 48 + c] = _rowmap(1024, q_ * 256 + p, c)
            else:
                ix[:, 48 + c] = _rowmap(1024, q_ * 256 + 128 + p, c - 4)
        m["idx"] = ix
        maps.append(m)
    return maps


_NC_CACHE = {}


def kernel(**inputs):
    inp = {k: np.asarray(v) for k, v in inputs.items()}
    if "nc" not in _NC_CACHE:
        _NC_CACHE["nc"] = build_fused()
    res = run_bass_kernel_spmd(_NC_CACHE["nc"], prep_fused(inp), core_ids=list(range(8))).results
    out = np.zeros((2, SEQ, 1024), np.float32)
    for core in range(8):
        b, q = core // 4, core % 4
        out[b, q * NTOK:(q + 1) * NTOK, :] = res[core]["outT"].T
    return out
```

```python
import numpy as np
import ml_dtypes
import concourse.bass as bass
import concourse.mybir as mybir
from concourse.bass_utils import run_bass_kernel_spmd

F32 = mybir.dt.float32
BF16 = mybir.dt.bfloat16
I32 = mybir.dt.int32
AF = mybir.ActivationFunctionType
ALU = mybir.AluOpType
AX = mybir.AxisListType
EPS = 1e-6
TWO_PI = 6.283185307179586
PI = 3.141592653589793
NTOK = 2048
TS = 512
class TT:
    __slots__ = ("h", "name", "last_w", "readers", "dsem", "dcount")

    def __init__(self, h, name):
        self.h = h
        self.name = name
        self.last_w = None
        self.readers = []
        self.dsem = None
        self.dcount = 0

    def __getitem__(self, idx):
        return self.h[idx]


class Prog:
    ENGS = ("pe", "act", "dve", "pool", "sp")

    def __init__(self, nc, prefix=""):
        self.nc = nc
        self.prefix = prefix
        self.ops = {e: [] for e in self.ENGS}
        self.count = {e: 0 for e in self.ENGS}
        self.waited = {e: {} for e in self.ENGS}
        self.sem_names = ["eng_" + e for e in self.ENGS]
        self.tiles = []
        self._ctx = []
        self.n_dsem = 0

    def sb(self, name, shape, dt=F32):
        g = self.nc.sbuf_tensor(self.prefix + "s_" + name, list(shape), dt)
        h = g.__enter__()
        self._ctx.append(g)
        t = TT(h, name)
        self.tiles.append(t)
        return t

    def ps(self, name, shape, dt=F32):
        g = self.nc.psum_tensor(self.prefix + "p_" + name, list(shape), dt)
        h = g.__enter__()
        self._ctx.append(g)
        t = TT(h, name)
        self.tiles.append(t)
        return t

    def _deps(self, eng, reads, writes):
        deps = []
        for r in reads:
            if r.last_w is not None:
                deps.append(r.last_w)
        for w in writes:
            if w.last_w is not None:
                deps.append(w.last_w)
            deps.extend(w.readers)
        waits = []
        wd = self.waited[eng]
        best = {}
        for (k, v) in deps:
            if eng == "pe" and k == "eng_pe":
                continue
            if wd.get(k, 0) >= v:
                continue
            if best.get(k, 0) < v:
                best[k] = v
        for k, v in best.items():
            wd[k] = v
            waits.append((k, v))
        return waits

    def op(self, eng, fn, reads=(), writes=()):
        waits = self._deps(eng, reads, writes)
        self.count[eng] += 1
        me = ("eng_" + eng, self.count[eng])
        self.ops[eng].append((waits, fn, (me[0], 1)))
        for r in reads:
            r.readers.append(me)
        for w in writes:
            w.last_w = me
            w.readers = []
        return me

    def dma(self, eng, out_ap, in_ap, reads=(), writes=(), **kw):
        waits = self._deps(eng, reads, writes)
        owner = (list(writes) + list(reads))[0]
        if owner.dsem is None:
            owner.dsem = "d%d" % self.n_dsem
            self.n_dsem += 1
            self.sem_names.append(owner.dsem)
        owner.dcount += 1
        me = (owner.dsem, 16 * owner.dcount)

        def fn(e, out_ap=out_ap, in_ap=in_ap, kw=kw):
            o_ = out_ap() if callable(out_ap) else out_ap
            i_ = in_ap() if callable(in_ap) else in_ap
            return e.dma_start(out=o_, in_=i_, **kw)
        self.ops[eng].append((waits, fn, (owner.dsem, 16)))
        for r in reads:
            r.readers.append(me)
        for w in writes:
            w.last_w = me
            w.readers = []
        return me

    def raw(self, eng, fn):
        self.ops[eng].append(([], fn, "raw"))

    def dram(self, name, shape, dt=F32):
        h = self.nc.dram_tensor(name, list(shape), dt).ap()
        t = TT(h, name)
        self.tiles.append(t)
        return t

    def collective(self, kind, in_t, out_t, groups):
        waits = self._deps("pool", [in_t], [out_t])
        key = "cc%d" % self.n_dsem
        self.n_dsem += 1
        self.sem_names.append(key)
        me = (key, 1)

        def fn(e):
            return e.collective_compute(kind, ALU.bypass, replica_groups=groups, ins=[in_t.h.opt()], outs=[out_t.h.opt()])
        self.ops["pool"].append((waits, fn, (key, None)))
        in_t.readers.append(me)
        out_t.last_w = me
        out_t.readers = []
        return me

    def gather(self, out_t, out_ap, in_ap, idx_t, idx_ap):
        waits = self._deps("pool", [idx_t], [out_t])
        if out_t.dsem is None:
            out_t.dsem = "d%d" % self.n_dsem
            self.n_dsem += 1
            self.sem_names.append(out_t.dsem)
        out_t.dcount += 1
        me = (out_t.dsem, 16 * out_t.dcount)

        def fn(e):
            return e.indirect_dma_start(out=out_ap, out_offset=None, in_=in_ap, in_offset=bass.IndirectOffsetOnAxis(ap=idx_ap, axis=0))
        self.ops["pool"].append((waits, fn, (out_t.dsem, 16)))
        idx_t.readers.append(me)
        out_t.last_w = me
        out_t.readers = []
        return me

    def collective_raw(self, kind, in_ap, out_ap, groups, wait=True):
        self.wait_all("pool", self.tiles)
        key = "cc%d" % self.n_dsem
        self.n_dsem += 1
        self.sem_names.append(key)

        def fn(e):
            return e.collective_compute(kind, ALU.bypass, replica_groups=groups, ins=[in_ap.opt()], outs=[out_ap.opt()])
        self.ops["pool"].append(([], fn, (key, None)))
        if wait:
            self.ops["pool"].append(([(key, 1)], None, None))
        return key

    def wait_all(self, eng, tiles):
        deps = []
        for t in tiles:
            if t.last_w is not None:
                deps.append(t.last_w)
            deps.extend(t.readers)
        wd = self.waited[eng]
        best = {}
        for k, v in deps:
            if wd.get(k, 0) < v and best.get(k, 0) < v:
                best[k] = v
        waits = []
        for k, v in best.items():
            wd[k] = v
            waits.append((k, v))
        self.ops[eng].append((waits, None, None))

    def emit(self):
        nc = self.nc
        sems = {}
        for n in self.sem_names:
            sems[n] = nc.alloc_semaphore(name=self.prefix + n)
        blk = nc.Block()
        block = blk.__enter__()

        def run(engname):
            def body(e):
                for waits, fn, inc in self.ops[engname]:
                    for k, v in waits:
                        e.wait_ge(sems[k], v)
                    if fn is not None:
                        ins = fn(e)
                        if inc == "raw":
                            continue
                        if inc[1] is None:
                            ins.then_inc(sems[inc[0]])
                        else:
                            ins.then_inc(sems[inc[0]], inc[1])
            return body

        block.tensor(run("pe"))
        block.scalar(run("act"))
        block.vector(run("dve"))
        block.gpsimd(run("pool"))
        block.sync(run("sp"))
        blk.__exit__(None, None, None)
        nc.all_engine_barrier()
        nc.clear_and_free_semaphores(list(sems.values()))
        nc.all_engine_barrier()
        for g in reversed(self._ctx):
            g.__exit__(None, None, None)
        self._ctx = []


class TV:
    def __init__(self, base, ap):
        self.__dict__["base"] = base
        self.__dict__["ap"] = ap

    def __getitem__(self, idx):
        return self.ap[idx]

    def __getattr__(self, k):
        return getattr(self.base, k)

    def __setattr__(self, k, v):
        setattr(self.base, k, v)


class Ctx:
    def __init__(self, P, nbanks=8):
        self.P = P
        self.nb = nbanks
        self.pbanks = [P.ps("pb%d" % i, [128, 512], F32) for i in range(nbanks)]
        self.pi = 0
        self.pinned = set()
        self.rings = {}

    def psum(self, pin=False):
        while (self.pi % self.nb) in self.pinned:
            self.pi += 1
        t = self.pbanks[self.pi % self.nb]
        if pin:
            self.pinned.add(self.pi % self.nb)
        self.pi += 1
        return t

    def unpin(self):
        self.pinned = set()

    def tmp(self, key, shape, dt=F32, n=2, own=False):
        if dt == F32 and len(shape) == 2 and shape[1] == 512 and not own:
            base = self.tmp("T32", [128, 512, 1], F32, 8)
            return TV(base, base.h[0:shape[0], :, 0])
        if key not in self.rings:
            self.rings[key] = [[self.P.sb("%s_%d" % (key, i), shape, dt) for i in range(n)], 0]
        r = self.rings[key]
        t = r[0][r[1] % len(r[0])]
        r[1] += 1
        return t


def dram_in(nc, name, shape, dt=F32):
    return nc.dram_tensor(name, list(shape), dt, kind="ExternalInput").ap()


def dram_out(nc, name, shape, dt=F32):
    return nc.dram_tensor(name, list(shape), dt, kind="ExternalOutput").ap()


def build_tok(mode, fz=None):
    nc = fz["nc"] if fz else bass.Bass("TRN2", target_bir_lowering=False)
    P = Prog(nc, mode + "_")
    C = Ctx(P)
    op = P.op
    L = 0 if mode == "A" else 1

    D = dict(fz["D"]) if fz else {}
    def din(name, shape, dt=F32):
        if name not in D:
            D[name] = dram_in(nc, name, shape, dt)
        return D[name]
    def dout(name, shape, dt=F32):
        if name not in D:
            D[name] = dram_out(nc, name, shape, dt)
        return D[name]

    din("hT", [1024, NTOK])
    din("pT", [256, NTOK])
    din("gains", [128, 32])
    din("consts", [128, 512])
    din("w_gate", [1024, 2816]); din("w_up", [1024, 2816]); din("w_down", [2816, 1024])
    din("w_ple_gate", [1024, 1024]); din("w_ple_proj", [256, 1024])
    din("w_out", [1024, 1024])
    if mode == "A":
        din("xhT", [1024, 2])
        din("posb", [96, NTOK], I32)
        din("ev_w_in", [1024, 2560])
        din("wconv", [128, 12]); din("gvb", [128, 512]); din("wsT", [128, 8, 128]); din("bsb", [128, 4, 128])
        din("od_w_in", [1024, 2248])
        din("g_lat", [128, 5])
        din("w_q_up", [384, 768]); din("w_q_up_sw", [384, 768]); din("w_kv_up", [256, 1024])
        din("gsm", [128, 8])
        dout("h1T", [1024, NTOK])
        dout("qT", [8, 96, NTOK], BF16); dout("knT", [512, NTOK], BF16); dout("kpeT", [32, NTOK], BF16)
        dout("vT", [512, NTOK], BF16)
        dout("mqT", [256, NTOK], BF16); dout("mkT", [256, NTOK], BF16)
        dout("mvT", [512, NTOK], BF16); dout("soT", [512, NTOK], BF16)
        dout("mifT", [8, NTOK])
    else:
        if not fz:
            din("ymT", [1024, NTOK], BF16)
        else:
            idxc = P.sb("idxc", [128, 64], I32)
            P.dma("sp", idxc[:, :], D["idx"], writes=[idxc])
            ystg = [P.sb("ystg%d" % c, [128, NTOK], BF16) for c in range(8)]
            for c in range(8):
                P.gather(ystg[c], ystg[c][:, :], D["recv2"][:, :], idxc, idxc[:, 48 + c:49 + c])
                if "dbg_y" in D:
                    P.dma("sp", D["dbg_y"][c * 128:(c + 1) * 128, :], ystg[c][:, :], reads=[ystg[c]])
        dout("outT", [1024, NTOK])

    h = [[P.sb("h%d_%d" % (s, c), [128, TS]) for c in range(8)] for s in range(2)]
    hn = [[P.sb("hn%d_%d" % (s, c), [128, TS], BF16) for c in range(8)] for s in range(2)]
    act = [[P.sb("act%d_%d" % (s, j), [128, TS], BF16) for j in range(22)] for s in range(2)]
    y = [a[0:8] for a in act]
    ringA = [P.sb("wA%d" % i, [128, 8, 512], BF16) for i in range(3)]
    ringD = [P.sb("wD%d" % i, [128, 22, 128], BF16) for i in range(2)]
    st = {"a": 0, "d": 0}
    wpp = P.sb("wpp", [128, 2, 1024], BF16)
    gains = P.sb("gains", [128, 32])
    cst = P.sb("cst", [128, 512])
    ones_bf = P.sb("ones_bf", [128, 128], BF16)
    P.dma("sp", gains[:, :], D["gains"], writes=[gains])
    P.dma("sp", cst[:, :], D["consts"], writes=[cst])
    op("pool", lambda e: e.memset(ones_bf[:, :], 1.0), writes=[ones_bf])
    P.dma("pool", wpp[:, :, :], D["w_ple_proj"].rearrange("(kc p) n -> p kc n", p=128), writes=[wpp])

    def slotA():
        t = ringA[st["a"] % 3]; st["a"] += 1; return t

    def slotD():
        t = ringD[st["d"] % 2]; st["d"] += 1; return t

    def loadA(w, c0, c1, dst=None, off=0):
        t = dst if dst is not None else slotA()
        P.dma("pool", t[:, :, off:off + (c1 - c0)], w.rearrange("(kc p) n -> p kc n", p=128)[:, :, c0:c1], writes=[t])
        return t

    def mm(ps_ap, lhs_fn, rhs_fn, nk, reads, ps):
        for kc in range(nk):
            l_ = lhs_fn(kc); r_ = rhs_fn(kc)
            op("pe", lambda e, kc=kc, l_=l_, r_=r_: e.matmul(ps_ap, l_, r_, start=(kc == 0), stop=(kc == nk - 1)),
               reads=reads, writes=[ps])

    def rstd_from_ps(ps, rows, n, scale, tag):
        sd = C.tmp("sd" + tag, [128, 512])
        rp = C.tmp("rp" + tag, [128, 512])
        rd = [ps] + ([cst] if not isinstance(scale, float) else [])
        op("act", lambda e: e.activation(sd[0:rows, 0:n], ps[0:rows, 0:n], AF.Sqrt, bias=EPS, scale=scale), reads=rd, writes=[sd])
        op("dve", lambda e: e.reciprocal(rp[0:rows, 0:n], sd[0:rows, 0:n]), reads=[sd], writes=[rp])
        return rp

    def norm(s, gcol, n=TS, src=None, dst=None):
        src = src or h[s]; dst = dst or hn[s]
        ps = C.psum()
        for c in range(8):
            sq = C.tmp("sq", [128, 512], BF16, 3)
            op("act", lambda e, c=c, sq=sq: e.activation(sq[:, 0:n], src[c][:, 0:n], AF.Square), reads=[src[c]], writes=[sq])
            op("pe", lambda e, c=c, sq=sq: e.matmul(ps[:, 0:n], ones_bf[:, :], sq[:, 0:n], start=(c == 0), stop=(c == 7)), reads=[sq, ones_bf], writes=[ps])
        rp = rstd_from_ps(ps, 128, n, 1.0 / 1024.0, "n")
        for c in range(8):
            op("dve", lambda e, c=c: e.scalar_tensor_tensor(dst[c][:, 0:n], src[c][:, 0:n], gains[:, gcol + c:gcol + c + 1], rp[:, 0:n], op0=ALU.mult, op1=ALU.mult),
               reads=[src[c], gains, rp], writes=[dst[c]])

    def resid_proj(w, src):
        for blk in range(2):
            slot = loadA(w, blk * 512, blk * 512 + 512)
            for s in range(2):
                for m in range(4):
                    ps = C.psum()
                    mm(ps[:, :], lambda kc, m=m: slot[:, kc, m * 128:(m + 1) * 128], lambda kc, s=s: src[s][kc][:, :], 8, [slot] + src[s], ps)
                    hc = h[s][blk * 4 + m]
                    op("dve", lambda e, ps=ps, hc=hc: e.tensor_tensor(hc[:, :], ps[:, :], hc[:, :], op=ALU.add), reads=[ps, hc], writes=[hc])

    def ffn(gcol):
        for s in range(2):
            norm(s, gcol)
        for j in range(11):
            slot = slotA()
            loadA(D["w_gate"], j * 256, j * 256 + 256, dst=slot, off=0)
            loadA(D["w_up"], j * 256, j * 256 + 256, dst=slot, off=256)
            for s in range(2):
                for jj in range(2):
                    pg = C.psum(); pu = C.psum()
                    mm(pg[:, :], lambda kc, jj=jj: slot[:, kc, jj * 128:(jj + 1) * 128], lambda kc, s=s: hn[s][kc][:, :], 8, [slot] + hn[s], pg)
                    mm(pu[:, :], lambda kc, jj=jj: slot[:, kc, 256 + jj * 128:256 + (jj + 1) * 128], lambda kc, s=s: hn[s][kc][:, :], 8, [slot] + hn[s], pu)
                    sg = C.tmp("sg", [128, 512], F32, 3)
                    op("act", lambda e, pg=pg, sg=sg: e.activation(sg[:, :], pg[:, :], AF.Silu), reads=[pg], writes=[sg])
                    a = act[s][2 * j + jj]
                    op("dve", lambda e, pu=pu, sg=sg, a=a: e.tensor_tensor(a[:, :], pu[:, :], sg[:, :], op=ALU.mult), reads=[pu, sg], writes=[a])
        for mb in range(8):
            slot = slotD()
            P.dma("pool", slot[:, :, :], D["w_down"].rearrange("(kc p) n -> p kc n", p=128)[:, :, mb * 128:(mb + 1) * 128], writes=[slot])
            for s in range(2):
                ps = C.psum()
                mm(ps[:, :], lambda kc: slot[:, kc, :], lambda kc, s=s: act[s][kc][:, :], 22, [slot] + act[s], ps)
                hc = h[s][mb]
                op("dve", lambda e, ps=ps, hc=hc: e.tensor_tensor(hc[:, :], ps[:, :], hc[:, :], op=ALU.add), reads=[ps, hc], writes=[hc])

    def ple(gcol, tok0):
        pt = []
        for s in range(2):
            norm(s, gcol)
            t = C.tmp("pt", [128, 2, TS], BF16, 2)
            P.dma("pool", t[:, :, :], D["pT"].rearrange("(kc p) n -> p kc n", p=128)[:, :, tok0 + s * TS:tok0 + (s + 1) * TS], writes=[t])
            pt.append(t)
        for blk in range(2):
            slot = loadA(D["w_ple_gate"], blk * 512, blk * 512 + 512)
            for s in range(2):
                for m in range(4):
                    mg = blk * 4 + m
                    pg = C.psum(); pp = C.psum()
                    mm(pg[:, :], lambda kc, m=m: slot[:, kc, m * 128:(m + 1) * 128], lambda kc, s=s: hn[s][kc][:, :], 8, [slot] + hn[s], pg)
                    mm(pp[:, :], lambda kc, mg=mg: wpp[:, kc, mg * 128:(mg + 1) * 128], lambda kc, s=s: pt[s][:, kc, :], 2, [wpp, pt[s]], pp)
                    sg = C.tmp("sg", [128, 512], F32, 3)
                    op("act", lambda e, pg=pg, sg=sg: e.activation(sg[:, :], pg[:, :], AF.Sigmoid), reads=[pg], writes=[sg])
                    t2 = C.tmp("t2", [128, 512], F32, 3)
                    op("dve", lambda e, pp=pp, sg=sg, t2=t2: e.tensor_tensor(t2[:, :], pp[:, :], sg[:, :], op=ALU.mult), reads=[pp, sg], writes=[t2])
                    hc = h[s][mg]
                    op("dve", lambda e, t2=t2, hc=hc: e.tensor_tensor(hc[:, :], t2[:, :], hc[:, :], op=ALU.add), reads=[t2, hc], writes=[hc])

    if mode == "A":
        wconv = P.sb("wconv", [128, 12]); gvb = P.sb("gvb", [128, 512]); bsb = P.sb("bsb", [128, 4, 128])
        wsT = P.sb("wsT", [128, 8, 128], BF16); maskb = P.sb("maskb", [128, 128], BF16)
        g_lat = P.sb("g_lat", [128, 5]); gsm = P.sb("gsm", [128, 8])
        b96 = P.sb("b96", [96, 96], BF16); bd64 = P.sb("bd64", [128, 128], BF16)
        for t, nm in ((wconv, "wconv"), (gvb, "gvb"), (g_lat, "g_lat"), (gsm, "gsm")):
            P.dma("sp", t[:, :], D[nm], writes=[t])
        P.dma("sp", bsb[:, :, :], D["bsb"], writes=[bsb])
        P.dma("pool", wsT[:, :, :], D["wsT"], writes=[wsT])
        op("dve", lambda e: e.tensor_copy(maskb[:, :], cst[:, 0:128]), reads=[cst], writes=[maskb])
        for hh in range(8):
            op("dve", lambda e, hh=hh: e.tensor_tensor(wsT[:, hh, :], wsT[:, hh, :], maskb[:, :], op=ALU.mult), reads=[wsT, maskb], writes=[wsT])
        op("dve", lambda e: e.tensor_copy(b96[:, :], cst[0:96, 128:224]), reads=[cst], writes=[b96])
        op("dve", lambda e: e.tensor_copy(bd64[:, :], cst[:, 224:352]), reads=[cst], writes=[bd64])
        hal = [P.sb("hal%d" % cc, [128, 2]) for cc in range(4)]
        gu = [act[s][8:12] for s in range(2)]
        hh_t = [P.sb("hh%d" % c, [128, 2]) for c in range(8)]
        hhn = [P.sb("hhn%d" % c, [128, 2], BF16) for c in range(8)]

    def even_mixer(sti, tok0):
        for s in range(2):
            norm(s, 0)
        if sti == 0:
            for c in range(8):
                P.dma("sp", hh_t[c][:, :], D["xhT"][c * 128:(c + 1) * 128, :], writes=[hh_t[c]])
            norm(0, 0, n=2, src=hh_t, dst=hhn)
        for cc in range(4):
            slot = loadA(D["ev_w_in"], cc * 384, cc * 384 + 384)
            for s in range(2):
                z = C.tmp("zt", [128, TS + 2], F32, 2)
                if not (s == 0 and sti == 0):
                    op("dve", lambda e, cc=cc, z=z: e.tensor_copy(z[:, 0:2], hal[cc][:, :]), reads=[hal[cc]], writes=[z])
                else:
                    pc = C.psum(); px = C.psum()
                    mm(pc[:, 0:2], lambda kc: slot[:, kc, 128:256], lambda kc: hhn[kc][:, :], 8, [slot] + hhn, pc)
                    mm(px[:, 0:2], lambda kc: slot[:, kc, 256:384], lambda kc: hhn[kc][:, :], 8, [slot] + hhn, px)
                    cs = C.tmp("cs", [128, 512], F32, 2)
                    op("act", lambda e, pc=pc, cs=cs: e.activation(cs[:, 0:2], pc[:, 0:2], AF.Copy), reads=[pc], writes=[cs])
                    op("dve", lambda e, px=px, cs=cs, z=z: e.tensor_tensor(z[:, 0:2], px[:, 0:2], cs[:, 0:2], op=ALU.mult), reads=[px, cs], writes=[z])
                pb = C.psum(); pc = C.psum(); px = C.psum()
                for pp_, c0 in ((pc, 128), (px, 256), (pb, 0)):
                    mm(pp_[:, :], lambda kc, c0=c0: slot[:, kc, c0:c0 + 128], lambda kc, s=s: hn[s][kc][:, :], 8, [slot] + hn[s], pp_)
                cs = C.tmp("cs", [128, 512], F32, 2)
                op("act", lambda e, pc=pc, cs=cs: e.activation(cs[:, :], pc[:, :], AF.Copy), reads=[pc], writes=[cs])
                op("dve", lambda e, px=px, cs=cs, z=z: e.tensor_tensor(z[:, 2:TS + 2], px[:, :], cs[:, :], op=ALU.mult), reads=[px, cs], writes=[z])
                op("dve", lambda e, cc=cc, z=z: e.tensor_copy(hal[cc][:, :], z[:, TS:TS + 2]), reads=[z], writes=[hal[cc]])
                acc = C.tmp("acc", [128, 512], F32, 2)
                op("dve", lambda e, z=z, acc=acc, cc=cc: e.tensor_scalar(acc[:, :], z[:, 0:TS], wconv[:, cc * 3:cc * 3 + 1], None, op0=ALU.mult), reads=[z, wconv], writes=[acc])
                op("dve", lambda e, z=z, acc=acc, cc=cc: e.scalar_tensor_tensor(acc[:, :], z[:, 1:TS + 1], wconv[:, cc * 3 + 1:cc * 3 + 2], acc[:, :], op0=ALU.mult, op1=ALU.add), reads=[z, wconv, acc], writes=[acc])
                op("dve", lambda e, z=z, acc=acc, cc=cc: e.scalar_tensor_tensor(acc[:, :], z[:, 2:TS + 2], wconv[:, cc * 3 + 2:cc * 3 + 3], acc[:, :], op0=ALU.mult, op1=ALU.add), reads=[z, wconv, acc], writes=[acc])
                yt = y[s][cc]
                op("dve", lambda e, pb=pb, acc=acc, yt=yt: e.tensor_tensor(yt[:, :], pb[:, :], acc[:, :], op=ALU.mult), reads=[pb, acc], writes=[yt])
        slot = loadA(D["ev_w_in"], 1536, 2048)
        for s in range(2):
            for uc in range(4):
                pu = C.psum()
                mm(pu[:, :], lambda kc, uc=uc: slot[:, kc, uc * 128:(uc + 1) * 128], lambda kc, s=s: hn[s][kc][:, :], 8, [slot] + hn[s], pu)
                g_ = gu[s][uc]
                op("act", lambda e, pu=pu, g_=g_: e.activation(g_[:, :], pu[:, :], AF.Gelu), reads=[pu], writes=[g_])
        slot = loadA(D["ev_w_in"], 2048, 2560)
        for s in range(2):
            pm = [C.psum(pin=True) for _ in range(4)]
            for tb in range(4):
                pv = C.psum()
                mm(pv[:, :], lambda kc, s=s, tb=tb: hn[s][kc][:, tb * 128:(tb + 1) * 128], lambda kc: slot[:, kc, :], 8, [slot] + hn[s], pv)
                gv = C.tmp("gv", [128, 512], F32, 2)
                op("act", lambda e, pv=pv, gv=gv: e.activation(gv[:, :], pv[:, :], AF.Gelu), reads=[pv], writes=[gv])
                sqv = C.tmp("sqv", [128, 512], F32, 2)
                op("act", lambda e, gv=gv, sqv=sqv: e.activation(sqv[:, :], gv[:, :], AF.Square), reads=[gv], writes=[sqv])
                ss = C.tmp("ss", [128, 8], F32, 2); sd = C.tmp("ssd", [128, 8], F32, 2); rs = C.tmp("srs", [128, 8], F32, 2)
                op("dve", lambda e, sqv=sqv, ss=ss: e.tensor_reduce(ss[:, :], sqv[:, :].rearrange("p (h d) -> p h d", d=64), axis=AX.X, op=ALU.add), reads=[sqv], writes=[ss])
                op("act", lambda e, ss=ss, sd=sd: e.activation(sd[:, :], ss[:, :], AF.Sqrt, bias=EPS, scale=1.0 / 64.0), reads=[ss], writes=[sd])
                op("dve", lambda e, sd=sd, rs=rs: e.reciprocal(rs[:, :], sd[:, :]), reads=[sd], writes=[rs])
                op("dve", lambda e, gv=gv, rs=rs: e.tensor_tensor(gv[:, :].rearrange("p (h d) -> p h d", d=64), gv[:, :].rearrange("p (h d) -> p h d", d=64),
                                                                 rs[:, :].unsqueeze(2).to_broadcast([128, 8, 64]), op=ALU.mult), reads=[gv, rs], writes=[gv])
                vn = C.tmp("vn", [128, 512], BF16, 2)
                op("dve", lambda e, gv=gv, vn=vn: e.tensor_tensor(vn[:, :], gv[:, :], gvb[:, :], op=ALU.mult), reads=[gv, gvb], writes=[vn])
                for hd in range(8):
                    op("pe", lambda e, hd=hd, tb=tb, vn=vn, pm=pm: e.matmul(pm[hd // 2][64 * (hd % 2):64 * (hd % 2) + 64, tb * 128:(tb + 1) * 128], vn[:, hd * 64:(hd + 1) * 64], wsT[:, hd, :], start=True, stop=True),
                       reads=[vn, wsT], writes=[pm[hd // 2]])
            for hc in range(4):
                t1 = C.tmp("t2", [128, 512], F32, 3)
                op("dve", lambda e, hc=hc, t1=t1, pm=pm: e.tensor_tensor(t1[:, :].rearrange("p (b t) -> p b t", t=128), pm[hc][:, :].rearrange("p (b t) -> p b t", t=128),
                                                                bsb[:, hc, :].unsqueeze(1).to_broadcast([128, 4, 128]), op=ALU.add), reads=[pm[hc], bsb], writes=[t1])
                yt = y[s][4 + hc]
                op("dve", lambda e, t1=t1, yt=yt, s=s, hc=hc: e.tensor_tensor(yt[:, :], t1[:, :], gu[s][hc][:, :], op=ALU.mult), reads=[t1, gu[s][hc]], writes=[yt])
            C.unpin()
        resid_proj(D["w_out"], y)

    def O(nm, r0, r1, t0):
        if fz and (nm + "_h0") in D:
            hf = t0 // 1024
            return D[nm + "_h%d" % hf][r0:r1, (t0 % 1024):(t0 % 1024) + TS]
        return D[nm][r0:r1, t0:t0 + TS]

    def odd_front(tok0):
        W = D["od_w_in"]
        for s in range(2):
            norm(s, 24)
        tabs = []
        for s in range(2):
            pi_ = C.tmp("ti", [96, TS], I32, 1)
            P.dma("sp", pi_[:, :], D["posb"][:, tok0 + s * TS:tok0 + (s + 1) * TS], writes=[pi_])
            ang = C.tmp("angp", [96, TS], F32, 1, own=True)
            op("dve", lambda e, pi_=pi_, ang=ang: e.tensor_copy(ang[:, :], pi_[:, :]), reads=[pi_], writes=[ang])
            op("dve", lambda e, ang=ang: e.tensor_scalar(ang[:, :], ang[:, :], cst[0:96, 353:354], None, op0=ALU.mult), reads=[ang, cst], writes=[ang])
            pair = []
            for nm, shift in (("cos", PI / 2.0), ("sin", 0.0)):
                a2 = C.tmp("a2", [96, TS], F32, 1); tf = C.tmp("tf", [96, TS], F32, 1); ti = C.tmp("ti", [96, TS], I32, 1)
                tab = C.tmp("tab" + nm, [96, TS], F32, 2, own=True)
                op("dve", lambda e, a2=a2, ang=ang, shift=shift: e.tensor_scalar(a2[:, :], ang[:, :], shift, None, op0=ALU.add), reads=[ang], writes=[a2])
                op("dve", lambda e, a2=a2, tf=tf: e.tensor_scalar(tf[:, :], a2[:, :], 1.0 / TWO_PI, None, op0=ALU.mult), reads=[a2], writes=[tf])
                op("dve", lambda e, tf=tf, ti=ti: e.tensor_copy(ti[:, :], tf[:, :]), reads=[tf], writes=[ti])
                op("dve", lambda e, tf=tf, ti=ti: e.tensor_copy(tf[:, :], ti[:, :]), reads=[ti], writes=[tf])
                op("dve", lambda e, tf=tf, a2=a2: e.scalar_tensor_tensor(a2[:, :], tf[:, :], -TWO_PI, a2[:, :], op0=ALU.mult, op1=ALU.add), reads=[tf, a2], writes=[a2])
                op("dve", lambda e, a2=a2: e.tensor_scalar(a2[:, :], a2[:, :], -PI, PI, op0=ALU.max, op1=ALU.min), reads=[a2], writes=[a2])
                if nm == "sin":
                    op("act", lambda e, a2=a2, tab=tab: e.activation(tab[:, :], a2[:, :], AF.Sin, scale=cst[0:96, 354:355]), reads=[a2, cst], writes=[tab])
                else:
                    op("act", lambda e, a2=a2, tab=tab: e.activation(tab[:, :], a2[:, :], AF.Sin), reads=[a2], writes=[tab])
                pair.append(tab)
            tabs.append(pair)

        def fm_out(ps, rows, dram_ap, scale=1.0, tag="fo"):
            ob = C.tmp(tag, [128, TS], BF16, 3)
            op("act", lambda e: e.mul(ob[0:rows, :], ps[0:rows, :], float(scale)), reads=[ps], writes=[ob])
            P.dma("sp", dram_ap, ob[0:rows, :], reads=[ob])

        def tok_out(ps, ncols, dram_fn, stage, tb, col0, func=AF.Copy):
            op("act", lambda e: e.activation(stage[:, tb, col0:col0 + ncols], ps[:, 0:ncols], func), reads=[ps], writes=[stage])

        def lat_norm(raws, nch, gc0, D_, outs, tag):
            ps = C.psum()
            for c in range(nch):
                sq = C.tmp("sq", [128, 512], BF16, 3)
                op("act", lambda e, c=c, sq=sq: e.activation(sq[:, :], raws[c][:, :], AF.Square), reads=[raws[c]], writes=[sq])
                op("pe", lambda e, c=c, sq=sq: e.matmul(ps[:, :], ones_bf[:, :], sq[:, :], start=(c == 0), stop=(c == nch - 1)), reads=[sq, ones_bf], writes=[ps])
            rp = rstd_from_ps(ps, 128, TS, 1.0 / D_, tag)
            for c in range(nch):
                op("dve", lambda e, c=c: e.scalar_tensor_tensor(outs[c][:, :], raws[c][:, :], g_lat[:, gc0 + c:gc0 + c + 1], rp[:, :], op0=ALU.mult, op1=ALU.mult),
                   reads=[raws[c], g_lat, rp], writes=[outs[c]])

        qlr = [act[s][12:15] for s in range(2)]
        kvr = [act[s][15:17] for s in range(2)]
        qln = [act[s][17:20] for s in range(2)]
        kvn = [act[s][20:22] for s in range(2)]

        def raw_copy(ps, rows, dst):
            op("act", lambda e: e.activation(dst[0:rows, :], ps[0:rows, :], AF.Copy), reads=[ps], writes=[dst])

        slot = loadA(W, 0, 512)
        for s in range(2):
            for c in range(4):
                ps = C.psum()
                mm(ps[:, :], lambda kc, c=c: slot[:, kc, c * 128:(c + 1) * 128], lambda kc, s=s: hn[s][kc][:, :], 8, [slot] + hn[s], ps)
                raw_copy(ps, 128, qlr[s][c] if c < 3 else kvr[s][0])
            lat_norm(qlr[s], 3, 0, 384.0, qln[s], "q")
        slot = loadA(W, 512, 960)
        for s in range(2):
            t0 = tok0 + s * TS
            cosT, sinT = tabs[s]
            ps = C.psum()
            mm(ps[:, :], lambda kc: slot[:, kc, 0:128], lambda kc, s=s: hn[s][kc][:, :], 8, [slot] + hn[s], ps)
            raw_copy(ps, 128, kvr[s][1])
            lat_norm(kvr[s], 2, 3, 256.0, kvn[s], "k")
            pk = C.psum(); pks = C.psum()
            mm(pk[0:32, :], lambda kc: slot[:, kc, 128:160], lambda kc, s=s: hn[s][kc][:, :], 8, [slot] + hn[s], pk)
            mm(pks[0:32, :], lambda kc: slot[:, kc, 160:192], lambda kc, s=s: hn[s][kc][:, :], 8, [slot] + hn[s], pks)
            kr = C.tmp("kr", [32, 512], F32)
            raw_copy(pk, 32, kr)
            sq = C.tmp("sq", [128, 512], BF16, 3)
            op("act", lambda e, sq=sq, kr=kr: e.activation(sq[0:32, :], kr[:, :], AF.Square), reads=[kr], writes=[sq])
            pn = C.psum()
            op("pe", lambda e, sq=sq, pn=pn: e.matmul(pn[0:32, :], ones_bf[0:32, 0:32], sq[0:32, :], start=True, stop=True), reads=[sq, ones_bf], writes=[pn])
            rp = rstd_from_ps(pn, 32, TS, 1.0 / 32.0, "kp")
            a = C.tmp("kpa", [32, 512], F32); b_ = C.tmp("kpb", [32, 512], F32)
            op("dve", lambda e, a=a, rp=rp, kr=kr: e.scalar_tensor_tensor(a[:, :], kr[:, :], gsm[0:32, 4:5], rp[0:32, :], op0=ALU.mult, op1=ALU.mult), reads=[kr, gsm, rp], writes=[a])
            op("dve", lambda e, b_=b_, rp=rp, pks=pks: e.scalar_tensor_tensor(b_[:, :], pks[0:32, :], gsm[0:32, 5:6], rp[0:32, :], op0=ALU.mult, op1=ALU.mult), reads=[pks, gsm, rp], writes=[b_])
            op("dve", lambda e, a=a, cosT=cosT: e.tensor_tensor(a[:, :], a[:, :], cosT[0:32, :], op=ALU.mult), reads=[a, cosT], writes=[a])
            op("dve", lambda e, b_=b_, sinT=sinT: e.tensor_tensor(b_[:, :], b_[:, :], sinT[0:32, :], op=ALU.mult), reads=[b_, sinT], writes=[b_])
            ob = C.tmp("fo", [128, TS], BF16, 3)
            op("dve", lambda e, a=a, b_=b_, ob=ob: e.tensor_tensor(ob[0:32, :], a[:, :], b_[:, :], op=ALU.add), reads=[a, b_], writes=[ob])
            P.dma("sp", O("kpeT", 0, 32, t0), ob[0:32, :], reads=[ob])
            for c in range(2):
                ps = C.psum()
                mm(ps[:, :], lambda kc, c=c: slot[:, kc, 192 + c * 128:192 + (c + 1) * 128], lambda kc, s=s: hn[s][kc][:, :], 8, [slot] + hn[s], ps)
                fm_out(ps, 128, O("mqT", c * 128, (c + 1) * 128, t0), scale=0.125)
        slot = loadA(W, 960, 1224)
        for s in range(2):
            t0 = tok0 + s * TS
            for c in range(2):
                ps = C.psum()
                mm(ps[:, :], lambda kc, c=c: slot[:, kc, c * 128:(c + 1) * 128], lambda kc, s=s: hn[s][kc][:, :], 8, [slot] + hn[s], ps)
                fm_out(ps, 128, O("mkT", c * 128, (c + 1) * 128, t0))
            ps = C.psum()
            mm(ps[0:8, :], lambda kc: slot[:, kc, 256:264], lambda kc, s=s: hn[s][kc][:, :], 8, [slot] + hn[s], ps)
            mo_ = C.tmp("mif", [8, 512], F32)
            raw_copy(ps, 8, mo_)
            P.dma("sp", D["mifT"][:, t0:t0 + TS], mo_[:, :], reads=[mo_])
        for gi_, (c0, nm) in enumerate(((1224, "mvT"), (1736, "soT"))):
            slot = loadA(W, c0, c0 + 512)
            for s in range(2):
                t0 = tok0 + s * TS
                for c in range(4):
                    ps = C.psum()
                    mm(ps[:, :], lambda kc, c=c: slot[:, kc, c * 128:(c + 1) * 128], lambda kc, s=s: hn[s][kc][:, :], 8, [slot] + hn[s], ps)
                    ob = C.tmp("fo", [128, TS], BF16, 3)
                    if gi_ == 0 and c % 2 == 0:
                        op("dve", lambda e, ps=ps, ob=ob: e.tensor_copy(ob[:, :], ps[:, :]), reads=[ps], writes=[ob])
                    else:
                        fn_ = AF.Copy if gi_ == 0 else AF.Sigmoid
                        op("act", lambda e, ps=ps, ob=ob, fn_=fn_: e.activation(ob[:, :], ps[:, :], fn_), reads=[ps], writes=[ob])
                    P.dma("sp", O(nm, c * 128, (c + 1) * 128, t0), ob[:, :], reads=[ob])
        def sview(slot, nk, n):
            return slot.h[:, :, :].rearrange("p k n -> p (k n)")[:, 0:nk * n].rearrange("p (k n) -> p k n", n=n)
        wq_t = slotA(); wqs_t = slotA(); wkv_t = slotA()
        wq = TV(wq_t, sview(wq_t, 3, 768)); wqs = TV(wqs_t, sview(wqs_t, 3, 768)); wkv = TV(wkv_t, sview(wkv_t, 2, 1024))
        P.dma("pool", wq[:, :, :], D["w_q_up"].rearrange("(kc p) n -> p kc n", p=128), writes=[wq])
        P.dma("pool", wqs[:, :, :], D["w_q_up_sw"].rearrange("(kc p) n -> p kc n", p=128), writes=[wqs])
        P.dma("pool", wkv[:, :, :], D["w_kv_up"].rearrange("(kc p) n -> p kc n", p=128), writes=[wkv])
        for hd in range(8):
            for s in range(2):
                t0 = tok0 + s * TS
                cosT, sinT = tabs[s]
                pq = C.psum(); pqs = C.psum()
                mm(pq[0:96, :], lambda kc, hd=hd: wq[:, kc, hd * 96:(hd + 1) * 96], lambda kc, s=s: qln[s][kc][:, :], 3, [wq] + qln[s], pq)
                mm(pqs[0:96, :], lambda kc, hd=hd: wqs[:, kc, hd * 96:(hd + 1) * 96], lambda kc, s=s: qln[s][kc][:, :], 3, [wqs] + qln[s], pqs)
                qr = C.tmp("qr", [96, 512], F32)
                raw_copy(pq, 96, qr)
                sq = C.tmp("sq", [128, 512], BF16, 3)
                op("act", lambda e, sq=sq, qr=qr: e.activation(sq[0:96, :], qr[:, :], AF.Square), reads=[qr], writes=[sq])
                pn = C.psum()
                op("pe", lambda e, sq=sq, pn=pn: e.matmul(pn[0:96, :], b96[:, :], sq[0:96, :], start=True, stop=True), reads=[sq, b96], writes=[pn])
                rp = rstd_from_ps(pn, 96, TS, cst[0:96, 352:353], "qh")
                qn = C.tmp("qn", [96, 512], F32); sw = C.tmp("qsw", [96, 512], F32)
                op("dve", lambda e, qn=qn, qr=qr, rp=rp: e.scalar_tensor_tensor(qn[:, :], qr[:, :], gsm[0:96, 0:1], rp[0:96, :], op0=ALU.mult, op1=ALU.mult), reads=[qr, gsm, rp], writes=[qn])
                op("dve", lambda e, sw=sw, pqs=pqs, rp=rp: e.scalar_tensor_tensor(sw[64:96, :], pqs[64:96, :], gsm[64:96, 1:2], rp[64:96, :], op0=ALU.mult, op1=ALU.mult), reads=[pqs, gsm, rp], writes=[sw])
                op("dve", lambda e, qn=qn, cosT=cosT: e.tensor_tensor(qn[64:96, :], qn[64:96, :], cosT[64:96, :], op=ALU.mult), reads=[qn, cosT], writes=[qn])
                op("dve", lambda e, sw=sw, sinT=sinT: e.tensor_tensor(sw[64:96, :], sw[64:96, :], sinT[64:96, :], op=ALU.mult), reads=[sw, sinT], writes=[sw])
                op("dve", lambda e, qn=qn, sw=sw: e.tensor_tensor(qn[64:96, :], qn[64:96, :], sw[64:96, :], op=ALU.add), reads=[qn, sw], writes=[qn])
                ob = C.tmp("fo", [128, TS], BF16, 3)
                op("act", lambda e, ob=ob, qn=qn: e.mul(ob[0:96, :], qn[:, :], 96.0 ** -0.5), reads=[qn], writes=[ob])
                P.dma("sp", (O("qTf", hd * 96, (hd + 1) * 96, t0) if fz else D["qT"][hd, :, t0:t0 + TS]), ob[0:96, :], reads=[ob])
        for s in range(2):
            t0 = tok0 + s * TS
            for hp in range(4):
                pk = C.psum()
                mm(pk[:, :], lambda kc, hp=hp: wkv[:, kc, hp * 128:(hp + 1) * 128], lambda kc, s=s: kvn[s][kc][:, :], 2, [wkv] + kvn[s], pk)
                kr = C.tmp("kr", [128, 512], F32)
                raw_copy(pk, 128, kr)
                sq = C.tmp("sq", [128, 512], BF16, 3)
                op("act", lambda e, sq=sq, kr=kr: e.activation(sq[:, :], kr[:, :], AF.Square), reads=[kr], writes=[sq])
                pn = C.psum()
                op("pe", lambda e, sq=sq, pn=pn: e.matmul(pn[:, :], bd64[:, :], sq[:, :], start=True, stop=True), reads=[sq, bd64], writes=[pn])
                rp = rstd_from_ps(pn, 128, TS, 1.0 / 64.0, "kh")
                ob = C.tmp("fo", [128, TS], BF16, 3)
                op("dve", lambda e, ob=ob, kr=kr, rp=rp: e.scalar_tensor_tensor(ob[:, :], kr[:, :], gsm[:, 2:3], rp[:, :], op0=ALU.mult, op1=ALU.mult), reads=[kr, gsm, rp], writes=[ob])
                P.dma("sp", O("knT", hp * 128, (hp + 1) * 128, t0), ob[:, :], reads=[ob])
            for hp in range(4):
                pv = C.psum()
                mm(pv[:, :], lambda kc, hp=hp: wkv[:, kc, 512 + hp * 128:512 + (hp + 1) * 128], lambda kc, s=s: kvn[s][kc][:, :], 2, [wkv] + kvn[s], pv)
                ob = C.tmp("fo", [128, TS], BF16, 3)
                op("act", lambda e, pv=pv, ob=ob: e.activation(ob[:, :], pv[:, :], AF.Copy), reads=[pv], writes=[ob])
                P.dma("sp", O("vT", hp * 128, (hp + 1) * 128, t0), ob[:, :], reads=[ob])

    for sti in range(NTOK // (2 * TS)):
        tok0 = sti * 2 * TS
        for s in range(2):
            for c in range(8):
                P.dma("sp", h[s][c][:, :], D["hT"][c * 128:(c + 1) * 128, tok0 + s * TS:tok0 + (s + 1) * TS], writes=[h[s][c]])
        if mode == "A":
            even_mixer(sti, tok0)
            ffn(8)
            ple(16, tok0)
            for s in range(2):
                for c in range(8):
                    P.dma("sp", D["h1T"][c * 128:(c + 1) * 128, tok0 + s * TS:tok0 + (s + 1) * TS], h[s][c][:, :], reads=[h[s][c]])
            odd_front(tok0)
            if fz and sti == 0 and "mid_hook" in fz:
                fz["mid_hook"](P)
        else:
            if fz:
                ysrc = [[TV(ystg[c], ystg[c].h[:, tok0 + s * TS:tok0 + (s + 1) * TS]) for c in range(8)] for s in range(2)]
            else:
                ysrc = y
                for s in range(2):
                    for c in range(8):
                        P.dma("pool", y[s][c][:, :], D["ymT"][c * 128:(c + 1) * 128, tok0 + s * TS:tok0 + (s + 1) * TS], writes=[y[s][c]])
            resid_proj(D["w_out"], ysrc)
            ffn(8)
            ple(16, tok0)
            for s in range(2):
                for c in range(8):
                    P.dma("sp", D["outT"][c * 128:(c + 1) * 128, tok0 + s * TS:tok0 + (s + 1) * TS], h[s][c][:, :], reads=[h[s][c]])
    P.wait_all("sp", P.tiles)
    if fz:
        return P
    P.emit()
    return nc


def _c(a):
    return np.ascontiguousarray(a)


def _chunkT(v):
    return _c(v.reshape(-1, 128).T)


def _consts():
    c = np.zeros((128, 512), np.float32)
    s = np.arange(128)
    c[:, 0:128] = (s[:, None] <= s[None, :]).astype(np.float32)
    k = np.arange(96)
    c[0:96, 128:224] = ((k[:, None] < 64) == (k[None, :] < 64)).astype(np.float32)
    c[:, 224:352] = ((s[:, None] // 64) == (s[None, :] // 64)).astype(np.float32)
    c[0:64, 352] = 1.0 / 64.0
    c[64:96, 352] = 1.0 / 32.0
    inv_freq = (10000.0 ** (-np.arange(0, 32, 2, dtype=np.float32) / np.float32(32))).astype(np.float32)
    c[0:96, 353] = inv_freq[np.arange(96) % 16]
    c[0:96, 354] = np.where((np.arange(96) % 32) < 16, -1.0, 1.0)
    return c


_SW = (np.arange(32) + 16) % 32


def prep_tok_common(inp, L, core):
    b, q = core // 4, core % 4
    s0 = q * NTOK
    m = {
        "pT": _c(inp["p"][L, b, s0:s0 + NTOK, :].T),
        "consts": _consts(),
        "w_gate": _c(inp["w_gate"][L]), "w_up": _c(inp["w_up"][L]), "w_down": _c(inp["w_down"][L]),
        "w_ple_gate": _c(inp["w_ple_gate"][L]), "w_ple_proj": _c(inp["w_ple_proj"][L]),
    }
    return m


def prep_A(inp):
    maps = []
    x = inp["x"]
    ev_w_in = inp["ev_w_in"][0]
    cols = []
    for cc in range(4):
        for base in (0, 512, 1024):
            cols.append(np.arange(base + cc * 128, base + (cc + 1) * 128))
    cols.append(np.arange(1536, 2560))
    ev_w_in_r = _c(ev_w_in[:, np.concatenate(cols)])
    od_w_in = inp["od_w_in"][0]
    ocols = np.concatenate([np.arange(0, 672), 640 + _SW, np.arange(672, 928), np.arange(928, 1184), np.arange(2208, 2216),
                            np.arange(1184, 1696), np.arange(1696, 2208)])
    od_ext = _c(od_w_in[:, ocols])
    wq = inp["od_w_q_up"][0]
    qcols = np.arange(768).reshape(8, 96).copy()
    qcols[:, 64:96] = qcols[:, 64:96][:, _SW]
    wq_sw = _c(wq[:, qcols.reshape(-1)])
    wkv = inp["od_w_kv_up"][0].reshape(256, 8, 128)
    wkv_r = _c(np.concatenate([wkv[:, :, :64].reshape(256, 512), wkv[:, :, 64:].reshape(256, 512)], axis=1))
    gq, gk = inp["od_g_q"][0], inp["od_g_k"][0]
    gsm = np.zeros((128, 8), np.float32)
    gsm[0:96, 0] = gq
    gsm[64:96, 1] = gq[64:96][_SW]
    gsm[0:64, 2] = gk[:64]; gsm[64:128, 2] = gk[:64]
    gsm[0:32, 4] = gk[64:96]; gsm[0:32, 5] = gk[64:96][_SW]
    g_lat = np.concatenate([_chunkT(inp["od_g_qa"][0]), _chunkT(inp["od_g_kva"][0])], axis=1)
    gains = np.zeros((128, 32), np.float32)
    gains[:, 0:8] = _chunkT(inp["g_mix"][0]); gains[:, 8:16] = _chunkT(inp["g_ffn"][0])
    gains[:, 16:24] = _chunkT(inp["g_ple"][0]); gains[:, 24:32] = _chunkT(inp["g_mix"][1])
    wconv = _c(inp["ev_w_conv"][0].T.reshape(4, 128, 3).transpose(1, 0, 2).reshape(128, 12))
    gvb = _c(np.tile(inp["ev_g_v"][0].reshape(1, 512), (128, 1)))
    wsT = _c(inp["ev_w_s"][0].transpose(2, 0, 1))
    bsb = _c(inp["ev_b_s"][0].reshape(4, 2, 1, 128).repeat(64, axis=2).reshape(4, 128, 128).transpose(1, 0, 2))
    for core in range(8):
        b, q = core // 4, core % 4
        s0 = q * NTOK
        m = prep_tok_common(inp, 0, core)
        m["hT"] = _c(x[b, s0:s0 + NTOK, :].T)
        m["xhT"] = _c(x[b, s0 - 2:s0, :].T) if q > 0 else np.zeros((1024, 2), np.float32)
        m["posb"] = _c(np.tile(inp["positions"][b, s0:s0 + NTOK].reshape(1, NTOK), (96, 1)).astype(np.int32))
        m.update({"gains": gains, "w_out": _c(inp["ev_w_out"][0]), "ev_w_in": ev_w_in_r, "wconv": wconv, "gvb": gvb, "wsT": wsT,
                  "bsb": bsb, "od_w_in": od_ext, "g_lat": _c(g_lat), "w_q_up": _c(wq), "w_q_up_sw": wq_sw, "w_kv_up": wkv_r, "gsm": gsm})
        maps.append(m)
    return maps


def prep_C(inp, h1T, ymT):
    maps = []
    gains = np.zeros((128, 32), np.float32)
    gains[:, 8:16] = _chunkT(inp["g_ffn"][1]); gains[:, 16:24] = _chunkT(inp["g_ple"][1])
    for core in range(8):
        m = prep_tok_common(inp, 1, core)
        m["hT"] = h1T[core]
        m["ymT"] = ymT[core]
        m["gains"] = gains
        m["w_out"] = _c(inp["od_w_out"][0])
        maps.append(m)
    return maps


SEQ = 8192


def build_mix(parts, fz):
    nc = fz["nc"]
    P = Prog(nc, ("M_" if "m" in parts else "T_"))
    C = Ctx(P, nbanks=6)
    op = P.op
    D = fz["D"]
    ptb = [P.ps("ptb%d" % i, [128, 1024], BF16) for i in range(2)]
    idx = P.sb("idx", [128, 64], I32)
    P.dma("sp", idx[:, :], D["idx"], writes=[idx])
    R1 = D["recv1"]

    def gat(t, rows, q, col):
        for hf in range(2):
            c0_ = q * NTOK + hf * 1024
            P.gather(t, t[0:rows, c0_:c0_ + 1024], R1[hf][:, :], idx, idx[0:rows, col:col + 1])
    send2 = D["send2"]

    cB = P.sb("cB", [128, 512])
    bcol = P.sb("bcol", [128, 2]); gmh = P.sb("gmh", [128, 128])
    P.dma("sp", cB[:, :], D["cB"], writes=[cB]); P.dma("sp", bcol[:, :], D["bcol"], writes=[bcol]); P.dma("sp", gmh[:, :], D["gmh"], writes=[gmh])
    maskb = P.sb("maskb", [128, 128], BF16)
    op("dve", lambda e: e.tensor_copy(maskb[:, :], cB[:, 0:128]), reads=[cB], writes=[maskb])
    idb = P.sb("idb", [128, 128], BF16)
    op("dve", lambda e: e.tensor_copy(idb[:, :], cB[:, 256:384]), reads=[cB], writes=[idb])
    ones_f = P.sb("ones_f", [128, 128])
    op("pool", lambda e: e.memset(ones_f[:, :], 1.0), writes=[ones_f])

    if True:
        pass
    if "s" in parts:
        gi = P.sb("gi", [128, 64]); gf = P.sb("gf", [128, 64])
        rmv = D["recvm"].rearrange("r (c t) -> (r c) t", t=64)
        P.gather(gi, gi[:, :], rmv, idx, idx[:, 40:41])
        P.gather(gf, gf[:, :], rmv, idx, idx[:, 41:42])
        zer = P.sb("zer", [128, 128]); op("pool", lambda e: e.memset(zer[:, :], 0.0), writes=[zer])
        nbf = P.sb("nbf", [128, 1])
        op("dve", lambda e: e.tensor_scalar(nbf[:, :], bcol[:, 1:2], -1.0, None, op0=ALU.mult), reads=[bcol], writes=[nbf])
        e1 = P.sb("e1", [128, 64]); lf = P.sb("lf", [128, 64]); Al = P.sb("Al", [128, 64]); vv = P.sb("vv", [128, 64])
        Ml = P.sb("Ml", [128, 64]); Mp = P.sb("Mp", [128, 64])
        op("act", lambda e: e.activation(e1[:, :], gf[:, :], AF.Exp, bias=nbf[:, 0:1], scale=-1.0), reads=[gf, nbf], writes=[e1])
        op("act", lambda e: e.activation(lf[:, :], e1[:, :], AF.Ln, bias=1.0), reads=[e1], writes=[lf])
        op("dve", lambda e: e.tensor_scalar(lf[:, :], lf[:, :], -1.0, None, op0=ALU.mult), reads=[lf], writes=[lf])
        op("dve", lambda e: e.tensor_tensor_scan(Al[:, :], lf[:, :], zer[:, 0:64], 0.0, op0=ALU.add, op1=ALU.add), reads=[lf, zer], writes=[Al])

        def col2row(col_ap, reads):
            ps = C.psum()
            op("pe", lambda e: e.matmul(ps[0:1, 0:128], col_ap, cB[:, 256:384], start=True, stop=True), reads=reads + [cB], writes=[ps])
            return ps

        rows = P.sb("rows", [1, 8, 128])
        ps = col2row(Al[:, 63:64], [Al])
        op("dve", lambda e, ps=ps: e.tensor_copy(rows[:, 0, :], ps[0:1, 0:128]), reads=[ps], writes=[rows])
        op("dve", lambda e: e.tensor_tensor_scan(rows[:, 1, :], rows[:, 0, :], zer[0:1, :], 0.0, op0=ALU.add, op1=ALU.add), reads=[rows, zer], writes=[rows])
        op("dve", lambda e: e.tensor_tensor(rows[:, 2, :], rows[:, 1, :], rows[:, 0, :], op=ALU.subtract), reads=[rows], writes=[rows])
        cols_ps = C.psum(pin=True)

        def row2col(k, j):
            op("pe", lambda e: e.matmul(cols_ps[:, j:j + 1], rows[0:1, k, :], ones_f[0:1, 0:1], start=True, stop=True), reads=[rows, ones_f], writes=[cols_ps])

        row2col(2, 0)
        cols = P.sb("cols", [128, 8])
        op("dve", lambda e: e.tensor_copy(cols[:, 0:1], cols_ps[:, 0:1]), reads=[cols_ps], writes=[cols])
        op("dve", lambda e: e.tensor_scalar(Al[:, :], Al[:, :], cols[:, 0:1], None, op0=ALU.add), reads=[Al, cols], writes=[Al])
        op("dve", lambda e: e.scalar_tensor_tensor(vv[:, :], gi[:, :], bcol[:, 0:1], Al[:, :], op0=ALU.add, op1=ALU.subtract), reads=[gi, bcol, Al], writes=[vv])
        op("dve", lambda e: e.tensor_tensor_scan(Ml[:, :], vv[:, :], vv[:, :], -1e30, op0=ALU.max, op1=ALU.max), reads=[vv], writes=[Ml])
        ps = col2row(Ml[:, 63:64], [Ml])
        op("dve", lambda e, ps=ps: e.tensor_copy(rows[:, 3, :], ps[0:1, 0:128]), reads=[ps], writes=[rows])
        op("dve", lambda e: e.tensor_tensor_scan(rows[:, 4, :], rows[:, 3, :], rows[:, 3, :], 0.0, op0=ALU.max, op1=ALU.max), reads=[rows], writes=[rows])
        op("dve", lambda e: e.memset(rows[:, 5, 0:1], 0.0), reads=[], writes=[rows])
        op("dve", lambda e: e.tensor_copy(rows[:, 5, 1:128], rows[:, 4, 0:127]), reads=[rows], writes=[rows])
        r4 = rows[:, 4, :].rearrange("p (c two) -> p c two", two=2)
        r6 = rows[:, 6, :].rearrange("p (c two) -> p c two", two=2)
        op("dve", lambda e: e.tensor_copy(r6[:, :, 0:1], r4[:, :, 1:2]), reads=[rows], writes=[rows])
        op("dve", lambda e: e.tensor_copy(r6[:, :, 1:2], r4[:, :, 1:2]), reads=[rows], writes=[rows])
        op("dve", lambda e: e.tensor_tensor(rows[:, 7, :], rows[:, 5, :], rows[:, 4, :], op=ALU.subtract), reads=[rows], writes=[rows])
        op("act", lambda e: e.activation(rows[:, 7, :], rows[:, 7, :], AF.Exp), reads=[rows], writes=[rows])
        row2col(5, 1); row2col(4, 2); row2col(6, 3)
        op("dve", lambda e: e.tensor_copy(cols[:, 1:4], cols_ps[:, 1:4]), reads=[cols_ps], writes=[cols])
        C.unpin()
        op("dve", lambda e: e.tensor_scalar(Mp[:, :], Ml[:, :], cols[:, 1:2], 0.0, op0=ALU.max, op1=ALU.max), reads=[Ml, cols], writes=[Mp])
        dec_b = P.sb("dec_b", [64, 128])
        ps = C.psum()
        op("pe", lambda e, ps=ps: e.matmul(ps[0:64, 0:128], ones_f[0:1, 0:64], rows[0:1, 7, :], start=True, stop=True), reads=[ones_f, rows], writes=[ps])
        op("dve", lambda e, ps=ps: e.tensor_copy(dec_b[:, :], ps[0:64, 0:128]), reads=[ps], writes=[dec_b])
        fac = P.sb("fac", [128, 5, 128])
        tmpc = P.sb("tmpc", [128, 64])
        op("dve", lambda e: e.tensor_scalar(tmpc[:, :], vv[:, :], cols[:, 3:4], None, op0=ALU.subtract), reads=[vv, cols], writes=[tmpc])
        op("act", lambda e: e.activation(fac[:, 0, 0:64], tmpc[:, :], AF.Exp), reads=[tmpc], writes=[fac])
        op("dve", lambda e: e.tensor_scalar(tmpc[:, :], Mp[:, :], cols[:, 3:4], None, op0=ALU.subtract), reads=[Mp, cols], writes=[tmpc])
        op("act", lambda e: e.activation(fac[:, 1, 0:64], tmpc[:, :], AF.Exp, scale=-1.0), reads=[tmpc], writes=[fac])
        op("dve", lambda e: e.tensor_scalar(tmpc[:, :], Mp[:, :], cols[:, 1:2], None, op0=ALU.subtract), reads=[Mp, cols], writes=[tmpc])
        op("act", lambda e: e.activation(fac[:, 2, 0:64], tmpc[:, :], AF.Exp, scale=-1.0), reads=[tmpc], writes=[fac])
        op("dve", lambda e: e.tensor_scalar(tmpc[:, :], vv[:, :], cols[:, 2:3], None, op0=ALU.subtract), reads=[vv, cols], writes=[tmpc])
        op("act", lambda e: e.activation(fac[:, 3, 0:64], tmpc[:, :], AF.Exp), reads=[tmpc], writes=[fac])
        op("dve", lambda e: e.tensor_tensor(tmpc[:, :], Al[:, :], Mp[:, :], op=ALU.add), reads=[Al, Mp], writes=[tmpc])
        op("act", lambda e: e.activation(fac[:, 4, 0:64], tmpc[:, :], AF.Exp, scale=-1.0), reads=[tmpc], writes=[fac])
        op("dve", lambda e: e.tensor_copy(fac[:, :, 64:128], fac[:, :, 0:64]), reads=[fac], writes=[fac])
        fcol = P.sb("fcol", [128, 5, 64])
        for k in range(5):
            ps = C.psum()
            op("pe", lambda e, k=k, ps=ps: e.transpose(ps[:, 0:128], fac[:, k, :], cB[:, 256:384]), reads=[fac, cB], writes=[ps])
            pv = ps[:, 0:128].rearrange("p (c two) -> p c two", two=2)
            op("dve", lambda e, k=k, ps=ps: e.tensor_copy(fcol[0:64, k, :], ps[0:64, 0:128].rearrange("p (c two) -> p c two", two=2)[:, :, 0]), reads=[ps], writes=[fcol])
            op("dve", lambda e, k=k, ps=ps: e.tensor_copy(fcol[64:128, k, :], ps[64:128, 0:128].rearrange("p (c two) -> p c two", two=2)[:, :, 1]), reads=[ps], writes=[fcol])

    if "m" in parts:
        mq = P.sb("mq", [64, SEQ], BF16); mk = P.sb("mk", [64, SEQ], BF16)
        ktok = P.sb("ktok", [128, 64, 64], BF16); vext = P.sb("vext", [128, 64, 129], BF16); so = P.sb("so", [128, 64, 128], BF16)
        Call = P.sb("Call", [64, 128, 129], BF16); Sxs = [P.sb("Sx%d" % i, [64, 129]) for i in range(2)]
        mvs = P.sb("mvs", [128, SEQ], BF16); sos = P.sb("sos", [128, SEQ], BF16)
        for q in range(4):
            cs_ = slice(q * NTOK, (q + 1) * NTOK)
            gat(mq, 64, q, 24 + q); gat(mk, 64, q, 28 + q); gat(mvs, 128, q, 32 + q); gat(sos, 128, q, 36 + q)
        op("pool", lambda e: e.memset(vext[:, :, :], 1.0), writes=[vext])
        tcount = [0]
        def tr_group(src, rows, nblk_per, width, dst_fn, g):
            pt_ = ptb[tcount[0] % 2]; tcount[0] += 1
            for b_ in range(nblk_per):
                blk = g * nblk_per + b_
                op("pe", lambda e, pt_=pt_, b_=b_, blk=blk: e.transpose(pt_[:, b_ * width:(b_ + 1) * width], src[0:rows, blk * 128:(blk + 1) * 128], idb[0:rows, 0:rows]),
                   reads=[src, idb], writes=[pt_])
            dst_t, dst_ap = dst_fn(g)
            eng = "act" if tcount[0] % 2 == 0 else "dve"
            if eng == "act":
                op("act", lambda e, pt_=pt_, dst_ap=dst_ap: e.activation(dst_ap, pt_[:, 0:nblk_per * width].rearrange("p (b w) -> p b w", w=width), AF.Copy), reads=[pt_], writes=[dst_t])
            else:
                op("dve", lambda e, pt_=pt_, dst_ap=dst_ap: e.tensor_copy(dst_ap, pt_[:, 0:nblk_per * width].rearrange("p (b w) -> p b w", w=width)), reads=[pt_], writes=[dst_t])
        for g in range(4):
            tr_group(mk, 64, 16, 64, lambda g: (ktok, ktok[:, g * 16:(g + 1) * 16, :]), g)
        for g in range(8):
            tr_group(mvs, 128, 8, 128, lambda g: (vext, vext[:, g * 8:(g + 1) * 8, 0:128]), g)
            tr_group(sos, 128, 8, 128, lambda g: (so, so[:, g * 8:(g + 1) * 8, :]), g)
        op("pool", lambda e: e.memset(Sxs[0][:, :], 0.0), writes=[Sxs[0]])
        for blk in range(64):
            kw = C.tmp("kw", [128, 64], BF16, 3)
            op("dve", lambda e, blk=blk, kw=kw: e.tensor_scalar(kw[:, :], ktok[:, blk, :], fcol[:, 3, blk:blk + 1], None, op0=ALU.mult), reads=[ktok, fcol], writes=[kw])
            for half in range(2):
                c = 2 * blk + half
                Sa = Sxs[c % 2]; Sb = Sxs[(c + 1) % 2]
                op("act", lambda e, c=c, Sa=Sa: e.activation(Call[:, c, :], Sa[:, :], AF.Copy), reads=[Sa], writes=[Call])
                pu = C.psum()
                op("pe", lambda e, pu=pu, kw=kw, half=half, blk=blk: e.matmul(pu[0:64, 0:129], kw[64 * half:64 * half + 64, :], vext[64 * half:64 * half + 64, blk, :], start=True, stop=True),
                   reads=[kw, vext], writes=[pu])
                op("dve", lambda e, pu=pu, c=c, Sa=Sa, Sb=Sb: e.scalar_tensor_tensor(Sb[:, :], Sa[:, :], dec_b[:, c:c + 1], pu[0:64, 0:129], op0=ALU.mult, op1=ALU.add), reads=[Sa, dec_b, pu], writes=[Sb])
        if "2" in parts:
            def stX(blk):
                t0 = blk * 128
                pS = C.psum()
                op("pe", lambda e, pS=pS, t0=t0: e.matmul(pS[:, 0:128], mk[:, t0:t0 + 128], mq[:, t0:t0 + 128], start=True, stop=True), reads=[mk, mq], writes=[pS])
                pt = C.tmp("mpt", [128, 128], BF16, 3)
                op("dve", lambda e, pS=pS, pt=pt, blk=blk: e.scalar_tensor_tensor(pt[:, :], pS[:, 0:128], fcol[:, 0, blk:blk + 1], cB[:, 128:256], op0=ALU.mult, op1=ALU.mult), reads=[pS, fcol, cB], writes=[pt])
                return pt

            def stY(blk, pt):
                t0 = blk * 128
                po1 = C.psum(); po2 = C.psum()
                op("pe", lambda e, po1=po1, pt=pt, blk=blk: e.matmul(po1[:, 0:129], pt[:, :], vext[:, blk, :], start=True, stop=True), reads=[pt, vext], writes=[po1])
                for half in range(2):
                    op("pe", lambda e, po2=po2, half=half, blk=blk, t0=t0: e.matmul(po2[64 * half:64 * half + 64, 0:129], mq[:, t0 + 64 * half:t0 + 64 * half + 64], Call[:, 2 * blk + half, :], start=True, stop=True),
                       reads=[mq, Call], writes=[po2])
                o1 = C.tmp("o1", [128, 129], F32, 2); hnum = C.tmp("hnum", [128, 129], F32, 2)
                op("dve", lambda e, o1=o1, po1=po1, blk=blk: e.tensor_scalar(o1[:, :], po1[:, 0:129], fcol[:, 1, blk:blk + 1], None, op0=ALU.mult), reads=[po1, fcol], writes=[o1])
                op("dve", lambda e, o1=o1, po2=po2, hnum=hnum, blk=blk: e.scalar_tensor_tensor(hnum[:, :], po2[:, 0:129], fcol[:, 2, blk:blk + 1], o1[:, :], op0=ALU.mult, op1=ALU.add), reads=[po2, fcol, o1], writes=[hnum])
                sm = C.tmp("sm", [128, 8], F32, 2)
                op("dve", lambda e, sm=sm, hnum=hnum: e.scalar_tensor_tensor(sm[:, 6:7], hnum[:, 128:129], -1.0, hnum[:, 128:129], op0=ALU.mult, op1=ALU.max), reads=[hnum], writes=[sm])
                op("dve", lambda e, sm=sm, blk=blk: e.tensor_tensor(sm[:, 0:1], sm[:, 6:7], fcol[:, 4, blk:blk + 1], op=ALU.max), reads=[sm, fcol], writes=[sm])
                op("dve", lambda e, sm=sm: e.reciprocal(sm[:, 1:2], sm[:, 0:1]), reads=[sm], writes=[sm])
                junk = C.tmp("junk", [128, 128], F32, 2)
                op("dve", lambda e, hnum=hnum, sm=sm: e.tensor_scalar(hnum[:, 0:128], hnum[:, 0:128], sm[:, 1:2], None, op0=ALU.mult), reads=[hnum, sm], writes=[hnum])
                op("act", lambda e, junk=junk, hnum=hnum, sm=sm: e.activation(junk[:, :], hnum[:, 0:128], AF.Square, accum_out=sm[:, 2:3]), reads=[hnum], writes=[junk, sm])
                op("act", lambda e, sm=sm: e.activation(sm[:, 3:4], sm[:, 2:3], AF.Sqrt, bias=EPS, scale=1.0 / 128.0), reads=[sm], writes=[sm])
                op("dve", lambda e, sm=sm: e.reciprocal(sm[:, 4:5], sm[:, 3:4]), reads=[sm], writes=[sm])
                yt = C.tmp("yt", [128, 128], F32, 2); yd = C.tmp("yd", [128, 128], F32, 3)
                op("dve", lambda e, yt=yt, hnum=hnum, sm=sm: e.scalar_tensor_tensor(yt[:, :], hnum[:, 0:128], sm[:, 4:5], gmh[:, :], op0=ALU.mult, op1=ALU.mult), reads=[hnum, sm, gmh], writes=[yt])
                op("pool", lambda e, yt=yt, yd=yd, blk=blk: e.tensor_tensor(yd[:, :], yt[:, :], so[:, blk, :], op=ALU.mult), reads=[yt, so], writes=[yd])
                return yd

            zst = {"pT": None}

            def stZ(blk, yd):
                g4, j4 = blk // 4, blk % 4
                if j4 == 0:
                    zst["pT"] = C.psum(pin=True)
                pT = zst["pT"]
                op("pe", lambda e, pT=pT, yd=yd, j4=j4: e.transpose(pT[:, j4 * 128:(j4 + 1) * 128], yd[:, :], cB[:, 256:384]), reads=[yd, cB], writes=[pT])
                if j4 == 3:
                    ob = C.tmp("ydo", [128, 512], BF16, 2)
                    op("act", lambda e, ob=ob, pT=pT: e.activation(ob[:, :], pT[:, :], AF.Copy), reads=[pT], writes=[ob])
                    P.dma("sp", send2[(g4 // 4) * 256 + 128:(g4 // 4) * 256 + 256, (g4 % 4) * 512:(g4 % 4) * 512 + 512], ob[:, :], reads=[ob])
                    C.unpin()

            pts = {0: stX(0)}
            yds = {}
            for blk in range(64):
                if blk + 1 < 64:
                    pts[blk + 1] = stX(blk + 1)
                yds[blk] = stY(blk, pts.pop(blk))
                if blk >= 1:
                    stZ(blk - 1, yds.pop(blk - 1))
            stZ(63, yds.pop(63))

    if "a" in parts:
        qTs = [P.sb("qT%d" % i, [96, SEQ], BF16) for i in range(2)]; kTs = [P.sb("kT%d" % i, [96, SEQ], BF16) for i in range(2)]
        V = P.sb("V", [128, 64, 65], BF16); vTss = [P.sb("vTs%d" % i, [64, SEQ], BF16) for i in range(2)]
        for hh in range(2):
            for q in range(4):
                cs_ = slice(q * NTOK, (q + 1) * NTOK)
                gat(qTs[hh], 96, q, hh * 4 + q); gat(kTs[hh], 96, q, 8 + hh * 4 + q); gat(vTss[hh], 64, q, 16 + hh * 4 + q)
        tails = []
        for hh in range(2):
            qT = qTs[hh]; kT = kTs[hh]; vTs = vTss[hh]
            op("pool", lambda e: e.memset(V[:, :, :], 1.0), writes=[V])
            for g in range(4):
                pt_ = ptb[g % 2]
                for b_ in range(16):
                    blk = g * 16 + b_
                    op("pe", lambda e, pt_=pt_, b_=b_, blk=blk, vTs=vTs: e.transpose(pt_[:, b_ * 64:(b_ + 1) * 64], vTs[0:64, blk * 128:(blk + 1) * 128], idb[0:64, 0:64]), reads=[vTs, idb], writes=[pt_])
                op("dve", lambda e, pt_=pt_, g=g: e.tensor_copy(V[:, g * 16:(g + 1) * 16, 0:64], pt_[:, :].rearrange("p (b w) -> p b w", w=64)), reads=[pt_], writes=[V])
            for qt in range(16):
                po = C.psum(pin=True)
                nkb = 4 * qt + 4
                LOOK = 3
                pend = []
                for it in range(nkb + LOOK):
                    if it < nkb:
                        kb = it
                        c0 = max(0, kb - 4 * qt) * 128
                        ps = C.psum()
                        op("pe", lambda e, ps=ps, kb=kb, c0=c0, qt=qt, kT=kT, qT=qT: e.matmul(ps[:, c0:512], kT[:, kb * 128:(kb + 1) * 128], qT[:, qt * 512 + c0:qt * 512 + 512], start=True, stop=True), reads=[kT, qT], writes=[ps])
                        pt = C.tmp("apt", [128, 512], BF16, 6)
                        op("act", lambda e, ps=ps, pt=pt, c0=c0: e.activation(pt[:, c0:512], ps[:, c0:512], AF.Exp), reads=[ps], writes=[pt])
                        if kb >= 4 * qt:
                            op("pool", lambda e, pt=pt, c0=c0: e.tensor_tensor(pt[:, c0:c0 + 128], pt[:, c0:c0 + 128], maskb[:, :], op=ALU.mult), reads=[pt, maskb], writes=[pt])
                        pend.append((kb, c0, pt))
                    if it == 2 and tails:
                        tails.pop(0)()
                    if it >= LOOK:
                        kb, c0, pt = pend[it - LOOK]
                        op("pe", lambda e, po=po, pt=pt, kb=kb, c0=c0, nkb=nkb: e.matmul(po[0:65, c0:512], V[:, kb, :], pt[:, c0:512], start=(kb == 0), stop=(kb == nkb - 1)), reads=[V, pt], writes=[po])
                osb = C.tmp("osb", [65, 512], F32, 2, own=True)
                op("act", lambda e, osb=osb, po=po: e.activation(osb[:, :], po[0:65, :], AF.Copy), reads=[po], writes=[osb])
                C.unpin()
                op("dve", lambda e, osb=osb: e.reciprocal(osb[64:65, :], osb[64:65, :]), reads=[osb], writes=[osb])

                def tail(osb=osb, hh=hh, qt=qt):
                    pb = C.psum()
                    op("pe", lambda e, pb=pb, osb=osb: e.matmul(pb[0:64, :], ones_f[64:65, 0:64], osb[64:65, :], start=True, stop=True), reads=[ones_f, osb], writes=[pb])
                    yc = C.tmp("yc", [64, 512], BF16, 2)
                    op("dve", lambda e, yc=yc, osb=osb, pb=pb: e.tensor_tensor(yc[:, :], osb[0:64, :], pb[0:64, :], op=ALU.mult), reads=[osb, pb], writes=[yc])
                    P.dma("sp", send2[(qt // 4) * 256 + hh * 64:(qt // 4) * 256 + hh * 64 + 64, (qt % 4) * 512:(qt % 4) * 512 + 512], yc[:, :], reads=[yc])
                tails.append(tail)
        while tails:
            tails.pop(0)()
    P.wait_all("sp", P.tiles)
    return P


R1ROWS = 3360
Q0, KN0, KP0, V0, MQ0, MK0, MV0, SO0 = 0, 768, 1280, 1312, 1824, 2080, 2336, 2848
GROUPS = [[0, 1, 2, 3], [4, 5, 6, 7]]
CCH = 240
CCH1 = 480


def _chunks(total, cch=CCH):
    out = []
    start = 0
    base = 0
    while start < total:
        size = min(cch, total - start)
        out.append((start, size, base))
        base += 4 * size
        start += size
    return out


def _rowmap(total, r, q, cch=CCH):
    k = r // cch
    start = k * cch
    size = np.minimum(cch, total - start)
    return 4 * start + q * size + (r - start)


def _gather_all(P, send, recv, total, cch=CCH, wait=True):
    keys = []
    for (start, size, base) in _chunks(total, cch):
        keys.append(P.collective_raw("AllGather", send[start:start + size, :], recv[base:base + 4 * size, :], GROUPS, wait=False))
    if wait:
        P.ops["pool"].append(([(k, 1) for k in keys], None, None))
    return keys


def build_fused(phases="AMTC"):
    nc = bass.Bass("TRN2", target_bir_lowering=False)
    send1 = [nc.dram_tensor("send1_%d" % i, [R1ROWS, 1024], BF16).ap() for i in range(2)]
    recv1 = [nc.dram_tensor("recv1_%d" % i, [4 * R1ROWS, 1024], BF16).ap() for i in range(2)]
    sendm = nc.dram_tensor("sendm", [8, NTOK], F32).ap()
    recvm = nc.dram_tensor("recvm", [32, NTOK], F32).ap()
    send2 = nc.dram_tensor("send2", [1024, NTOK], BF16).ap()
    recv2 = nc.dram_tensor("recv2", [4096, NTOK], BF16).ap()
    h1d = nc.dram_tensor("h1d", [1024, NTOK], F32).ap()
    idx = dram_in(nc, "idx", [128, 64], I32)
    consts = dram_in(nc, "consts", [128, 512])

    def tokD(L):
        sfx = str(L)
        return {
            "pT": dram_in(nc, "pT" + sfx, [256, NTOK]), "gains": dram_in(nc, "gains" + sfx, [128, 32]), "consts": consts,
            "w_gate": dram_in(nc, "w_gate" + sfx, [1024, 2816]), "w_up": dram_in(nc, "w_up" + sfx, [1024, 2816]),
            "w_down": dram_in(nc, "w_down" + sfx, [2816, 1024]), "w_ple_gate": dram_in(nc, "w_ple_gate" + sfx, [1024, 1024]),
            "w_ple_proj": dram_in(nc, "w_ple_proj" + sfx, [256, 1024]), "w_out": dram_in(nc, "w_out" + sfx, [1024, 1024]),
        }
    DA = tokD(0)
    DA.update({
        "hT": dram_in(nc, "xT", [1024, NTOK]), "h1T": h1d,
        "mifT": sendm,
    })
    for hf in range(2):
        for nm, r0, nr in (("qTf", Q0, 768), ("knT", KN0, 512), ("kpeT", KP0, 32), ("vT", V0, 512), ("mqT", MQ0, 256), ("mkT", MK0, 256),
                           ("mvT", MV0, 512), ("soT", SO0, 512)):
            DA["%s_h%d" % (nm, hf)] = send1[hf][r0:r0 + nr, :]
    for nm in ("qT", "knT", "kpeT", "vT", "mqT", "mkT", "mvT", "soT"):
        DA[nm] = send1[0]
    keys1 = []

    def mid_hook(P):
        P.wait_all("pool", P.tiles)
        keys1.extend(_gather_all(P, send1[0], recv1[0], R1ROWS, CCH1, wait=False))
    PA = build_tok("A", {"nc": nc, "D": DA, "mid_hook": mid_hook})
    PA.wait_all("pool", PA.tiles)
    keys1.extend(_gather_all(PA, send1[1], recv1[1], R1ROWS, CCH1, wait=False))
    PA.ops["pool"].append(([(k, 1) for k in keys1], None, None))
    PA.collective_raw("AllGather", sendm, recvm, GROUPS)
    PA.emit()
    nc.all_engine_barrier()
    if "M" not in phases and "T" not in phases and "C" not in phases:
        dram_out(nc, "outT", [1024, NTOK])
        return nc

    DB = {"idx": idx, "recv1": recv1, "recvm": recvm, "send2": send2,
          "cB": dram_in(nc, "cB", [128, 512]), "bcol": dram_in(nc, "bcol", [128, 2]), "gmh": dram_in(nc, "gmh", [128, 128])}
    if "M" in phases:
        PM = build_mix("sm2", {"nc": nc, "D": DB})
        PM.wait_all("pool", PM.tiles)
        PM.emit()
        nc.all_engine_barrier()
    if "T" in phases:
        PT = build_mix("a", {"nc": nc, "D": DB})
        _gather_all(PT, send2, recv2, 1024)
        PT.emit()
        nc.all_engine_barrier()
    if "C" not in phases:
        dram_out(nc, "outT", [1024, NTOK])
        return nc

    DC = tokD(1)
    DC.update({"hT": h1d, "idx": idx, "recv2": recv2, "outT": dram_out(nc, "outT", [1024, NTOK])})
    if "D" in phases:
        DC["dbg_y"] = dram_out(nc, "dbg_y", [1024, NTOK], BF16)
    PC = build_tok("C", {"nc": nc, "D": DC})
    PC.wait_all("pool", PC.tiles)
    if "D" in phases:
        d1 = dram_out(nc, "dbg_send1", [R1ROWS, NTOK], BF16); d2 = dram_out(nc, "dbg_send2", [1024, NTOK], BF16); d3 = dram_out(nc, "dbg_sendm", [8, NTOK]); d4 = dram_out(nc, "dbg_h1", [1024, NTOK])
        dummy = PC.sb("dbgdummy", [1, 8])
        PC.dma("sp", d1, send1, reads=[dummy]); PC.dma("sp", d2, send2, reads=[dummy]); PC.dma("sp", d3, sendm, reads=[dummy]); PC.dma("sp", d4, h1d, reads=[dummy])
        PC.wait_all("sp", [dummy])
    PC.emit()
    return nc


def prep_fused(inp):
    mapsA = prep_A(inp)
    maps = []
    cB = np.zeros((128, 512), np.float32)
    s = np.arange(128)
    cB[:, 0:128] = (s[:, None] <= s[None, :])
    cB[:, 128:256] = (s[:, None] <= s[None, :]) & ((s[:, None] // 64) == (s[None, :] // 64))
    cB[:, 256:384] = np.eye(128, dtype=np.float32)
    gains1 = np.zeros((128, 32), np.float32)
    gains1[:, 8:16] = _chunkT(inp["g_ffn"][1]); gains1[:, 16:24] = _chunkT(inp["g_ple"][1])
    bg = inp["od_b_gate"][0]
    p = np.arange(128)
    for core in range(8):
        b, j = core // 4, core % 4
        q_ = j
        a = mapsA[core]
        m = {"xT": a["hT"], "xhT": a["xhT"], "posb": a["posb"], "consts": a["consts"], "idx": None,
             "pT0": a["pT"], "gains0": a["gains"], "w_gate0": a["w_gate"], "w_up0": a["w_up"], "w_down0": a["w_down"],
             "w_ple_gate0": a["w_ple_gate"], "w_ple_proj0": a["w_ple_proj"], "w_out0": a["w_out"]}
        for k in ("ev_w_in", "wconv", "gvb", "wsT", "bsb", "od_w_in", "g_lat", "w_q_up", "w_q_up_sw", "w_kv_up", "gsm"):
            m[k] = a[k]
        c1 = prep_tok_common(inp, 1, core)
        m.update({"pT1": c1["pT"], "gains1": gains1, "w_gate1": c1["w_gate"], "w_up1": c1["w_up"], "w_down1": c1["w_down"],
                  "w_ple_gate1": c1["w_ple_gate"], "w_ple_proj1": c1["w_ple_proj"], "w_out1": _c(inp["od_w_out"][0])})
        m["cB"] = cB
        m["bcol"] = _c(np.tile(np.array([[bg[j], bg[4 + j]]], np.float32), (128, 1)))
        m["gmh"] = _c(np.tile(inp["od_g_mh"][0][j].reshape(1, 128), (128, 1)))
        ix = np.zeros((128, 64), np.int32)
        def rm(r, q):
            return _rowmap(R1ROWS, np.asarray(r), q, CCH1)
        for q in range(4):
            for hh in range(2):
                hd = 2 * j + hh
                ix[0:96, hh * 4 + q] = rm(Q0 + hd * 96 + p[0:96], q)
                ix[0:64, 8 + hh * 4 + q] = rm(KN0 + hd * 64 + p[0:64], q)
                ix[64:96, 8 + hh * 4 + q] = rm(KP0 + p[0:32], q)
                ix[0:64, 16 + hh * 4 + q] = rm(V0 + hd * 64 + p[0:64], q)
            ix[0:64, 24 + q] = rm(MQ0 + j * 64 + p[0:64], q)
            ix[0:64, 28 + q] = rm(MK0 + j * 64 + p[0:64], q)
            ix[:, 32 + q] = rm(MV0 + j * 128 + p, q)
            ix[:, 36 + q] = rm(SO0 + j * 128 + p, q)
            ix[q * 32:(q + 1) * 32, 40] = (q * 8 + j) * 32 + p[0:32]
            ix[q * 32:(q + 1) * 32, 41] = (q * 8 + 4 + j) * 32 + p[0:32]
        for c in range(8):
            if c < 4:
                ix[:, 48 + c] = _rowmap(1024, q_ * 256 + p, c)
            else:
                ix[:, 48 + c] = _rowmap(1024, q_ * 256 + 128 + p, c - 4)
        m["idx"] = ix
        maps.append(m)
    return maps


_NC_CACHE = {}


def kernel(**inputs):
    inp = {k: np.asarray(v) for k, v in inputs.items()}
    if "nc" not in _NC_CACHE:
        _NC_CACHE["nc"] = build_fused()
    res = run_bass_kernel_spmd(_NC_CACHE["nc"], prep_fused(inp), core_ids=list(range(8))).results
    out = np.zeros((2, SEQ, 1024), np.float32)
    for core in range(8):
        b, q = core // 4, core % 4
        out[b, q * NTOK:(q + 1) * NTOK, :] = res[core]["outT"].T
    return out
```

```python
import numpy as np
import ml_dtypes
import concourse.bass as bass
import concourse.mybir as mybir
from concourse.bass_utils import run_bass_kernel_spmd

F32 = mybir.dt.float32
BF16 = mybir.dt.bfloat16
I32 = mybir.dt.int32
AF = mybir.ActivationFunctionType
ALU = mybir.AluOpType
AX = mybir.AxisListType
EPS = 1e-6
TWO_PI = 6.283185307179586
PI = 3.141592653589793
NTOK = 2048
TS = 512
class TT:
    __slots__ = ("h", "name", "last_w", "readers", "dsem", "dcount")

    def __init__(self, h, name):
        self.h = h
        self.name = name
        self.last_w = None
        self.readers = []
        self.dsem = None
        self.dcount = 0

    def __getitem__(self, idx):
        return self.h[idx]


class Prog:
    ENGS = ("pe", "act", "dve", "pool", "sp")

    def __init__(self, nc, prefix=""):
        self.nc = nc
        self.prefix = prefix
        self.ops = {e: [] for e in self.ENGS}
        self.count = {e: 0 for e in self.ENGS}
        self.waited = {e: {} for e in self.ENGS}
        self.sem_names = ["eng_" + e for e in self.ENGS]
        self.tiles = []
        self._ctx = []
        self.n_dsem = 0

    def sb(self, name, shape, dt=F32):
        g = self.nc.sbuf_tensor(self.prefix + "s_" + name, list(shape), dt)
        h = g.__enter__()
        self._ctx.append(g)
        t = TT(h, name)
        self.tiles.append(t)
        return t

    def ps(self, name, shape, dt=F32):
        g = self.nc.psum_tensor(self.prefix + "p_" + name, list(shape), dt)
        h = g.__enter__()
        self._ctx.append(g)
        t = TT(h, name)
        self.tiles.append(t)
        return t

    def _deps(self, eng, reads, writes):
        deps = []
        for r in reads:
            if r.last_w is not None:
                deps.append(r.last_w)
        for w in writes:
            if w.last_w is not None:
                deps.append(w.last_w)
            deps.extend(w.readers)
        waits = []
        wd = self.waited[eng]
        best = {}
        for (k, v) in deps:
            if eng == "pe" and k == "eng_pe":
                continue
            if wd.get(k, 0) >= v:
                continue
            if best.get(k, 0) < v:
                best[k] = v
        for k, v in best.items():
            wd[k] = v
            waits.append((k, v))
        return waits

    def op(self, eng, fn, reads=(), writes=()):
        waits = self._deps(eng, reads, writes)
        self.count[eng] += 1
        me = ("eng_" + eng, self.count[eng])
        self.ops[eng].append((waits, fn, (me[0], 1)))
        for r in reads:
            r.readers.append(me)
        for w in writes:
            w.last_w = me
            w.readers = []
        return me

    def dma(self, eng, out_ap, in_ap, reads=(), writes=(), **kw):
        waits = self._deps(eng, reads, writes)
        owner = (list(writes) + list(reads))[0]
        if owner.dsem is None:
            owner.dsem = "d%d" % self.n_dsem
            self.n_dsem += 1
            self.sem_names.append(owner.dsem)
        owner.dcount += 1
        me = (owner.dsem, 16 * owner.dcount)

        def fn(e, out_ap=out_ap, in_ap=in_ap, kw=kw):
            o_ = out_ap() if callable(out_ap) else out_ap
            i_ = in_ap() if callable(in_ap) else in_ap
            return e.dma_start(out=o_, in_=i_, **kw)
        self.ops[eng].append((waits, fn, (owner.dsem, 16)))
        for r in reads:
            r.readers.append(me)
        for w in writes:
            w.last_w = me
            w.readers = []
        return me

    def raw(self, eng, fn):
        self.ops[eng].append(([], fn, "raw"))

    def dram(self, name, shape, dt=F32):
        h = self.nc.dram_tensor(name, list(shape), dt).ap()
        t = TT(h, name)
        self.tiles.append(t)
        return t

    def collective(self, kind, in_t, out_t, groups):
        waits = self._deps("pool", [in_t], [out_t])
        key = "cc%d" % self.n_dsem
        self.n_dsem += 1
        self.sem_names.append(key)
        me = (key, 1)

        def fn(e):
            return e.collective_compute(kind, ALU.bypass, replica_groups=groups, ins=[in_t.h.opt()], outs=[out_t.h.opt()])
        self.ops["pool"].append((waits, fn, (key, None)))
        in_t.readers.append(me)
        out_t.last_w = me
        out_t.readers = []
        return me

    def gather(self, out_t, out_ap, in_ap, idx_t, idx_ap):
        waits = self._deps("pool", [idx_t], [out_t])
        if out_t.dsem is None:
            out_t.dsem = "d%d" % self.n_dsem
            self.n_dsem += 1
            self.sem_names.append(out_t.dsem)
        out_t.dcount += 1
        me = (out_t.dsem, 16 * out_t.dcount)

        def fn(e):
            return e.indirect_dma_start(out=out_ap, out_offset=None, in_=in_ap, in_offset=bass.IndirectOffsetOnAxis(ap=idx_ap, axis=0))
        self.ops["pool"].append((waits, fn, (out_t.dsem, 16)))
        idx_t.readers.append(me)
        out_t.last_w = me
        out_t.readers = []
        return me

    def collective_raw(self, kind, in_ap, out_ap, groups, wait=True):
        self.wait_all("pool", self.tiles)
        key = "cc%d" % self.n_dsem
        self.n_dsem += 1
        self.sem_names.append(key)

        def fn(e):
            return e.collective_compute(kind, ALU.bypass, replica_groups=groups, ins=[in_ap.opt()], outs=[out_ap.opt()])
        self.ops["pool"].append(([], fn, (key, None)))
        if wait:
            self.ops["pool"].append(([(key, 1)], None, None))
        return key

    def wait_all(self, eng, tiles):
        deps = []
        for t in tiles:
            if t.last_w is not None:
                deps.append(t.last_w)
            deps.extend(t.readers)
        wd = self.waited[eng]
        best = {}
        for k, v in deps:
            if wd.get(k, 0) < v and best.get(k, 0) < v:
                best[k] = v
        waits = []
        for k, v in best.items():
            wd[k] = v
            waits.append((k, v))
        self.ops[eng].append((waits, None, None))

    def emit(self):
        nc = self.nc
        sems = {}
        for n in self.sem_names:
            sems[n] = nc.alloc_semaphore(name=self.prefix + n)
        blk = nc.Block()
        block = blk.__enter__()

        def run(engname):
            def body(e):
                for waits, fn, inc in self.ops[engname]:
                    for k, v in waits:
                        e.wait_ge(sems[k], v)
                    if fn is not None:
                        ins = fn(e)
                        if inc == "raw":
                            continue
                        if inc[1] is None:
                            ins.then_inc(sems[inc[0]])
                        else:
                            ins.then_inc(sems[inc[0]], inc[1])
            return body

        block.tensor(run("pe"))
        block.scalar(run("act"))
        block.vector(run("dve"))
        block.gpsimd(run("pool"))
        block.sync(run("sp"))
        blk.__exit__(None, None, None)
        nc.all_engine_barrier()
        nc.clear_and_free_semaphores(list(sems.values()))
        nc.all_engine_barrier()
        for g in reversed(self._ctx):
            g.__exit__(None, None, None)
        self._ctx = []


class TV:
    def __init__(self, base, ap):
        self.__dict__["base"] = base
        self.__dict__["ap"] = ap

    def __getitem__(self, idx):
        return self.ap[idx]

    def __getattr__(self, k):
        return getattr(self.base, k)

    def __setattr__(self, k, v):
        setattr(self.base, k, v)


class Ctx:
    def __init__(self, P, nbanks=8):
        self.P = P
        self.nb = nbanks
        self.pbanks = [P.ps("pb%d" % i, [128, 512], F32) for i in range(nbanks)]
        self.pi = 0
        self.pinned = set()
        self.rings = {}

    def psum(self, pin=False):
        while (self.pi % self.nb) in self.pinned:
            self.pi += 1
        t = self.pbanks[self.pi % self.nb]
        if pin:
            self.pinned.add(self.pi % self.nb)
        self.pi += 1
        return t

    def unpin(self):
        self.pinned = set()

    def tmp(self, key, shape, dt=F32, n=2, own=False):
        if dt == F32 and len(shape) == 2 and shape[1] == 512 and not own:
            base = self.tmp("T32", [128, 512, 1], F32, 8)
            return TV(base, base.h[0:shape[0], :, 0])
        if key not in self.rings:
            self.rings[key] = [[self.P.sb("%s_%d" % (key, i), shape, dt) for i in range(n)], 0]
        r = self.rings[key]
        t = r[0][r[1] % len(r[0])]
        r[1] += 1
        return t


def dram_in(nc, name, shape, dt=F32):
    return nc.dram_tensor(name, list(shape), dt, kind="ExternalInput").ap()


def dram_out(nc, name, shape, dt=F32):
    return nc.dram_tensor(name, list(shape), dt, kind="ExternalOutput").ap()


def build_tok(mode, fz=None):
    nc = fz["nc"] if fz else bass.Bass("TRN2", target_bir_lowering=False)
    P = Prog(nc, mode + "_")
    C = Ctx(P)
    op = P.op
    L = 0 if mode == "A" else 1

    D = dict(fz["D"]) if fz else {}
    def din(name, shape, dt=F32):
        if name not in D:
            D[name] = dram_in(nc, name, shape, dt)
        return D[name]
    def dout(name, shape, dt=F32):
        if name not in D:
            D[name] = dram_out(nc, name, shape, dt)
        return D[name]

    din("hT", [1024, NTOK])
    din("pT", [256, NTOK])
    din("gains", [128, 32])
    din("consts", [128, 512])
    din("w_gate", [1024, 2816]); din("w_up", [1024, 2816]); din("w_down", [2816, 1024])
    din("w_ple_gate", [1024, 1024]); din("w_ple_proj", [256, 1024])
    din("w_out", [1024, 1024])
    if mode == "A":
        din("xhT", [1024, 2])
        din("posb", [96, NTOK], I32)
        din("ev_w_in", [1024, 2560])
        din("wconv", [128, 12]); din("gvb", [128, 512]); din("wsT", [128, 8, 128]); din("bsb", [128, 4, 128])
        din("od_w_in", [1024, 2248])
        din("g_lat", [128, 5])
        din("w_q_up", [384, 768]); din("w_q_up_sw", [384, 768]); din("w_kv_up", [256, 1024])
        din("gsm", [128, 8])
        dout("h1T", [1024, NTOK])
        dout("qT", [8, 96, NTOK], BF16); dout("knT", [512, NTOK], BF16); dout("kpeT", [32, NTOK], BF16)
        dout("vT", [512, NTOK], BF16)
        dout("mqT", [256, NTOK], BF16); dout("mkT", [256, NTOK], BF16)
        dout("mvT", [512, NTOK], BF16); dout("soT", [512, NTOK], BF16)
        dout("mifT", [8, NTOK])
    else:
        if not fz:
            din("ymT", [1024, NTOK], BF16)
        else:
            idxc = P.sb("idxc", [128, 64], I32)
            P.dma("sp", idxc[:, :], D["idx"], writes=[idxc])
            ystg = [P.sb("ystg%d" % c, [128, NTOK], BF16) for c in range(8)]
            for c in range(8):
                P.gather(ystg[c], ystg[c][:, :], D["recv2"][:, :], idxc, idxc[:, 48 + c:49 + c])
                if "dbg_y" in D:
                    P.dma("sp", D["dbg_y"][c * 128:(c + 1) * 128, :], ystg[c][:, :], reads=[ystg[c]])
        dout("outT", [1024, NTOK])

    h = [[P.sb("h%d_%d" % (s, c), [128, TS]) for c in range(8)] for s in range(2)]
    hn = [[P.sb("hn%d_%d" % (s, c), [128, TS], BF16) for c in range(8)] for s in range(2)]
    act = [[P.sb("act%d_%d" % (s, j), [128, TS], BF16) for j in range(22)] for s in range(2)]
    y = [a[0:8] for a in act]
    ringA = [P.sb("wA%d" % i, [128, 8, 512], BF16) for i in range(3)]
    ringD = [P.sb("wD%d" % i, [128, 22, 128], BF16) for i in range(2)]
    st = {"a": 0, "d": 0}
    wpp = P.sb("wpp", [128, 2, 1024], BF16)
    gains = P.sb("gains", [128, 32])
    cst = P.sb("cst", [128, 512])
    ones_bf = P.sb("ones_bf", [128, 128], BF16)
    P.dma("sp", gains[:, :], D["gains"], writes=[gains])
    P.dma("sp", cst[:, :], D["consts"], writes=[cst])
    op("pool", lambda e: e.memset(ones_bf[:, :], 1.0), writes=[ones_bf])
    P.dma("pool", wpp[:, :, :], D["w_ple_proj"].rearrange("(kc p) n -> p kc n", p=128), writes=[wpp])

    def slotA():
        t = ringA[st["a"] % 3]; st["a"] += 1; return t

    def slotD():
        t = ringD[st["d"] % 2]; st["d"] += 1; return t

    def loadA(w, c0, c1, dst=None, off=0):
        t = dst if dst is not None else slotA()
        P.dma("pool", t[:, :, off:off + (c1 - c0)], w.rearrange("(kc p) n -> p kc n", p=128)[:, :, c0:c1], writes=[t])
        return t

    def mm(ps_ap, lhs_fn, rhs_fn, nk, reads, ps):
        for kc in range(nk):
            l_ = lhs_fn(kc); r_ = rhs_fn(kc)
            op("pe", lambda e, kc=kc, l_=l_, r_=r_: e.matmul(ps_ap, l_, r_, start=(kc == 0), stop=(kc == nk - 1)),
               reads=reads, writes=[ps])

    def rstd_from_ps(ps, rows, n, scale, tag):
        sd = C.tmp("sd" + tag, [128, 512])
        rp = C.tmp("rp" + tag, [128, 512])
        rd = [ps] + ([cst] if not isinstance(scale, float) else [])
        op("act", lambda e: e.activation(sd[0:rows, 0:n], ps[0:rows, 0:n], AF.Sqrt, bias=EPS, scale=scale), reads=rd, writes=[sd])
        op("dve", lambda e: e.reciprocal(rp[0:rows, 0:n], sd[0:rows, 0:n]), reads=[sd], writes=[rp])
        return rp

    def norm(s, gcol, n=TS, src=None, dst=None):
        src = src or h[s]; dst = dst or hn[s]
        ps = C.psum()
        for c in range(8):
            sq = C.tmp("sq", [128, 512], BF16, 3)
            op("act", lambda e, c=c, sq=sq: e.activation(sq[:, 0:n], src[c][:, 0:n], AF.Square), reads=[src[c]], writes=[sq])
            op("pe", lambda e, c=c, sq=sq: e.matmul(ps[:, 0:n], ones_bf[:, :], sq[:, 0:n], start=(c == 0), stop=(c == 7)), reads=[sq, ones_bf], writes=[ps])
        rp = rstd_from_ps(ps, 128, n, 1.0 / 1024.0, "n")
        for c in range(8):
            op("dve", lambda e, c=c: e.scalar_tensor_tensor(dst[c][:, 0:n], src[c][:, 0:n], gains[:, gcol + c:gcol + c + 1], rp[:, 0:n], op0=ALU.mult, op1=ALU.mult),
               reads=[src[c], gains, rp], writes=[dst[c]])

    def resid_proj(w, src):
        for blk in range(2):
            slot = loadA(w, blk * 512, blk * 512 + 512)
            for s in range(2):
                for m in range(4):
                    ps = C.psum()
                    mm(ps[:, :], lambda kc, m=m: slot[:, kc, m * 128:(m + 1) * 128], lambda kc, s=s: src[s][kc][:, :], 8, [slot] + src[s], ps)
                    hc = h[s][blk * 4 + m]
                    op("dve", lambda e, ps=ps, hc=hc: e.tensor_tensor(hc[:, :], ps[:, :], hc[:, :], op=ALU.add), reads=[ps, hc], writes=[hc])

    def ffn(gcol):
        for s in range(2):
            norm(s, gcol)
        for j in range(11):
            slot = slotA()
            loadA(D["w_gate"], j * 256, j * 256 + 256, dst=slot, off=0)
            loadA(D["w_up"], j * 256, j * 256 + 256, dst=slot, off=256)
            for s in range(2):
                for jj in range(2):
                    pg = C.psum(); pu = C.psum()
                    mm(pg[:, :], lambda kc, jj=jj: slot[:, kc, jj * 128:(jj + 1) * 128], lambda kc, s=s: hn[s][kc][:, :], 8, [slot] + hn[s], pg)
                    mm(pu[:, :], lambda kc, jj=jj: slot[:, kc, 256 + jj * 128:256 + (jj + 1) * 128], lambda kc, s=s: hn[s][kc][:, :], 8, [slot] + hn[s], pu)
                    sg = C.tmp("sg", [128, 512], F32, 3)
                    op("act", lambda e, pg=pg, sg=sg: e.activation(sg[:, :], pg[:, :], AF.Silu), reads=[pg], writes=[sg])
                    a = act[s][2 * j + jj]
                    op("dve", lambda e, pu=pu, sg=sg, a=a: e.tensor_tensor(a[:, :], pu[:, :], sg[:, :], op=ALU.mult), reads=[pu, sg], writes=[a])
        for mb in range(8):
            slot = slotD()
            P.dma("pool", slot[:, :, :], D["w_down"].rearrange("(kc p) n -> p kc n", p=128)[:, :, mb * 128:(mb + 1) * 128], writes=[slot])
            for s in range(2):
                ps = C.psum()
                mm(ps[:, :], lambda kc: slot[:, kc, :], lambda kc, s=s: act[s][kc][:, :], 22, [slot] + act[s], ps)
                hc = h[s][mb]
                op("dve", lambda e, ps=ps, hc=hc: e.tensor_tensor(hc[:, :], ps[:, :], hc[:, :], op=ALU.add), reads=[ps, hc], writes=[hc])

    def ple(gcol, tok0):
        pt = []
        for s in range(2):
            norm(s, gcol)
            t = C.tmp("pt", [128, 2, TS], BF16, 2)
            P.dma("pool", t[:, :, :], D["pT"].rearrange("(kc p) n -> p kc n", p=128)[:, :, tok0 + s * TS:tok0 + (s + 1) * TS], writes=[t])
            pt.append(t)
        for blk in range(2):
            slot = loadA(D["w_ple_gate"], blk * 512, blk * 512 + 512)
            for s in range(2):
                for m in range(4):
                    mg = blk * 4 + m
                    pg = C.psum(); pp = C.psum()
                    mm(pg[:, :], lambda kc, m=m: slot[:, kc, m * 128:(m + 1) * 128], lambda kc, s=s: hn[s][kc][:, :], 8, [slot] + hn[s], pg)
                    mm(pp[:, :], lambda kc, mg=mg: wpp[:, kc, mg * 128:(mg + 1) * 128], lambda kc, s=s: pt[s][:, kc, :], 2, [wpp, pt[s]], pp)
                    sg = C.tmp("sg", [128, 512], F32, 3)
                    op("act", lambda e, pg=pg, sg=sg: e.activation(sg[:, :], pg[:, :], AF.Sigmoid), reads=[pg], writes=[sg])
                    t2 = C.tmp("t2", [128, 512], F32, 3)
                    op("dve", lambda e, pp=pp, sg=sg, t2=t2: e.tensor_tensor(t2[:, :], pp[:, :], sg[:, :], op=ALU.mult), reads=[pp, sg], writes=[t2])
                    hc = h[s][mg]
                    op("dve", lambda e, t2=t2, hc=hc: e.tensor_tensor(hc[:, :], t2[:, :], hc[:, :], op=ALU.add), reads=[t2, hc], writes=[hc])

    if mode == "A":
        wconv = P.sb("wconv", [128, 12]); gvb = P.sb("gvb", [128, 512]); bsb = P.sb("bsb", [128, 4, 128])
        wsT = P.sb("wsT", [128, 8, 128], BF16); maskb = P.sb("maskb", [128, 128], BF16)
        g_lat = P.sb("g_lat", [128, 5]); gsm = P.sb("gsm", [128, 8])
        b96 = P.sb("b96", [96, 96], BF16); bd64 = P.sb("bd64", [128, 128], BF16)
        for t, nm in ((wconv, "wconv"), (gvb, "gvb"), (g_lat, "g_lat"), (gsm, "gsm")):
            P.dma("sp", t[:, :], D[nm], writes=[t])
        P.dma("sp", bsb[:, :, :], D["bsb"], writes=[bsb])
        P.dma("pool", wsT[:, :, :], D["wsT"], writes=[wsT])
        op("dve", lambda e: e.tensor_copy(maskb[:, :], cst[:, 0:128]), reads=[cst], writes=[maskb])
        for hh in range(8):
            op("dve", lambda e, hh=hh: e.tensor_tensor(wsT[:, hh, :], wsT[:, hh, :], maskb[:, :], op=ALU.mult), reads=[wsT, maskb], writes=[wsT])
        op("dve", lambda e: e.tensor_copy(b96[:, :], cst[0:96, 128:224]), reads=[cst], writes=[b96])
        op("dve", lambda e: e.tensor_copy(bd64[:, :], cst[:, 224:352]), reads=[cst], writes=[bd64])
        hal = [P.sb("hal%d" % cc, [128, 2]) for cc in range(4)]
        gu = [act[s][8:12] for s in range(2)]
        hh_t = [P.sb("hh%d" % c, [128, 2]) for c in range(8)]
        hhn = [P.sb("hhn%d" % c, [128, 2], BF16) for c in range(8)]

    def even_mixer(sti, tok0):
        for s in range(2):
            norm(s, 0)
        if sti == 0:
            for c in range(8):
                P.dma("sp", hh_t[c][:, :], D["xhT"][c * 128:(c + 1) * 128, :], writes=[hh_t[c]])
            norm(0, 0, n=2, src=hh_t, dst=hhn)
        for cc in range(4):
            slot = loadA(D["ev_w_in"], cc * 384, cc * 384 + 384)
            for s in range(2):
                z = C.tmp("zt", [128, TS + 2], F32, 2)
                if not (s == 0 and sti == 0):
                    op("dve", lambda e, cc=cc, z=z: e.tensor_copy(z[:, 0:2], hal[cc][:, :]), reads=[hal[cc]], writes=[z])
                else:
                    pc = C.psum(); px = C.psum()
                    mm(pc[:, 0:2], lambda kc: slot[:, kc, 128:256], lambda kc: hhn[kc][:, :], 8, [slot] + hhn, pc)
                    mm(px[:, 0:2], lambda kc: slot[:, kc, 256:384], lambda kc: hhn[kc][:, :], 8, [slot] + hhn, px)
                    cs = C.tmp("cs", [128, 512], F32, 2)
                    op("act", lambda e, pc=pc, cs=cs: e.activation(cs[:, 0:2], pc[:, 0:2], AF.Copy), reads=[pc], writes=[cs])
                    op("dve", lambda e, px=px, cs=cs, z=z: e.tensor_tensor(z[:, 0:2], px[:, 0:2], cs[:, 0:2], op=ALU.mult), reads=[px, cs], writes=[z])
                pb = C.psum(); pc = C.psum(); px = C.psum()
                for pp_, c0 in ((pc, 128), (px, 256), (pb, 0)):
                    mm(pp_[:, :], lambda kc, c0=c0: slot[:, kc, c0:c0 + 128], lambda kc, s=s: hn[s][kc][:, :], 8, [slot] + hn[s], pp_)
                cs = C.tmp("cs", [128, 512], F32, 2)
                op("act", lambda e, pc=pc, cs=cs: e.activation(cs[:, :], pc[:, :], AF.Copy), reads=[pc], writes=[cs])
                op("dve", lambda e, px=px, cs=cs, z=z: e.tensor_tensor(z[:, 2:TS + 2], px[:, :], cs[:, :], op=ALU.mult), reads=[px, cs], writes=[z])
                op("dve", lambda e, cc=cc, z=z: e.tensor_copy(hal[cc][:, :], z[:, TS:TS + 2]), reads=[z], writes=[hal[cc]])
                acc = C.tmp("acc", [128, 512], F32, 2)
                op("dve", lambda e, z=z, acc=acc, cc=cc: e.tensor_scalar(acc[:, :], z[:, 0:TS], wconv[:, cc * 3:cc * 3 + 1], None, op0=ALU.mult), reads=[z, wconv], writes=[acc])
                op("dve", lambda e, z=z, acc=acc, cc=cc: e.scalar_tensor_tensor(acc[:, :], z[:, 1:TS + 1], wconv[:, cc * 3 + 1:cc * 3 + 2], acc[:, :], op0=ALU.mult, op1=ALU.add), reads=[z, wconv, acc], writes=[acc])
                op("dve", lambda e, z=z, acc=acc, cc=cc: e.scalar_tensor_tensor(acc[:, :], z[:, 2:TS + 2], wconv[:, cc * 3 + 2:cc * 3 + 3], acc[:, :], op0=ALU.mult, op1=ALU.add), reads=[z, wconv, acc], writes=[acc])
                yt = y[s][cc]
                op("dve", lambda e, pb=pb, acc=acc, yt=yt: e.tensor_tensor(yt[:, :], pb[:, :], acc[:, :], op=ALU.mult), reads=[pb, acc], writes=[yt])
        slot = loadA(D["ev_w_in"], 1536, 2048)
        for s in range(2):
            for uc in range(4):
                pu = C.psum()
                mm(pu[:, :], lambda kc, uc=uc: slot[:, kc, uc * 128:(uc + 1) * 128], lambda kc, s=s: hn[s][kc][:, :], 8, [slot] + hn[s], pu)
                g_ = gu[s][uc]
                op("act", lambda e, pu=pu, g_=g_: e.activation(g_[:, :], pu[:, :], AF.Gelu), reads=[pu], writes=[g_])
        slot = loadA(D["ev_w_in"], 2048, 2560)
        for s in range(2):
            pm = [C.psum(pin=True) for _ in range(4)]
            for tb in range(4):
                pv = C.psum()
                mm(pv[:, :], lambda kc, s=s, tb=tb: hn[s][kc][:, tb * 128:(tb + 1) * 128], lambda kc: slot[:, kc, :], 8, [slot] + hn[s], pv)
                gv = C.tmp("gv", [128, 512], F32, 2)
                op("act", lambda e, pv=pv, gv=gv: e.activation(gv[:, :], pv[:, :], AF.Gelu), reads=[pv], writes=[gv])
                sqv = C.tmp("sqv", [128, 512], F32, 2)
                op("act", lambda e, gv=gv, sqv=sqv: e.activation(sqv[:, :], gv[:, :], AF.Square), reads=[gv], writes=[sqv])
                ss = C.tmp("ss", [128, 8], F32, 2); sd = C.tmp("ssd", [128, 8], F32, 2); rs = C.tmp("srs", [128, 8], F32, 2)
                op("dve", lambda e, sqv=sqv, ss=ss: e.tensor_reduce(ss[:, :], sqv[:, :].rearrange("p (h d) -> p h d", d=64), axis=AX.X, op=ALU.add), reads=[sqv], writes=[ss])
                op("act", lambda e, ss=ss, sd=sd: e.activation(sd[:, :], ss[:, :], AF.Sqrt, bias=EPS, scale=1.0 / 64.0), reads=[ss], writes=[sd])
                op("dve", lambda e, sd=sd, rs=rs: e.reciprocal(rs[:, :], sd[:, :]), reads=[sd], writes=[rs])
                op("dve", lambda e, gv=gv, rs=rs: e.tensor_tensor(gv[:, :].rearrange("p (h d) -> p h d", d=64), gv[:, :].rearrange("p (h d) -> p h d", d=64),
                                                                 rs[:, :].unsqueeze(2).to_broadcast([128, 8, 64]), op=ALU.mult), reads=[gv, rs], writes=[gv])
                vn = C.tmp("vn", [128, 512], BF16, 2)
                op("dve", lambda e, gv=gv, vn=vn: e.tensor_tensor(vn[:, :], gv[:, :], gvb[:, :], op=ALU.mult), reads=[gv, gvb], writes=[vn])
                for hd in range(8):
                    op("pe", lambda e, hd=hd, tb=tb, vn=vn, pm=pm: e.matmul(pm[hd // 2][64 * (hd % 2):64 * (hd % 2) + 64, tb * 128:(tb + 1) * 128], vn[:, hd * 64:(hd + 1) * 64], wsT[:, hd, :], start=True, stop=True),
                       reads=[vn, wsT], writes=[pm[hd // 2]])
            for hc in range(4):
                t1 = C.tmp("t2", [128, 512], F32, 3)
                op("dve", lambda e, hc=hc, t1=t1, pm=pm: e.tensor_tensor(t1[:, :].rearrange("p (b t) -> p b t", t=128), pm[hc][:, :].rearrange("p (b t) -> p b t", t=128),
                                                                bsb[:, hc, :].unsqueeze(1).to_broadcast([128, 4, 128]), op=ALU.add), reads=[pm[hc], bsb], writes=[t1])
                yt = y[s][4 + hc]
                op("dve", lambda e, t1=t1, yt=yt, s=s, hc=hc: e.tensor_tensor(yt[:, :], t1[:, :], gu[s][hc][:, :], op=ALU.mult), reads=[t1, gu[s][hc]], writes=[yt])
            C.unpin()
        resid_proj(D["w_out"], y)

    def O(nm, r0, r1, t0):
        if fz and (nm + "_h0") in D:
            hf = t0 // 1024
            return D[nm + "_h%d" % hf][r0:r1, (t0 % 1024):(t0 % 1024) + TS]
        return D[nm][r0:r1, t0:t0 + TS]

    def odd_front(tok0):
        W = D["od_w_in"]
        for s in range(2):
            norm(s, 24)
        tabs = []
        for s in range(2):
            pi_ = C.tmp("ti", [96, TS], I32, 1)
            P.dma("sp", pi_[:, :], D["posb"][:, tok0 + s * TS:tok0 + (s + 1) * TS], writes=[pi_])
            ang = C.tmp("angp", [96, TS], F32, 1, own=True)
            op("dve", lambda e, pi_=pi_, ang=ang: e.tensor_copy(ang[:, :], pi_[:, :]), reads=[pi_], writes=[ang])
            op("dve", lambda e, ang=ang: e.tensor_scalar(ang[:, :], ang[:, :], cst[0:96, 353:354], None, op0=ALU.mult), reads=[ang, cst], writes=[ang])
            pair = []
            for nm, shift in (("cos", PI / 2.0), ("sin", 0.0)):
                a2 = C.tmp("a2", [96, TS], F32, 1); tf = C.tmp("tf", [96, TS], F32, 1); ti = C.tmp("ti", [96, TS], I32, 1)
                tab = C.tmp("tab" + nm, [96, TS], F32, 2, own=True)
                op("dve", lambda e, a2=a2, ang=ang, shift=shift: e.tensor_scalar(a2[:, :], ang[:, :], shift, None, op0=ALU.add), reads=[ang], writes=[a2])
                op("dve", lambda e, a2=a2, tf=tf: e.tensor_scalar(tf[:, :], a2[:, :], 1.0 / TWO_PI, None, op0=ALU.mult), reads=[a2], writes=[tf])
                op("dve", lambda e, tf=tf, ti=ti: e.tensor_copy(ti[:, :], tf[:, :]), reads=[tf], writes=[ti])
                op("dve", lambda e, tf=tf, ti=ti: e.tensor_copy(tf[:, :], ti[:, :]), reads=[ti], writes=[tf])
                op("dve", lambda e, tf=tf, a2=a2: e.scalar_tensor_tensor(a2[:, :], tf[:, :], -TWO_PI, a2[:, :], op0=ALU.mult, op1=ALU.add), reads=[tf, a2], writes=[a2])
                op("dve", lambda e, a2=a2: e.tensor_scalar(a2[:, :], a2[:, :], -PI, PI, op0=ALU.max, op1=ALU.min), reads=[a2], writes=[a2])
                if nm == "sin":
                    op("act", lambda e, a2=a2, tab=tab: e.activation(tab[:, :], a2[:, :], AF.Sin, scale=cst[0:96, 354:355]), reads=[a2, cst], writes=[tab])
                else:
                    op("act", lambda e, a2=a2, tab=tab: e.activation(tab[:, :], a2[:, :], AF.Sin), reads=[a2], writes=[tab])
                pair.append(tab)
            tabs.append(pair)

        def fm_out(ps, rows, dram_ap, scale=1.0, tag="fo"):
            ob = C.tmp(tag, [128, TS], BF16, 3)
            op("act", lambda e: e.mul(ob[0:rows, :], ps[0:rows, :], float(scale)), reads=[ps], writes=[ob])
            P.dma("sp", dram_ap, ob[0:rows, :], reads=[ob])

        def tok_out(ps, ncols, dram_fn, stage, tb, col0, func=AF.Copy):
            op("act", lambda e: e.activation(stage[:, tb, col0:col0 + ncols], ps[:, 0:ncols], func), reads=[ps], writes=[stage])

        def lat_norm(raws, nch, gc0, D_, outs, tag):
            ps = C.psum()
            for c in range(nch):
                sq = C.tmp("sq", [128, 512], BF16, 3)
                op("act", lambda e, c=c, sq=sq: e.activation(sq[:, :], raws[c][:, :], AF.Square), reads=[raws[c]], writes=[sq])
                op("pe", lambda e, c=c, sq=sq: e.matmul(ps[:, :], ones_bf[:, :], sq[:, :], start=(c == 0), stop=(c == nch - 1)), reads=[sq, ones_bf], writes=[ps])
            rp = rstd_from_ps(ps, 128, TS, 1.0 / D_, tag)
            for c in range(nch):
                op("dve", lambda e, c=c: e.scalar_tensor_tensor(outs[c][:, :], raws[c][:, :], g_lat[:, gc0 + c:gc0 + c + 1], rp[:, :], op0=ALU.mult, op1=ALU.mult),
                   reads=[raws[c], g_lat, rp], writes=[outs[c]])

        qlr = [act[s][12:15] for s in range(2)]
        kvr = [act[s][15:17] for s in range(2)]
        qln = [act[s][17:20] for s in range(2)]
        kvn = [act[s][20:22] for s in range(2)]

        def raw_copy(ps, rows, dst):
            op("act", lambda e: e.activation(dst[0:rows, :], ps[0:rows, :], AF.Copy), reads=[ps], writes=[dst])

        slot = loadA(W, 0, 512)
        for s in range(2):
            for c in range(4):
                ps = C.psum()
                mm(ps[:, :], lambda kc, c=c: slot[:, kc, c * 128:(c + 1) * 128], lambda kc, s=s: hn[s][kc][:, :], 8, [slot] + hn[s], ps)
                raw_copy(ps, 128, qlr[s][c] if c < 3 else kvr[s][0])
            lat_norm(qlr[s], 3, 0, 384.0, qln[s], "q")
        slot = loadA(W, 512, 960)
        for s in range(2):
            t0 = tok0 + s * TS
            cosT, sinT = tabs[s]
            ps = C.psum()
            mm(ps[:, :], lambda kc: slot[:, kc, 0:128], lambda kc, s=s: hn[s][kc][:, :], 8, [slot] + hn[s], ps)
            raw_copy(ps, 128, kvr[s][1])
            lat_norm(kvr[s], 2, 3, 256.0, kvn[s], "k")
            pk = C.psum(); pks = C.psum()
            mm(pk[0:32, :], lambda kc: slot[:, kc, 128:160], lambda kc, s=s: hn[s][kc][:, :], 8, [slot] + hn[s], pk)
            mm(pks[0:32, :], lambda kc: slot[:, kc, 160:192], lambda kc, s=s: hn[s][kc][:, :], 8, [slot] + hn[s], pks)
            kr = C.tmp("kr", [32, 512], F32)
            raw_copy(pk, 32, kr)
            sq = C.tmp("sq", [128, 512], BF16, 3)
            op("act", lambda e, sq=sq, kr=kr: e.activation(sq[0:32, :], kr[:, :], AF.Square), reads=[kr], writes=[sq])
            pn = C.psum()
            op("pe", lambda e, sq=sq, pn=pn: e.matmul(pn[0:32, :], ones_bf[0:32, 0:32], sq[0:32, :], start=True, stop=True), reads=[sq, ones_bf], writes=[pn])
            rp = rstd_from_ps(pn, 32, TS, 1.0 / 32.0, "kp")
            a = C.tmp("kpa", [32, 512], F32); b_ = C.tmp("kpb", [32, 512], F32)
            op("dve", lambda e, a=a, rp=rp, kr=kr: e.scalar_tensor_tensor(a[:, :], kr[:, :], gsm[0:32, 4:5], rp[0:32, :], op0=ALU.mult, op1=ALU.mult), reads=[kr, gsm, rp], writes=[a])
            op("dve", lambda e, b_=b_, rp=rp, pks=pks: e.scalar_tensor_tensor(b_[:, :], pks[0:32, :], gsm[0:32, 5:6], rp[0:32, :], op0=ALU.mult, op1=ALU.mult), reads=[pks, gsm, rp], writes=[b_])
            op("dve", lambda e, a=a, cosT=cosT: e.tensor_tensor(a[:, :], a[:, :], cosT[0:32, :], op=ALU.mult), reads=[a, cosT], writes=[a])
            op("dve", lambda e, b_=b_, sinT=sinT: e.tensor_tensor(b_[:, :], b_[:, :], sinT[0:32, :], op=ALU.mult), reads=[b_, sinT], writes=[b_])
            ob = C.tmp("fo", [128, TS], BF16, 3)
            op("dve", lambda e, a=a, b_=b_, ob=ob: e.tensor_tensor(ob[0:32, :], a[:, :], b_[:, :], op=ALU.add), reads=[a, b_], writes=[ob])
            P.dma("sp", O("kpeT", 0, 32, t0), ob[0:32, :], reads=[ob])
            for c in range(2):
                ps = C.psum()
                mm(ps[:, :], lambda kc, c=c: slot[:, kc, 192 + c * 128:192 + (c + 1) * 128], lambda kc, s=s: hn[s][kc][:, :], 8, [slot] + hn[s], ps)
                fm_out(ps, 128, O("mqT", c * 128, (c + 1) * 128, t0), scale=0.125)
        slot = loadA(W, 960, 1224)
        for s in range(2):
            t0 = tok0 + s * TS
            for c in range(2):
                ps = C.psum()
                mm(ps[:, :], lambda kc, c=c: slot[:, kc, c * 128:(c + 1) * 128], lambda kc, s=s: hn[s][kc][:, :], 8, [slot] + hn[s], ps)
                fm_out(ps, 128, O("mkT", c * 128, (c + 1) * 128, t0))
            ps = C.psum()
            mm(ps[0:8, :], lambda kc: slot[:, kc, 256:264], lambda kc, s=s: hn[s][kc][:, :], 8, [slot] + hn[s], ps)
            mo_ = C.tmp("mif", [8, 512], F32)
            raw_copy(ps, 8, mo_)
            P.dma("sp", D["mifT"][:, t0:t0 + TS], mo_[:, :], reads=[mo_])
        for gi_, (c0, nm) in enumerate(((1224, "mvT"), (1736, "soT"))):
            slot = loadA(W, c0, c0 + 512)
            for s in range(2):
                t0 = tok0 + s * TS
                for c in range(4):
                    ps = C.psum()
                    mm(ps[:, :], lambda kc, c=c: slot[:, kc, c * 128:(c + 1) * 128], lambda kc, s=s: hn[s][kc][:, :], 8, [slot] + hn[s], ps)
                    ob = C.tmp("fo", [128, TS], BF16, 3)
                    if gi_ == 0 and c % 2 == 0:
                        op("dve", lambda e, ps=ps, ob=ob: e.tensor_copy(ob[:, :], ps[:, :]), reads=[ps], writes=[ob])
                    else:
                        fn_ = AF.Copy if gi_ == 0 else AF.Sigmoid
                        op("act", lambda e, ps=ps, ob=ob, fn_=fn_: e.activation(ob[:, :], ps[:, :], fn_), reads=[ps], writes=[ob])
                    P.dma("sp", O(nm, c * 128, (c + 1) * 128, t0), ob[:, :], reads=[ob])
        def sview(slot, nk, n):
            return slot.h[:, :, :].rearrange("p k n -> p (k n)")[:, 0:nk * n].rearrange("p (k n) -> p k n", n=n)
        wq_t = slotA(); wqs_t = slotA(); wkv_t = slotA()
        wq = TV(wq_t, sview(wq_t, 3, 768)); wqs = TV(wqs_t, sview(wqs_t, 3, 768)); wkv = TV(wkv_t, sview(wkv_t, 2, 1024))
        P.dma("pool", wq[:, :, :], D["w_q_up"].rearrange("(kc p) n -> p kc n", p=128), writes=[wq])
        P.dma("pool", wqs[:, :, :], D["w_q_up_sw"].rearrange("(kc p) n -> p kc n", p=128), writes=[wqs])
        P.dma("pool", wkv[:, :, :], D["w_kv_up"].rearrange("(kc p) n -> p kc n", p=128), writes=[wkv])
        for hd in range(8):
            for s in range(2):
                t0 = tok0 + s * TS
                cosT, sinT = tabs[s]
                pq = C.psum(); pqs = C.psum()
                mm(pq[0:96, :], lambda kc, hd=hd: wq[:, kc, hd * 96:(hd + 1) * 96], lambda kc, s=s: qln[s][kc][:, :], 3, [wq] + qln[s], pq)
                mm(pqs[0:96, :], lambda kc, hd=hd: wqs[:, kc, hd * 96:(hd + 1) * 96], lambda kc, s=s: qln[s][kc][:, :], 3, [wqs] + qln[s], pqs)
                qr = C.tmp("qr", [96, 512], F32)
                raw_copy(pq, 96, qr)
                sq = C.tmp("sq", [128, 512], BF16, 3)
                op("act", lambda e, sq=sq, qr=qr: e.activation(sq[0:96, :], qr[:, :], AF.Square), reads=[qr], writes=[sq])
                pn = C.psum()
                op("pe", lambda e, sq=sq, pn=pn: e.matmul(pn[0:96, :], b96[:, :], sq[0:96, :], start=True, stop=True), reads=[sq, b96], writes=[pn])
                rp = rstd_from_ps(pn, 96, TS, cst[0:96, 352:353], "qh")
                qn = C.tmp("qn", [96, 512], F32); sw = C.tmp("qsw", [96, 512], F32)
                op("dve", lambda e, qn=qn, qr=qr, rp=rp: e.scalar_tensor_tensor(qn[:, :], qr[:, :], gsm[0:96, 0:1], rp[0:96, :], op0=ALU.mult, op1=ALU.mult), reads=[qr, gsm, rp], writes=[qn])
                op("dve", lambda e, sw=sw, pqs=pqs, rp=rp: e.scalar_tensor_tensor(sw[64:96, :], pqs[64:96, :], gsm[64:96, 1:2], rp[64:96, :], op0=ALU.mult, op1=ALU.mult), reads=[pqs, gsm, rp], writes=[sw])
                op("dve", lambda e, qn=qn, cosT=cosT: e.tensor_tensor(qn[64:96, :], qn[64:96, :], cosT[64:96, :], op=ALU.mult), reads=[qn, cosT], writes=[qn])
                op("dve", lambda e, sw=sw, sinT=sinT: e.tensor_tensor(sw[64:96, :], sw[64:96, :], sinT[64:96, :], op=ALU.mult), reads=[sw, sinT], writes=[sw])
                op("dve", lambda e, qn=qn, sw=sw: e.tensor_tensor(qn[64:96, :], qn[64:96, :], sw[64:96, :], op=ALU.add), reads=[qn, sw], writes=[qn])
                ob = C.tmp("fo", [128, TS], BF16, 3)
                op("act", lambda e, ob=ob, qn=qn: e.mul(ob[0:96, :], qn[:, :], 96.0 ** -0.5), reads=[qn], writes=[ob])
                P.dma("sp", (O("qTf", hd * 96, (hd + 1) * 96, t0) if fz else D["qT"][hd, :, t0:t0 + TS]), ob[0:96, :], reads=[ob])
        for s in range(2):
            t0 = tok0 + s * TS
            for hp in range(4):
                pk = C.psum()
                mm(pk[:, :], lambda kc, hp=hp: wkv[:, kc, hp * 128:(hp + 1) * 128], lambda kc, s=s: kvn[s][kc][:, :], 2, [wkv] + kvn[s], pk)
                kr = C.tmp("kr", [128, 512], F32)
                raw_copy(pk, 128, kr)
                sq = C.tmp("sq", [128, 512], BF16, 3)
                op("act", lambda e, sq=sq, kr=kr: e.activation(sq[:, :], kr[:, :], AF.Square), reads=[kr], writes=[sq])
                pn = C.psum()
                op("pe", lambda e, sq=sq, pn=pn: e.matmul(pn[:, :], bd64[:, :], sq[:, :], start=True, stop=True), reads=[sq, bd64], writes=[pn])
                rp = rstd_from_ps(pn, 128, TS, 1.0 / 64.0, "kh")
                ob = C.tmp("fo", [128, TS], BF16, 3)
                op("dve", lambda e, ob=ob, kr=kr, rp=rp: e.scalar_tensor_tensor(ob[:, :], kr[:, :], gsm[:, 2:3], rp[:, :], op0=ALU.mult, op1=ALU.mult), reads=[kr, gsm, rp], writes=[ob])
                P.dma("sp", O("knT", hp * 128, (hp + 1) * 128, t0), ob[:, :], reads=[ob])
            for hp in range(4):
                pv = C.psum()
                mm(pv[:, :], lambda kc, hp=hp: wkv[:, kc, 512 + hp * 128:512 + (hp + 1) * 128], lambda kc, s=s: kvn[s][kc][:, :], 2, [wkv] + kvn[s], pv)
                ob = C.tmp("fo", [128, TS], BF16, 3)
                op("act", lambda e, pv=pv, ob=ob: e.activation(ob[:, :], pv[:, :], AF.Copy), reads=[pv], writes=[ob])
                P.dma("sp", O("vT", hp * 128, (hp + 1) * 128, t0), ob[:, :], reads=[ob])

    for sti in range(NTOK // (2 * TS)):
        tok0 = sti * 2 * TS
        for s in range(2):
            for c in range(8):
                P.dma("sp", h[s][c][:, :], D["hT"][c * 128:(c + 1) * 128, tok0 + s * TS:tok0 + (s + 1) * TS], writes=[h[s][c]])
        if mode == "A":
            even_mixer(sti, tok0)
            ffn(8)
            ple(16, tok0)
            for s in range(2):
                for c in range(8):
                    P.dma("sp", D["h1T"][c * 128:(c + 1) * 128, tok0 + s * TS:tok0 + (s + 1) * TS], h[s][c][:, :], reads=[h[s][c]])
            odd_front(tok0)
            if fz and sti == 0 and "mid_hook" in fz:
                fz["mid_hook"](P)
        else:
            if fz:
                ysrc = [[TV(ystg[c], ystg[c].h[:, tok0 + s * TS:tok0 + (s + 1) * TS]) for c in range(8)] for s in range(2)]
            else:
                ysrc = y
                for s in range(2):
                    for c in range(8):
                        P.dma("pool", y[s][c][:, :], D["ymT"][c * 128:(c + 1) * 128, tok0 + s * TS:tok0 + (s + 1) * TS], writes=[y[s][c]])
            resid_proj(D["w_out"], ysrc)
            ffn(8)
            ple(16, tok0)
            for s in range(2):
                for c in range(8):
                    P.dma("sp", D["outT"][c * 128:(c + 1) * 128, tok0 + s * TS:tok0 + (s + 1) * TS], h[s][c][:, :], reads=[h[s][c]])
    P.wait_all("sp", P.tiles)
    if fz:
        return P
    P.emit()
    return nc


def _c(a):
    return np.ascontiguousarray(a)


def _chunkT(v):
    return _c(v.reshape(-1, 128).T)


def _consts():
    c = np.zeros((128, 512), np.float32)
    s = np.arange(128)
    c[:, 0:128] = (s[:, None] <= s[None, :]).astype(np.float32)
    k = np.arange(96)
    c[0:96, 128:224] = ((k[:, None] < 64) == (k[None, :] < 64)).astype(np.float32)
    c[:, 224:352] = ((s[:, None] // 64) == (s[None, :] // 64)).astype(np.float32)
    c[0:64, 352] = 1.0 / 64.0
    c[64:96, 352] = 1.0 / 32.0
    inv_freq = (10000.0 ** (-np.arange(0, 32, 2, dtype=np.float32) / np.float32(32))).astype(np.float32)
    c[0:96, 353] = inv_freq[np.arange(96) % 16]
    c[0:96, 354] = np.where((np.arange(96) % 32) < 16, -1.0, 1.0)
    return c


_SW = (np.arange(32) + 16) % 32


def prep_tok_common(inp, L, core):
    b, q = core // 4, core % 4
    s0 = q * NTOK
    m = {
        "pT": _c(inp["p"][L, b, s0:s0 + NTOK, :].T),
        "consts": _consts(),
        "w_gate": _c(inp["w_gate"][L]), "w_up": _c(inp["w_up"][L]), "w_down": _c(inp["w_down"][L]),
        "w_ple_gate": _c(inp["w_ple_gate"][L]), "w_ple_proj": _c(inp["w_ple_proj"][L]),
    }
    return m


def prep_A(inp):
    maps = []
    x = inp["x"]
    ev_w_in = inp["ev_w_in"][0]
    cols = []
    for cc in range(4):
        for base in (0, 512, 1024):
            cols.append(np.arange(base + cc * 128, base + (cc + 1) * 128))
    cols.append(np.arange(1536, 2560))
    ev_w_in_r = _c(ev_w_in[:, np.concatenate(cols)])
    od_w_in = inp["od_w_in"][0]
    ocols = np.concatenate([np.arange(0, 672), 640 + _SW, np.arange(672, 928), np.arange(928, 1184), np.arange(2208, 2216),
                            np.arange(1184, 1696), np.arange(1696, 2208)])
    od_ext = _c(od_w_in[:, ocols])
    wq = inp["od_w_q_up"][0]
    qcols = np.arange(768).reshape(8, 96).copy()
    qcols[:, 64:96] = qcols[:, 64:96][:, _SW]
    wq_sw = _c(wq[:, qcols.reshape(-1)])
    wkv = inp["od_w_kv_up"][0].reshape(256, 8, 128)
    wkv_r = _c(np.concatenate([wkv[:, :, :64].reshape(256, 512), wkv[:, :, 64:].reshape(256, 512)], axis=1))
    gq, gk = inp["od_g_q"][0], inp["od_g_k"][0]
    gsm = np.zeros((128, 8), np.float32)
    gsm[0:96, 0] = gq
    gsm[64:96, 1] = gq[64:96][_SW]
    gsm[0:64, 2] = gk[:64]; gsm[64:128, 2] = gk[:64]
    gsm[0:32, 4] = gk[64:96]; gsm[0:32, 5] = gk[64:96][_SW]
    g_lat = np.concatenate([_chunkT(inp["od_g_qa"][0]), _chunkT(inp["od_g_kva"][0])], axis=1)
    gains = np.zeros((128, 32), np.float32)
    gains[:, 0:8] = _chunkT(inp["g_mix"][0]); gains[:, 8:16] = _chunkT(inp["g_ffn"][0])
    gains[:, 16:24] = _chunkT(inp["g_ple"][0]); gains[:, 24:32] = _chunkT(inp["g_mix"][1])
    wconv = _c(inp["ev_w_conv"][0].T.reshape(4, 128, 3).transpose(1, 0, 2).reshape(128, 12))
    gvb = _c(np.tile(inp["ev_g_v"][0].reshape(1, 512), (128, 1)))
    wsT = _c(inp["ev_w_s"][0].transpose(2, 0, 1))
    bsb = _c(inp["ev_b_s"][0].reshape(4, 2, 1, 128).repeat(64, axis=2).reshape(4, 128, 128).transpose(1, 0, 2))
    for core in range(8):
        b, q = core // 4, core % 4
        s0 = q * NTOK
        m = prep_tok_common(inp, 0, core)
        m["hT"] = _c(x[b, s0:s0 + NTOK, :].T)
        m["xhT"] = _c(x[b, s0 - 2:s0, :].T) if q > 0 else np.zeros((1024, 2), np.float32)
        m["posb"] = _c(np.tile(inp["positions"][b, s0:s0 + NTOK].reshape(1, NTOK), (96, 1)).astype(np.int32))
        m.update({"gains": gains, "w_out": _c(inp["ev_w_out"][0]), "ev_w_in": ev_w_in_r, "wconv": wconv, "gvb": gvb, "wsT": wsT,
                  "bsb": bsb, "od_w_in": od_ext, "g_lat": _c(g_lat), "w_q_up": _c(wq), "w_q_up_sw": wq_sw, "w_kv_up": wkv_r, "gsm": gsm})
        maps.append(m)
    return maps


def prep_C(inp, h1T, ymT):
    maps = []
    gains = np.zeros((128, 32), np.float32)
    gains[:, 8:16] = _chunkT(inp["g_ffn"][1]); gains[:, 16:24] = _chunkT(inp["g_ple"][1])
    for core in range(8):
        m = prep_tok_common(inp, 1, core)
        m["hT"] = h1T[core]
        m["ymT"] = ymT[core]
        m["gains"] = gains
        m["w_out"] = _c(inp["od_w_out"][0])
        maps.append(m)
    return maps


SEQ = 8192


def build_mix(parts, fz):
    nc = fz["nc"]
    P = Prog(nc, ("M_" if "m" in parts else "T_"))
    C = Ctx(P, nbanks=6)
    op = P.op
    D = fz["D"]
    ptb = [P.ps("ptb%d" % i, [128, 1024], BF16) for i in range(2)]
    idx = P.sb("idx", [128, 64], I32)
    P.dma("sp", idx[:, :], D["idx"], writes=[idx])
    R1 = D["recv1"]

    def gat(t, rows, q, col):
        for hf in range(2):
            c0_ = q * NTOK + hf * 1024
            P.gather(t, t[0:rows, c0_:c0_ + 1024], R1[hf][:, :], idx, idx[0:rows, col:col + 1])
    send2 = D["send2"]

    cB = P.sb("cB", [128, 512])
    bcol = P.sb("bcol", [128, 2]); gmh = P.sb("gmh", [128, 128])
    P.dma("sp", cB[:, :], D["cB"], writes=[cB]); P.dma("sp", bcol[:, :], D["bcol"], writes=[bcol]); P.dma("sp", gmh[:, :], D["gmh"], writes=[gmh])
    maskb = P.sb("maskb", [128, 128], BF16)
    op("dve", lambda e: e.tensor_copy(maskb[:, :], cB[:, 0:128]), reads=[cB], writes=[maskb])
    idb = P.sb("idb", [128, 128], BF16)
    op("dve", lambda e: e.tensor_copy(idb[:, :], cB[:, 256:384]), reads=[cB], writes=[idb])
    ones_f = P.sb("ones_f", [128, 128])
    op("pool", lambda e: e.memset(ones_f[:, :], 1.0), writes=[ones_f])

    if True:
        pass
    if "s" in parts:
        gi = P.sb("gi", [128, 64]); gf = P.sb("gf", [128, 64])
        rmv = D["recvm"].rearrange("r (c t) -> (r c) t", t=64)
        P.gather(gi, gi[:, :], rmv, idx, idx[:, 40:41])
        P.gather(gf, gf[:, :], rmv, idx, idx[:, 41:42])
        zer = P.sb("zer", [128, 128]); op("pool", lambda e: e.memset(zer[:, :], 0.0), writes=[zer])
        nbf = P.sb("nbf", [128, 1])
        op("dve", lambda e: e.tensor_scalar(nbf[:, :], bcol[:, 1:2], -1.0, None, op0=ALU.mult), reads=[bcol], writes=[nbf])
        e1 = P.sb("e1", [128, 64]); lf = P.sb("lf", [128, 64]); Al = P.sb("Al", [128, 64]); vv = P.sb("vv", [128, 64])
        Ml = P.sb("Ml", [128, 64]); Mp = P.sb("Mp", [128, 64])
        op("act", lambda e: e.activation(e1[:, :], gf[:, :], AF.Exp, bias=nbf[:, 0:1], scale=-1.0), reads=[gf, nbf], writes=[e1])
        op("act", lambda e: e.activation(lf[:, :], e1[:, :], AF.Ln, bias=1.0), reads=[e1], writes=[lf])
        op("dve", lambda e: e.tensor_scalar(lf[:, :], lf[:, :], -1.0, None, op0=ALU.mult), reads=[lf], writes=[lf])
        op("dve", lambda e: e.tensor_tensor_scan(Al[:, :], lf[:, :], zer[:, 0:64], 0.0, op0=ALU.add, op1=ALU.add), reads=[lf, zer], writes=[Al])

        def col2row(col_ap, reads):
            ps = C.psum()
            op("pe", lambda e: e.matmul(ps[0:1, 0:128], col_ap, cB[:, 256:384], start=True, stop=True), reads=reads + [cB], writes=[ps])
            return ps

        rows = P.sb("rows", [1, 8, 128])
        ps = col2row(Al[:, 63:64], [Al])
        op("dve", lambda e, ps=ps: e.tensor_copy(rows[:, 0, :], ps[0:1, 0:128]), reads=[ps], writes=[rows])
        op("dve", lambda e: e.tensor_tensor_scan(rows[:, 1, :], rows[:, 0, :], zer[0:1, :], 0.0, op0=ALU.add, op1=ALU.add), reads=[rows, zer], writes=[rows])
        op("dve", lambda e: e.tensor_tensor(rows[:, 2, :], rows[:, 1, :], rows[:, 0, :], op=ALU.subtract), reads=[rows], writes=[rows])
        cols_ps = C.psum(pin=True)

        def row2col(k, j):
            op("pe", lambda e: e.matmul(cols_ps[:, j:j + 1], rows[0:1, k, :], ones_f[0:1, 0:1], start=True, stop=True), reads=[rows, ones_f], writes=[cols_ps])

        row2col(2, 0)
        cols = P.sb("cols", [128, 8])
        op("dve", lambda e: e.tensor_copy(cols[:, 0:1], cols_ps[:, 0:1]), reads=[cols_ps], writes=[cols])
        op("dve", lambda e: e.tensor_scalar(Al[:, :], Al[:, :], cols[:, 0:1], None, op0=ALU.add), reads=[Al, cols], writes=[Al])
        op("dve", lambda e: e.scalar_tensor_tensor(vv[:, :], gi[:, :], bcol[:, 0:1], Al[:, :], op0=ALU.add, op1=ALU.subtract), reads=[gi, bcol, Al], writes=[vv])
        op("dve", lambda e: e.tensor_tensor_scan(Ml[:, :], vv[:, :], vv[:, :], -1e30, op0=ALU.max, op1=ALU.max), reads=[vv], writes=[Ml])
        ps = col2row(Ml[:, 63:64], [Ml])
        op("dve", lambda e, ps=ps: e.tensor_copy(rows[:, 3, :], ps[0:1, 0:128]), reads=[ps], writes=[rows])
        op("dve", lambda e: e.tensor_tensor_scan(rows[:, 4, :], rows[:, 3, :], rows[:, 3, :], 0.0, op0=ALU.max, op1=ALU.max), reads=[rows], writes=[rows])
        op("dve", lambda e: e.memset(rows[:, 5, 0:1], 0.0), reads=[], writes=[rows])
        op("dve", lambda e: e.tensor_copy(rows[:, 5, 1:128], rows[:, 4, 0:127]), reads=[rows], writes=[rows])
        r4 = rows[:, 4, :].rearrange("p (c two) -> p c two", two=2)
        r6 = rows[:, 6, :].rearrange("p (c two) -> p c two", two=2)
        op("dve", lambda e: e.tensor_copy(r6[:, :, 0:1], r4[:, :, 1:2]), reads=[rows], writes=[rows])
        op("dve", lambda e: e.tensor_copy(r6[:, :, 1:2], r4[:, :, 1:2]), reads=[rows], writes=[rows])
        op("dve", lambda e: e.tensor_tensor(rows[:, 7, :], rows[:, 5, :], rows[:, 4, :], op=ALU.subtract), reads=[rows], writes=[rows])
        op("act", lambda e: e.activation(rows[:, 7, :], rows[:, 7, :], AF.Exp), reads=[rows], writes=[rows])
        row2col(5, 1); row2col(4, 2); row2col(6, 3)
        op("dve", lambda e: e.tensor_copy(cols[:, 1:4], cols_ps[:, 1:4]), reads=[cols_ps], writes=[cols])
        C.unpin()
        op("dve", lambda e: e.tensor_scalar(Mp[:, :], Ml[:, :], cols[:, 1:2], 0.0, op0=ALU.max, op1=ALU.max), reads=[Ml, cols], writes=[Mp])
        dec_b = P.sb("dec_b", [64, 128])
        ps = C.psum()
        op("pe", lambda e, ps=ps: e.matmul(ps[0:64, 0:128], ones_f[0:1, 0:64], rows[0:1, 7, :], start=True, stop=True), reads=[ones_f, rows], writes=[ps])
        op("dve", lambda e, ps=ps: e.tensor_copy(dec_b[:, :], ps[0:64, 0:128]), reads=[ps], writes=[dec_b])
        fac = P.sb("fac", [128, 5, 128])
        tmpc = P.sb("tmpc", [128, 64])
        op("dve", lambda e: e.tensor_scalar(tmpc[:, :], vv[:, :], cols[:, 3:4], None, op0=ALU.subtract), reads=[vv, cols], writes=[tmpc])
        op("act", lambda e: e.activation(fac[:, 0, 0:64], tmpc[:, :], AF.Exp), reads=[tmpc], writes=[fac])
        op("dve", lambda e: e.tensor_scalar(tmpc[:, :], Mp[:, :], cols[:, 3:4], None, op0=ALU.subtract), reads=[Mp, cols], writes=[tmpc])
        op("act", lambda e: e.activation(fac[:, 1, 0:64], tmpc[:, :], AF.Exp, scale=-1.0), reads=[tmpc], writes=[fac])
        op("dve", lambda e: e.tensor_scalar(tmpc[:, :], Mp[:, :], cols[:, 1:2], None, op0=ALU.subtract), reads=[Mp, cols], writes=[tmpc])
        op("act", lambda e: e.activation(fac[:, 2, 0:64], tmpc[:, :], AF.Exp, scale=-1.0), reads=[tmpc], writes=[fac])
        op("dve", lambda e: e.tensor_scalar(tmpc[:, :], vv[:, :], cols[:, 2:3], None, op0=ALU.subtract), reads=[vv, cols], writes=[tmpc])
        op("act", lambda e: e.activation(fac[:, 3, 0:64], tmpc[:, :], AF.Exp), reads=[tmpc], writes=[fac])
        op("dve", lambda e: e.tensor_tensor(tmpc[:, :], Al[:, :], Mp[:, :], op=ALU.add), reads=[Al, Mp], writes=[tmpc])
        op("act", lambda e: e.activation(fac[:, 4, 0:64], tmpc[:, :], AF.Exp, scale=-1.0), reads=[tmpc], writes=[fac])
        op("dve", lambda e: e.tensor_copy(fac[:, :, 64:128], fac[:, :, 0:64]), reads=[fac], writes=[fac])
        fcol = P.sb("fcol", [128, 5, 64])
        for k in range(5):
            ps = C.psum()
            op("pe", lambda e, k=k, ps=ps: e.transpose(ps[:, 0:128], fac[:, k, :], cB[:, 256:384]), reads=[fac, cB], writes=[ps])
            pv = ps[:, 0:128].rearrange("p (c two) -> p c two", two=2)
            op("dve", lambda e, k=k, ps=ps: e.tensor_copy(fcol[0:64, k, :], ps[0:64, 0:128].rearrange("p (c two) -> p c two", two=2)[:, :, 0]), reads=[ps], writes=[fcol])
            op("dve", lambda e, k=k, ps=ps: e.tensor_copy(fcol[64:128, k, :], ps[64:128, 0:128].rearrange("p (c two) -> p c two", two=2)[:, :, 1]), reads=[ps], writes=[fcol])

    if "m" in parts:
        mq = P.sb("mq", [64, SEQ], BF16); mk = P.sb("mk", [64, SEQ], BF16)
        ktok = P.sb("ktok", [128, 64, 64], BF16); vext = P.sb("vext", [128, 64, 129], BF16); so = P.sb("so", [128, 64, 128], BF16)
        Call = P.sb("Call", [64, 128, 129], BF16); Sxs = [P.sb("Sx%d" % i, [64, 129]) for i in range(2)]
        mvs = P.sb("mvs", [128, SEQ], BF16); sos = P.sb("sos", [128, SEQ], BF16)
        for q in range(4):
            cs_ = slice(q * NTOK, (q + 1) * NTOK)
            gat(mq, 64, q, 24 + q); gat(mk, 64, q, 28 + q); gat(mvs, 128, q, 32 + q); gat(sos, 128, q, 36 + q)
        op("pool", lambda e: e.memset(vext[:, :, :], 1.0), writes=[vext])
        tcount = [0]
        def tr_group(src, rows, nblk_per, width, dst_fn, g):
            pt_ = ptb[tcount[0] % 2]; tcount[0] += 1
            for b_ in range(nblk_per):
                blk = g * nblk_per + b_
                op("pe", lambda e, pt_=pt_, b_=b_, blk=blk: e.transpose(pt_[:, b_ * width:(b_ + 1) * width], src[0:rows, blk * 128:(blk + 1) * 128], idb[0:rows, 0:rows]),
                   reads=[src, idb], writes=[pt_])
            dst_t, dst_ap = dst_fn(g)
            eng = "act" if tcount[0] % 2 == 0 else "dve"
            if eng == "act":
                op("act", lambda e, pt_=pt_, dst_ap=dst_ap: e.activation(dst_ap, pt_[:, 0:nblk_per * width].rearrange("p (b w) -> p b w", w=width), AF.Copy), reads=[pt_], writes=[dst_t])
            else:
                op("dve", lambda e, pt_=pt_, dst_ap=dst_ap: e.tensor_copy(dst_ap, pt_[:, 0:nblk_per * width].rearrange("p (b w) -> p b w", w=width)), reads=[pt_], writes=[dst_t])
        for g in range(4):
            tr_group(mk, 64, 16, 64, lambda g: (ktok, ktok[:, g * 16:(g + 1) * 16, :]), g)
        for g in range(8):
            tr_group(mvs, 128, 8, 128, lambda g: (vext, vext[:, g * 8:(g + 1) * 8, 0:128]), g)
            tr_group(sos, 128, 8, 128, lambda g: (so, so[:, g * 8:(g + 1) * 8, :]), g)
        op("pool", lambda e: e.memset(Sxs[0][:, :], 0.0), writes=[Sxs[0]])
        for blk in range(64):
            kw = C.tmp("kw", [128, 64], BF16, 3)
            op("dve", lambda e, blk=blk, kw=kw: e.tensor_scalar(kw[:, :], ktok[:, blk, :], fcol[:, 3, blk:blk + 1], None, op0=ALU.mult), reads=[ktok, fcol], writes=[kw])
            for half in range(2):
                c = 2 * blk + half
                Sa = Sxs[c % 2]; Sb = Sxs[(c + 1) % 2]
                op("act", lambda e, c=c, Sa=Sa: e.activation(Call[:, c, :], Sa[:, :], AF.Copy), reads=[Sa], writes=[Call])
                pu = C.psum()
                op("pe", lambda e, pu=pu, kw=kw, half=half, blk=blk: e.matmul(pu[0:64, 0:129], kw[64 * half:64 * half + 64, :], vext[64 * half:64 * half + 64, blk, :], start=True, stop=True),
                   reads=[kw, vext], writes=[pu])
                op("dve", lambda e, pu=pu, c=c, Sa=Sa, Sb=Sb: e.scalar_tensor_tensor(Sb[:, :], Sa[:, :], dec_b[:, c:c + 1], pu[0:64, 0:129], op0=ALU.mult, op1=ALU.add), reads=[Sa, dec_b, pu], writes=[Sb])
        if "2" in parts:
            def stX(blk):
                t0 = blk * 128
                pS = C.psum()
                op("pe", lambda e, pS=pS, t0=t0: e.matmul(pS[:, 0:128], mk[:, t0:t0 + 128], mq[:, t0:t0 + 128], start=True, stop=True), reads=[mk, mq], writes=[pS])
                pt = C.tmp("mpt", [128, 128], BF16, 3)
                op("dve", lambda e, pS=pS, pt=pt, blk=blk: e.scalar_tensor_tensor(pt[:, :], pS[:, 0:128], fcol[:, 0, blk:blk + 1], cB[:, 128:256], op0=ALU.mult, op1=ALU.mult), reads=[pS, fcol, cB], writes=[pt])
                return pt

            def stY(blk, pt):
                t0 = blk * 128
                po1 = C.psum(); po2 = C.psum()
                op("pe", lambda e, po1=po1, pt=pt, blk=blk: e.matmul(po1[:, 0:129], pt[:, :], vext[:, blk, :], start=True, stop=True), reads=[pt, vext], writes=[po1])
                for half in range(2):
                    op("pe", lambda e, po2=po2, half=half, blk=blk, t0=t0: e.matmul(po2[64 * half:64 * half + 64, 0:129], mq[:, t0 + 64 * half:t0 + 64 * half + 64], Call[:, 2 * blk + half, :], start=True, stop=True),
                       reads=[mq, Call], writes=[po2])
                o1 = C.tmp("o1", [128, 129], F32, 2); hnum = C.tmp("hnum", [128, 129], F32, 2)
                op("dve", lambda e, o1=o1, po1=po1, blk=blk: e.tensor_scalar(o1[:, :], po1[:, 0:129], fcol[:, 1, blk:blk + 1], None, op0=ALU.mult), reads=[po1, fcol], writes=[o1])
                op("dve", lambda e, o1=o1, po2=po2, hnum=hnum, blk=blk: e.scalar_tensor_tensor(hnum[:, :], po2[:, 0:129], fcol[:, 2, blk:blk + 1], o1[:, :], op0=ALU.mult, op1=ALU.add), reads=[po2, fcol, o1], writes=[hnum])
                sm = C.tmp("sm", [128, 8], F32, 2)
                op("dve", lambda e, sm=sm, hnum=hnum: e.scalar_tensor_tensor(sm[:, 6:7], hnum[:, 128:129], -1.0, hnum[:, 128:129], op0=ALU.mult, op1=ALU.max), reads=[hnum], writes=[sm])
                op("dve", lambda e, sm=sm, blk=blk: e.tensor_tensor(sm[:, 0:1], sm[:, 6:7], fcol[:, 4, blk:blk + 1], op=ALU.max), reads=[sm, fcol], writes=[sm])
                op("dve", lambda e, sm=sm: e.reciprocal(sm[:, 1:2], sm[:, 0:1]), reads=[sm], writes=[sm])
                junk = C.tmp("junk", [128, 128], F32, 2)
                op("dve", lambda e, hnum=hnum, sm=sm: e.tensor_scalar(hnum[:, 0:128], hnum[:, 0:128], sm[:, 1:2], None, op0=ALU.mult), reads=[hnum, sm], writes=[hnum])
                op("act", lambda e, junk=junk, hnum=hnum, sm=sm: e.activation(junk[:, :], hnum[:, 0:128], AF.Square, accum_out=sm[:, 2:3]), reads=[hnum], writes=[junk, sm])
                op("act", lambda e, sm=sm: e.activation(sm[:, 3:4], sm[:, 2:3], AF.Sqrt, bias=EPS, scale=1.0 / 128.0), reads=[sm], writes=[sm])
                op("dve", lambda e, sm=sm: e.reciprocal(sm[:, 4:5], sm[:, 3:4]), reads=[sm], writes=[sm])
                yt = C.tmp("yt", [128, 128], F32, 2); yd = C.tmp("yd", [128, 128], F32, 3)
                op("dve", lambda e, yt=yt, hnum=hnum, sm=sm: e.scalar_tensor_tensor(yt[:, :], hnum[:, 0:128], sm[:, 4:5], gmh[:, :], op0=ALU.mult, op1=ALU.mult), reads=[hnum, sm, gmh], writes=[yt])
                op("pool", lambda e, yt=yt, yd=yd, blk=blk: e.tensor_tensor(yd[:, :], yt[:, :], so[:, blk, :], op=ALU.mult), reads=[yt, so], writes=[yd])
                return yd

            zst = {"pT": None}

            def stZ(blk, yd):
                g4, j4 = blk // 4, blk % 4
                if j4 == 0:
                    zst["pT"] = C.psum(pin=True)
                pT = zst["pT"]
                op("pe", lambda e, pT=pT, yd=yd, j4=j4: e.transpose(pT[:, j4 * 128:(j4 + 1) * 128], yd[:, :], cB[:, 256:384]), reads=[yd, cB], writes=[pT])
                if j4 == 3:
                    ob = C.tmp("ydo", [128, 512], BF16, 2)
                    op("act", lambda e, ob=ob, pT=pT: e.activation(ob[:, :], pT[:, :], AF.Copy), reads=[pT], writes=[ob])
                    P.dma("sp", send2[(g4 // 4) * 256 + 128:(g4 // 4) * 256 + 256, (g4 % 4) * 512:(g4 % 4) * 512 + 512], ob[:, :], reads=[ob])
                    C.unpin()

            pts = {0: stX(0)}
            yds = {}
            for blk in range(64):
                if blk + 1 < 64:
                    pts[blk + 1] = stX(blk + 1)
                yds[blk] = stY(blk, pts.pop(blk))
                if blk >= 1:
                    stZ(blk - 1, yds.pop(blk - 1))
            stZ(63, yds.pop(63))

    if "a" in parts:
        qTs = [P.sb("qT%d" % i, [96, SEQ], BF16) for i in range(2)]; kTs = [P.sb("kT%d" % i, [96, SEQ], BF16) for i in range(2)]
        V = P.sb("V", [128, 64, 65], BF16); vTss = [P.sb("vTs%d" % i, [64, SEQ], BF16) for i in range(2)]
        for hh in range(2):
            for q in range(4):
                cs_ = slice(q * NTOK, (q + 1) * NTOK)
                gat(qTs[hh], 96, q, hh * 4 + q); gat(kTs[hh], 96, q, 8 + hh * 4 + q); gat(vTss[hh], 64, q, 16 + hh * 4 + q)
        tails = []
        for hh in range(2):
            qT = qTs[hh]; kT = kTs[hh]; vTs = vTss[hh]
            op("pool", lambda e: e.memset(V[:, :, :], 1.0), writes=[V])
            for g in range(4):
                pt_ = ptb[g % 2]
                for b_ in range(16):
                    blk = g * 16 + b_
                    op("pe", lambda e, pt_=pt_, b_=b_, blk=blk, vTs=vTs: e.transpose(pt_[:, b_ * 64:(b_ + 1) * 64], vTs[0:64, blk * 128:(blk + 1) * 128], idb[0:64, 0:64]), reads=[vTs, idb], writes=[pt_])
                op("dve", lambda e, pt_=pt_, g=g: e.tensor_copy(V[:, g * 16:(g + 1) * 16, 0:64], pt_[:, :].rearrange("p (b w) -> p b w", w=64)), reads=[pt_], writes=[V])
            for qt in range(16):
                po = C.psum(pin=True)
                nkb = 4 * qt + 4
                LOOK = 3
                pend = []
                for it in range(nkb + LOOK):
                    if it < nkb:
                        kb = it
                        c0 = max(0, kb - 4 * qt) * 128
                        ps = C.psum()
                        op("pe", lambda e, ps=ps, kb=kb, c0=c0, qt=qt, kT=kT, qT=qT: e.matmul(ps[:, c0:512], kT[:, kb * 128:(kb + 1) * 128], qT[:, qt * 512 + c0:qt * 512 + 512], start=True, stop=True), reads=[kT, qT], writes=[ps])
                        pt = C.tmp("apt", [128, 512], BF16, 6)
                        op("act", lambda e, ps=ps, pt=pt, c0=c0: e.activation(pt[:, c0:512], ps[:, c0:512], AF.Exp), reads=[ps], writes=[pt])
                        if kb >= 4 * qt:
                            op("dve", lambda e, pt=pt, c0=c0: e.tensor_tensor(pt[:, c0:c0 + 128], pt[:, c0:c0 + 128], maskb[:, :], op=ALU.mult), reads=[pt, maskb], writes=[pt])
                        pend.append((kb, c0, pt))
                    if it == 2 and tails:
                        tails.pop(0)()
                    if it >= LOOK:
                        kb, c0, pt = pend[it - LOOK]
                        op("pe", lambda e, po=po, pt=pt, kb=kb, c0=c0, nkb=nkb: e.matmul(po[0:65, c0:512], V[:, kb, :], pt[:, c0:512], start=(kb == 0), stop=(kb == nkb - 1)), reads=[V, pt], writes=[po])
                osb = C.tmp("osb", [65, 512], F32, 2, own=True)
                op("act", lambda e, osb=osb, po=po: e.activation(osb[:, :], po[0:65, :], AF.Copy), reads=[po], writes=[osb])
                C.unpin()
                op("dve", lambda e, osb=osb: e.reciprocal(osb[64:65, :], osb[64:65, :]), reads=[osb], writes=[osb])

                def tail(osb=osb, hh=hh, qt=qt):
                    pb = C.psum()
                    op("pe", lambda e, pb=pb, osb=osb: e.matmul(pb[0:64, :], ones_f[64:65, 0:64], osb[64:65, :], start=True, stop=True), reads=[ones_f, osb], writes=[pb])
                    yc = C.tmp("yc", [64, 512], BF16, 2)
                    op("dve", lambda e, yc=yc, osb=osb, pb=pb: e.tensor_tensor(yc[:, :], osb[0:64, :], pb[0:64, :], op=ALU.mult), reads=[osb, pb], writes=[yc])
                    P.dma("sp", send2[(qt // 4) * 256 + hh * 64:(qt // 4) * 256 + hh * 64 + 64, (qt % 4) * 512:(qt % 4) * 512 + 512], yc[:, :], reads=[yc])
                tails.append(tail)
        while tails:
            tails.pop(0)()
    P.wait_all("sp", P.tiles)
    return P


R1ROWS = 3360
Q0, KN0, KP0, V0, MQ0, MK0, MV0, SO0 = 0, 768, 1280, 1312, 1824, 2080, 2336, 2848
GROUPS = [[0, 1, 2, 3], [4, 5, 6, 7]]
CCH = 240
CCH1 = 480


def _chunks(total, cch=CCH):
    out = []
    start = 0
    base = 0
    while start < total:
        size = min(cch, total - start)
        out.append((start, size, base))
        base += 4 * size
        start += size
    return out


def _rowmap(total, r, q, cch=CCH):
    k = r // cch
    start = k * cch
    size = np.minimum(cch, total - start)
    return 4 * start + q * size + (r - start)


def _gather_all(P, send, recv, total, cch=CCH, wait=True):
    keys = []
    for (start, size, base) in _chunks(total, cch):
        keys.append(P.collective_raw("AllGather", send[start:start + size, :], recv[base:base + 4 * size, :], GROUPS, wait=False))
    if wait:
        P.ops["pool"].append(([(k, 1) for k in keys], None, None))
    return keys


def build_fused(phases="AMTC"):
    nc = bass.Bass("TRN2", target_bir_lowering=False)
    send1 = [nc.dram_tensor("send1_%d" % i, [R1ROWS, 1024], BF16).ap() for i in range(2)]
    recv1 = [nc.dram_tensor("recv1_%d" % i, [4 * R1ROWS, 1024], BF16).ap() for i in range(2)]
    sendm = nc.dram_tensor("sendm", [8, NTOK], F32).ap()
    recvm = nc.dram_tensor("recvm", [32, NTOK], F32).ap()
    send2 = nc.dram_tensor("send2", [1024, NTOK], BF16).ap()
    recv2 = nc.dram_tensor("recv2", [4096, NTOK], BF16).ap()
    h1d = nc.dram_tensor("h1d", [1024, NTOK], F32).ap()
    idx = dram_in(nc, "idx", [128, 64], I32)
    consts = dram_in(nc, "consts", [128, 512])

    def tokD(L):
        sfx = str(L)
        return {
            "pT": dram_in(nc, "pT" + sfx, [256, NTOK]), "gains": dram_in(nc, "gains" + sfx, [128, 32]), "consts": consts,
            "w_gate": dram_in(nc, "w_gate" + sfx, [1024, 2816]), "w_up": dram_in(nc, "w_up" + sfx, [1024, 2816]),
            "w_down": dram_in(nc, "w_down" + sfx, [2816, 1024]), "w_ple_gate": dram_in(nc, "w_ple_gate" + sfx, [1024, 1024]),
            "w_ple_proj": dram_in(nc, "w_ple_proj" + sfx, [256, 1024]), "w_out": dram_in(nc, "w_out" + sfx, [1024, 1024]),
        }
    DA = tokD(0)
    DA.update({
        "hT": dram_in(nc, "xT", [1024, NTOK]), "h1T": h1d,
        "mifT": sendm,
    })
    for hf in range(2):
        for nm, r0, nr in (("qTf", Q0, 768), ("knT", KN0, 512), ("kpeT", KP0, 32), ("vT", V0, 512), ("mqT", MQ0, 256), ("mkT", MK0, 256),
                           ("mvT", MV0, 512), ("soT", SO0, 512)):
            DA["%s_h%d" % (nm, hf)] = send1[hf][r0:r0 + nr, :]
    for nm in ("qT", "knT", "kpeT", "vT", "mqT", "mkT", "mvT", "soT"):
        DA[nm] = send1[0]
    keys1 = []

    def mid_hook(P):
        P.wait_all("pool", P.tiles)
        keys1.extend(_gather_all(P, send1[0], recv1[0], R1ROWS, CCH1, wait=False))
    PA = build_tok("A", {"nc": nc, "D": DA, "mid_hook": mid_hook})
    PA.wait_all("pool", PA.tiles)
    keys1.extend(_gather_all(PA, send1[1], recv1[1], R1ROWS, CCH1, wait=False))
    PA.ops["pool"].append(([(k, 1) for k in keys1], None, None))
    PA.collective_raw("AllGather", sendm, recvm, GROUPS)
    PA.emit()
    nc.all_engine_barrier()
    if "M" not in phases and "T" not in phases and "C" not in phases:
        dram_out(nc, "outT", [1024, NTOK])
        return nc

    DB = {"idx": idx, "recv1": recv1, "recvm": recvm, "send2": send2,
          "cB": dram_in(nc, "cB", [128, 512]), "bcol": dram_in(nc, "bcol", [128, 2]), "gmh": dram_in(nc, "gmh", [128, 128])}
    if "M" in phases:
        PM = build_mix("sm2", {"nc": nc, "D": DB})
        PM.wait_all("pool", PM.tiles)
        PM.emit()
        nc.all_engine_barrier()
    if "T" in phases:
        PT = build_mix("a", {"nc": nc, "D": DB})
        _gather_all(PT, send2, recv2, 1024)
        PT.emit()
        nc.all_engine_barrier()
    if "C" not in phases:
        dram_out(nc, "outT", [1024, NTOK])
        return nc

    DC = tokD(1)
    DC.update({"hT": h1d, "idx": idx, "recv2": recv2, "outT": dram_out(nc, "outT", [1024, NTOK])})
    if "D" in phases:
        DC["dbg_y"] = dram_out(nc, "dbg_y", [1024, NTOK], BF16)
    PC = build_tok("C", {"nc": nc, "D": DC})
    PC.wait_all("pool", PC.tiles)
    if "D" in phases:
        d1 = dram_out(nc, "dbg_send1", [R1ROWS, NTOK], BF16); d2 = dram_out(nc, "dbg_send2", [1024, NTOK], BF16); d3 = dram_out(nc, "dbg_sendm", [8, NTOK]); d4 = dram_out(nc, "dbg_h1", [1024, NTOK])
        dummy = PC.sb("dbgdummy", [1, 8])
        PC.dma("sp", d1, send1, reads=[dummy]); PC.dma("sp", d2, send2, reads=[dummy]); PC.dma("sp", d3, sendm, reads=[dummy]); PC.dma("sp", d4, h1d, reads=[dummy])
        PC.wait_all("sp", [dummy])
    PC.emit()
    return nc


def prep_fused(inp):
    mapsA = prep_A(inp)
    maps = []
    cB = np.zeros((128, 512), np.float32)
    s = np.arange(128)
    cB[:, 0:128] = (s[:, None] <= s[None, :])
    cB[:, 128:256] = (s[:, None] <= s[None, :]) & ((s[:, None] // 64) == (s[None, :] // 64))
    cB[:, 256:384] = np.eye(128, dtype=np.float32)
    gains1 = np.zeros((128, 32), np.float32)
    gains1[:, 8:16] = _chunkT(inp["g_ffn"][1]); gains1[:, 16:24] = _chunkT(inp["g_ple"][1])
    bg = inp["od_b_gate"][0]
    p = np.arange(128)
    for core in range(8):
        b, j = core // 4, core % 4
        q_ = j
        a = mapsA[core]
        m = {"xT": a["hT"], "xhT": a["xhT"], "posb": a["posb"], "consts": a["consts"], "idx": None,
             "pT0": a["pT"], "gains0": a["gains"], "w_gate0": a["w_gate"], "w_up0": a["w_up"], "w_down0": a["w_down"],
             "w_ple_gate0": a["w_ple_gate"], "w_ple_proj0": a["w_ple_proj"], "w_out0": a["w_out"]}
        for k in ("ev_w_in", "wconv", "gvb", "wsT", "bsb", "od_w_in", "g_lat", "w_q_up", "w_q_up_sw", "w_kv_up", "gsm"):
            m[k] = a[k]
        c1 = prep_tok_common(inp, 1, core)
        m.update({"pT1": c1["pT"], "gains1": gains1, "w_gate1": c1["w_gate"], "w_up1": c1["w_up"], "w_down1": c1["w_down"],
                  "w_ple_gate1": c1["w_ple_gate"], "w_ple_proj1": c1["w_ple_proj"], "w_out1": _c(inp["od_w_out"][0])})
        m["cB"] = cB
        m["bcol"] = _c(np.tile(np.array([[bg[j], bg[4 + j]]], np.float32), (128, 1)))
        m["gmh"] = _c(np.tile(inp["od_g_mh"][0][j].reshape(1, 128), (128, 1)))
        ix = np.zeros((128, 64), np.int32)
        def rm(r, q):
            return _rowmap(R1ROWS, np.asarray(r), q, CCH1)
        for q in range(4):
            for hh in range(2):
                hd = 2 * j + hh
                ix[0:96, hh * 4 + q] = rm(Q0 + hd * 96 + p[0:96], q)
                ix[0:64, 8 + hh * 4 + q] = rm(KN0 + hd * 64 + p[0:64], q)
                ix[64:96, 8 + hh * 4 + q] = rm(KP0 + p[0:32], q)
                ix[0:64, 16 + hh * 4 + q] = rm(V0 + hd * 64 + p[0:64], q)
            ix[0:64, 24 + q] = rm(MQ0 + j * 64 + p[0:64], q)
            ix[0:64, 28 + q] = rm(MK0 + j * 64 + p[0:64], q)
            ix[:, 32 + q] = rm(MV0 + j * 128 + p, q)
            ix[:, 36 + q] = rm(SO0 + j * 128 + p, q)
            ix[q * 32:(q + 1) * 32, 40] = (q * 8 + j) * 32 + p[0:32]
            ix[q * 32:(q + 1) * 32, 41] = (q * 8 + 4 + j) * 32 + p[0:32]
        for c in range(8):
            if c < 4:
                ix[:, 48 + c] = _rowmap(1024, q_ * 256 + p, c)
            else:
                ix[:, 48 + c] = _rowmap(1024, q_ * 256 + 128 + p, c - 4)
        m["idx"] = ix
        maps.append(m)
    return maps


_NC_CACHE = {}


def kernel(**inputs):
    inp = {k: np.asarray(v) for k, v in inputs.items()}
    if "nc" not in _NC_CACHE:
        _NC_CACHE["nc"] = build_fused()
    res = run_bass_kernel_spmd(_NC_CACHE["nc"], prep_fused(inp), core_ids=list(range(8))).results
    out = np.zeros((2, SEQ, 1024), np.float32)
    for core in range(8):
        b, q = core // 4, core % 4
        out[b, q * NTOK:(q + 1) * NTOK, :] = res[core]["outT"].T
    return out
```

```python
import numpy as np
import ml_dtypes
import concourse.bass as bass
import concourse.mybir as mybir
from concourse.bass_utils import run_bass_kernel_spmd

F32 = mybir.dt.float32
BF16 = mybir.dt.bfloat16
I32 = mybir.dt.int32
AF = mybir.ActivationFunctionType
ALU = mybir.AluOpType
AX = mybir.AxisListType
EPS = 1e-6
TWO_PI = 6.283185307179586
PI = 3.141592653589793
NTOK = 2048
TS = 512
class TT:
    __slots__ = ("h", "name", "last_w", "readers", "dsem", "dcount")

    def __init__(self, h, name):
        self.h = h
        self.name = name
        self.last_w = None
        self.readers = []
        self.dsem = None
        self.dcount = 0

    def __getitem__(self, idx):
        return self.h[idx]


class Prog:
    ENGS = ("pe", "act", "dve", "pool", "sp")

    def __init__(self, nc, prefix=""):
        self.nc = nc
        self.prefix = prefix
        self.ops = {e: [] for e in self.ENGS}
        self.count = {e: 0 for e in self.ENGS}
        self.waited = {e: {} for e in self.ENGS}
        self.sem_names = ["eng_" + e for e in self.ENGS]
        self.tiles = []
        self._ctx = []
        self.n_dsem = 0

    def sb(self, name, shape, dt=F32):
        g = self.nc.sbuf_tensor(self.prefix + "s_" + name, list(shape), dt)
        h = g.__enter__()
        self._ctx.append(g)
        t = TT(h, name)
        self.tiles.append(t)
        return t

    def ps(self, name, shape, dt=F32):
        g = self.nc.psum_tensor(self.prefix + "p_" + name, list(shape), dt)
        h = g.__enter__()
        self._ctx.append(g)
        t = TT(h, name)
        self.tiles.append(t)
        return t

    def _deps(self, eng, reads, writes):
        deps = []
        for r in reads:
            if r.last_w is not None:
                deps.append(r.last_w)
        for w in writes:
            if w.last_w is not None:
                deps.append(w.last_w)
            deps.extend(w.readers)
        waits = []
        wd = self.waited[eng]
        best = {}
        for (k, v) in deps:
            if eng == "pe" and k == "eng_pe":
                continue
            if wd.get(k, 0) >= v:
                continue
            if best.get(k, 0) < v:
                best[k] = v
        for k, v in best.items():
            wd[k] = v
            waits.append((k, v))
        return waits

    def op(self, eng, fn, reads=(), writes=()):
        waits = self._deps(eng, reads, writes)
        self.count[eng] += 1
        me = ("eng_" + eng, self.count[eng])
        self.ops[eng].append((waits, fn, (me[0], 1)))
        for r in reads:
            r.readers.append(me)
        for w in writes:
            w.last_w = me
            w.readers = []
        return me

    def dma(self, eng, out_ap, in_ap, reads=(), writes=(), **kw):
        waits = self._deps(eng, reads, writes)
        owner = (list(writes) + list(reads))[0]
        if owner.dsem is None:
            owner.dsem = "d%d" % self.n_dsem
            self.n_dsem += 1
            self.sem_names.append(owner.dsem)
        owner.dcount += 1
        me = (owner.dsem, 16 * owner.dcount)

        def fn(e, out_ap=out_ap, in_ap=in_ap, kw=kw):
            o_ = out_ap() if callable(out_ap) else out_ap
            i_ = in_ap() if callable(in_ap) else in_ap
            return e.dma_start(out=o_, in_=i_, **kw)
        self.ops[eng].append((waits, fn, (owner.dsem, 16)))
        for r in reads:
            r.readers.append(me)
        for w in writes:
            w.last_w = me
            w.readers = []
        return me

    def raw(self, eng, fn):
        self.ops[eng].append(([], fn, "raw"))

    def dram(self, name, shape, dt=F32):
        h = self.nc.dram_tensor(name, list(shape), dt).ap()
        t = TT(h, name)
        self.tiles.append(t)
        return t

    def collective(self, kind, in_t, out_t, groups):
        waits = self._deps("pool", [in_t], [out_t])
        key = "cc%d" % self.n_dsem
        self.n_dsem += 1
        self.sem_names.append(key)
        me = (key, 1)

        def fn(e):
            return e.collective_compute(kind, ALU.bypass, replica_groups=groups, ins=[in_t.h.opt()], outs=[out_t.h.opt()])
        self.ops["pool"].append((waits, fn, (key, None)))
        in_t.readers.append(me)
        out_t.last_w = me
        out_t.readers = []
        return me

    def gather(self, out_t, out_ap, in_ap, idx_t, idx_ap):
        waits = self._deps("pool", [idx_t], [out_t])
        if out_t.dsem is None:
            out_t.dsem = "d%d" % self.n_dsem
            self.n_dsem += 1
            self.sem_names.append(out_t.dsem)
        out_t.dcount += 1
        me = (out_t.dsem, 16 * out_t.dcount)

        def fn(e):
            return e.indirect_dma_start(out=out_ap, out_offset=None, in_=in_ap, in_offset=bass.IndirectOffsetOnAxis(ap=idx_ap, axis=0))
        self.ops["pool"].append((waits, fn, (out_t.dsem, 16)))
        idx_t.readers.append(me)
        out_t.last_w = me
        out_t.readers = []
        return me

    def collective_raw(self, kind, in_ap, out_ap, groups, wait=True):
        self.wait_all("pool", self.tiles)
        key = "cc%d" % self.n_dsem
        self.n_dsem += 1
        self.sem_names.append(key)

        def fn(e):
            return e.collective_compute(kind, ALU.bypass, replica_groups=groups, ins=[in_ap.opt()], outs=[out_ap.opt()])
        self.ops["pool"].append(([], fn, (key, None)))
        if wait:
            self.ops["pool"].append(([(key, 1)], None, None))
        return key

    def wait_all(self, eng, tiles):
        deps = []
        for t in tiles:
            if t.last_w is not None:
                deps.append(t.last_w)
            deps.extend(t.readers)
        wd = self.waited[eng]
        best = {}
        for k, v in deps:
            if wd.get(k, 0) < v and best.get(k, 0) < v:
                best[k] = v
        waits = []
        for k, v in best.items():
            wd[k] = v
            waits.append((k, v))
        self.ops[eng].append((waits, None, None))

    def emit(self):
        nc = self.nc
        sems = {}
        for n in self.sem_names:
            sems[n] = nc.alloc_semaphore(name=self.prefix + n)
        blk = nc.Block()
        block = blk.__enter__()

        def run(engname):
            def body(e):
                for waits, fn, inc in self.ops[engname]:
                    for k, v in waits:
                        e.wait_ge(sems[k], v)
                    if fn is not None:
                        ins = fn(e)
                        if inc == "raw":
                            continue
                        if inc[1] is None:
                            ins.then_inc(sems[inc[0]])
                        else:
                            ins.then_inc(sems[inc[0]], inc[1])
            return body

        block.tensor(run("pe"))
        block.scalar(run("act"))
        block.vector(run("dve"))
        block.gpsimd(run("pool"))
        block.sync(run("sp"))
        blk.__exit__(None, None, None)
        nc.all_engine_barrier()
        nc.clear_and_free_semaphores(list(sems.values()))
        nc.all_engine_barrier()
        for g in reversed(self._ctx):
            g.__exit__(None, None, None)
        self._ctx = []


class TV:
    def __init__(self, base, ap):
        self.__dict__["base"] = base
        self.__dict__["ap"] = ap

    def __getitem__(self, idx):
        return self.ap[idx]

    def __getattr__(self, k):
        return getattr(self.base, k)

    def __setattr__(self, k, v):
        setattr(self.base, k, v)


class Ctx:
    def __init__(self, P, nbanks=8):
        self.P = P
        self.nb = nbanks
        self.pbanks = [P.ps("pb%d" % i, [128, 512], F32) for i in range(nbanks)]
        self.pi = 0
        self.pinned = set()
        self.rings = {}

    def psum(self, pin=False):
        while (self.pi % self.nb) in self.pinned:
            self.pi += 1
        t = self.pbanks[self.pi % self.nb]
        if pin:
            self.pinned.add(self.pi % self.nb)
        self.pi += 1
        return t

    def unpin(self):
        self.pinned = set()

    def tmp(self, key, shape, dt=F32, n=2, own=False):
        if dt == F32 and len(shape) == 2 and shape[1] == 512 and not own:
            base = self.tmp("T32", [128, 512, 1], F32, 8)
            return TV(base, base.h[0:shape[0], :, 0])
        if key not in self.rings:
            self.rings[key] = [[self.P.sb("%s_%d" % (key, i), shape, dt) for i in range(n)], 0]
        r = self.rings[key]
        t = r[0][r[1] % len(r[0])]
        r[1] += 1
        return t


def dram_in(nc, name, shape, dt=F32):
    return nc.dram_tensor(name, list(shape), dt, kind="ExternalInput").ap()


def dram_out(nc, name, shape, dt=F32):
    return nc.dram_tensor(name, list(shape), dt, kind="ExternalOutput").ap()


def build_tok(mode, fz=None):
    nc = fz["nc"] if fz else bass.Bass("TRN2", target_bir_lowering=False)
    P = Prog(nc, mode + "_")
    C = Ctx(P)
    op = P.op
    L = 0 if mode == "A" else 1

    D = dict(fz["D"]) if fz else {}
    def din(name, shape, dt=F32):
        if name not in D:
            D[name] = dram_in(nc, name, shape, dt)
        return D[name]
    def dout(name, shape, dt=F32):
        if name not in D:
            D[name] = dram_out(nc, name, shape, dt)
        return D[name]

    din("hT", [1024, NTOK])
    din("pT", [256, NTOK])
    din("gains", [128, 32])
    din("consts", [128, 512])
    din("w_gate", [1024, 2816]); din("w_up", [1024, 2816]); din("w_down", [2816, 1024])
    din("w_ple_gate", [1024, 1024]); din("w_ple_proj", [256, 1024])
    din("w_out", [1024, 1024])
    if mode == "A":
        din("xhT", [1024, 2])
        din("posb", [96, NTOK], I32)
        din("ev_w_in", [1024, 2560])
        din("wconv", [128, 12]); din("gvb", [128, 512]); din("wsT", [128, 8, 128]); din("bsb", [128, 4, 128])
        din("od_w_in", [1024, 2248])
        din("g_lat", [128, 5])
        din("w_q_up", [384, 768]); din("w_q_up_sw", [384, 768]); din("w_kv_up", [256, 1024])
        din("gsm", [128, 8])
        dout("h1T", [1024, NTOK])
        dout("qT", [8, 96, NTOK], BF16); dout("knT", [512, NTOK], BF16); dout("kpeT", [32, NTOK], BF16)
        dout("vT", [512, NTOK], BF16)
        dout("mqT", [256, NTOK], BF16); dout("mkT", [256, NTOK], BF16)
        dout("mvT", [512, NTOK], BF16); dout("soT", [512, NTOK], BF16)
        dout("mifT", [8, NTOK])
    else:
        if not fz:
            din("ymT", [1024, NTOK], BF16)
        else:
            idxc = P.sb("idxc", [128, 64], I32)
            P.dma("sp", idxc[:, :], D["idx"], writes=[idxc])
            ystg = [P.sb("ystg%d" % c, [128, NTOK], BF16) for c in range(8)]
            for c in range(8):
                P.gather(ystg[c], ystg[c][:, :], D["recv2"][:, :], idxc, idxc[:, 48 + c:49 + c])
                if "dbg_y" in D:
                    P.dma("sp", D["dbg_y"][c * 128:(c + 1) * 128, :], ystg[c][:, :], reads=[ystg[c]])
        dout("outT", [1024, NTOK])

    h = [[P.sb("h%d_%d" % (s, c), [128, TS]) for c in range(8)] for s in range(2)]
    hn = [[P.sb("hn%d_%d" % (s, c), [128, TS], BF16) for c in range(8)] for s in range(2)]
    act = [[P.sb("act%d_%d" % (s, j), [128, TS], BF16) for j in range(22)] for s in range(2)]
    y = [a[0:8] for a in act]
    ringA = [P.sb("wA%d" % i, [128, 8, 512], BF16) for i in range(4)]
    ringD = [P.sb("wD%d" % i, [128, 22, 128], BF16) for i in range(2)]
    st = {"a": 0, "d": 0}
    wpp = P.sb("wpp", [128, 2, 1024], BF16)
    gains = P.sb("gains", [128, 32])
    cst = P.sb("cst", [128, 512])
    ones_bf = P.sb("ones_bf", [128, 128], BF16)
    P.dma("sp", gains[:, :], D["gains"], writes=[gains])
    P.dma("sp", cst[:, :], D["consts"], writes=[cst])
    op("pool", lambda e: e.memset(ones_bf[:, :], 1.0), writes=[ones_bf])
    P.dma("pool", wpp[:, :, :], D["w_ple_proj"].rearrange("(kc p) n -> p kc n", p=128), writes=[wpp])

    def slotA():
        t = ringA[st["a"] % 4]; st["a"] += 1; return t

    def slotD():
        t = ringD[st["d"] % 2]; st["d"] += 1; return t

    def loadA(w, c0, c1, dst=None, off=0):
        t = dst if dst is not None else slotA()
        P.dma("pool", t[:, :, off:off + (c1 - c0)], w.rearrange("(kc p) n -> p kc n", p=128)[:, :, c0:c1], writes=[t])
        return t

    def mm(ps_ap, lhs_fn, rhs_fn, nk, reads, ps):
        for kc in range(nk):
            l_ = lhs_fn(kc); r_ = rhs_fn(kc)
            op("pe", lambda e, kc=kc, l_=l_, r_=r_: e.matmul(ps_ap, l_, r_, start=(kc == 0), stop=(kc == nk - 1)),
               reads=reads, writes=[ps])

    def rstd_from_ps(ps, rows, n, scale, tag):
        sd = C.tmp("sd" + tag, [128, 512])
        rp = C.tmp("rp" + tag, [128, 512])
        rd = [ps] + ([cst] if not isinstance(scale, float) else [])
        op("act", lambda e: e.activation(sd[0:rows, 0:n], ps[0:rows, 0:n], AF.Sqrt, bias=EPS, scale=scale), reads=rd, writes=[sd])
        op("dve", lambda e: e.reciprocal(rp[0:rows, 0:n], sd[0:rows, 0:n]), reads=[sd], writes=[rp])
        return rp

    def norm(s, gcol, n=TS, src=None, dst=None):
        src = src or h[s]; dst = dst or hn[s]
        ps = C.psum()
        for c in range(8):
            sq = C.tmp("sq", [128, 512], BF16, 3)
            op("act", lambda e, c=c, sq=sq: e.activation(sq[:, 0:n], src[c][:, 0:n], AF.Square), reads=[src[c]], writes=[sq])
            op("pe", lambda e, c=c, sq=sq: e.matmul(ps[:, 0:n], ones_bf[:, :], sq[:, 0:n], start=(c == 0), stop=(c == 7)), reads=[sq, ones_bf], writes=[ps])
        rp = rstd_from_ps(ps, 128, n, 1.0 / 1024.0, "n")
        for c in range(8):
            op("dve", lambda e, c=c: e.scalar_tensor_tensor(dst[c][:, 0:n], src[c][:, 0:n], gains[:, gcol + c:gcol + c + 1], rp[:, 0:n], op0=ALU.mult, op1=ALU.mult),
               reads=[src[c], gains, rp], writes=[dst[c]])

    def resid_proj(w, src):
        for blk in range(2):
            slot = loadA(w, blk * 512, blk * 512 + 512)
            for s in range(2):
                for m in range(4):
                    ps = C.psum()
                    mm(ps[:, :], lambda kc, m=m: slot[:, kc, m * 128:(m + 1) * 128], lambda kc, s=s: src[s][kc][:, :], 8, [slot] + src[s], ps)
                    hc = h[s][blk * 4 + m]
                    op("dve", lambda e, ps=ps, hc=hc: e.tensor_tensor(hc[:, :], ps[:, :], hc[:, :], op=ALU.add), reads=[ps, hc], writes=[hc])

    def ffn(gcol):
        for s in range(2):
            norm(s, gcol)
        for j in range(11):
            slot = slotA()
            loadA(D["w_gate"], j * 256, j * 256 + 256, dst=slot, off=0)
            loadA(D["w_up"], j * 256, j * 256 + 256, dst=slot, off=256)
            for s in range(2):
                for jj in range(2):
                    pg = C.psum(); pu = C.psum()
                    mm(pg[:, :], lambda kc, jj=jj: slot[:, kc, jj * 128:(jj + 1) * 128], lambda kc, s=s: hn[s][kc][:, :], 8, [slot] + hn[s], pg)
                    mm(pu[:, :], lambda kc, jj=jj: slot[:, kc, 256 + jj * 128:256 + (jj + 1) * 128], lambda kc, s=s: hn[s][kc][:, :], 8, [slot] + hn[s], pu)
                    sg = C.tmp("sg", [128, 512], F32, 3)
                    op("act", lambda e, pg=pg, sg=sg: e.activation(sg[:, :], pg[:, :], AF.Silu), reads=[pg], writes=[sg])
                    a = act[s][2 * j + jj]
                    op("dve", lambda e, pu=pu, sg=sg, a=a: e.tensor_tensor(a[:, :], pu[:, :], sg[:, :], op=ALU.mult), reads=[pu, sg], writes=[a])
        for mb in range(8):
            slot = slotD()
            P.dma("pool", slot[:, :, :], D["w_down"].rearrange("(kc p) n -> p kc n", p=128)[:, :, mb * 128:(mb + 1) * 128], writes=[slot])
            for s in range(2):
                ps = C.psum()
                mm(ps[:, :], lambda kc: slot[:, kc, :], lambda kc, s=s: act[s][kc][:, :], 22, [slot] + act[s], ps)
                hc = h[s][mb]
                op("dve", lambda e, ps=ps, hc=hc: e.tensor_tensor(hc[:, :], ps[:, :], hc[:, :], op=ALU.add), reads=[ps, hc], writes=[hc])

    def ple(gcol, tok0):
        pt = []
        for s in range(2):
            norm(s, gcol)
            t = C.tmp("pt", [128, 2, TS], BF16, 2)
            P.dma("pool", t[:, :, :], D["pT"].rearrange("(kc p) n -> p kc n", p=128)[:, :, tok0 + s * TS:tok0 + (s + 1) * TS], writes=[t])
            pt.append(t)
        for blk in range(2):
            slot = loadA(D["w_ple_gate"], blk * 512, blk * 512 + 512)
            for s in range(2):
                for m in range(4):
                    mg = blk * 4 + m
                    pg = C.psum(); pp = C.psum()
                    mm(pg[:, :], lambda kc, m=m: slot[:, kc, m * 128:(m + 1) * 128], lambda kc, s=s: hn[s][kc][:, :], 8, [slot] + hn[s], pg)
                    mm(pp[:, :], lambda kc, mg=mg: wpp[:, kc, mg * 128:(mg + 1) * 128], lambda kc, s=s: pt[s][:, kc, :], 2, [wpp, pt[s]], pp)
                    sg = C.tmp("sg", [128, 512], F32, 3)
                    op("act", lambda e, pg=pg, sg=sg: e.activation(sg[:, :], pg[:, :], AF.Sigmoid), reads=[pg], writes=[sg])
                    t2 = C.tmp("t2", [128, 512], F32, 3)
                    op("dve", lambda e, pp=pp, sg=sg, t2=t2: e.tensor_tensor(t2[:, :], pp[:, :], sg[:, :], op=ALU.mult), reads=[pp, sg], writes=[t2])
                    hc = h[s][mg]
                    op("dve", lambda e, t2=t2, hc=hc: e.tensor_tensor(hc[:, :], t2[:, :], hc[:, :], op=ALU.add), reads=[t2, hc], writes=[hc])

    if mode == "A":
        wconv = P.sb("wconv", [128, 12]); gvb = P.sb("gvb", [128, 512]); bsb = P.sb("bsb", [128, 4, 128])
        wsT = P.sb("wsT", [128, 8, 128], BF16); maskb = P.sb("maskb", [128, 128], BF16)
        g_lat = P.sb("g_lat", [128, 5]); gsm = P.sb("gsm", [128, 8])
        b96 = P.sb("b96", [96, 96], BF16); bd64 = P.sb("bd64", [128, 128], BF16)
        for t, nm in ((wconv, "wconv"), (gvb, "gvb"), (g_lat, "g_lat"), (gsm, "gsm")):
            P.dma("sp", t[:, :], D[nm], writes=[t])
        P.dma("sp", bsb[:, :, :], D["bsb"], writes=[bsb])
        P.dma("pool", wsT[:, :, :], D["wsT"], writes=[wsT])
        op("dve", lambda e: e.tensor_copy(maskb[:, :], cst[:, 0:128]), reads=[cst], writes=[maskb])
        for hh in range(8):
            op("dve", lambda e, hh=hh: e.tensor_tensor(wsT[:, hh, :], wsT[:, hh, :], maskb[:, :], op=ALU.mult), reads=[wsT, maskb], writes=[wsT])
        op("dve", lambda e: e.tensor_copy(b96[:, :], cst[0:96, 128:224]), reads=[cst], writes=[b96])
        op("dve", lambda e: e.tensor_copy(bd64[:, :], cst[:, 224:352]), reads=[cst], writes=[bd64])
        hal = [P.sb("hal%d" % cc, [128, 2]) for cc in range(4)]
        gu = [act[s][8:12] for s in range(2)]
        hh_t = [P.sb("hh%d" % c, [128, 2]) for c in range(8)]
        hhn = [P.sb("hhn%d" % c, [128, 2], BF16) for c in range(8)]

    def even_mixer(sti, tok0):
        for s in range(2):
            norm(s, 0)
        if sti == 0:
            for c in range(8):
                P.dma("sp", hh_t[c][:, :], D["xhT"][c * 128:(c + 1) * 128, :], writes=[hh_t[c]])
            norm(0, 0, n=2, src=hh_t, dst=hhn)
        for cc in range(4):
            slot = loadA(D["ev_w_in"], cc * 384, cc * 384 + 384)
            for s in range(2):
                z = C.tmp("zt", [128, TS + 2], F32, 2)
                if not (s == 0 and sti == 0):
                    op("dve", lambda e, cc=cc, z=z: e.tensor_copy(z[:, 0:2], hal[cc][:, :]), reads=[hal[cc]], writes=[z])
                else:
                    pc = C.psum(); px = C.psum()
                    mm(pc[:, 0:2], lambda kc: slot[:, kc, 128:256], lambda kc: hhn[kc][:, :], 8, [slot] + hhn, pc)
                    mm(px[:, 0:2], lambda kc: slot[:, kc, 256:384], lambda kc: hhn[kc][:, :], 8, [slot] + hhn, px)
                    cs = C.tmp("cs", [128, 512], F32, 2)
                    op("act", lambda e, pc=pc, cs=cs: e.activation(cs[:, 0:2], pc[:, 0:2], AF.Copy), reads=[pc], writes=[cs])
                    op("dve", lambda e, px=px, cs=cs, z=z: e.tensor_tensor(z[:, 0:2], px[:, 0:2], cs[:, 0:2], op=ALU.mult), reads=[px, cs], writes=[z])
                pb = C.psum(); pc = C.psum(); px = C.psum()
                for pp_, c0 in ((pc, 128), (px, 256), (pb, 0)):
                    mm(pp_[:, :], lambda kc, c0=c0: slot[:, kc, c0:c0 + 128], lambda kc, s=s: hn[s][kc][:, :], 8, [slot] + hn[s], pp_)
                cs = C.tmp("cs", [128, 512], F32, 2)
                op("act", lambda e, pc=pc, cs=cs: e.activation(cs[:, :], pc[:, :], AF.Copy), reads=[pc], writes=[cs])
                op("dve", lambda e, px=px, cs=cs, z=z: e.tensor_tensor(z[:, 2:TS + 2], px[:, :], cs[:, :], op=ALU.mult), reads=[px, cs], writes=[z])
                op("dve", lambda e, cc=cc, z=z: e.tensor_copy(hal[cc][:, :], z[:, TS:TS + 2]), reads=[z], writes=[hal[cc]])
                acc = C.tmp("acc", [128, 512], F32, 2)
                op("dve", lambda e, z=z, acc=acc, cc=cc: e.tensor_scalar(acc[:, :], z[:, 0:TS], wconv[:, cc * 3:cc * 3 + 1], None, op0=ALU.mult), reads=[z, wconv], writes=[acc])
                op("dve", lambda e, z=z, acc=acc, cc=cc: e.scalar_tensor_tensor(acc[:, :], z[:, 1:TS + 1], wconv[:, cc * 3 + 1:cc * 3 + 2], acc[:, :], op0=ALU.mult, op1=ALU.add), reads=[z, wconv, acc], writes=[acc])
                op("dve", lambda e, z=z, acc=acc, cc=cc: e.scalar_tensor_tensor(acc[:, :], z[:, 2:TS + 2], wconv[:, cc * 3 + 2:cc * 3 + 3], acc[:, :], op0=ALU.mult, op1=ALU.add), reads=[z, wconv, acc], writes=[acc])
                yt = y[s][cc]
                op("dve", lambda e, pb=pb, acc=acc, yt=yt: e.tensor_tensor(yt[:, :], pb[:, :], acc[:, :], op=ALU.mult), reads=[pb, acc], writes=[yt])
        slot = loadA(D["ev_w_in"], 1536, 2048)
        for s in range(2):
            for uc in range(4):
                pu = C.psum()
                mm(pu[:, :], lambda kc, uc=uc: slot[:, kc, uc * 128:(uc + 1) * 128], lambda kc, s=s: hn[s][kc][:, :], 8, [slot] + hn[s], pu)
                g_ = gu[s][uc]
                op("act", lambda e, pu=pu, g_=g_: e.activation(g_[:, :], pu[:, :], AF.Gelu), reads=[pu], writes=[g_])
        slot = loadA(D["ev_w_in"], 2048, 2560)
        for s in range(2):
            pm = [C.psum(pin=True) for _ in range(4)]
            for tb in range(4):
                pv = C.psum()
                mm(pv[:, :], lambda kc, s=s, tb=tb: hn[s][kc][:, tb * 128:(tb + 1) * 128], lambda kc: slot[:, kc, :], 8, [slot] + hn[s], pv)
                gv = C.tmp("gv", [128, 512], F32, 2)
                op("act", lambda e, pv=pv, gv=gv: e.activation(gv[:, :], pv[:, :], AF.Gelu), reads=[pv], writes=[gv])
                sqv = C.tmp("sqv", [128, 512], F32, 2)
                op("act", lambda e, gv=gv, sqv=sqv: e.activation(sqv[:, :], gv[:, :], AF.Square), reads=[gv], writes=[sqv])
                ss = C.tmp("ss", [128, 8], F32, 2); sd = C.tmp("ssd", [128, 8], F32, 2); rs = C.tmp("srs", [128, 8], F32, 2)
                op("dve", lambda e, sqv=sqv, ss=ss: e.tensor_reduce(ss[:, :], sqv[:, :].rearrange("p (h d) -> p h d", d=64), axis=AX.X, op=ALU.add), reads=[sqv], writes=[ss])
                op("act", lambda e, ss=ss, sd=sd: e.activation(sd[:, :], ss[:, :], AF.Sqrt, bias=EPS, scale=1.0 / 64.0), reads=[ss], writes=[sd])
                op("dve", lambda e, sd=sd, rs=rs: e.reciprocal(rs[:, :], sd[:, :]), reads=[sd], writes=[rs])
                op("dve", lambda e, gv=gv, rs=rs: e.tensor_tensor(gv[:, :].rearrange("p (h d) -> p h d", d=64), gv[:, :].rearrange("p (h d) -> p h d", d=64),
                                                                 rs[:, :].unsqueeze(2).to_broadcast([128, 8, 64]), op=ALU.mult), reads=[gv, rs], writes=[gv])
                vn = C.tmp("vn", [128, 512], BF16, 2)
                op("dve", lambda e, gv=gv, vn=vn: e.tensor_tensor(vn[:, :], gv[:, :], gvb[:, :], op=ALU.mult), reads=[gv, gvb], writes=[vn])
                for hd in range(8):
                    op("pe", lambda e, hd=hd, tb=tb, vn=vn, pm=pm: e.matmul(pm[hd // 2][64 * (hd % 2):64 * (hd % 2) + 64, tb * 128:(tb + 1) * 128], vn[:, hd * 64:(hd + 1) * 64], wsT[:, hd, :], start=True, stop=True),
                       reads=[vn, wsT], writes=[pm[hd // 2]])
            for hc in range(4):
                t1 = C.tmp("t2", [128, 512], F32, 3)
                op("dve", lambda e, hc=hc, t1=t1, pm=pm: e.tensor_tensor(t1[:, :].rearrange("p (b t) -> p b t", t=128), pm[hc][:, :].rearrange("p (b t) -> p b t", t=128),
                                                                bsb[:, hc, :].unsqueeze(1).to_broadcast([128, 4, 128]), op=ALU.add), reads=[pm[hc], bsb], writes=[t1])
                yt = y[s][4 + hc]
                op("dve", lambda e, t1=t1, yt=yt, s=s, hc=hc: e.tensor_tensor(yt[:, :], t1[:, :], gu[s][hc][:, :], op=ALU.mult), reads=[t1, gu[s][hc]], writes=[yt])
            C.unpin()
        resid_proj(D["w_out"], y)

    def O(nm, r0, r1, t0):
        if fz and (nm + "_h0") in D:
            hf = t0 // 1024
            return D[nm + "_h%d" % hf][r0:r1, (t0 % 1024):(t0 % 1024) + TS]
        return D[nm][r0:r1, t0:t0 + TS]

    def odd_front(tok0):
        W = D["od_w_in"]
        for s in range(2):
            norm(s, 24)
        tabs = []
        for s in range(2):
            pi_ = C.tmp("ti", [96, TS], I32, 1)
            P.dma("sp", pi_[:, :], D["posb"][:, tok0 + s * TS:tok0 + (s + 1) * TS], writes=[pi_])
            ang = C.tmp("angp", [96, TS], F32, 1, own=True)
            op("dve", lambda e, pi_=pi_, ang=ang: e.tensor_copy(ang[:, :], pi_[:, :]), reads=[pi_], writes=[ang])
            op("dve", lambda e, ang=ang: e.tensor_scalar(ang[:, :], ang[:, :], cst[0:96, 353:354], None, op0=ALU.mult), reads=[ang, cst], writes=[ang])
            pair = []
            for nm, shift in (("cos", PI / 2.0), ("sin", 0.0)):
                a2 = C.tmp("a2", [96, TS], F32, 1); tf = C.tmp("tf", [96, TS], F32, 1); ti = C.tmp("ti", [96, TS], I32, 1)
                tab = C.tmp("tab" + nm, [96, TS], F32, 2, own=True)
                op("dve", lambda e, a2=a2, ang=ang, shift=shift: e.tensor_scalar(a2[:, :], ang[:, :], shift, None, op0=ALU.add), reads=[ang], writes=[a2])
                op("dve", lambda e, a2=a2, tf=tf: e.tensor_scalar(tf[:, :], a2[:, :], 1.0 / TWO_PI, None, op0=ALU.mult), reads=[a2], writes=[tf])
                op("dve", lambda e, tf=tf, ti=ti: e.tensor_copy(ti[:, :], tf[:, :]), reads=[tf], writes=[ti])
                op("dve", lambda e, tf=tf, ti=ti: e.tensor_copy(tf[:, :], ti[:, :]), reads=[ti], writes=[tf])
                op("dve", lambda e, tf=tf, a2=a2: e.scalar_tensor_tensor(a2[:, :], tf[:, :], -TWO_PI, a2[:, :], op0=ALU.mult, op1=ALU.add), reads=[tf, a2], writes=[a2])
                op("dve", lambda e, a2=a2: e.tensor_scalar(a2[:, :], a2[:, :], -PI, PI, op0=ALU.max, op1=ALU.min), reads=[a2], writes=[a2])
                if nm == "sin":
                    op("act", lambda e, a2=a2, tab=tab: e.activation(tab[:, :], a2[:, :], AF.Sin, scale=cst[0:96, 354:355]), reads=[a2, cst], writes=[tab])
                else:
                    op("act", lambda e, a2=a2, tab=tab: e.activation(tab[:, :], a2[:, :], AF.Sin), reads=[a2], writes=[tab])
                pair.append(tab)
            tabs.append(pair)

        def fm_out(ps, rows, dram_ap, scale=1.0, tag="fo"):
            ob = C.tmp(tag, [128, TS], BF16, 3)
            op("act", lambda e: e.mul(ob[0:rows, :], ps[0:rows, :], float(scale)), reads=[ps], writes=[ob])
            P.dma("sp", dram_ap, ob[0:rows, :], reads=[ob])

        def tok_out(ps, ncols, dram_fn, stage, tb, col0, func=AF.Copy):
            op("act", lambda e: e.activation(stage[:, tb, col0:col0 + ncols], ps[:, 0:ncols], func), reads=[ps], writes=[stage])

        def lat_norm(raws, nch, gc0, D_, outs, tag):
            ps = C.psum()
            for c in range(nch):
                sq = C.tmp("sq", [128, 512], BF16, 3)
                op("act", lambda e, c=c, sq=sq: e.activation(sq[:, :], raws[c][:, :], AF.Square), reads=[raws[c]], writes=[sq])
                op("pe", lambda e, c=c, sq=sq: e.matmul(ps[:, :], ones_bf[:, :], sq[:, :], start=(c == 0), stop=(c == nch - 1)), reads=[sq, ones_bf], writes=[ps])
            rp = rstd_from_ps(ps, 128, TS, 1.0 / D_, tag)
            for c in range(nch):
                op("dve", lambda e, c=c: e.scalar_tensor_tensor(outs[c][:, :], raws[c][:, :], g_lat[:, gc0 + c:gc0 + c + 1], rp[:, :], op0=ALU.mult, op1=ALU.mult),
                   reads=[raws[c], g_lat, rp], writes=[outs[c]])

        qlr = [act[s][12:15] for s in range(2)]
        kvr = [act[s][15:17] for s in range(2)]
        qln = [act[s][17:20] for s in range(2)]
        kvn = [act[s][20:22] for s in range(2)]

        def raw_copy(ps, rows, dst):
            op("act", lambda e: e.activation(dst[0:rows, :], ps[0:rows, :], AF.Copy), reads=[ps], writes=[dst])

        slot = loadA(W, 0, 512)
        for s in range(2):
            for c in range(4):
                ps = C.psum()
                mm(ps[:, :], lambda kc, c=c: slot[:, kc, c * 128:(c + 1) * 128], lambda kc, s=s: hn[s][kc][:, :], 8, [slot] + hn[s], ps)
                raw_copy(ps, 128, qlr[s][c] if c < 3 else kvr[s][0])
            lat_norm(qlr[s], 3, 0, 384.0, qln[s], "q")
        slot = loadA(W, 512, 960)
        for s in range(2):
            t0 = tok0 + s * TS
            cosT, sinT = tabs[s]
            ps = C.psum()
            mm(ps[:, :], lambda kc: slot[:, kc, 0:128], lambda kc, s=s: hn[s][kc][:, :], 8, [slot] + hn[s], ps)
            raw_copy(ps, 128, kvr[s][1])
            lat_norm(kvr[s], 2, 3, 256.0, kvn[s], "k")
            pk = C.psum(); pks = C.psum()
            mm(pk[0:32, :], lambda kc: slot[:, kc, 128:160], lambda kc, s=s: hn[s][kc][:, :], 8, [slot] + hn[s], pk)
            mm(pks[0:32, :], lambda kc: slot[:, kc, 160:192], lambda kc, s=s: hn[s][kc][:, :], 8, [slot] + hn[s], pks)
            kr = C.tmp("kr", [32, 512], F32)
            raw_copy(pk, 32, kr)
            sq = C.tmp("sq", [128, 512], BF16, 3)
            op("act", lambda e, sq=sq, kr=kr: e.activation(sq[0:32, :], kr[:, :], AF.Square), reads=[kr], writes=[sq])
            pn = C.psum()
            op("pe", lambda e, sq=sq, pn=pn: e.matmul(pn[0:32, :], ones_bf[0:32, 0:32], sq[0:32, :], start=True, stop=True), reads=[sq, ones_bf], writes=[pn])
            rp = rstd_from_ps(pn, 32, TS, 1.0 / 32.0, "kp")
            a = C.tmp("kpa", [32, 512], F32); b_ = C.tmp("kpb", [32, 512], F32)
            op("dve", lambda e, a=a, rp=rp, kr=kr: e.scalar_tensor_tensor(a[:, :], kr[:, :], gsm[0:32, 4:5], rp[0:32, :], op0=ALU.mult, op1=ALU.mult), reads=[kr, gsm, rp], writes=[a])
            op("dve", lambda e, b_=b_, rp=rp, pks=pks: e.scalar_tensor_tensor(b_[:, :], pks[0:32, :], gsm[0:32, 5:6], rp[0:32, :], op0=ALU.mult, op1=ALU.mult), reads=[pks, gsm, rp], writes=[b_])
            op("dve", lambda e, a=a, cosT=cosT: e.tensor_tensor(a[:, :], a[:, :], cosT[0:32, :], op=ALU.mult), reads=[a, cosT], writes=[a])
            op("dve", lambda e, b_=b_, sinT=sinT: e.tensor_tensor(b_[:, :], b_[:, :], sinT[0:32, :], op=ALU.mult), reads=[b_, sinT], writes=[b_])
            ob = C.tmp("fo", [128, TS], BF16, 3)
            op("dve", lambda e, a=a, b_=b_, ob=ob: e.tensor_tensor(ob[0:32, :], a[:, :], b_[:, :], op=ALU.add), reads=[a, b_], writes=[ob])
            P.dma("sp", O("kpeT", 0, 32, t0), ob[0:32, :], reads=[ob])
            for c in range(2):
                ps = C.psum()
                mm(ps[:, :], lambda kc, c=c: slot[:, kc, 192 + c * 128:192 + (c + 1) * 128], lambda kc, s=s: hn[s][kc][:, :], 8, [slot] + hn[s], ps)
                fm_out(ps, 128, O("mqT", c * 128, (c + 1) * 128, t0), scale=0.125)
        slot = loadA(W, 960, 1224)
        for s in range(2):
            t0 = tok0 + s * TS
            for c in range(2):
                ps = C.psum()
                mm(ps[:, :], lambda kc, c=c: slot[:, kc, c * 128:(c + 1) * 128], lambda kc, s=s: hn[s][kc][:, :], 8, [slot] + hn[s], ps)
                fm_out(ps, 128, O("mkT", c * 128, (c + 1) * 128, t0))
            ps = C.psum()
            mm(ps[0:8, :], lambda kc: slot[:, kc, 256:264], lambda kc, s=s: hn[s][kc][:, :], 8, [slot] + hn[s], ps)
            mo_ = C.tmp("mif", [8, 512], F32)
            raw_copy(ps, 8, mo_)
            P.dma("sp", D["mifT"][:, t0:t0 + TS], mo_[:, :], reads=[mo_])
        for gi_, (c0, nm) in enumerate(((1224, "mvT"), (1736, "soT"))):
            slot = loadA(W, c0, c0 + 512)
            for s in range(2):
                t0 = tok0 + s * TS
                for c in range(4):
                    ps = C.psum()
                    mm(ps[:, :], lambda kc, c=c: slot[:, kc, c * 128:(c + 1) * 128], lambda kc, s=s: hn[s][kc][:, :], 8, [slot] + hn[s], ps)
                    ob = C.tmp("fo", [128, TS], BF16, 3)
                    if gi_ == 0 and c % 2 == 0:
                        op("dve", lambda e, ps=ps, ob=ob: e.tensor_copy(ob[:, :], ps[:, :]), reads=[ps], writes=[ob])
                    else:
                        fn_ = AF.Copy if gi_ == 0 else AF.Sigmoid
                        op("act", lambda e, ps=ps, ob=ob, fn_=fn_: e.activation(ob[:, :], ps[:, :], fn_), reads=[ps], writes=[ob])
                    P.dma("sp", O(nm, c * 128, (c + 1) * 128, t0), ob[:, :], reads=[ob])
        def sview(slot, nk, n):
            return slot.h[:, :, :].rearrange("p k n -> p (k n)")[:, 0:nk * n].rearrange("p (k n) -> p k n", n=n)
        wq_t = slotA(); wqs_t = slotA(); wkv_t = slotA()
        wq = TV(wq_t, sview(wq_t, 3, 768)); wqs = TV(wqs_t, sview(wqs_t, 3, 768)); wkv = TV(wkv_t, sview(wkv_t, 2, 1024))
        P.dma("pool", wq[:, :, :], D["w_q_up"].rearrange("(kc p) n -> p kc n", p=128), writes=[wq])
        P.dma("pool", wqs[:, :, :], D["w_q_up_sw"].rearrange("(kc p) n -> p kc n", p=128), writes=[wqs])
        P.dma("pool", wkv[:, :, :], D["w_kv_up"].rearrange("(kc p) n -> p kc n", p=128), writes=[wkv])
        for hd in range(8):
            for s in range(2):
                t0 = tok0 + s * TS
                cosT, sinT = tabs[s]
                pq = C.psum(); pqs = C.psum()
                mm(pq[0:96, :], lambda kc, hd=hd: wq[:, kc, hd * 96:(hd + 1) * 96], lambda kc, s=s: qln[s][kc][:, :], 3, [wq] + qln[s], pq)
                mm(pqs[0:96, :], lambda kc, hd=hd: wqs[:, kc, hd * 96:(hd + 1) * 96], lambda kc, s=s: qln[s][kc][:, :], 3, [wqs] + qln[s], pqs)
                qr = C.tmp("qr", [96, 512], F32)
                raw_copy(pq, 96, qr)
                sq = C.tmp("sq", [128, 512], BF16, 3)
                op("act", lambda e, sq=sq, qr=qr: e.activation(sq[0:96, :], qr[:, :], AF.Square), reads=[qr], writes=[sq])
                pn = C.psum()
                op("pe", lambda e, sq=sq, pn=pn: e.matmul(pn[0:96, :], b96[:, :], sq[0:96, :], start=True, stop=True), reads=[sq, b96], writes=[pn])
                rp = rstd_from_ps(pn, 96, TS, cst[0:96, 352:353], "qh")
                qn = C.tmp("qn", [96, 512], F32); sw = C.tmp("qsw", [96, 512], F32)
                op("dve", lambda e, qn=qn, qr=qr, rp=rp: e.scalar_tensor_tensor(qn[:, :], qr[:, :], gsm[0:96, 0:1], rp[0:96, :], op0=ALU.mult, op1=ALU.mult), reads=[qr, gsm, rp], writes=[qn])
                op("dve", lambda e, sw=sw, pqs=pqs, rp=rp: e.scalar_tensor_tensor(sw[64:96, :], pqs[64:96, :], gsm[64:96, 1:2], rp[64:96, :], op0=ALU.mult, op1=ALU.mult), reads=[pqs, gsm, rp], writes=[sw])
                op("dve", lambda e, qn=qn, cosT=cosT: e.tensor_tensor(qn[64:96, :], qn[64:96, :], cosT[64:96, :], op=ALU.mult), reads=[qn, cosT], writes=[qn])
                op("dve", lambda e, sw=sw, sinT=sinT: e.tensor_tensor(sw[64:96, :], sw[64:96, :], sinT[64:96, :], op=ALU.mult), reads=[sw, sinT], writes=[sw])
                op("dve", lambda e, qn=qn, sw=sw: e.tensor_tensor(qn[64:96, :], qn[64:96, :], sw[64:96, :], op=ALU.add), reads=[qn, sw], writes=[qn])
                ob = C.tmp("fo", [128, TS], BF16, 3)
                op("act", lambda e, ob=ob, qn=qn: e.mul(ob[0:96, :], qn[:, :], 96.0 ** -0.5), reads=[qn], writes=[ob])
                P.dma("sp", (O("qTf", hd * 96, (hd + 1) * 96, t0) if fz else D["qT"][hd, :, t0:t0 + TS]), ob[0:96, :], reads=[ob])
        for s in range(2):
            t0 = tok0 + s * TS
            for hp in range(4):
                pk = C.psum()
                mm(pk[:, :], lambda kc, hp=hp: wkv[:, kc, hp * 128:(hp + 1) * 128], lambda kc, s=s: kvn[s][kc][:, :], 2, [wkv] + kvn[s], pk)
                kr = C.tmp("kr", [128, 512], F32)
                raw_copy(pk, 128, kr)
                sq = C.tmp("sq", [128, 512], BF16, 3)
                op("act", lambda e, sq=sq, kr=kr: e.activation(sq[:, :], kr[:, :], AF.Square), reads=[kr], writes=[sq])
                pn = C.psum()
                op("pe", lambda e, sq=sq, pn=pn: e.matmul(pn[:, :], bd64[:, :], sq[:, :], start=True, stop=True), reads=[sq, bd64], writes=[pn])
                rp = rstd_from_ps(pn, 128, TS, 1.0 / 64.0, "kh")
                ob = C.tmp("fo", [128, TS], BF16, 3)
                op("dve", lambda e, ob=ob, kr=kr, rp=rp: e.scalar_tensor_tensor(ob[:, :], kr[:, :], gsm[:, 2:3], rp[:, :], op0=ALU.mult, op1=ALU.mult), reads=[kr, gsm, rp], writes=[ob])
                P.dma("sp", O("knT", hp * 128, (hp + 1) * 128, t0), ob[:, :], reads=[ob])
            for hp in range(4):
                pv = C.psum()
                mm(pv[:, :], lambda kc, hp=hp: wkv[:, kc, 512 + hp * 128:512 + (hp + 1) * 128], lambda kc, s=s: kvn[s][kc][:, :], 2, [wkv] + kvn[s], pv)
                ob = C.tmp("fo", [128, TS], BF16, 3)
                op("act", lambda e, pv=pv, ob=ob: e.activation(ob[:, :], pv[:, :], AF.Copy), reads=[pv], writes=[ob])
                P.dma("sp", O("vT", hp * 128, (hp + 1) * 128, t0), ob[:, :], reads=[ob])

    for sti in range(NTOK // (2 * TS)):
        tok0 = sti * 2 * TS
        for s in range(2):
            for c in range(8):
                P.dma("sp", h[s][c][:, :], D["hT"][c * 128:(c + 1) * 128, tok0 + s * TS:tok0 + (s + 1) * TS], writes=[h[s][c]])
        if mode == "A":
            even_mixer(sti, tok0)
            ffn(8)
            ple(16, tok0)
            for s in range(2):
                for c in range(8):
                    P.dma("sp", D["h1T"][c * 128:(c + 1) * 128, tok0 + s * TS:tok0 + (s + 1) * TS], h[s][c][:, :], reads=[h[s][c]])
            odd_front(tok0)
            if fz and sti == 0 and "mid_hook" in fz:
                fz["mid_hook"](P)
        else:
            if fz:
                ysrc = [[TV(ystg[c], ystg[c].h[:, tok0 + s * TS:tok0 + (s + 1) * TS]) for c in range(8)] for s in range(2)]
            else:
                ysrc = y
                for s in range(2):
                    for c in range(8):
                        P.dma("pool", y[s][c][:, :], D["ymT"][c * 128:(c + 1) * 128, tok0 + s * TS:tok0 + (s + 1) * TS], writes=[y[s][c]])
            resid_proj(D["w_out"], ysrc)
            ffn(8)
            ple(16, tok0)
            for s in range(2):
                for c in range(8):
                    P.dma("sp", D["outT"][c * 128:(c + 1) * 128, tok0 + s * TS:tok0 + (s + 1) * TS], h[s][c][:, :], reads=[h[s][c]])
    P.wait_all("sp", P.tiles)
    if fz:
        return P
    P.emit()
    return nc


def _c(a):
    return np.ascontiguousarray(a)


def _chunkT(v):
    return _c(v.reshape(-1, 128).T)


def _consts():
    c = np.zeros((128, 512), np.float32)
    s = np.arange(128)
    c[:, 0:128] = (s[:, None] <= s[None, :]).astype(np.float32)
    k = np.arange(96)
    c[0:96, 128:224] = ((k[:, None] < 64) == (k[None, :] < 64)).astype(np.float32)
    c[:, 224:352] = ((s[:, None] // 64) == (s[None, :] // 64)).astype(np.float32)
    c[0:64, 352] = 1.0 / 64.0
    c[64:96, 352] = 1.0 / 32.0
    inv_freq = (10000.0 ** (-np.arange(0, 32, 2, dtype=np.float32) / np.float32(32))).astype(np.float32)
    c[0:96, 353] = inv_freq[np.arange(96) % 16]
    c[0:96, 354] = np.where((np.arange(96) % 32) < 16, -1.0, 1.0)
    return c


_SW = (np.arange(32) + 16) % 32


def prep_tok_common(inp, L, core):
    b, q = core // 4, core % 4
    s0 = q * NTOK
    m = {
        "pT": _c(inp["p"][L, b, s0:s0 + NTOK, :].T),
        "consts": _consts(),
        "w_gate": _c(inp["w_gate"][L]), "w_up": _c(inp["w_up"][L]), "w_down": _c(inp["w_down"][L]),
        "w_ple_gate": _c(inp["w_ple_gate"][L]), "w_ple_proj": _c(inp["w_ple_proj"][L]),
    }
    return m


def prep_A(inp):
    maps = []
    x = inp["x"]
    ev_w_in = inp["ev_w_in"][0]
    cols = []
    for cc in range(4):
        for base in (0, 512, 1024):
            cols.append(np.arange(base + cc * 128, base + (cc + 1) * 128))
    cols.append(np.arange(1536, 2560))
    ev_w_in_r = _c(ev_w_in[:, np.concatenate(cols)])
    od_w_in = inp["od_w_in"][0]
    ocols = np.concatenate([np.arange(0, 672), 640 + _SW, np.arange(672, 928), np.arange(928, 1184), np.arange(2208, 2216),
                            np.arange(1184, 1696), np.arange(1696, 2208)])
    od_ext = _c(od_w_in[:, ocols])
    wq = inp["od_w_q_up"][0]
    qcols = np.arange(768).reshape(8, 96).copy()
    qcols[:, 64:96] = qcols[:, 64:96][:, _SW]
    wq_sw = _c(wq[:, qcols.reshape(-1)])
    wkv = inp["od_w_kv_up"][0].reshape(256, 8, 128)
    wkv_r = _c(np.concatenate([wkv[:, :, :64].reshape(256, 512), wkv[:, :, 64:].reshape(256, 512)], axis=1))
    gq, gk = inp["od_g_q"][0], inp["od_g_k"][0]
    gsm = np.zeros((128, 8), np.float32)
    gsm[0:96, 0] = gq
    gsm[64:96, 1] = gq[64:96][_SW]
    gsm[0:64, 2] = gk[:64]; gsm[64:128, 2] = gk[:64]
    gsm[0:32, 4] = gk[64:96]; gsm[0:32, 5] = gk[64:96][_SW]
    g_lat = np.concatenate([_chunkT(inp["od_g_qa"][0]), _chunkT(inp["od_g_kva"][0])], axis=1)
    gains = np.zeros((128, 32), np.float32)
    gains[:, 0:8] = _chunkT(inp["g_mix"][0]); gains[:, 8:16] = _chunkT(inp["g_ffn"][0])
    gains[:, 16:24] = _chunkT(inp["g_ple"][0]); gains[:, 24:32] = _chunkT(inp["g_mix"][1])
    wconv = _c(inp["ev_w_conv"][0].T.reshape(4, 128, 3).transpose(1, 0, 2).reshape(128, 12))
    gvb = _c(np.tile(inp["ev_g_v"][0].reshape(1, 512), (128, 1)))
    wsT = _c(inp["ev_w_s"][0].transpose(2, 0, 1))
    bsb = _c(inp["ev_b_s"][0].reshape(4, 2, 1, 128).repeat(64, axis=2).reshape(4, 128, 128).transpose(1, 0, 2))
    for core in range(8):
        b, q = core // 4, core % 4
        s0 = q * NTOK
        m = prep_tok_common(inp, 0, core)
        m["hT"] = _c(x[b, s0:s0 + NTOK, :].T)
        m["xhT"] = _c(x[b, s0 - 2:s0, :].T) if q > 0 else np.zeros((1024, 2), np.float32)
        m["posb"] = _c(np.tile(inp["positions"][b, s0:s0 + NTOK].reshape(1, NTOK), (96, 1)).astype(np.int32))
        m.update({"gains": gains, "w_out": _c(inp["ev_w_out"][0]), "ev_w_in": ev_w_in_r, "wconv": wconv, "gvb": gvb, "wsT": wsT,
                  "bsb": bsb, "od_w_in": od_ext, "g_lat": _c(g_lat), "w_q_up": _c(wq), "w_q_up_sw": wq_sw, "w_kv_up": wkv_r, "gsm": gsm})
        maps.append(m)
    return maps


def prep_C(inp, h1T, ymT):
    maps = []
    gains = np.zeros((128, 32), np.float32)
    gains[:, 8:16] = _chunkT(inp["g_ffn"][1]); gains[:, 16:24] = _chunkT(inp["g_ple"][1])
    for core in range(8):
        m = prep_tok_common(inp, 1, core)
        m["hT"] = h1T[core]
        m["ymT"] = ymT[core]
        m["gains"] = gains
        m["w_out"] = _c(inp["od_w_out"][0])
        maps.append(m)
    return maps


SEQ = 8192


def build_mix(parts, fz):
    nc = fz["nc"]
    P = Prog(nc, ("M_" if "m" in parts else "T_"))
    C = Ctx(P, nbanks=6)
    op = P.op
    D = fz["D"]
    ptb = [P.ps("ptb%d" % i, [128, 1024], BF16) for i in range(2)]
    idx = P.sb("idx", [128, 64], I32)
    P.dma("sp", idx[:, :], D["idx"], writes=[idx])
    R1 = D["recv1"]

    def gat(t, rows, q, col):
        for hf in range(2):
            c0_ = q * NTOK + hf * 1024
            P.gather(t, t[0:rows, c0_:c0_ + 1024], R1[hf][:, :], idx, idx[0:rows, col:col + 1])
    send2 = D["send2"]

    cB = P.sb("cB", [128, 512])
    bcol = P.sb("bcol", [128, 2]); gmh = P.sb("gmh", [128, 128])
    P.dma("sp", cB[:, :], D["cB"], writes=[cB]); P.dma("sp", bcol[:, :], D["bcol"], writes=[bcol]); P.dma("sp", gmh[:, :], D["gmh"], writes=[gmh])
    maskb = P.sb("maskb", [128, 128], BF16)
    op("dve", lambda e: e.tensor_copy(maskb[:, :], cB[:, 0:128]), reads=[cB], writes=[maskb])
    idb = P.sb("idb", [128, 128], BF16)
    op("dve", lambda e: e.tensor_copy(idb[:, :], cB[:, 256:384]), reads=[cB], writes=[idb])
    ones_f = P.sb("ones_f", [128, 128])
    op("pool", lambda e: e.memset(ones_f[:, :], 1.0), writes=[ones_f])

    if True:
        pass
    if "s" in parts:
        gi = P.sb("gi", [128, 64]); gf = P.sb("gf", [128, 64])
        rmv = D["recvm"].rearrange("r (c t) -> (r c) t", t=64)
        P.gather(gi, gi[:, :], rmv, idx, idx[:, 40:41])
        P.gather(gf, gf[:, :], rmv, idx, idx[:, 41:42])
        zer = P.sb("zer", [128, 128]); op("pool", lambda e: e.memset(zer[:, :], 0.0), writes=[zer])
        nbf = P.sb("nbf", [128, 1])
        op("dve", lambda e: e.tensor_scalar(nbf[:, :], bcol[:, 1:2], -1.0, None, op0=ALU.mult), reads=[bcol], writes=[nbf])
        e1 = P.sb("e1", [128, 64]); lf = P.sb("lf", [128, 64]); Al = P.sb("Al", [128, 64]); vv = P.sb("vv", [128, 64])
        Ml = P.sb("Ml", [128, 64]); Mp = P.sb("Mp", [128, 64])
        op("act", lambda e: e.activation(e1[:, :], gf[:, :], AF.Exp, bias=nbf[:, 0:1], scale=-1.0), reads=[gf, nbf], writes=[e1])
        op("act", lambda e: e.activation(lf[:, :], e1[:, :], AF.Ln, bias=1.0), reads=[e1], writes=[lf])
        op("dve", lambda e: e.tensor_scalar(lf[:, :], lf[:, :], -1.0, None, op0=ALU.mult), reads=[lf], writes=[lf])
        op("dve", lambda e: e.tensor_tensor_scan(Al[:, :], lf[:, :], zer[:, 0:64], 0.0, op0=ALU.add, op1=ALU.add), reads=[lf, zer], writes=[Al])

        def col2row(col_ap, reads):
            ps = C.psum()
            op("pe", lambda e: e.matmul(ps[0:1, 0:128], col_ap, cB[:, 256:384], start=True, stop=True), reads=reads + [cB], writes=[ps])
            return ps

        rows = P.sb("rows", [1, 8, 128])
        ps = col2row(Al[:, 63:64], [Al])
        op("dve", lambda e, ps=ps: e.tensor_copy(rows[:, 0, :], ps[0:1, 0:128]), reads=[ps], writes=[rows])
        op("dve", lambda e: e.tensor_tensor_scan(rows[:, 1, :], rows[:, 0, :], zer[0:1, :], 0.0, op0=ALU.add, op1=ALU.add), reads=[rows, zer], writes=[rows])
        op("dve", lambda e: e.tensor_tensor(rows[:, 2, :], rows[:, 1, :], rows[:, 0, :], op=ALU.subtract), reads=[rows], writes=[rows])
        cols_ps = C.psum(pin=True)

        def row2col(k, j):
            op("pe", lambda e: e.matmul(cols_ps[:, j:j + 1], rows[0:1, k, :], ones_f[0:1, 0:1], start=True, stop=True), reads=[rows, ones_f], writes=[cols_ps])

        row2col(2, 0)
        cols = P.sb("cols", [128, 8])
        op("dve", lambda e: e.tensor_copy(cols[:, 0:1], cols_ps[:, 0:1]), reads=[cols_ps], writes=[cols])
        op("dve", lambda e: e.tensor_scalar(Al[:, :], Al[:, :], cols[:, 0:1], None, op0=ALU.add), reads=[Al, cols], writes=[Al])
        op("dve", lambda e: e.scalar_tensor_tensor(vv[:, :], gi[:, :], bcol[:, 0:1], Al[:, :], op0=ALU.add, op1=ALU.subtract), reads=[gi, bcol, Al], writes=[vv])
        op("dve", lambda e: e.tensor_tensor_scan(Ml[:, :], vv[:, :], vv[:, :], -1e30, op0=ALU.max, op1=ALU.max), reads=[vv], writes=[Ml])
        ps = col2row(Ml[:, 63:64], [Ml])
        op("dve", lambda e, ps=ps: e.tensor_copy(rows[:, 3, :], ps[0:1, 0:128]), reads=[ps], writes=[rows])
        op("dve", lambda e: e.tensor_tensor_scan(rows[:, 4, :], rows[:, 3, :], rows[:, 3, :], 0.0, op0=ALU.max, op1=ALU.max), reads=[rows], writes=[rows])
        op("dve", lambda e: e.memset(rows[:, 5, 0:1], 0.0), reads=[], writes=[rows])
        op("dve", lambda e: e.tensor_copy(rows[:, 5, 1:128], rows[:, 4, 0:127]), reads=[rows], writes=[rows])
        r4 = rows[:, 4, :].rearrange("p (c two) -> p c two", two=2)
        r6 = rows[:, 6, :].rearrange("p (c two) -> p c two", two=2)
        op("dve", lambda e: e.tensor_copy(r6[:, :, 0:1], r4[:, :, 1:2]), reads=[rows], writes=[rows])
        op("dve", lambda e: e.tensor_copy(r6[:, :, 1:2], r4[:, :, 1:2]), reads=[rows], writes=[rows])
        op("dve", lambda e: e.tensor_tensor(rows[:, 7, :], rows[:, 5, :], rows[:, 4, :], op=ALU.subtract), reads=[rows], writes=[rows])
        op("act", lambda e: e.activation(rows[:, 7, :], rows[:, 7, :], AF.Exp), reads=[rows], writes=[rows])
        row2col(5, 1); row2col(4, 2); row2col(6, 3)
        op("dve", lambda e: e.tensor_copy(cols[:, 1:4], cols_ps[:, 1:4]), reads=[cols_ps], writes=[cols])
        C.unpin()
        op("dve", lambda e: e.tensor_scalar(Mp[:, :], Ml[:, :], cols[:, 1:2], 0.0, op0=ALU.max, op1=ALU.max), reads=[Ml, cols], writes=[Mp])
        dec_b = P.sb("dec_b", [64, 128])
        ps = C.psum()
        op("pe", lambda e, ps=ps: e.matmul(ps[0:64, 0:128], ones_f[0:1, 0:64], rows[0:1, 7, :], start=True, stop=True), reads=[ones_f, rows], writes=[ps])
        op("dve", lambda e, ps=ps: e.tensor_copy(dec_b[:, :], ps[0:64, 0:128]), reads=[ps], writes=[dec_b])
        fac = P.sb("fac", [128, 5, 128])
        tmpc = P.sb("tmpc", [128, 64])
        op("dve", lambda e: e.tensor_scalar(tmpc[:, :], vv[:, :], cols[:, 3:4], None, op0=ALU.subtract), reads=[vv, cols], writes=[tmpc])
        op("act", lambda e: e.activation(fac[:, 0, 0:64], tmpc[:, :], AF.Exp), reads=[tmpc], writes=[fac])
        op("dve", lambda e: e.tensor_scalar(tmpc[:, :], Mp[:, :], cols[:, 3:4], None, op0=ALU.subtract), reads=[Mp, cols], writes=[tmpc])
        op("act", lambda e: e.activation(fac[:, 1, 0:64], tmpc[:, :], AF.Exp, scale=-1.0), reads=[tmpc], writes=[fac])
        op("dve", lambda e: e.tensor_scalar(tmpc[:, :], Mp[:, :], cols[:, 1:2], None, op0=ALU.subtract), reads=[Mp, cols], writes=[tmpc])
        op("act", lambda e: e.activation(fac[:, 2, 0:64], tmpc[:, :], AF.Exp, scale=-1.0), reads=[tmpc], writes=[fac])
        op("dve", lambda e: e.tensor_scalar(tmpc[:, :], vv[:, :], cols[:, 2:3], None, op0=ALU.subtract), reads=[vv, cols], writes=[tmpc])
        op("act", lambda e: e.activation(fac[:, 3, 0:64], tmpc[:, :], AF.Exp), reads=[tmpc], writes=[fac])
        op("dve", lambda e: e.tensor_tensor(tmpc[:, :], Al[:, :], Mp[:, :], op=ALU.add), reads=[Al, Mp], writes=[tmpc])
        op("act", lambda e: e.activation(fac[:, 4, 0:64], tmpc[:, :], AF.Exp, scale=-1.0), reads=[tmpc], writes=[fac])
        op("dve", lambda e: e.tensor_copy(fac[:, :, 64:128], fac[:, :, 0:64]), reads=[fac], writes=[fac])
        fcol = P.sb("fcol", [128, 5, 64])
        for k in range(5):
            ps = C.psum()
            op("pe", lambda e, k=k, ps=ps: e.transpose(ps[:, 0:128], fac[:, k, :], cB[:, 256:384]), reads=[fac, cB], writes=[ps])
            pv = ps[:, 0:128].rearrange("p (c two) -> p c two", two=2)
            op("dve", lambda e, k=k, ps=ps: e.tensor_copy(fcol[0:64, k, :], ps[0:64, 0:128].rearrange("p (c two) -> p c two", two=2)[:, :, 0]), reads=[ps], writes=[fcol])
            op("dve", lambda e, k=k, ps=ps: e.tensor_copy(fcol[64:128, k, :], ps[64:128, 0:128].rearrange("p (c two) -> p c two", two=2)[:, :, 1]), reads=[ps], writes=[fcol])

    if "m" in parts:
        mq = P.sb("mq", [64, SEQ], BF16); mk = P.sb("mk", [64, SEQ], BF16)
        ktok = P.sb("ktok", [128, 64, 64], BF16); vext = P.sb("vext", [128, 64, 129], BF16); so = P.sb("so", [128, 64, 128], BF16)
        Call = P.sb("Call", [64, 128, 129], BF16); Sxs = [P.sb("Sx%d" % i, [64, 129]) for i in range(2)]
        mvs = P.sb("mvs", [128, SEQ], BF16); sos = P.sb("sos", [128, SEQ], BF16)
        for q in range(4):
            cs_ = slice(q * NTOK, (q + 1) * NTOK)
            gat(mq, 64, q, 24 + q); gat(mk, 64, q, 28 + q); gat(mvs, 128, q, 32 + q); gat(sos, 128, q, 36 + q)
        op("pool", lambda e: e.memset(vext[:, :, :], 1.0), writes=[vext])
        tcount = [0]
        def tr_group(src, rows, nblk_per, width, dst_fn, g):
            pt_ = ptb[tcount[0] % 2]; tcount[0] += 1
            for b_ in range(nblk_per):
                blk = g * nblk_per + b_
                op("pe", lambda e, pt_=pt_, b_=b_, blk=blk: e.transpose(pt_[:, b_ * width:(b_ + 1) * width], src[0:rows, blk * 128:(blk + 1) * 128], idb[0:rows, 0:rows]),
                   reads=[src, idb], writes=[pt_])
            dst_t, dst_ap = dst_fn(g)
            eng = "act" if tcount[0] % 2 == 0 else "dve"
            if eng == "act":
                op("act", lambda e, pt_=pt_, dst_ap=dst_ap: e.activation(dst_ap, pt_[:, 0:nblk_per * width].rearrange("p (b w) -> p b w", w=width), AF.Copy), reads=[pt_], writes=[dst_t])
            else:
                op("dve", lambda e, pt_=pt_, dst_ap=dst_ap: e.tensor_copy(dst_ap, pt_[:, 0:nblk_per * width].rearrange("p (b w) -> p b w", w=width)), reads=[pt_], writes=[dst_t])
        for g in range(4):
            tr_group(mk, 64, 16, 64, lambda g: (ktok, ktok[:, g * 16:(g + 1) * 16, :]), g)
        for g in range(8):
            tr_group(mvs, 128, 8, 128, lambda g: (vext, vext[:, g * 8:(g + 1) * 8, 0:128]), g)
            tr_group(sos, 128, 8, 128, lambda g: (so, so[:, g * 8:(g + 1) * 8, :]), g)
        op("pool", lambda e: e.memset(Sxs[0][:, :], 0.0), writes=[Sxs[0]])
        for blk in range(64):
            kw = C.tmp("kw", [128, 64], BF16, 3)
            op("dve", lambda e, blk=blk, kw=kw: e.tensor_scalar(kw[:, :], ktok[:, blk, :], fcol[:, 3, blk:blk + 1], None, op0=ALU.mult), reads=[ktok, fcol], writes=[kw])
            for half in range(2):
                c = 2 * blk + half
                Sa = Sxs[c % 2]; Sb = Sxs[(c + 1) % 2]
                op("act", lambda e, c=c, Sa=Sa: e.activation(Call[:, c, :], Sa[:, :], AF.Copy), reads=[Sa], writes=[Call])
                pu = C.psum()
                op("pe", lambda e, pu=pu, kw=kw, half=half, blk=blk: e.matmul(pu[0:64, 0:129], kw[64 * half:64 * half + 64, :], vext[64 * half:64 * half + 64, blk, :], start=True, stop=True),
                   reads=[kw, vext], writes=[pu])
                op("dve", lambda e, pu=pu, c=c, Sa=Sa, Sb=Sb: e.scalar_tensor_tensor(Sb[:, :], Sa[:, :], dec_b[:, c:c + 1], pu[0:64, 0:129], op0=ALU.mult, op1=ALU.add), reads=[Sa, dec_b, pu], writes=[Sb])
        if "2" in parts:
            def stX(blk):
                t0 = blk * 128
                pS = C.psum()
                op("pe", lambda e, pS=pS, t0=t0: e.matmul(pS[:, 0:128], mk[:, t0:t0 + 128], mq[:, t0:t0 + 128], start=True, stop=True), reads=[mk, mq], writes=[pS])
                pt = C.tmp("mpt", [128, 128], BF16, 3)
                op("dve", lambda e, pS=pS, pt=pt, blk=blk: e.scalar_tensor_tensor(pt[:, :], pS[:, 0:128], fcol[:, 0, blk:blk + 1], cB[:, 128:256], op0=ALU.mult, op1=ALU.mult), reads=[pS, fcol, cB], writes=[pt])
                return pt

            def stY(blk, pt):
                t0 = blk * 128
                po1 = C.psum(); po2 = C.psum()
                op("pe", lambda e, po1=po1, pt=pt, blk=blk: e.matmul(po1[:, 0:129], pt[:, :], vext[:, blk, :], start=True, stop=True), reads=[pt, vext], writes=[po1])
                for half in range(2):
                    op("pe", lambda e, po2=po2, half=half, blk=blk, t0=t0: e.matmul(po2[64 * half:64 * half + 64, 0:129], mq[:, t0 + 64 * half:t0 + 64 * half + 64], Call[:, 2 * blk + half, :], start=True, stop=True),
                       reads=[mq, Call], writes=[po2])
                o1 = C.tmp("o1", [128, 129], F32, 2); hnum = C.tmp("hnum", [128, 129], F32, 2)
                op("dve", lambda e, o1=o1, po1=po1, blk=blk: e.tensor_scalar(o1[:, :], po1[:, 0:129], fcol[:, 1, blk:blk + 1], None, op0=ALU.mult), reads=[po1, fcol], writes=[o1])
                op("dve", lambda e, o1=o1, po2=po2, hnum=hnum, blk=blk: e.scalar_tensor_tensor(hnum[:, :], po2[:, 0:129], fcol[:, 2, blk:blk + 1], o1[:, :], op0=ALU.mult, op1=ALU.add), reads=[po2, fcol, o1], writes=[hnum])
                sm = C.tmp("sm", [128, 8], F32, 2)
                op("dve", lambda e, sm=sm, hnum=hnum: e.scalar_tensor_tensor(sm[:, 6:7], hnum[:, 128:129], -1.0, hnum[:, 128:129], op0=ALU.mult, op1=ALU.max), reads=[hnum], writes=[sm])
                op("dve", lambda e, sm=sm, blk=blk: e.tensor_tensor(sm[:, 0:1], sm[:, 6:7], fcol[:, 4, blk:blk + 1], op=ALU.max), reads=[sm, fcol], writes=[sm])
                op("dve", lambda e, sm=sm: e.reciprocal(sm[:, 1:2], sm[:, 0:1]), reads=[sm], writes=[sm])
                junk = C.tmp("junk", [128, 128], F32, 2)
                op("dve", lambda e, hnum=hnum, sm=sm: e.tensor_scalar(hnum[:, 0:128], hnum[:, 0:128], sm[:, 1:2], None, op0=ALU.mult), reads=[hnum, sm], writes=[hnum])
                op("act", lambda e, junk=junk, hnum=hnum, sm=sm: e.activation(junk[:, :], hnum[:, 0:128], AF.Square, accum_out=sm[:, 2:3]), reads=[hnum], writes=[junk, sm])
                op("act", lambda e, sm=sm: e.activation(sm[:, 3:4], sm[:, 2:3], AF.Sqrt, bias=EPS, scale=1.0 / 128.0), reads=[sm], writes=[sm])
                op("dve", lambda e, sm=sm: e.reciprocal(sm[:, 4:5], sm[:, 3:4]), reads=[sm], writes=[sm])
                yt = C.tmp("yt", [128, 128], F32, 2); yd = C.tmp("yd", [128, 128], F32, 3)
                op("dve", lambda e, yt=yt, hnum=hnum, sm=sm: e.scalar_tensor_tensor(yt[:, :], hnum[:, 0:128], sm[:, 4:5], gmh[:, :], op0=ALU.mult, op1=ALU.mult), reads=[hnum, sm, gmh], writes=[yt])
                op("pool", lambda e, yt=yt, yd=yd, blk=blk: e.tensor_tensor(yd[:, :], yt[:, :], so[:, blk, :], op=ALU.mult), reads=[yt, so], writes=[yd])
                return yd

            zst = {"pT": None}

            def stZ(blk, yd):
                g4, j4 = blk // 4, blk % 4
                if j4 == 0:
                    zst["pT"] = C.psum(pin=True)
                pT = zst["pT"]
                op("pe", lambda e, pT=pT, yd=yd, j4=j4: e.transpose(pT[:, j4 * 128:(j4 + 1) * 128], yd[:, :], cB[:, 256:384]), reads=[yd, cB], writes=[pT])
                if j4 == 3:
                    ob = C.tmp("ydo", [128, 512], BF16, 2)
                    op("act", lambda e, ob=ob, pT=pT: e.activation(ob[:, :], pT[:, :], AF.Copy), reads=[pT], writes=[ob])
                    P.dma("sp", send2[(g4 // 4) * 256 + 128:(g4 // 4) * 256 + 256, (g4 % 4) * 512:(g4 % 4) * 512 + 512], ob[:, :], reads=[ob])
                    C.unpin()

            pts = {0: stX(0)}
            yds = {}
            for blk in range(64):
                if blk + 1 < 64:
                    pts[blk + 1] = stX(blk + 1)
                yds[blk] = stY(blk, pts.pop(blk))
                if blk >= 1:
                    stZ(blk - 1, yds.pop(blk - 1))
            stZ(63, yds.pop(63))

    if "a" in parts:
        qTs = [P.sb("qT%d" % i, [96, SEQ], BF16) for i in range(2)]; kTs = [P.sb("kT%d" % i, [96, SEQ], BF16) for i in range(2)]
        V = P.sb("V", [128, 64, 65], BF16); vTss = [P.sb("vTs%d" % i, [64, SEQ], BF16) for i in range(2)]
        for hh in range(2):
            for q in range(4):
                cs_ = slice(q * NTOK, (q + 1) * NTOK)
                gat(qTs[hh], 96, q, hh * 4 + q); gat(kTs[hh], 96, q, 8 + hh * 4 + q); gat(vTss[hh], 64, q, 16 + hh * 4 + q)
        tails = []
        for hh in range(2):
            qT = qTs[hh]; kT = kTs[hh]; vTs = vTss[hh]
            op("pool", lambda e: e.memset(V[:, :, :], 1.0), writes=[V])
            for g in range(4):
                pt_ = ptb[g % 2]
                for b_ in range(16):
                    blk = g * 16 + b_
                    op("pe", lambda e, pt_=pt_, b_=b_, blk=blk, vTs=vTs: e.transpose(pt_[:, b_ * 64:(b_ + 1) * 64], vTs[0:64, blk * 128:(blk + 1) * 128], idb[0:64, 0:64]), reads=[vTs, idb], writes=[pt_])
                op("dve", lambda e, pt_=pt_, g=g: e.tensor_copy(V[:, g * 16:(g + 1) * 16, 0:64], pt_[:, :].rearrange("p (b w) -> p b w", w=64)), reads=[pt_], writes=[V])
            for qt in range(16):
                po = C.psum(pin=True)
                nkb = 4 * qt + 4
                LOOK = 3
                pend = []
                for it in range(nkb + LOOK):
                    if it < nkb:
                        kb = it
                        c0 = max(0, kb - 4 * qt) * 128
                        ps = C.psum()
                        op("pe", lambda e, ps=ps, kb=kb, c0=c0, qt=qt, kT=kT, qT=qT: e.matmul(ps[:, c0:512], kT[:, kb * 128:(kb + 1) * 128], qT[:, qt * 512 + c0:qt * 512 + 512], start=True, stop=True), reads=[kT, qT], writes=[ps])
                        pt = C.tmp("apt", [128, 512], BF16, 6)
                        op("act", lambda e, ps=ps, pt=pt, c0=c0: e.activation(pt[:, c0:512], ps[:, c0:512], AF.Exp), reads=[ps], writes=[pt])
                        if kb >= 4 * qt:
                            op("dve", lambda e, pt=pt, c0=c0: e.tensor_tensor(pt[:, c0:c0 + 128], pt[:, c0:c0 + 128], maskb[:, :], op=ALU.mult), reads=[pt, maskb], writes=[pt])
                        pend.append((kb, c0, pt))
                    if it == 2 and tails:
                        tails.pop(0)()
                    if it >= LOOK:
                        kb, c0, pt = pend[it - LOOK]
                        op("pe", lambda e, po=po, pt=pt, kb=kb, c0=c0, nkb=nkb: e.matmul(po[0:65, c0:512], V[:, kb, :], pt[:, c0:512], start=(kb == 0), stop=(kb == nkb - 1)), reads=[V, pt], writes=[po])
                osb = C.tmp("osb", [65, 512], F32, 2, own=True)
                op("act", lambda e, osb=osb, po=po: e.activation(osb[:, :], po[0:65, :], AF.Copy), reads=[po], writes=[osb])
                C.unpin()
                op("dve", lambda e, osb=osb: e.reciprocal(osb[64:65, :], osb[64:65, :]), reads=[osb], writes=[osb])

                def tail(osb=osb, hh=hh, qt=qt):
                    pb = C.psum()
                    op("pe", lambda e, pb=pb, osb=osb: e.matmul(pb[0:64, :], ones_f[64:65, 0:64], osb[64:65, :], start=True, stop=True), reads=[ones_f, osb], writes=[pb])
                    yc = C.tmp("yc", [64, 512], BF16, 2)
                    op("dve", lambda e, yc=yc, osb=osb, pb=pb: e.tensor_tensor(yc[:, :], osb[0:64, :], pb[0:64, :], op=ALU.mult), reads=[osb, pb], writes=[yc])
                    P.dma("sp", send2[(qt // 4) * 256 + hh * 64:(qt // 4) * 256 + hh * 64 + 64, (qt % 4) * 512:(qt % 4) * 512 + 512], yc[:, :], reads=[yc])
                tails.append(tail)
        while tails:
            tails.pop(0)()
    P.wait_all("sp", P.tiles)
    return P


R1ROWS = 3360
Q0, KN0, KP0, V0, MQ0, MK0, MV0, SO0 = 0, 768, 1280, 1312, 1824, 2080, 2336, 2848
GROUPS = [[0, 1, 2, 3], [4, 5, 6, 7]]
CCH = 240
CCH1 = 480


def _chunks(total, cch=CCH):
    out = []
    start = 0
    base = 0
    while start < total:
        size = min(cch, total - start)
        out.append((start, size, base))
        base += 4 * size
        start += size
    return out


def _rowmap(total, r, q, cch=CCH):
    k = r // cch
    start = k * cch
    size = np.minimum(cch, total - start)
    return 4 * start + q * size + (r - start)


def _gather_all(P, send, recv, total, cch=CCH, wait=True):
    keys = []
    for (start, size, base) in _chunks(total, cch):
        keys.append(P.collective_raw("AllGather", send[start:start + size, :], recv[base:base + 4 * size, :], GROUPS, wait=False))
    if wait:
        P.ops["pool"].append(([(k, 1) for k in keys], None, None))
    return keys


def build_fused(phases="AMTC"):
    nc = bass.Bass("TRN2", target_bir_lowering=False)
    send1 = [nc.dram_tensor("send1_%d" % i, [R1ROWS, 1024], BF16).ap() for i in range(2)]
    recv1 = [nc.dram_tensor("recv1_%d" % i, [4 * R1ROWS, 1024], BF16).ap() for i in range(2)]
    sendm = nc.dram_tensor("sendm", [8, NTOK], F32).ap()
    recvm = nc.dram_tensor("recvm", [32, NTOK], F32).ap()
    send2 = nc.dram_tensor("send2", [1024, NTOK], BF16).ap()
    recv2 = nc.dram_tensor("recv2", [4096, NTOK], BF16).ap()
    h1d = nc.dram_tensor("h1d", [1024, NTOK], F32).ap()
    idx = dram_in(nc, "idx", [128, 64], I32)
    consts = dram_in(nc, "consts", [128, 512])

    def tokD(L):
        sfx = str(L)
        return {
            "pT": dram_in(nc, "pT" + sfx, [256, NTOK]), "gains": dram_in(nc, "gains" + sfx, [128, 32]), "consts": consts,
            "w_gate": dram_in(nc, "w_gate" + sfx, [1024, 2816]), "w_up": dram_in(nc, "w_up" + sfx, [1024, 2816]),
            "w_down": dram_in(nc, "w_down" + sfx, [2816, 1024]), "w_ple_gate": dram_in(nc, "w_ple_gate" + sfx, [1024, 1024]),
            "w_ple_proj": dram_in(nc, "w_ple_proj" + sfx, [256, 1024]), "w_out": dram_in(nc, "w_out" + sfx, [1024, 1024]),
        }
    DA = tokD(0)
    DA.update({
        "hT": dram_in(nc, "xT", [1024, NTOK]), "h1T": h1d,
        "mifT": sendm,
    })
    for hf in range(2):
        for nm, r0, nr in (("qTf", Q0, 768), ("knT", KN0, 512), ("kpeT", KP0, 32), ("vT", V0, 512), ("mqT", MQ0, 256), ("mkT", MK0, 256),
                           ("mvT", MV0, 512), ("soT", SO0, 512)):
            DA["%s_h%d" % (nm, hf)] = send1[hf][r0:r0 + nr, :]
    for nm in ("qT", "knT", "kpeT", "vT", "mqT", "mkT", "mvT", "soT"):
        DA[nm] = send1[0]
    keys1 = []

    def mid_hook(P):
        P.wait_all("pool", P.tiles)
        keys1.extend(_gather_all(P, send1[0], recv1[0], R1ROWS, CCH1, wait=False))
    PA = build_tok("A", {"nc": nc, "D": DA, "mid_hook": mid_hook})
    PA.wait_all("pool", PA.tiles)
    keys1.extend(_gather_all(PA, send1[1], recv1[1], R1ROWS, CCH1, wait=False))
    PA.ops["pool"].append(([(k, 1) for k in keys1], None, None))
    PA.collective_raw("AllGather", sendm, recvm, GROUPS)
    PA.emit()
    nc.all_engine_barrier()
    if "M" not in phases and "T" not in phases and "C" not in phases:
        dram_out(nc, "outT", [1024, NTOK])
        return nc

    DB = {"idx": idx, "recv1": recv1, "recvm": recvm, "send2": send2,
          "cB": dram_in(nc, "cB", [128, 512]), "bcol": dram_in(nc, "bcol", [128, 2]), "gmh": dram_in(nc, "gmh", [128, 128])}
    if "M" in phases:
        PM = build_mix("sm2", {"nc": nc, "D": DB})
        PM.wait_all("pool", PM.tiles)
        PM.emit()
        nc.all_engine_barrier()
    if "T" in phases:
        PT = build_mix("a", {"nc": nc, "D": DB})
        _gather_all(PT, send2, recv2, 1024)
        PT.emit()
        nc.all_engine_barrier()
    if "C" not in phases:
        dram_out(nc, "outT", [1024, NTOK])
        return nc

    DC = tokD(1)
    DC.update({"hT": h1d, "idx": idx, "recv2": recv2, "outT": dram_out(nc, "outT", [1024, NTOK])})
    if "D" in phases:
        DC["dbg_y"] = dram_out(nc, "dbg_y", [1024, NTOK], BF16)
    PC = build_tok("C", {"nc": nc, "D": DC})
    PC.wait_all("pool", PC.tiles)
    if "D" in phases:
        d1 = dram_out(nc, "dbg_send1", [R1ROWS, NTOK], BF16); d2 = dram_out(nc, "dbg_send2", [1024, NTOK], BF16); d3 = dram_out(nc, "dbg_sendm", [8, NTOK]); d4 = dram_out(nc, "dbg_h1", [1024, NTOK])
        dummy = PC.sb("dbgdummy", [1, 8])
        PC.dma("sp", d1, send1, reads=[dummy]); PC.dma("sp", d2, send2, reads=[dummy]); PC.dma("sp", d3, sendm, reads=[dummy]); PC.dma("sp", d4, h1d, reads=[dummy])
        PC.wait_all("sp", [dummy])
    PC.emit()
    return nc


def prep_fused(inp):
    mapsA = prep_A(inp)
    maps = []
    cB = np.zeros((128, 512), np.float32)
    s = np.arange(128)
    cB[:, 0:128] = (s[:, None] <= s[None, :])
    cB[:, 128:256] = (s[:, None] <= s[None, :]) & ((s[:, None] // 64) == (s[None, :] // 64))
    cB[:, 256:384] = np.eye(128, dtype=np.float32)
    gains1 = np.zeros((128, 32), np.float32)
    gains1[:, 8:16] = _chunkT(inp["g_ffn"][1]); gains1[:, 16:24] = _chunkT(inp["g_ple"][1])
    bg = inp["od_b_gate"][0]
    p = np.arange(128)
    for core in range(8):
        b, j = core // 4, core % 4
        q_ = j
        a = mapsA[core]
        m = {"xT": a["hT"], "xhT": a["xhT"], "posb": a["posb"], "consts": a["consts"], "idx": None,
             "pT0": a["pT"], "gains0": a["gains"], "w_gate0": a["w_gate"], "w_up0": a["w_up"], "w_down0": a["w_down"],
             "w_ple_gate0": a["w_ple_gate"], "w_ple_proj0": a["w_ple_proj"], "w_out0": a["w_out"]}
        for k in ("ev_w_in", "wconv", "gvb", "wsT", "bsb", "od_w_in", "g_lat", "w_q_up", "w_q_up_sw", "w_kv_up", "gsm"):
            m[k] = a[k]
        c1 = prep_tok_common(inp, 1, core)
        m.update({"pT1": c1["pT"], "gains1": gains1, "w_gate1": c1["w_gate"], "w_up1": c1["w_up"], "w_down1": c1["w_down"],
                  "w_ple_gate1": c1["w_ple_gate"], "w_ple_proj1": c1["w_ple_proj"], "w_out1": _c(inp["od_w_out"][0])})
        m["cB"] = cB
        m["bcol"] = _c(np.tile(np.array([[bg[j], bg[4 + j]]], np.float32), (128, 1)))
        m["gmh"] = _c(np.tile(inp["od_g_mh"][0][j].reshape(1, 128), (128, 1)))
        ix = np.zeros((128, 64), np.int32)
        def rm(r, q):
            return _rowmap(R1ROWS, np.asarray(r), q, CCH1)
        for q in range(4):
            for hh in range(2):
                hd = 2 * j + hh
                ix[0:96, hh * 4 + q] = rm(Q0 + hd * 96 + p[0:96], q)
                ix[0:64, 8 + hh * 4 + q] = rm(KN0 + hd * 64 + p[0:64], q)
                ix[64:96, 8 + hh * 4 + q] = rm(KP0 + p[0:32], q)
                ix[0:64, 16 + hh * 4 + q] = rm(V0 + hd * 64 + p[0:64], q)
            ix[0:64, 24 + q] = rm(MQ0 + j * 64 + p[0:64], q)
            ix[0:64, 28 + q] = rm(MK0 + j * 64 + p[0:64], q)
            ix[:, 32 + q] = rm(MV0 + j * 128 + p, q)
            ix[:, 36 + q] = rm(SO0 + j * 128 + p, q)
            ix[q * 32:(q + 1) * 32, 40] = (q * 8 + j) * 32 + p[0:32]
            ix[q * 32:(q + 1) * 32, 41] = (q * 8 + 4 + j) * 32 + p[0:32]
        for c in range(8):
            if c < 4:
                ix[:, 48 + c] = _rowmap(1024, q_ * 256 + p, c)
            else:
                ix[:, 48 + c] = _rowmap(1024, q_ * 256 + 128 + p, c - 4)
        m["idx"] = ix
        maps.append(m)
    return maps


_NC_CACHE = {}


def kernel(**inputs):
    inp = {k: np.asarray(v) for k, v in inputs.items()}
    if "nc" not in _NC_CACHE:
        _NC_CACHE["nc"] = build_fused()
    res = run_bass_kernel_spmd(_NC_CACHE["nc"], prep_fused(inp), core_ids=list(range(8))).results
    out = np.zeros((2, SEQ, 1024), np.float32)
    for core in range(8):
        b, q = core // 4, core % 4
        out[b, q * NTOK:(q + 1) * NTOK, :] = res[core]["outT"].T
    return out
```
